# Optimizing a Trainium2 kernel written in Bass

```python
import math
import jax, jax.numpy as jnp
from jax import lax
import numpy as np

D_MODEL = 1024
BATCH = 2
SEQ = 8192
DEPTH = 1

EPS = 1e-6
Q_BLOCK = 128
MLA_HEADS = 8
MLA_NOPE_DIM = 64
MLA_ROPE_DIM = 32
MLA_V_DIM = 64
Q_LORA_RANK = 256
KV_LORA_RANK = 128
ROPE_THETA = 10000.0
MLA_QK_DIM = MLA_NOPE_DIM + MLA_ROPE_DIM
SB_HEADS = 8
SB_HEAD_DIM = 64
MLA_WIDTH = MLA_HEADS * MLA_V_DIM
SB_WIDTH = SB_HEADS * SB_HEAD_DIM
MIX_WIDTH = MLA_WIDTH + SB_WIDTH
IN_SPLITS = (Q_LORA_RANK, KV_LORA_RANK, MLA_ROPE_DIM, SB_WIDTH, SB_WIDTH, SB_WIDTH)
IN_PROJ_WIDTH = sum(IN_SPLITS)
IN_SPLIT_POINTS = tuple(int(v) for v in np.cumsum(IN_SPLITS)[:-1])
D_FF = ((8 * D_MODEL + 3 * 256 - 1) // (3 * 256)) * 256

kernel_name = "hymba_mla_stickbreaking_swiglu"


def rmsnorm(x, g):
    xf = x.astype(jnp.float32)
    y = xf * lax.rsqrt(jnp.mean(xf * xf, axis=-1, keepdims=True) + EPS)
    return (y * g.astype(jnp.float32)).astype(x.dtype)


def rope_tables(positions, dim):
    inv_freq = ROPE_THETA ** (-jnp.arange(0, dim, 2, dtype=jnp.float32) / dim)
    ang = positions.astype(jnp.float32)[:, :, None] * inv_freq[None, None, :]
    return jnp.cos(ang)[:, None], jnp.sin(ang)[:, None]


def apply_rope(x, cos, sin):
    xf = x.astype(jnp.float32)
    x1, x2 = jnp.split(xf, 2, axis=-1)
    out = jnp.concatenate([x1 * cos - x2 * sin, x2 * cos + x1 * sin], axis=-1)
    return out.astype(x.dtype)


def to_query_blocks(q):
    b, h, s, d = q.shape
    return q.reshape(b, h, s // Q_BLOCK, Q_BLOCK, d).transpose(2, 0, 1, 3, 4)


def from_query_blocks(o):
    nb, b, h, qb, d = o.shape
    return o.transpose(1, 2, 0, 3, 4).reshape(b, h, nb * qb, d)


def mla_causal_attention(q, k, v):
    seq = q.shape[2]
    scale = 1.0 / math.sqrt(q.shape[-1])
    kf = k.astype(jnp.float32)
    vf = v.astype(jnp.float32)
    k_pos = jnp.arange(seq)
    nb = seq // Q_BLOCK

    def block(args):
        i, qb = args
        s = jnp.einsum("bhqd,bhkd->bhqk", qb.astype(jnp.float32), kf) * scale
        q_pos = i * Q_BLOCK + jnp.arange(Q_BLOCK)
        causal = k_pos[None, :] <= q_pos[:, None]
        p = jax.nn.softmax(jnp.where(causal, s, -jnp.inf), axis=-1)
        return jnp.einsum("bhqk,bhkd->bhqd", p, vf)

    o = lax.map(block, (jnp.arange(nb), to_query_blocks(q)))
    return from_query_blocks(o).astype(q.dtype)


def stick_breaking_attention(q, k, v):
    seq = q.shape[2]
    scale = 1.0 / math.sqrt(q.shape[-1])
    kf = k.astype(jnp.float32)
    vf = v.astype(jnp.float32)
    k_pos = jnp.arange(seq)
    nb = seq // Q_BLOCK

    def block(args):
        i, qb = args
        z = jnp.einsum("bhqd,bhkd->bhqk", qb.astype(jnp.float32), kf) * scale
        q_pos = i * Q_BLOCK + jnp.arange(Q_BLOCK)
        strict = k_pos[None, :] < q_pos[:, None]
        log_beta = jax.nn.log_sigmoid(z)
        log_one_minus = jnp.where(strict, jax.nn.log_sigmoid(-z), 0.0)
        suffix = lax.cumsum(log_one_minus, axis=3, reverse=True) - log_one_minus
        a = jnp.where(strict, jnp.exp(log_beta + suffix), 0.0)
        return jnp.einsum("bhqk,bhkd->bhqd", a, vf)

    o = lax.map(block, (jnp.arange(nb), to_query_blocks(q)))
    return from_query_blocks(o).astype(q.dtype)


def split_heads(t, n_heads):
    b, s, _ = t.shape
    return t.reshape(b, s, n_heads, -1).transpose(0, 2, 1, 3)


def merge_heads(t):
    b, h, s, d = t.shape
    return t.transpose(0, 2, 1, 3).reshape(b, s, h * d)


def setup_inputs(seed: int = 0) -> dict:
    key = jax.random.key(seed)
    ks = jax.random.split(key, 20)
    f32 = jnp.float32

    def w(k, shape, fan_in):
        return jax.random.normal(k, shape, f32) * (fan_in ** -0.5)

    def gain(k, shape):
        return 1.0 + 0.02 * jax.random.normal(k, shape, f32)

    x = jax.random.normal(ks[0], (BATCH, SEQ, D_MODEL), f32)
    positions = jnp.broadcast_to(jnp.arange(SEQ, dtype=jnp.int32), (BATCH, SEQ))
    return {
        "x": x,
        "positions": positions,
        "norm_mix": gain(ks[1], (DEPTH, D_MODEL)),
        "w_in": w(ks[2], (DEPTH, D_MODEL, IN_PROJ_WIDTH), D_MODEL),
        "q_latent_norm": gain(ks[3], (DEPTH, Q_LORA_RANK)),
        "w_uq": w(ks[4], (DEPTH, Q_LORA_RANK, MLA_HEADS * MLA_QK_DIM), Q_LORA_RANK),
        "kv_latent_norm": gain(ks[5], (DEPTH, KV_LORA_RANK)),
        "w_ukv": w(ks[6], (DEPTH, KV_LORA_RANK, MLA_HEADS * (MLA_NOPE_DIM + MLA_V_DIM)), KV_LORA_RANK),
        "out_norm_mla": gain(ks[7], (DEPTH, MLA_WIDTH)),
        "out_norm_sb": gain(ks[8], (DEPTH, SB_WIDTH)),
        "w_o": w(ks[9], (DEPTH, MIX_WIDTH, D_MODEL), MIX_WIDTH),
        "norm_ffn": gain(ks[10], (DEPTH, D_MODEL)),
        "w_gate": w(ks[11], (DEPTH, D_MODEL, D_FF), D_MODEL),
        "w_up": w(ks[12], (DEPTH, D_MODEL, D_FF), D_MODEL),
        "w_down": w(ks[13], (DEPTH, D_FF, D_MODEL), D_FF),
        "norm_final": gain(ks[14], (D_MODEL,)),
    }


def reference(x, positions, norm_mix, w_in, q_latent_norm, w_uq, kv_latent_norm,
              w_ukv, out_norm_mla, out_norm_sb, w_o, norm_ffn, w_gate, w_up,
              w_down, norm_final):
    b, s, _ = x.shape
    cos, sin = rope_tables(positions, MLA_ROPE_DIM)
    h = x
    for l in range(DEPTH):
        u = rmsnorm(h, norm_mix[l])
        proj = jnp.einsum("bsd,de->bse", u, w_in[l])
        c_q, c_kv, k_r, q_sb, k_sb, v_sb = jnp.split(proj, IN_SPLIT_POINTS, axis=-1)

        q = split_heads(jnp.einsum("bsr,re->bse", rmsnorm(c_q, q_latent_norm[l]), w_uq[l]), MLA_HEADS)
        q_nope, q_rope = q[..., :MLA_NOPE_DIM], q[..., MLA_NOPE_DIM:]
        q_rope = apply_rope(q_rope, cos, sin)
        kv = split_heads(jnp.einsum("bsr,re->bse", rmsnorm(c_kv, kv_latent_norm[l]), w_ukv[l]), MLA_HEADS)
        k_nope, v_mla = kv[..., :MLA_NOPE_DIM], kv[..., MLA_NOPE_DIM:]
        k_rope = apply_rope(k_r[:, None], cos, sin)
        q_mla = jnp.concatenate([q_nope, q_rope], axis=-1)
        k_mla = jnp.concatenate([k_nope, jnp.broadcast_to(k_rope, (b, MLA_HEADS, s, MLA_ROPE_DIM))], axis=-1)
        o_mla = merge_heads(mla_causal_attention(q_mla, k_mla, v_mla))

        o_sb = merge_heads(stick_breaking_attention(
            split_heads(q_sb, SB_HEADS), split_heads(k_sb, SB_HEADS), split_heads(v_sb, SB_HEADS)))

        merged = jnp.concatenate([rmsnorm(o_mla, out_norm_mla[l]), rmsnorm(o_sb, out_norm_sb[l])], axis=-1)
        h = h + jnp.einsum("bse,ed->bsd", merged, w_o[l])

        f = rmsnorm(h, norm_ffn[l])
        gate = jnp.einsum("bsd,df->bsf", f, w_gate[l])
        up = jnp.einsum("bsd,df->bsf", f, w_up[l])
        h = h + jnp.einsum("bsf,fd->bsd", jax.nn.silu(gate) * up, w_down[l])
    return rmsnorm(h, norm_final)
```

```python
import math
from contextlib import ExitStack
import numpy as np
import concourse.bass as bass
import concourse.mybir as mybir
from concourse.bass_utils import run_bass_kernel_spmd

F32 = mybir.dt.float32
BF16 = mybir.dt.bfloat16
I32 = mybir.dt.int32
AF = mybir.ActivationFunctionType
ALU = mybir.AluOpType


ENGS = ("pe", "act", "dve", "pool", "sp")


class Res:
    __slots__ = ("name", "w", "r", "excl")

    def __init__(self, name, excl=False):
        self.name = name
        self.w = None
        self.r = []
        self.excl = excl


class Op:
    __slots__ = ("fn", "waits", "sig", "dma", "idx")

    def __init__(self, fn, waits, dma, idx):
        self.fn = fn
        self.waits = waits
        self.sig = False
        self.dma = dma
        self.idx = idx


class Prog:
    NDMA = 40

    def __init__(self, nc):
        self.nc = nc
        self.sem = {e: nc.alloc_semaphore("s_" + e) for e in ENGS}
        self.dsem = [nc.alloc_semaphore("d%d" % i) for i in range(self.NDMA)]
        self.dcount = [0] * self.NDMA
        self.dnext = 0
        self.ops = {e: [] for e in ENGS}
        self.start = {e: 0 for e in ENGS}
        self.base = {e: 0 for e in ENGS}
        self.seen = {e: {x: 0 for x in ENGS} for e in ENGS}
        self.seen_d = {e: [0] * self.NDMA for e in ENGS}

    def _deps(self, reads, writes):
        ev = []
        for r in reads:
            if r.excl:
                writes = list(writes) + [r]
                continue
            if r.w is not None:
                ev.append(r.w)
        for w in writes:
            if w.w is not None:
                ev.append(w.w)
            ev.extend(w.r)
        return ev

    def _filter(self, eng, evs):
        out = []
        best = {}
        for e in evs:
            if e[0] == "c":
                _, x, i = e
                if x == "pe" and eng == "pe":
                    continue
                if i <= self.seen[eng][x]:
                    continue
                if i > best.get(("c", x), 0):
                    best[("c", x)] = i
            else:
                _, s, c = e
                if c <= self.seen_d[eng][s]:
                    continue
                if c > best.get(("d", s), 0):
                    best[("d", s)] = c
        for k, v in best.items():
            if k[0] == "c":
                self.seen[eng][k[1]] = v
                self.ops[k[1]][v - 1].sig = True
                out.append(("c", k[1], v))
            else:
                self.seen_d[eng][k[1]] = v
                out.append(("d", k[1], v))
        return out

    def _commit(self, ev, reads, writes):
        for r in reads:
            if r.excl:
                r.w = ev
                r.r = []
            else:
                r.r.append(ev)
        for w in writes:
            w.w = ev
            w.r = []

    def op(self, eng, fn, reads=(), writes=()):
        waits = self._filter(eng, self._deps(reads, writes))
        lst = self.ops[eng]
        o = Op(fn, waits, None, len(lst) + 1)
        lst.append(o)
        self._commit(("c", eng, o.idx), reads, writes)
        return o

    def dma(self, q, fn, reads=(), writes=()):
        s = self.dnext
        self.dnext = (self.dnext + 1) % self.NDMA
        evs = self._deps(reads, writes)
        if self.dcount[s] > 0:
            evs.append(("d", s, self.dcount[s]))
        waits = self._filter(q, evs)
        self.dcount[s] += 16
        lst = self.ops[q]
        o = Op(fn, waits, (s, self.dcount[s]), len(lst) + 1)
        lst.append(o)
        self._commit(("d", s, self.dcount[s]), reads, writes)
        return o

    def barrier(self):
        evs = [("d", s, self.dcount[s]) for s in range(self.NDMA) if self.dcount[s] > 0]
        waits = self._filter("sp", evs)
        lst = self.ops["sp"]
        o = Op(lambda e: e.nop(), waits, None, len(lst) + 1)
        lst.append(o)
        for x in ENGS:
            for s in range(self.NDMA):
                self.seen_d[x][s] = self.dcount[s]
            for y in ENGS:
                self.seen[x][y] = len(self.ops[y])

    def end_phase(self):
        self.barrier()
        self.replay()

    def replay(self):
        nc = self.nc
        sigcnt = {}
        for e in ENGS:
            c = self.base[e]
            arr = []
            for o in self.ops[e][self.start[e]:]:
                if o.sig:
                    c += 1
                arr.append(c)
            sigcnt[e] = arr

        def val(x, idx1):
            st = self.start[x]
            if idx1 - 1 < st:
                return self._hist[x][idx1 - 1]
            return sigcnt[x][idx1 - 1 - st]

        if not hasattr(self, "_hist"):
            self._hist = {e: [] for e in ENGS}

        def emit(ename, eng):
            for o in self.ops[ename][self.start[ename]:]:
                for w in o.waits:
                    if w[0] == "c":
                        eng.wait_ge(self.sem[w[1]], val(w[1], w[2]))
                    else:
                        eng.wait_ge(self.dsem[w[1]], w[2])
                ins = o.fn(eng)
                if o.dma is not None:
                    ins.then_inc(self.dsem[o.dma[0]], 16)
                elif o.sig:
                    ins.then_inc(self.sem[ename], 1)

        with nc.Block() as block:
            @block.tensor
            def _(t):
                emit("pe", t)

            @block.scalar
            def _(t):
                emit("act", t)

            @block.vector
            def _(t):
                emit("dve", t)

            @block.gpsimd
            def _(t):
                emit("pool", t)

            @block.sync
            def _(t):
                emit("sp", t)

        for e in ENGS:
            self._hist[e].extend(sigcnt[e])
            self.base[e] = sigcnt[e][-1] if sigcnt[e] else self.base[e]
            self.start[e] = len(self.ops[e])


D = 1024
DC = 8
H = 8
DFF = 2816
FC = 22
EPS = 1e-6
INW = 1952
C_CQ, C_CKV, C_KR, C_QSB, C_KSB, C_VSB = 0, 256, 384, 416, 928, 1440
MLA_SCALE = 1.0 / math.sqrt(96.0)
NEG = -30000.0


def own_blocks(j, NB):
    half = NB // 8
    return [4 * m + j for m in range(half)] + [4 * m + 3 - j for m in range(half, 2 * half)]


def build(S, Prog, Res, stop_after=None):
    NB = S // 128
    NBO = NB // 4
    NG = NBO // 4
    NCH = NB // 4
    SO = NBO * 128
    nc = bass.Bass("TRN2", target_bir_lowering=False)

    def din(name, shape, dt=F32):
        return nc.dram_tensor(name, list(shape), dt, kind="ExternalInput").ap()

    xb = din("xb", [S, D])
    xo = din("xo", [SO, D])
    posb = din("posb", [128, NB], I32)
    poso = din("poso", [128, NBO], I32)
    ident_d = din("ident", [128, 128])
    tri_d = din("tri", [128, 2, 128])
    w_in_d = din("w_in", [128, DC, INW])
    w_uq_d = din("w_uq", [128, 2 * 768])
    w_uk_d = din("w_uk", [128, 512])
    w_uv_d = din("w_uv", [128, 512])
    w_o_d = din("w_o", [128, DC, D])
    w_g_d = din("w_gate", [128, DC, DFF])
    w_u_d = din("w_up", [128, DC, DFF])
    w_d_d = din("w_down", [128, FC, D])
    g_d = {"mix": din("g_mix", [128, D]), "q": din("g_q", [128, 256]), "kv": din("g_kv", [128, 128]),
           "mla": din("g_mla", [128, 512]), "sb": din("g_sb", [128, 512]), "ffn": din("g_ffn", [128, D]),
           "fin": din("g_fin", [128, D])}
    g_o_d = din("g_o", [128, DC])
    mask_d = din("masks", [128, 16 * 128])
    y = nc.dram_tensor("y", [SO, D], F32, kind="ExternalOutput").ap()

    KTm = nc.dram_tensor("KTm", [4, 128, S], BF16).ap()
    KR = nc.dram_tensor("KR", [32, S], BF16).ap()
    Vm = nc.dram_tensor("Vm", [H, 128, NB, 65], BF16).ap()
    KTs = nc.dram_tensor("KTs", [4, 128, S], BF16).ap()
    Vs = nc.dram_tensor("Vs", [H, 128, NB, 65], BF16).ap()
    QTm = nc.dram_tensor("QTm", [H, 96, SO], BF16).ap()
    QTs = nc.dram_tensor("QTs", [4, 128, SO], BF16).ap()

    P = Prog(nc)
    inv_freq = (10000.0 ** (-np.arange(0, 32, 2, dtype=np.float32) / np.float32(32))).astype(np.float32)

    class T:
        def __init__(self, t, excl=False):
            self.t = t
            self.r = Res("r", excl)

    def sb(es, name, shape, dt):
        return T(es.enter_context(nc.sbuf_tensor(name, list(shape), dt)))

    def ps(es, name, shape, dt=F32):
        return T(es.enter_context(nc.psum_tensor(name, list(shape), dt)), excl=True)

    rrs = {"evac": 0}

    def evac_eng():
        rrs["evac"] ^= 1
        return "act" if rrs["evac"] else "dve"

    def copy_op(eng, out, in_, reads, writes, scale=None):
        if eng == "act":
            if scale is None:
                P.op("act", lambda e: e.activation(out=out, in_=in_, func=AF.Copy), reads=reads, writes=writes)
            else:
                P.op("act", lambda e: e.activation(out=out, in_=in_, func=AF.Copy, scale=scale), reads=reads, writes=writes)
        elif scale is None:
            P.op(eng, lambda e: e.tensor_copy(out=out, in_=in_), reads=reads, writes=writes)
        else:
            P.op(eng, lambda e: e.tensor_scalar(out=out, in0=in_, scalar1=scale, scalar2=None, op0=ALU.mult), reads=reads, writes=writes)

    def mm(out, lhsT, rhs, start, stop, reads, writes, skip=False):
        if skip:
            P.op("pe", lambda e: e.matmul(out, lhsT=lhsT, rhs=rhs, start=start, stop=stop, skip_group_check=True), reads=reads, writes=writes)
        else:
            P.op("pe", lambda e: e.matmul(out, lhsT=lhsT, rhs=rhs, start=start, stop=stop), reads=reads, writes=writes)

    def tr(out, in_, ident, reads, writes):
        P.op("pe", lambda e: e.transpose(out=out, in_=in_, identity=ident), reads=reads, writes=writes)

    def load(q, out, in_, writes, reads=()):
        P.dma(q, lambda e: e.dma_start(out=out, in_=in_), reads=reads, writes=writes)

    with ExitStack() as g:
        idf = sb(g, "idf", [128, 128], F32)
        idb = sb(g, "idb", [128, 128], BF16)
        zerob = sb(g, "zerob", [128, 128], BF16)
        epsc = sb(g, "epsc", [128, 1], F32)
        onec = sb(g, "onec", [128, 1], F32)
        pic = sb(g, "pic", [128, 1], F32)
        gm = {}

        def load_gains(es_, names):
            for nm in names:
                gm[nm] = sb(es_, "gs_" + nm, [128, g_d[nm].shape[1]], F32)
                load("sp", gm[nm].t[:], g_d[nm][:, :], [gm[nm].r])
        load("sp", idf.t[:], ident_d[:, :], [idf.r])
        P.op("pool", lambda e: e.tensor_copy(out=idb.t[:], in_=idf.t[:]), reads=[idf.r], writes=[idb.r])
        P.op("pool", lambda e: e.memset(zerob.t[:], 0.0), writes=[zerob.r])
        P.op("pool", lambda e: e.memset(epsc.t[:], EPS), writes=[epsc.r])
        P.op("pool", lambda e: e.memset(onec.t[:], 1.0), writes=[onec.r])
        P.op("pool", lambda e: e.memset(pic.t[:], math.pi), writes=[pic.r])

        def rstd_from_ss(ssap, rsap, rd, wr, n):
            P.op("act", lambda e: e.activation(out=rsap, in_=ssap, func=AF.Ln, scale=1.0 / n, bias=epsc.t[:]),
                 reads=[rd, epsc.r], writes=[wr])
            P.op("act", lambda e: e.activation(out=rsap, in_=rsap, func=AF.Exp, scale=-0.5), reads=[wr], writes=[wr])

        with ExitStack() as es:
            win = sb(es, "win", [128, DC, INW], BF16)
            wuq = sb(es, "wuq", [128, 2 * 768], BF16)
            wuk = sb(es, "wuk", [128, 512], BF16)
            wuv = sb(es, "wuv", [128, 512], BF16)
            cosb = sb(es, "cosb", [128, NB, 16], F32)
            sinb = sb(es, "sinb", [128, NB, 16], F32)
            coso = sb(es, "coso", [128, NBO, 16], F32)
            sino = sb(es, "sino", [128, NBO, 16], F32)
            load_gains(es, ("mix", "q", "kv"))
            es0 = es
            es = ExitStack()
            es.__enter__()
            wst = [sb(es, "wst%d" % i, [128, 2048], F32) for i in range(4)]
            ropi = sb(es, "ropi", [128, NB], I32)
            ropf = sb(es, "ropf", [128, NB], F32)
            ropt = sb(es, "ropt", [128, NB, 16], F32)
            ropk = sb(es, "ropk", [128, NB, 16], I32)
            ropg = sb(es, "ropg", [128, NB, 16], F32)
            roph = sb(es, "roph", [128, NB, 16], F32)

            def rope_table(pos_d, n, sink_sin, sink_cos):
                load("sp", ropi.t[:, 0:n], pos_d[:, :], [ropi.r])
                P.op("dve", lambda e: e.tensor_copy(out=ropf.t[:, 0:n], in_=ropi.t[:, 0:n]), reads=[ropi.r], writes=[ropf.r])
                for i in range(16):
                    P.op("dve", lambda e, i=i: e.tensor_scalar(out=ropt.t[:, 0:n, i], in0=ropf.t[:, 0:n], scalar1=float(inv_freq[i]),
                                                                scalar2=1.0 / (2 * math.pi), op0=ALU.mult, op1=ALU.mult),
                         reads=[ropf.r], writes=[ropt.r])
                for ph, sink in ((0.0, sink_sin), (0.25, sink_cos)):
                    P.op("dve", lambda e, ph=ph: e.tensor_scalar(out=ropg.t[:, 0:n, :], in0=ropt.t[:, 0:n, :], scalar1=ph, scalar2=None, op0=ALU.add),
                         reads=[ropt.r], writes=[ropg.r])
                    P.op("dve", lambda e: e.tensor_copy(out=ropk.t[:, 0:n, :], in_=ropg.t[:, 0:n, :]), reads=[ropg.r], writes=[ropk.r])
                    P.op("dve", lambda e: e.tensor_copy(out=roph.t[:, 0:n, :], in_=ropk.t[:, 0:n, :]), reads=[ropk.r], writes=[roph.r])
                    P.op("dve", lambda e: e.tensor_tensor(out=ropg.t[:, 0:n, :], in0=ropg.t[:, 0:n, :], in1=roph.t[:, 0:n, :], op=ALU.subtract),
                         reads=[ropg.r, roph.r], writes=[ropg.r])
                    P.op("dve", lambda e: e.scalar_tensor_tensor(out=roph.t[:, 0:n, :], in0=ropg.t[:, 0:n, :], scalar=0.0, in1=ropg.t[:, 0:n, :],
                                                                 op0=ALU.is_lt, op1=ALU.add), reads=[ropg.r], writes=[roph.r])
                    sink(roph, n)

            def sink_b(tile):
                def f(src, n):
                    P.op("act", lambda e: e.activation(out=tile.t[:], in_=src.t[:, 0:n, :], func=AF.Sin, scale=-2.0 * math.pi, bias=pic.t[:]),
                         reads=[src.r, pic.r], writes=[tile.r])
                return f

            rope_table(posb, NB, sink_b(sinb), sink_b(cosb))
            rope_table(poso, NBO, sink_b(sino), sink_b(coso))

            k = 0
            for c in range(DC):
                st = wst[k % 4]; k += 1
                load("sp", st.t[:, 0:INW], w_in_d[:, c, :], [st.r])
                copy_op("pool" if c % 2 == 0 else "act", win.t[:, c, :], st.t[:, 0:INW], [st.r], [win.r])
            st = wst[k % 4]; k += 1
            load("sp", st.t[:, 0:1536], w_uq_d[:, :], [st.r])
            P.op("pool", lambda e, st=st: e.tensor_copy(out=wuq.t[:], in_=st.t[:, 0:1536]), reads=[st.r], writes=[wuq.r])
            st = wst[k % 4]; k += 1
            load("sp", st.t[:, 0:512], w_uk_d[:, :], [st.r])
            load("sp", st.t[:, 512:1024], w_uv_d[:, :], [st.r])
            P.op("pool", lambda e, st=st: e.tensor_copy(out=wuk.t[:], in_=st.t[:, 0:512]), reads=[st.r], writes=[wuk.r])
            P.op("pool", lambda e, st=st: e.tensor_copy(out=wuv.t[:], in_=st.t[:, 512:1024]), reads=[st.r], writes=[wuv.r])

            P.end_phase()
            es.__exit__(None, None, None)
            es = ExitStack()
            es.__enter__()
            xbuf = [sb(es, "xbuf%d" % i, [128, D], F32) for i in range(8)]
            junk = sb(es, "junk", [128, D], F32)
            ubuf = [sb(es, "ubuf%d" % i, [128, D], BF16) for i in range(3)]
            uT = [sb(es, "uT%d" % i, [128, DC, 512], BF16) for i in range(2)]
            ss4 = [sb(es, "ss4_%d" % i, [128, 4], F32) for i in range(2)]
            rs4 = [sb(es, "rs4_%d" % i, [128, 4], F32) for i in range(2)]
            ssc = [sb(es, "ssc_%d" % i, [128, 4], F32) for i in range(2)]
            rsc = [sb(es, "rsc_%d" % i, [128, 4], F32) for i in range(2)]
            st4 = [sb(es, "st4_%d" % i, [128, 4, 512], BF16) for i in range(3)]
            stv = [sb(es, "stv%d" % i, [128, H, 4, 65], BF16) for i in range(3)]
            ckr = [sb(es, "ckr%d" % i, [128, 160], BF16) for i in range(4)]
            rtmp = [sb(es, "rtmp%d" % i, [128, 2, 16], F32) for i in range(2)]
            ckvT = [sb(es, "ckvT%d" % i, [128, 512], BF16) for i in range(2)]
            krT = [sb(es, "krT%d" % i, [32, 512], BF16) for i in range(2)]
            cqn = [sb(es, "cqn%d" % i, [128, 256], BF16) for i in range(4)]
            cqT = [sb(es, "cqT%d" % i, [128, 2, 512], BF16) for i in range(2)]
            qtok = [sb(es, "qtok%d" % i, [128, H, 96], BF16) for i in range(2)]
            qrt = [sb(es, "qrt%d" % i, [128, 2, H, 16], F32) for i in range(2)]
            qmst = [sb(es, "qmst%d" % i, [128, H, 512], BF16) for i in range(2)]
            pT = [ps(es, "pT%d" % i, [128, DC, 128], BF16) for i in range(2)]
            pP = [ps(es, "pP%d" % i, [128, 512], F32) for i in range(3)]
            pC = ps(es, "pC", [128, 4, 256], F32)
            pQ = ps(es, "pQ", [128, 512], F32)
            for v in stv:
                P.op("pool", lambda e, v=v: e.memset(v.t[:], 1.0), writes=[v.r])
            ctr = {"x": 0, "u": 0, "pT": 0, "pP": 0, "st4": 0, "stv": 0}

            def nxt(lst, key):
                v = lst[ctr[key] % len(lst)]
                ctr[key] += 1
                return v

            def rope_tok(x1, x2, cs, sn, o1, o2, tA, tB, rd, tmp_r, out_r):
                P.op("dve", lambda e: e.tensor_tensor(out=tA, in0=x1, in1=cs, op=ALU.mult), reads=rd, writes=[tmp_r])
                P.op("dve", lambda e: e.tensor_tensor(out=tB, in0=x2, in1=sn, op=ALU.mult), reads=rd, writes=[tmp_r])
                P.op("dve", lambda e: e.tensor_tensor(out=o1, in0=tA, in1=tB, op=ALU.subtract), reads=[tmp_r], writes=[out_r])
                P.op("dve", lambda e: e.tensor_tensor(out=tA, in0=x2, in1=cs, op=ALU.mult), reads=rd, writes=[tmp_r])
                P.op("dve", lambda e: e.tensor_tensor(out=tB, in0=x1, in1=sn, op=ALU.mult), reads=rd, writes=[tmp_r])
                P.op("dve", lambda e: e.tensor_tensor(out=o2, in0=tA, in1=tB, op=ALU.add), reads=[tmp_r], writes=[out_r])

            chunks = [("A", ci) for ci in range(NCH)] + [("B", gi) for gi in range(NG)]
            xtiles = {}

            def chunk_src(k):
                typ, i = chunks[k]
                return (xb if typ == "A" else xo), i

            def front_loads(k):
                src, i = chunk_src(k)
                xs = [nxt(xbuf, "x") for _ in range(4)]
                xtiles[k] = xs
                for t in range(4):
                    load("sp", xs[t].t[:], src[(4 * i + t) * 128:(4 * i + t + 1) * 128, :], [xs[t].r])

            def front_stats(k):
                xs = xtiles[k]
                s4 = ss4[k % 2]; r4 = rs4[k % 2]
                for t in range(4):
                    P.op("act", lambda e, t=t: e.activation(out=junk.t[:], in_=xs[t].t[:], func=AF.Square, accum_out=s4.t[:, t:t + 1]),
                         reads=[xs[t].r], writes=[junk.r, s4.r])
                rstd_from_ss(s4.t[:], r4.t[:], s4.r, r4.r, D)

            def front_norm_T(k):
                xs = xtiles[k]
                r4 = rs4[k % 2]
                uTt = uT[k % 2]
                for t in range(4):
                    ub = nxt(ubuf, "u"); pt = nxt(pT, "pT")
                    P.op("dve", lambda e, t=t, ub=ub: e.scalar_tensor_tensor(out=ub.t[:], in0=xs[t].t[:], scalar=r4.t[:, t:t + 1], in1=gm["mix"].t[:],
                                                                       op0=ALU.mult, op1=ALU.mult),
                         reads=[xs[t].r, r4.r, gm["mix"].r], writes=[ub.r])
                    for c in range(DC):
                        tr(pt.t[:, c, :], ub.t[:, c * 128:(c + 1) * 128], idb.t[:], [ub.r, idb.r], [pt.r])
                    copy_op("act", uTt.t[:, :, t * 128:(t + 1) * 128], pt.t[:], [pt.r], [uTt.r])

            def fm_proj(uTt, col0, dst4, scale=None):
                for pr in range(4):
                    p_ = nxt(pP, "pP")
                    for c in range(DC):
                        mm(p_.t[:, :], win.t[:, c, col0 + pr * 128:col0 + (pr + 1) * 128], uTt.t[:, c, :],
                           c == 0, c == DC - 1, [win.r, uTt.r], [p_.r])
                    copy_op(evac_eng(), dst4.t[:, pr, :], p_.t[:, :], [p_.r], [dst4.r], scale=scale)

            def proj_A1(k):
                ci = chunks[k][1]
                uTt = uT[k % 2]
                ks = nxt(st4, "st4")
                fm_proj(uTt, C_KSB, ks)
                load("sp", KTs[:, :, ci * 512:(ci + 1) * 512].rearrange("a p n -> p a n"), ks.t[:], [], reads=[ks.r])
                for t in range(4):
                    for c in range(DC):
                        mm(pC.t[:, t, 0:160], uTt.t[:, c, t * 128:(t + 1) * 128], win.t[:, c, C_CKV:C_CKV + 160],
                           c == 0, c == DC - 1, [win.r, uTt.r], [pC.r])
                s4 = ssc[k % 2]; r4 = rsc[k % 2]
                for t in range(4):
                    P.op("act", lambda e, t=t: e.activation(out=junk.t[:, 0:128], in_=pC.t[:, t, 0:128], func=AF.Square, accum_out=s4.t[:, t:t + 1]),
                         reads=[pC.r], writes=[junk.r, s4.r])
                rstd_from_ss(s4.t[:], r4.t[:], s4.r, r4.r, 128)

            def proj_A2(k):
                ci = chunks[k][1]
                uTt = uT[k % 2]
                r4 = rsc[k % 2]
                ckT = ckvT[k % 2]
                krt = krT[k % 2]
                cks = []
                for t in range(4):
                    kb = 4 * ci + t
                    ck = ckr[t]
                    tm = rtmp[t % 2]
                    cks.append(ck)
                    P.op("dve", lambda e, ck=ck, t=t: e.scalar_tensor_tensor(out=ck.t[:, 0:128], in0=pC.t[:, t, 0:128], scalar=r4.t[:, t:t + 1],
                                                                       in1=gm["kv"].t[:], op0=ALU.mult, op1=ALU.mult),
                         reads=[pC.r, r4.r, gm["kv"].r], writes=[ck.r])
                    rope_tok(pC.t[:, t, 128:144], pC.t[:, t, 144:160], cosb.t[:, kb, :], sinb.t[:, kb, :],
                             ck.t[:, 128:144], ck.t[:, 144:160], tm.t[:, 0, :], tm.t[:, 1, :],
                             [pC.r, cosb.r, sinb.r], tm.r, ck.r)
                vs_ = nxt(stv, "stv")
                for t in range(4):
                    p_ = nxt(pP, "pP")
                    for c in range(DC):
                        mm(p_.t[:, :], uTt.t[:, c, t * 128:(t + 1) * 128], win.t[:, c, C_VSB:C_VSB + 512],
                           c == 0, c == DC - 1, [win.r, uTt.r], [p_.r])
                    copy_op("act", vs_.t[:, :, t, 0:64], p_.t[:, :].rearrange("p (h d) -> p h d", d=64), [p_.r], [vs_.r])
                load("sp", Vs[:, :, 4 * ci:4 * ci + 4, :].rearrange("h p t e -> p h t e"), vs_.t[:], [], reads=[vs_.r])
                pt = nxt(pT, "pT")
                for t in range(4):
                    tr(pt.t[:, t, :], cks[t].t[:, 0:128], idb.t[:], [cks[t].r, idb.r], [pt.r])
                    tr(pt.t[0:32, 4 + t, :], cks[t].t[:, 128:160], idb.t[:], [cks[t].r, idb.r], [pt.r])
                copy_op("dve", ckT.t[:, :].rearrange("p (t n) -> p t n", n=128), pt.t[:, 0:4, :], [pt.r], [ckT.r])
                copy_op("dve", krt.t[:, :].rearrange("p (t n) -> p t n", n=128), pt.t[0:32, 4:8, :], [pt.r], [krt.r])
                load("sp", KR[:, ci * 512:(ci + 1) * 512], krt.t[:], [], reads=[krt.r])
                kn = nxt(st4, "st4")
                for pr in range(4):
                    p_ = nxt(pP, "pP")
                    mm(p_.t[:, :], wuk.t[:, pr * 128:(pr + 1) * 128], ckT.t[:, :], True, True, [wuk.r, ckT.r], [p_.r])
                    copy_op(evac_eng(), kn.t[:, pr, :], p_.t[:, :], [p_.r], [kn.r])
                load("sp", KTm[:, :, ci * 512:(ci + 1) * 512].rearrange("a p n -> p a n"), kn.t[:], [], reads=[kn.r])
                vm_ = nxt(stv, "stv")
                for t in range(4):
                    p_ = nxt(pP, "pP")
                    mm(p_.t[:, :], ckT.t[:, t * 128:(t + 1) * 128], wuv.t[:, :], True, True, [wuv.r, ckT.r], [p_.r])
                    copy_op(evac_eng(), vm_.t[:, :, t, 0:64], p_.t[:, :].rearrange("p (h d) -> p h d", d=64), [p_.r], [vm_.r])
                load("sp", Vm[:, :, 4 * ci:4 * ci + 4, :].rearrange("h p t e -> p h t e"), vm_.t[:], [], reads=[vm_.r])

            def proj_B1(k):
                gi = chunks[k][1]
                uTt = uT[k % 2]
                qs = nxt(st4, "st4")
                fm_proj(uTt, C_QSB, qs, scale=0.125)
                load("sp", QTs[:, :, gi * 512:(gi + 1) * 512].rearrange("a p n -> p a n"), qs.t[:], [], reads=[qs.r])
                for t in range(4):
                    for c in range(DC):
                        mm(pC.t[:, t, 0:256], uTt.t[:, c, t * 128:(t + 1) * 128], win.t[:, c, C_CQ:C_CQ + 256],
                           c == 0, c == DC - 1, [win.r, uTt.r], [pC.r])
                s4 = ssc[k % 2]; r4 = rsc[k % 2]
                for t in range(4):
                    P.op("act", lambda e, t=t: e.activation(out=junk.t[:, 0:256], in_=pC.t[:, t, 0:256], func=AF.Square, accum_out=s4.t[:, t:t + 1]),
                         reads=[pC.r], writes=[junk.r, s4.r])
                rstd_from_ss(s4.t[:], r4.t[:], s4.r, r4.r, 256)

            def proj_B2(k):
                gi = chunks[k][1]
                r4 = rsc[k % 2]
                cqt = cqT[k % 2]
                for t in range(4):
                    cq = cqn[t]
                    P.op("dve", lambda e, cq=cq, t=t: e.scalar_tensor_tensor(out=cq.t[:], in0=pC.t[:, t, 0:256], scalar=r4.t[:, t:t + 1],
                                                                       in1=gm["q"].t[:], op0=ALU.mult, op1=ALU.mult),
                         reads=[pC.r, r4.r, gm["q"].r], writes=[cq.r])
                pt = nxt(pT, "pT")
                for t in range(4):
                    for k2 in range(2):
                        tr(pt.t[:, 2 * t + k2, :], cqn[t].t[:, k2 * 128:(k2 + 1) * 128], idb.t[:], [cqn[t].r, idb.r], [pt.r])
                for k2 in range(2):
                    copy_op(evac_eng(), cqt.t[:, k2, :].rearrange("p (t n) -> p t n", n=128),
                            pt.t[:, :, :].rearrange("p (t k) n -> p t k n", k=2)[:, :, k2, :], [pt.r], [cqt.r])
                qm = qmst[k % 2]
                for t in range(4):
                    pos = 4 * gi + t
                    qt = qtok[t % 2]
                    qr = qrt[t % 2]
                    pa = nxt(pP, "pP")
                    for k2 in range(2):
                        mm(pa.t[:, 0:512], cqt.t[:, k2, t * 128:(t + 1) * 128], wuq.t[:, k2 * 768:k2 * 768 + 512],
                           k2 == 0, k2 == 1, [wuq.r, cqt.r], [pa.r])
                    for k2 in range(2):
                        mm(pQ.t[:, 0:256], cqt.t[:, k2, t * 128:(t + 1) * 128], wuq.t[:, k2 * 768 + 512:k2 * 768 + 768],
                           k2 == 0, k2 == 1, [wuq.r, cqt.r], [pQ.r])
                    qf = junk
                    copy_op("act", qf.t[:, 0:512], pa.t[:, 0:512], [pa.r], [qf.r])
                    copy_op("dve", qf.t[:, 512:768], pQ.t[:, 0:256], [pQ.r], [qf.r])
                    q3 = qf.t[:, 0:768].rearrange("p (h d) -> p h d", d=96)
                    copy_op("act", qt.t[:, :, 0:64], q3[:, :, 0:64], [qf.r], [qt.r])
                    rope_tok(q3[:, :, 64:80], q3[:, :, 80:96], coso.t[:, pos, :].unsqueeze(1).broadcast_to([128, H, 16]),
                             sino.t[:, pos, :].unsqueeze(1).broadcast_to([128, H, 16]),
                             qt.t[:, :, 64:80], qt.t[:, :, 80:96], qr.t[:, 0, :, :], qr.t[:, 1, :, :],
                             [qf.r, coso.r, sino.r], qr.r, qt.r)
                    pt = nxt(pT, "pT")
                    for h in range(H):
                        tr(pt.t[0:96, h, :], qt.t[:, h, :], idb.t[:], [qt.r, idb.r], [pt.r])
                    copy_op(evac_eng(), qm.t[0:96, :, t * 128:(t + 1) * 128], pt.t[0:96, :, :], [pt.r], [qm.r])
                load("sp", QTm[:, :, gi * 512:(gi + 1) * 512].rearrange("h r n -> r h n"), qm.t[0:96, :, :], [], reads=[qm.r])

            NK = len(chunks)
            front_loads(0)
            if NK > 1:
                front_loads(1)
            front_stats(0)
            front_norm_T(0)
            for k in range(NK):
                if k + 2 < NK:
                    front_loads(k + 2)
                if k + 1 < NK:
                    front_stats(k + 1)
                (proj_A1 if chunks[k][0] == "A" else proj_B1)(k)
                if k + 1 < NK:
                    front_norm_T(k + 1)
                (proj_A2 if chunks[k][0] == "A" else proj_B2)(k)
            P.end_phase()
            es.__exit__(None, None, None)
        if stop_after == "AB":
            return nc, dict(KTm=KTm, KR=KR, Vm=Vm, KTs=KTs, Vs=Vs, QTm=QTm, QTs=QTs)

        merged = sb(g, "merged", [128, NBO, D], F32)
        mres = [Res("m") for _ in range(NBO)]
        with ExitStack() as es:
            trib = sb(es, "trib", [128, 2, 128], BF16)
            maskb = sb(es, "maskb", [128, 16 * 128], BF16)
            mst = sb(es, "mst", [128, 2048], F32)
            mst2 = sb(es, "mst2", [128, 256], F32)
            load("sp", mst.t[:, 0:2048], mask_d[:, :], [mst.r])
            P.op("pool", lambda e: e.tensor_copy(out=maskb.t[:], in_=mst.t[:, 0:2048]), reads=[mst.r], writes=[maskb.r])
            load("sp", mst2.t[:, 0:256], tri_d.rearrange("p a b -> p (a b)"), [mst2.r])
            P.op("pool", lambda e: e.tensor_copy(out=trib.t[:].rearrange("p a b -> p (a b)"), in_=mst2.t[:, 0:256]),
                 reads=[mst2.r], writes=[trib.r])
            Kb = [sb(es, "Kb%d" % i, [128, S], BF16) for i in range(2)]
            Vb = [sb(es, "Vb%d" % i, [128, NB, 65], BF16) for i in range(2)]
            Qb = [sb(es, "Qb%d" % i, [128, SO], BF16) for i in range(2)]
            eb2 = [sb(es, "eb2_%d" % i, [128, 2, 512], F32) for i in range(4)]
            lb2 = [sb(es, "lb2_%d" % i, [128, 2, 512], BF16) for i in range(2)]
            gb2 = [sb(es, "gb2_%d" % i, [128, 2, 512], F32) for i in range(2)]
            ab2 = [sb(es, "ab2_%d" % i, [128, 2, 512], BF16) for i in range(2)]
            ab = [sb(es, "ab%d" % i, [128, 512], BF16) for i in range(2)]
            ot = [sb(es, "ot%d" % i, [128, 512], F32) for i in range(2)]
            ot2 = sb(es, "ot2", [128, 4, 128], F32)
            rinv = [sb(es, "rinv%d" % i, [128, 4, 1], F32) for i in range(2)]
            zb2 = [ps(es, "zb2_%d" % i, [128, 2, 512], F32) for i in range(2)]
            zres = [[Res("z", True), Res("z", True)] for _ in range(2)]
            cb = ps(es, "cb", [128, 512], F32)
            oacc = [ps(es, "oacc%d" % i, [128, 512], F32) for i in range(2)]
            tp = ps(es, "tp", [128, 4, 128], F32)
            mk4 = maskb.t[:].rearrange("p (a b c q) -> p a b c q", a=2, b=2, c=4)
            halfpos = NBO // 2

            jobs = [("m", h) for h in range(H)] + [("s", h) for h in range(H)]

            def job_loads(k):
                typ, h = jobs[k]
                sl = k % 2
                Kt, Vt, Qt = Kb[sl], Vb[sl], Qb[sl]
                r0 = (h % 2) * 64
                if typ == "m":
                    load("sp", Kt.t[0:64, :], KTm[h // 2, r0:r0 + 64, :], [Kt.r])
                    load("sp", Kt.t[64:96, :], KR[:, :], [Kt.r])
                    load("sp", Vt.t[:], Vm[h], [Vt.r])
                    load("sp", Qt.t[0:96, :], QTm[h], [Qt.r])
                else:
                    load("sp", Kt.t[0:64, :], KTs[h // 2, r0:r0 + 64, :], [Kt.r])
                    load("sp", Kt.t[64:128, :], KTs[h // 2, r0:r0 + 64, :], [Kt.r])
                    load("sp", Vt.t[:], Vs[h], [Vt.r])
                    load("sp", Qt.t[0:64, :], QTs[h // 2, r0:r0 + 64, :], [Qt.r])
                    load("sp", Qt.t[64:128, :], QTs[h // 2, r0:r0 + 64, :], [Qt.r])

            gctr = [0]

            def tiles_for(gi):
                out = []
                for kb in range(16 * gi + 15, -1, -1):
                    pm = kb // 4
                    if pm >= 4 * gi:
                        lo = pm - 4 * gi
                        out.append((kb, lo * 128, True, 0 if pm < halfpos else 1, kb % 4))
                    else:
                        out.append((kb, 0, False, 0, 0))
                return out

            def finish_group(typ, h, gi, oa):
                o_ = ot[gctr[0] % 2]
                rv = rinv[gctr[0] % 2]
                nr = 65 if typ == "m" else 64
                copy_op("dve", o_.t[0:nr, :], oa.t[0:nr, :], [oa.r], [o_.r])
                for t in range(4):
                    tr(tp.t[:, t, 0:nr], o_.t[0:nr, t * 128:(t + 1) * 128], idf.t[0:nr, 0:nr], [o_.r, idf.r], [tp.r])
                if typ == "m":
                    P.op("dve", lambda e: e.reciprocal(out=rv.t[:], in_=tp.t[:, :, 64:65]), reads=[tp.r], writes=[rv.r])
                    for t in range(4):
                        P.op("dve", lambda e, t=t: e.tensor_scalar(out=merged.t[:, 4 * gi + t, h * 64:(h + 1) * 64], in0=tp.t[:, t, 0:64],
                                                                   scalar1=rv.t[:, t, :], scalar2=None, op0=ALU.mult),
                             reads=[tp.r, rv.r], writes=[mres[4 * gi + t]])
                else:
                    P.op("dve", lambda e: e.tensor_copy(out=merged.t[:, 4 * gi:4 * gi + 4, 512 + h * 64:512 + (h + 1) * 64], in_=tp.t[:, :, 0:64]),
                         reads=[tp.r], writes=[mres[4 * gi + t_] for t_ in range(4)])

            def zero_bank(bank, Qt):
                mm(bank.t[:, :], zerob.t[:, :], maskb.t[:, 0:512], True, True, [zerob.r, maskb.r], [bank.r])

            def run_mla(k, h):
                sl = k % 2
                Kt, Vt, Qt = Kb[sl], Vb[sl], Qb[sl]
                for gi in range(NG):
                    oa = oacc[gctr[0] % 2]
                    zero_bank(oa, Qt)
                    tl = tiles_for(gi)
                    npair = len(tl) // 2

                    def S2(p, tl=tl, gi=gi):
                        (kba, c0, msk, hf, ra), (kbb, c0b, mskb, hfb, rb) = tl[2 * p], tl[2 * p + 1]
                        assert c0 == c0b and msk == mskb and hf == hfb
                        zt = zb2[p % 2].t; zr = zres[p % 2]
                        q0 = gi * 512 + c0
                        for j, kb_, r_ in ((0, kba, ra), (1, kbb, rb)):
                            mm(zt[:, j, c0:512], Kt.t[0:96, kb_ * 128:(kb_ + 1) * 128], Qt.t[0:96, q0:(gi + 1) * 512],
                               True, not msk, [Kt.r, Qt.r], [zr[j]])
                            if msk:
                                mm(zt[:, j, c0:c0 + 128], idb.t[:, :], mk4[:, 0, hf, r_, :], False, True, [idb.r, maskb.r], [zr[j]])

                    def P2(p, tl=tl, oa=oa):
                        (kba, c0, msk, hf, ra), (kbb, _, _, _, rb) = tl[2 * p], tl[2 * p + 1]
                        zt = zb2[p % 2].t; zr = zres[p % 2]
                        a_ = ab2[p % 2]
                        o_ap = a_.t[:, :, c0:512]; i_ap = zt[:, :, c0:512]
                        P.op("act", lambda e, o_ap=o_ap, i_ap=i_ap: e.activation(out=o_ap, in_=i_ap, func=AF.Exp, scale=MLA_SCALE),
                             reads=[zr[0], zr[1]], writes=[a_.r])
                        mm(oa.t[0:65, c0:512], Vt.t[:, kba, 0:65], a_.t[:, 0, c0:512], False, False, [Vt.r, a_.r], [oa.r], skip=True)
                        mm(oa.t[0:65, c0:512], Vt.t[:, kbb, 0:65], a_.t[:, 1, c0:512], False, False, [Vt.r, a_.r], [oa.r], skip=True)

                    S2(0)
                    for p in range(npair):
                        if p + 1 < npair:
                            S2(p + 1)
                        P2(p)
                    finish_group("m", h, gi, oa)
                    gctr[0] += 1

            def finish_sb(h, gi, oa):
                o_ = ot[gctr[0] % 2]
                copy_op("dve", o_.t[:, :], oa.t[:, :], [oa.r], [o_.r])
                for t in range(4):
                    tr(tp.t[:, t, :], o_.t[:, t * 128:(t + 1) * 128], idf.t[:, :], [o_.r, idf.r], [tp.r])
                copy_op("dve", ot2.t[:], tp.t[:], [tp.r], [ot2.r])
                P.op("dve", lambda e: e.tensor_tensor(out=merged.t[:, 4 * gi:4 * gi + 4, 512 + h * 64:512 + (h + 1) * 64],
                                                      in0=ot2.t[:, :, 0:64], in1=ot2.t[:, :, 64:128], op=ALU.add),
                     reads=[ot2.r], writes=[mres[4 * gi + t_] for t_ in range(4)])

            def run_sb(k, h):
                sl = k % 2
                Kt, Vt, Qt = Kb[sl], Vb[sl], Qb[sl]
                for gi in range(NG):
                    oa = oacc[gctr[0] % 2]
                    zero_bank(oa, Qt)
                    zero_bank(cb, Qt)
                    tl = tiles_for(gi)
                    npair = len(tl) // 2

                    def pr_(p):
                        ta, tb_ = tl[2 * p], tl[2 * p + 1]
                        assert ta[1] == tb_[1] and ta[2] == tb_[2] and ta[3] == tb_[3]
                        return ta, tb_

                    def Zp(p):
                        (kba, c0, msk, hf, ra), (kbb, _, _, _, rb) = pr_(p)
                        zt = zb2[p % 2].t; zr = zres[p % 2]
                        q0 = gi * 512 + c0
                        mm(zt[:, 0, c0:512], Kt.t[0:64, kba * 128:(kba + 1) * 128], Qt.t[0:64, q0:(gi + 1) * 512],
                           True, not msk, [Kt.r, Qt.r], [zr[0]])
                        mm(zt[:, 1, c0:512], Kt.t[64:128, kbb * 128:(kbb + 1) * 128], Qt.t[64:128, q0:(gi + 1) * 512],
                           True, not msk, [Kt.r, Qt.r], [zr[1]])
                        if msk:
                            mm(zt[:, 0, c0:c0 + 128], idb.t[:, :], mk4[:, 1, hf, ra, :], False, True, [idb.r, maskb.r], [zr[0]])
                            mm(zt[:, 1, c0:c0 + 128], idb.t[:, :], mk4[:, 1, hf, rb, :], False, True, [idb.r, maskb.r], [zr[1]])

                    def Ep(p):
                        c0 = pr_(p)[0][1]
                        zt = zb2[p % 2].t; zr = zres[p % 2]; e_ = eb2[p % 4]
                        P.op("act", lambda e: e.activation(out=e_.t[:, :, c0:512], in_=zt[:, :, c0:512], func=AF.Exp),
                             reads=[zr[0], zr[1]], writes=[e_.r])

                    def Lp(p):
                        c0 = pr_(p)[0][1]
                        e_ = eb2[p % 4]; l_ = lb2[p % 2]
                        P.op("act", lambda e: e.activation(out=l_.t[:, :, c0:512], in_=e_.t[:, :, c0:512], func=AF.Ln, bias=onec.t[:]),
                             reads=[e_.r, onec.r], writes=[l_.r])

                    def Tri(p, j):
                        c0 = pr_(p)[0][1]
                        l_ = lb2[p % 2]
                        mm(cb.t[:, c0:512], trib.t[:, 0, :], l_.t[:, j, c0:512], False, False, [trib.r, l_.r], [cb.r], skip=True)

                    def G(p, j):
                        c0 = pr_(p)[0][1]
                        g_ = gb2[p % 2]
                        P.op("act", lambda e: e.activation(out=g_.t[:, j, c0:512], in_=cb.t[:, c0:512], func=AF.Exp, scale=-1.0),
                             reads=[cb.r], writes=[g_.r])

                    def OmT(p, j):
                        c0 = pr_(p)[0][1]
                        l_ = lb2[p % 2]
                        mm(cb.t[:, c0:512], trib.t[:, 1, :], l_.t[:, j, c0:512], False, False, [trib.r, l_.r], [cb.r], skip=True)

                    def Amul(p):
                        c0 = pr_(p)[0][1]
                        g_ = gb2[p % 2]; a_ = ab2[p % 2]; e_ = eb2[p % 4]
                        P.op("dve", lambda e: e.tensor_tensor(out=a_.t[:, :, c0:512], in0=e_.t[:, :, c0:512], in1=g_.t[:, :, c0:512], op=ALU.mult),
                             reads=[e_.r, g_.r], writes=[a_.r])

                    def AVp(p):
                        (kba, c0, _, _, _), (kbb, _, _, _, _) = pr_(p)
                        a_ = ab2[p % 2]
                        mm(oa.t[0:64, c0:512], Vt.t[:, kba, 0:64], a_.t[:, 0, c0:512], False, False, [Vt.r, a_.r], [oa.r], skip=True)
                        o_ap = oa.t[64:128, c0:512]; l_ap = Vt.t[:, kbb, 0:64]; r_ap = a_.t[:, 1, c0:512]
                        P.op("pe", lambda e, o_ap=o_ap, l_ap=l_ap, r_ap=r_ap: e.matmul(o_ap, lhsT=l_ap, rhs=r_ap, start=False, stop=False,
                                                                                     skip_group_check=True, tile_position=(0, 64)),
                             reads=[Vt.r, a_.r], writes=[oa.r])

                    Zp(0)
                    if npair > 1:
                        Zp(1)
                    Ep(0)
                    if npair > 2:
                        Zp(2)
                    if npair > 1:
                        Ep(1)
                    Lp(0)
                    for p in range(npair + 1):
                        if 0 <= p - 1 < npair:
                            OmT(p - 1, 1)
                        if p < npair:
                            Tri(p, 0)
                            G(p, 0)
                        if p + 3 < npair:
                            Zp(p + 3)
                        if p + 1 < npair:
                            Lp(p + 1)
                        if 0 <= p - 1 < npair:
                            Amul(p - 1)
                            AVp(p - 1)
                        if p < npair:
                            OmT(p, 0)
                            Tri(p, 1)
                            G(p, 1)
                        if p + 2 < npair:
                            Ep(p + 2)
                    finish_sb(h, gi, oa)
                    gctr[0] += 1

            job_loads(0)
            for k in range(len(jobs)):
                if k + 1 < len(jobs):
                    job_loads(k + 1)
                typ, h = jobs[k]
                if typ == "m":
                    run_mla(k, h)
                else:
                    run_sb(k, h)
            P.end_phase()
        if stop_after == "C":
            return nc, dict(merged=merged)

        with ExitStack() as es:
            fT = sb(es, "fT", [128, DC, SO], BF16)
            load_gains(es, ("ffn", "fin"))
            g_o = sb(es, "g_o_sb", [128, DC], F32)
            load("sp", g_o.t[:], g_o_d[:, :], [g_o.r])
            sst = sb(es, "sst", [128, NBO, 2], F32)
            rst = sb(es, "rst", [128, NBO, 2], F32)
            junk = sb(es, "junkd", [128, D], F32)
            wg = [sb(es, "wg0", [128, DC, 512], BF16), None]
            wu = [sb(es, "wu0", [128, DC, 512], BF16), None]
            wd = [sb(es, "wd0", [128, 4, D], BF16), None]
            wstf = [sb(es, "wstf%d" % i, [128, 1024], F32) for i in range(2)]
            NFG = (FC + 3) // 4
            wctr = [0]

            def stage_cast(dst_ap, src_ap, n, dst_r):
                i = wctr[0]; wctr[0] += 1
                st = wstf[i % 2]
                load("sp", st.t[:, 0:n], src_ap, [st.r])
                copy_op("pool", dst_ap, st.t[:, 0:n], [st.r], [dst_r])

            def ffn_load_list(fg):
                f0 = fg * 512
                nf = min(512, DFF - f0)
                sl = fg % 2
                lst = []
                for c in range(DC):
                    lst.append((wg[sl].t[:, c, 0:nf], w_g_d[:, c, f0:f0 + nf], nf, wg[sl].r))
                    lst.append((wu[sl].t[:, c, 0:nf], w_u_d[:, c, f0:f0 + nf], nf, wu[sl].r))
                for fc in range(nf // 128):
                    lst.append((wd[sl].t[:, fc, :], w_d_d[:, fg * 4 + fc, :], D, wd[sl].r))
                return lst

            def ffn_loads(fg):
                for a_ in ffn_load_list(fg):
                    stage_cast(*a_)

            with ExitStack() as e1:
                wo = sb(e1, "wo", [128, DC, D], BF16)
                mn = [sb(e1, "mn%d" % i, [128, D], BF16) for i in range(3)]
                mT = [sb(e1, "mT%d" % i, [128, DC, 128], BF16) for i in range(3)]
                xs2 = [sb(e1, "xs2_%d" % i, [128, D], F32) for i in range(3)]
                pT = [ps(e1, "pTd%d" % i, [128, DC, 128], BF16) for i in range(3)]
                pO = [ps(e1, "pO%d" % i, [128, 1024], F32) for i in range(2)]
                for c in range(DC):
                    st = wstf[c % 2]
                    load("sp", st.t[:, 0:D], w_o_d[:, c, :], [st.r])
                    P.op("act", lambda e, st=st, c=c: e.activation(out=wo.t[:, c, :], in_=st.t[:, 0:D], func=AF.Copy, scale=g_o.t[:, c:c + 1]),
                         reads=[st.r, g_o.r], writes=[wo.r])
                for t in range(NBO):
                    for hf in range(2):
                        P.op("act", lambda e, t=t, hf=hf: e.activation(out=junk.t[:, 0:512], in_=merged.t[:, t, hf * 512:(hf + 1) * 512],
                                                                       func=AF.Square, accum_out=sst.t[:, t, hf:hf + 1]),
                             reads=[mres[t]], writes=[junk.r, sst.r])
                rstd_from_ss(sst.t[:], rst.t[:], sst.r, rst.r, 512)
                pre0 = ffn_load_list(0)

                def prep(t):
                    m_ = mn[t % 3]; mt = mT[t % 3]; pt = pT[t % 3]; xs = xs2[t % 3]
                    load("sp", xs.t[:], xo[t * 128:(t + 1) * 128, :], [xs.r])
                    for hf in range(2):
                        P.op("act", lambda e, t=t, hf=hf, m_=m_: e.activation(
                            out=m_.t[:, hf * 512:(hf + 1) * 512], in_=merged.t[:, t, hf * 512:(hf + 1) * 512],
                            func=AF.Copy, scale=rst.t[:, t, hf:hf + 1]),
                            reads=[mres[t], rst.r], writes=[m_.r])
                    for c in range(DC):
                        tr(pt.t[:, c, :], m_.t[:, c * 128:(c + 1) * 128], idb.t[:], [m_.r, idb.r], [pt.r])
                    copy_op("act", mt.t[:], pt.t[:], [pt.r], [mt.r])

                def fin(t):
                    mt = mT[t % 3]; po = pO[t % 2]; xs = xs2[t % 3]
                    for nh in range(2):
                        for c in range(DC):
                            mm(po.t[:, nh * 512:(nh + 1) * 512], mt.t[:, c, :], wo.t[:, c, nh * 512:(nh + 1) * 512],
                               c == 0, c == DC - 1, [mt.r, wo.r], [po.r])
                    P.op("dve", lambda e, t=t, po=po, xs=xs: e.tensor_tensor(out=merged.t[:, t, :], in0=po.t[:, :], in1=xs.t[:], op=ALU.add),
                         reads=[po.r, xs.r], writes=[mres[t]])
                    P.op("act", lambda e, t=t: e.activation(out=junk.t[:], in_=merged.t[:, t, :], func=AF.Square, accum_out=sst.t[:, t, 0:1]),
                         reads=[mres[t]], writes=[junk.r, sst.r])

                prep(0)
                if NBO > 1:
                    prep(1)
                for t in range(NBO):
                    if t + 2 < NBO:
                        prep(t + 2)
                    for _ in range(2):
                        if pre0:
                            stage_cast(*pre0.pop(0))
                    fin(t)
                while pre0:
                    stage_cast(*pre0.pop(0))
                rstd_from_ss(sst.t[:], rst.t[:], sst.r, rst.r, D)
                for t in range(NBO):
                    m_ = mn[t % 2]; pt = pT[t % 2]
                    P.op("dve", lambda e, t=t, m_=m_: e.scalar_tensor_tensor(out=m_.t[:], in0=merged.t[:, t, :], scalar=rst.t[:, t, 0:1],
                                                                       in1=gm["ffn"].t[:], op0=ALU.mult, op1=ALU.mult),
                         reads=[mres[t], rst.r, gm["ffn"].r], writes=[m_.r])
                    for c in range(DC):
                        tr(pt.t[:, c, :], m_.t[:, c * 128:(c + 1) * 128], idb.t[:], [m_.r, idb.r], [pt.r])
                    copy_op("act" if t % 2 == 0 else "dve", fT.t[:, :, t * 128:(t + 1) * 128], pt.t[:], [pt.r], [fT.r])
                P.end_phase()

            with ExitStack() as e2:
                wg[1] = sb(e2, "wg1", [128, DC, 512], BF16)
                wu[1] = sb(e2, "wu1", [128, DC, 512], BF16)
                wd[1] = sb(e2, "wd1", [128, 4, D], BF16)
                aT = [sb(e2, "aT%d" % i, [128, 4, 512], BF16) for i in range(2)]
                sg = [sb(e2, "sg%d" % i, [128, 512], F32) for i in range(2)]
                yt = [sb(e2, "yt%d" % i, [128, D], F32) for i in range(3)]
                pg = [ps(e2, "pg%d" % i, [128, 512], F32) for i in range(2)]
                pu = [ps(e2, "pu%d" % i, [128, 512], F32) for i in range(2)]
                pd = [ps(e2, "pd%d" % i, [128, 1024], F32) for i in range(2)]
                k = 0
                pend = None

                def down(k_, sl_, nfc_, tg_):
                    a_ = aT[k_ % 2]
                    for tt in range(4):
                        t = tg_ * 4 + tt
                        pd_ = pd[(k_ * 4 + tt) % 2]
                        for nh in range(2):
                            for fc in range(nfc_):
                                mm(pd_.t[:, nh * 512:(nh + 1) * 512], a_.t[:, fc, tt * 128:(tt + 1) * 128], wd[sl_].t[:, fc, nh * 512:(nh + 1) * 512],
                                   fc == 0, fc == nfc_ - 1, [a_.r, wd[sl_].r], [pd_.r])
                        P.op("dve", lambda e, t=t, pd_=pd_: e.tensor_tensor(out=merged.t[:, t, :], in0=pd_.t[:, :], in1=merged.t[:, t, :], op=ALU.add),
                             reads=[pd_.r, mres[t]], writes=[mres[t]])

                if NFG > 1:
                    ffn_loads(1)
                for fg in range(NFG):
                    f0 = fg * 512
                    nfc = min(512, DFF - f0) // 128
                    sl = fg % 2
                    for tg in range(NG):
                        a_ = aT[k % 2]
                        for fc in range(nfc):
                            pg_ = pg[(k * 4 + fc) % 2]; pu_ = pu[(k * 4 + fc) % 2]; sg_ = sg[(k * 4 + fc) % 2]
                            for c in range(DC):
                                mm(pg_.t[:, :], wg[sl].t[:, c, fc * 128:(fc + 1) * 128], fT.t[:, c, tg * 512:(tg + 1) * 512],
                                   c == 0, c == DC - 1, [wg[sl].r, fT.r], [pg_.r])
                            for c in range(DC):
                                mm(pu_.t[:, :], wu[sl].t[:, c, fc * 128:(fc + 1) * 128], fT.t[:, c, tg * 512:(tg + 1) * 512],
                                   c == 0, c == DC - 1, [wu[sl].r, fT.r], [pu_.r])
                            P.op("act", lambda e, pg_=pg_, sg_=sg_: e.activation(out=sg_.t[:], in_=pg_.t[:, :], func=AF.Silu),
                                 reads=[pg_.r], writes=[sg_.r])
                            P.op("dve", lambda e, pu_=pu_, sg_=sg_, a_=a_, fc=fc: e.tensor_tensor(out=a_.t[:, fc, :], in0=pu_.t[:, :], in1=sg_.t[:],
                                                                                             op=ALU.mult),
                                 reads=[pu_.r, sg_.r], writes=[a_.r])
                        if pend is not None:
                            down(*pend)
                        pend = (k, sl, nfc, tg)
                        k += 1
                        if tg == 0 and fg >= 1 and fg + 1 < NFG:
                            ffn_loads(fg + 1)
                down(*pend)
                for t in range(NBO):
                    P.op("act", lambda e, t=t: e.activation(out=junk.t[:], in_=merged.t[:, t, :], func=AF.Square, accum_out=sst.t[:, t, 0:1]),
                         reads=[mres[t]], writes=[junk.r, sst.r])
                rstd_from_ss(sst.t[:], rst.t[:], sst.r, rst.r, D)
                for t in range(NBO):
                    y_ = yt[t % 3]
                    P.op("dve", lambda e, t=t, y_=y_: e.scalar_tensor_tensor(out=y_.t[:], in0=merged.t[:, t, :], scalar=rst.t[:, t, 0:1],
                                                                       in1=gm["fin"].t[:], op0=ALU.mult, op1=ALU.mult),
                         reads=[mres[t], rst.r, gm["fin"].r], writes=[y_.r])
                    load("sp", y[t * 128:(t + 1) * 128, :], y_.t[:], [], reads=[y_.r])
                P.end_phase()
    return nc, {}


def host_inputs(inputs, S):
    NB = S // 128
    f = np.float32
    x = np.asarray(inputs["x"], f)
    pos = np.asarray(inputs["positions"], np.int32)

    def kchunk(w):
        K, E = w.shape
        return np.ascontiguousarray(w.reshape(K // 128, 128, E).transpose(1, 0, 2))

    def rep(v):
        return np.ascontiguousarray(np.broadcast_to(np.asarray(v, f).reshape(1, -1), (128, v.size)))

    w_in = kchunk(np.asarray(inputs["w_in"], f)[0])
    w_uq = kchunk(np.asarray(inputs["w_uq"], f)[0]).reshape(128, 2 * 768)
    wukv = np.asarray(inputs["w_ukv"], f)[0].reshape(128, H, 128)
    w_uk = np.ascontiguousarray(wukv[:, :, 0:64].reshape(128, 512))
    w_uv = np.ascontiguousarray(wukv[:, :, 64:128].reshape(128, 512))
    w_o = kchunk(np.asarray(inputs["w_o"], f)[0])
    w_g = kchunk(np.asarray(inputs["w_gate"], f)[0])
    w_u = kchunk(np.asarray(inputs["w_up"], f)[0])
    w_d = kchunk(np.asarray(inputs["w_down"], f)[0])
    ident = np.eye(128, dtype=f)
    jj = np.arange(128)[:, None]
    ss_ = np.arange(128)[None, :]
    tri = np.stack([(jj >= ss_).astype(f), (jj < ss_).astype(f)], axis=1)
    causal = (jj <= ss_).astype(f)
    strict = (jj < ss_).astype(f)
    common = dict(ident=ident, tri=np.ascontiguousarray(tri), w_in=w_in, w_uq=w_uq, w_uk=w_uk, w_uv=w_uv, w_o=w_o,
                  w_gate=w_g, w_up=w_u, w_down=w_d,
                  g_mix=rep(inputs["norm_mix"][0]), g_q=rep(inputs["q_latent_norm"][0]), g_kv=rep(inputs["kv_latent_norm"][0]),
                  g_mla=rep(inputs["out_norm_mla"][0]), g_sb=rep(inputs["out_norm_sb"][0]), g_ffn=rep(inputs["norm_ffn"][0]),
                  g_fin=rep(inputs["norm_final"]),
                  g_o=np.ascontiguousarray(np.concatenate([np.asarray(inputs["out_norm_mla"], f)[0],
                                                           np.asarray(inputs["out_norm_sb"], f)[0]]).reshape(DC, 128).T))
    maps = []
    for c in range(8):
        b, j = c // 4, c % 4
        ob = own_blocks(j, NB)
        rows = np.concatenate([np.arange(k * 128, (k + 1) * 128) for k in ob])
        masks = np.zeros((128, 2, 2, 4, 128), f)
        for ti, tm in enumerate((causal, strict)):
            for hf, off in enumerate((j, 3 - j)):
                for r in range(4):
                    if r < off:
                        masks[:, ti, hf, r, :] = 0.0
                    elif r == off:
                        masks[:, ti, hf, r, :] = NEG * (1.0 - tm)
                    else:
                        masks[:, ti, hf, r, :] = NEG
        m = dict(common)
        m["xb"] = np.ascontiguousarray(x[b])
        m["xo"] = np.ascontiguousarray(x[b][rows])
        m["posb"] = np.ascontiguousarray(pos[b].reshape(NB, 128).T)
        m["poso"] = np.ascontiguousarray(pos[b][rows].reshape(len(ob), 128).T)
        m["masks"] = masks.reshape(128, 16 * 128)
        maps.append(m)
    return maps


def assemble(results, S, B=2):
    NB = S // 128
    out = np.zeros((B, S, D), np.float32)
    for c in range(8):
        b, j = c // 4, c % 4
        ob = own_blocks(j, NB)
        yy = np.asarray(results[c]["y"], np.float32)
        for i, k in enumerate(ob):
            out[b, k * 128:(k + 1) * 128, :] = yy[i * 128:(i + 1) * 128, :]
    return out


_NC_CACHE = {}


def kernel(**inputs):
    S = int(np.asarray(inputs["x"]).shape[1])
    if S not in _NC_CACHE:
        _NC_CACHE[S] = build(S, Prog, Res)[0]
    nc = _NC_CACHE[S]
    maps = host_inputs(inputs, S)
    res = run_bass_kernel_spmd(nc, maps, core_ids=list(range(8)))
    return assemble(res.results, S, B=int(np.asarray(inputs["x"]).shape[0]))
```

```python
import math
from contextlib import ExitStack
import numpy as np
import concourse.bass as bass
import concourse.mybir as mybir
from concourse.bass_utils import run_bass_kernel_spmd

F32 = mybir.dt.float32
BF16 = mybir.dt.bfloat16
I32 = mybir.dt.int32
AF = mybir.ActivationFunctionType
ALU = mybir.AluOpType


ENGS = ("pe", "act", "dve", "pool", "sp")


class Res:
    __slots__ = ("name", "w", "r", "excl")

    def __init__(self, name, excl=False):
        self.name = name
        self.w = None
        self.r = []
        self.excl = excl


class Op:
    __slots__ = ("fn", "waits", "sig", "dma", "idx")

    def __init__(self, fn, waits, dma, idx):
        self.fn = fn
        self.waits = waits
        self.sig = False
        self.dma = dma
        self.idx = idx


class Prog:
    NDMA = 40

    def __init__(self, nc):
        self.nc = nc
        self.sem = {e: nc.alloc_semaphore("s_" + e) for e in ENGS}
        self.dsem = [nc.alloc_semaphore("d%d" % i) for i in range(self.NDMA)]
        self.dcount = [0] * self.NDMA
        self.dnext = 0
        self.ops = {e: [] for e in ENGS}
        self.start = {e: 0 for e in ENGS}
        self.base = {e: 0 for e in ENGS}
        self.seen = {e: {x: 0 for x in ENGS} for e in ENGS}
        self.seen_d = {e: [0] * self.NDMA for e in ENGS}

    def _deps(self, reads, writes):
        ev = []
        for r in reads:
            if r.excl:
                writes = list(writes) + [r]
                continue
            if r.w is not None:
                ev.append(r.w)
        for w in writes:
            if w.w is not None:
                ev.append(w.w)
            ev.extend(w.r)
        return ev

    def _filter(self, eng, evs):
        out = []
        best = {}
        for e in evs:
            if e[0] == "c":
                _, x, i = e
                if x == "pe" and eng == "pe":
                    continue
                if i <= self.seen[eng][x]:
                    continue
                if i > best.get(("c", x), 0):
                    best[("c", x)] = i
            else:
                _, s, c = e
                if c <= self.seen_d[eng][s]:
                    continue
                if c > best.get(("d", s), 0):
                    best[("d", s)] = c
        for k, v in best.items():
            if k[0] == "c":
                self.seen[eng][k[1]] = v
                self.ops[k[1]][v - 1].sig = True
                out.append(("c", k[1], v))
            else:
                self.seen_d[eng][k[1]] = v
                out.append(("d", k[1], v))
        return out

    def _commit(self, ev, reads, writes):
        for r in reads:
            if r.excl:
                r.w = ev
                r.r = []
            else:
                r.r.append(ev)
        for w in writes:
            w.w = ev
            w.r = []

    def op(self, eng, fn, reads=(), writes=()):
        waits = self._filter(eng, self._deps(reads, writes))
        lst = self.ops[eng]
        o = Op(fn, waits, None, len(lst) + 1)
        lst.append(o)
        self._commit(("c", eng, o.idx), reads, writes)
        return o

    def dma(self, q, fn, reads=(), writes=()):
        s = self.dnext
        self.dnext = (self.dnext + 1) % self.NDMA
        evs = self._deps(reads, writes)
        if self.dcount[s] > 0:
            evs.append(("d", s, self.dcount[s]))
        waits = self._filter(q, evs)
        self.dcount[s] += 16
        lst = self.ops[q]
        o = Op(fn, waits, (s, self.dcount[s]), len(lst) + 1)
        lst.append(o)
        self._commit(("d", s, self.dcount[s]), reads, writes)
        return o

    def barrier(self):
        evs = [("d", s, self.dcount[s]) for s in range(self.NDMA) if self.dcount[s] > 0]
        waits = self._filter("sp", evs)
        lst = self.ops["sp"]
        o = Op(lambda e: e.nop(), waits, None, len(lst) + 1)
        lst.append(o)
        for x in ENGS:
            for s in range(self.NDMA):
                self.seen_d[x][s] = self.dcount[s]
            for y in ENGS:
                self.seen[x][y] = len(self.ops[y])

    def end_phase(self):
        self.barrier()
        self.replay()

    def replay(self):
        nc = self.nc
        sigcnt = {}
        for e in ENGS:
            c = self.base[e]
            arr = []
            for o in self.ops[e][self.start[e]:]:
                if o.sig:
                    c += 1
                arr.append(c)
            sigcnt[e] = arr

        def val(x, idx1):
            st = self.start[x]
            if idx1 - 1 < st:
                return self._hist[x][idx1 - 1]
            return sigcnt[x][idx1 - 1 - st]

        if not hasattr(self, "_hist"):
            self._hist = {e: [] for e in ENGS}

        def emit(ename, eng):
            for o in self.ops[ename][self.start[ename]:]:
                for w in o.waits:
                    if w[0] == "c":
                        eng.wait_ge(self.sem[w[1]], val(w[1], w[2]))
                    else:
                        eng.wait_ge(self.dsem[w[1]], w[2])
                ins = o.fn(eng)
                if o.dma is not None:
                    ins.then_inc(self.dsem[o.dma[0]], 16)
                elif o.sig:
                    ins.then_inc(self.sem[ename], 1)

        with nc.Block() as block:
            @block.tensor
            def _(t):
                emit("pe", t)

            @block.scalar
            def _(t):
                emit("act", t)

            @block.vector
            def _(t):
                emit("dve", t)

            @block.gpsimd
            def _(t):
                emit("pool", t)

            @block.sync
            def _(t):
                emit("sp", t)

        for e in ENGS:
            self._hist[e].extend(sigcnt[e])
            self.base[e] = sigcnt[e][-1] if sigcnt[e] else self.base[e]
            self.start[e] = len(self.ops[e])


D = 1024
DC = 8
H = 8
DFF = 2816
FC = 22
EPS = 1e-6
INW = 1952
C_CQ, C_CKV, C_KR, C_QSB, C_KSB, C_VSB = 0, 256, 384, 416, 928, 1440
MLA_SCALE = 1.0 / math.sqrt(96.0)
NEG = -30000.0


def own_blocks(j, NB):
    half = NB // 8
    return [4 * m + j for m in range(half)] + [4 * m + 3 - j for m in range(half, 2 * half)]


def build(S, Prog, Res, stop_after=None):
    NB = S // 128
    NBO = NB // 4
    NG = NBO // 4
    NCH = NB // 4
    SO = NBO * 128
    nc = bass.Bass("TRN2", target_bir_lowering=False)

    def din(name, shape, dt=F32):
        return nc.dram_tensor(name, list(shape), dt, kind="ExternalInput").ap()

    xb = din("xb", [S, D])
    xo = din("xo", [SO, D])
    posb = din("posb", [128, NB], I32)
    poso = din("poso", [128, NBO], I32)
    ident_d = din("ident", [128, 128])
    tri_d = din("tri", [128, 2, 128])
    w_in_d = din("w_in", [128, DC, INW])
    w_uq_d = din("w_uq", [128, 2 * 768])
    w_uk_d = din("w_uk", [128, 512])
    w_uv_d = din("w_uv", [128, 512])
    w_o_d = din("w_o", [128, DC, D])
    w_g_d = din("w_gate", [128, DC, DFF])
    w_u_d = din("w_up", [128, DC, DFF])
    w_d_d = din("w_down", [128, FC, D])
    g_d = {"mix": din("g_mix", [128, D]), "q": din("g_q", [128, 256]), "kv": din("g_kv", [128, 128]),
           "mla": din("g_mla", [128, 512]), "sb": din("g_sb", [128, 512]), "ffn": din("g_ffn", [128, D]),
           "fin": din("g_fin", [128, D])}
    g_o_d = din("g_o", [128, DC])
    mask_d = din("masks", [128, 16 * 128])
    y = nc.dram_tensor("y", [SO, D], F32, kind="ExternalOutput").ap()

    KTm = nc.dram_tensor("KTm", [4, 128, S], BF16).ap()
    KR = nc.dram_tensor("KR", [32, S], BF16).ap()
    Vm = nc.dram_tensor("Vm", [H, 128, NB, 65], BF16).ap()
    KTs = nc.dram_tensor("KTs", [4, 128, S], BF16).ap()
    Vs = nc.dram_tensor("Vs", [H, 128, NB, 65], BF16).ap()
    QTm = nc.dram_tensor("QTm", [H, 96, SO], BF16).ap()
    QTs = nc.dram_tensor("QTs", [4, 128, SO], BF16).ap()

    P = Prog(nc)
    inv_freq = (10000.0 ** (-np.arange(0, 32, 2, dtype=np.float32) / np.float32(32))).astype(np.float32)

    class T:
        def __init__(self, t, excl=False):
            self.t = t
            self.r = Res("r", excl)

    def sb(es, name, shape, dt):
        return T(es.enter_context(nc.sbuf_tensor(name, list(shape), dt)))

    def ps(es, name, shape, dt=F32):
        return T(es.enter_context(nc.psum_tensor(name, list(shape), dt)), excl=True)

    rrs = {"evac": 0}

    def evac_eng():
        rrs["evac"] ^= 1
        return "act" if rrs["evac"] else "dve"

    def copy_op(eng, out, in_, reads, writes, scale=None):
        if eng == "act":
            if scale is None:
                P.op("act", lambda e: e.activation(out=out, in_=in_, func=AF.Copy), reads=reads, writes=writes)
            else:
                P.op("act", lambda e: e.activation(out=out, in_=in_, func=AF.Copy, scale=scale), reads=reads, writes=writes)
        elif scale is None:
            P.op(eng, lambda e: e.tensor_copy(out=out, in_=in_), reads=reads, writes=writes)
        else:
            P.op(eng, lambda e: e.tensor_scalar(out=out, in0=in_, scalar1=scale, scalar2=None, op0=ALU.mult), reads=reads, writes=writes)

    def mm(out, lhsT, rhs, start, stop, reads, writes, skip=False):
        if skip:
            P.op("pe", lambda e: e.matmul(out, lhsT=lhsT, rhs=rhs, start=start, stop=stop, skip_group_check=True), reads=reads, writes=writes)
        else:
            P.op("pe", lambda e: e.matmul(out, lhsT=lhsT, rhs=rhs, start=start, stop=stop), reads=reads, writes=writes)

    def tr(out, in_, ident, reads, writes):
        P.op("pe", lambda e: e.transpose(out=out, in_=in_, identity=ident), reads=reads, writes=writes)

    def load(q, out, in_, writes, reads=()):
        P.dma(q, lambda e: e.dma_start(out=out, in_=in_), reads=reads, writes=writes)

    with ExitStack() as g:
        idf = sb(g, "idf", [128, 128], F32)
        idb = sb(g, "idb", [128, 128], BF16)
        zerob = sb(g, "zerob", [128, 128], BF16)
        epsc = sb(g, "epsc", [128, 1], F32)
        onec = sb(g, "onec", [128, 1], F32)
        pic = sb(g, "pic", [128, 1], F32)
        gm = {}

        def load_gains(es_, names):
            for nm in names:
                gm[nm] = sb(es_, "gs_" + nm, [128, g_d[nm].shape[1]], F32)
                load("sp", gm[nm].t[:], g_d[nm][:, :], [gm[nm].r])
        load("sp", idf.t[:], ident_d[:, :], [idf.r])
        P.op("pool", lambda e: e.tensor_copy(out=idb.t[:], in_=idf.t[:]), reads=[idf.r], writes=[idb.r])
        P.op("pool", lambda e: e.memset(zerob.t[:], 0.0), writes=[zerob.r])
        P.op("pool", lambda e: e.memset(epsc.t[:], EPS), writes=[epsc.r])
        P.op("pool", lambda e: e.memset(onec.t[:], 1.0), writes=[onec.r])
        P.op("pool", lambda e: e.memset(pic.t[:], math.pi), writes=[pic.r])

        def rstd_from_ss(ssap, rsap, rd, wr, n):
            P.op("act", lambda e: e.activation(out=rsap, in_=ssap, func=AF.Ln, scale=1.0 / n, bias=epsc.t[:]),
                 reads=[rd, epsc.r], writes=[wr])
            P.op("act", lambda e: e.activation(out=rsap, in_=rsap, func=AF.Exp, scale=-0.5), reads=[wr], writes=[wr])

        with ExitStack() as es:
            win = sb(es, "win", [128, DC, INW], BF16)
            wuq = sb(es, "wuq", [128, 2 * 768], BF16)
            wuk = sb(es, "wuk", [128, 512], BF16)
            wuv = sb(es, "wuv", [128, 512], BF16)
            cosb = sb(es, "cosb", [128, NB, 16], F32)
            sinb = sb(es, "sinb", [128, NB, 16], F32)
            coso = sb(es, "coso", [128, NBO, 16], F32)
            sino = sb(es, "sino", [128, NBO, 16], F32)
            load_gains(es, ("mix", "q", "kv"))
            es0 = es
            es = ExitStack()
            es.__enter__()
            wst = [sb(es, "wst%d" % i, [128, 2048], F32) for i in range(4)]
            ropi = sb(es, "ropi", [128, NB], I32)
            ropf = sb(es, "ropf", [128, NB], F32)
            ropt = sb(es, "ropt", [128, NB, 16], F32)
            ropk = sb(es, "ropk", [128, NB, 16], I32)
            ropg = sb(es, "ropg", [128, NB, 16], F32)
            roph = sb(es, "roph", [128, NB, 16], F32)

            def rope_table(pos_d, n, sink_sin, sink_cos):
                load("sp", ropi.t[:, 0:n], pos_d[:, :], [ropi.r])
                P.op("dve", lambda e: e.tensor_copy(out=ropf.t[:, 0:n], in_=ropi.t[:, 0:n]), reads=[ropi.r], writes=[ropf.r])
                for i in range(16):
                    P.op("dve", lambda e, i=i: e.tensor_scalar(out=ropt.t[:, 0:n, i], in0=ropf.t[:, 0:n], scalar1=float(inv_freq[i]),
                                                                scalar2=1.0 / (2 * math.pi), op0=ALU.mult, op1=ALU.mult),
                         reads=[ropf.r], writes=[ropt.r])
                for ph, sink in ((0.0, sink_sin), (0.25, sink_cos)):
                    P.op("dve", lambda e, ph=ph: e.tensor_scalar(out=ropg.t[:, 0:n, :], in0=ropt.t[:, 0:n, :], scalar1=ph, scalar2=None, op0=ALU.add),
                         reads=[ropt.r], writes=[ropg.r])
                    P.op("dve", lambda e: e.tensor_copy(out=ropk.t[:, 0:n, :], in_=ropg.t[:, 0:n, :]), reads=[ropg.r], writes=[ropk.r])
                    P.op("dve", lambda e: e.tensor_copy(out=roph.t[:, 0:n, :], in_=ropk.t[:, 0:n, :]), reads=[ropk.r], writes=[roph.r])
                    P.op("dve", lambda e: e.tensor_tensor(out=ropg.t[:, 0:n, :], in0=ropg.t[:, 0:n, :], in1=roph.t[:, 0:n, :], op=ALU.subtract),
                         reads=[ropg.r, roph.r], writes=[ropg.r])
                    P.op("dve", lambda e: e.scalar_tensor_tensor(out=roph.t[:, 0:n, :], in0=ropg.t[:, 0:n, :], scalar=0.0, in1=ropg.t[:, 0:n, :],
                                                                 op0=ALU.is_lt, op1=ALU.add), reads=[ropg.r], writes=[roph.r])
                    sink(roph, n)

            def sink_b(tile):
                def f(src, n):
                    P.op("act", lambda e: e.activation(out=tile.t[:], in_=src.t[:, 0:n, :], func=AF.Sin, scale=-2.0 * math.pi, bias=pic.t[:]),
                         reads=[src.r, pic.r], writes=[tile.r])
                return f

            rope_table(posb, NB, sink_b(sinb), sink_b(cosb))
            rope_table(poso, NBO, sink_b(sino), sink_b(coso))

            k = 0
            for c in range(DC):
                st = wst[k % 4]; k += 1
                load("sp", st.t[:, 0:INW], w_in_d[:, c, :], [st.r])
                copy_op(("act", "dve", "act", "pool")[c % 4], win.t[:, c, :], st.t[:, 0:INW], [st.r], [win.r])
            st = wst[k % 4]; k += 1
            load("sp", st.t[:, 0:1536], w_uq_d[:, :], [st.r])
            copy_op("act", wuq.t[:], st.t[:, 0:1536], [st.r], [wuq.r])
            st = wst[k % 4]; k += 1
            load("sp", st.t[:, 0:512], w_uk_d[:, :], [st.r])
            load("sp", st.t[:, 512:1024], w_uv_d[:, :], [st.r])
            copy_op("dve", wuk.t[:], st.t[:, 0:512], [st.r], [wuk.r])
            copy_op("dve", wuv.t[:], st.t[:, 512:1024], [st.r], [wuv.r])

            P.end_phase()
            es.__exit__(None, None, None)
            es = ExitStack()
            es.__enter__()
            xbuf = [sb(es, "xbuf%d" % i, [128, D], F32) for i in range(8)]
            junk = sb(es, "junk", [128, D], F32)
            ubuf = [sb(es, "ubuf%d" % i, [128, D], BF16) for i in range(3)]
            uT = [sb(es, "uT%d" % i, [128, DC, 512], BF16) for i in range(2)]
            ss4 = [sb(es, "ss4_%d" % i, [128, 4], F32) for i in range(2)]
            rs4 = [sb(es, "rs4_%d" % i, [128, 4], F32) for i in range(2)]
            ssc = [sb(es, "ssc_%d" % i, [128, 4], F32) for i in range(2)]
            rsc = [sb(es, "rsc_%d" % i, [128, 4], F32) for i in range(2)]
            st4 = [sb(es, "st4_%d" % i, [128, 4, 512], BF16) for i in range(3)]
            stv = [sb(es, "stv%d" % i, [128, H, 4, 65], BF16) for i in range(3)]
            ckr = [sb(es, "ckr%d" % i, [128, 160], BF16) for i in range(4)]
            rtmp = [sb(es, "rtmp%d" % i, [128, 2, 16], F32) for i in range(2)]
            ckvT = [sb(es, "ckvT%d" % i, [128, 512], BF16) for i in range(2)]
            krT = [sb(es, "krT%d" % i, [32, 512], BF16) for i in range(2)]
            cqn = [sb(es, "cqn%d" % i, [128, 256], BF16) for i in range(4)]
            cqT = [sb(es, "cqT%d" % i, [128, 2, 512], BF16) for i in range(2)]
            qtok = [sb(es, "qtok%d" % i, [128, H, 96], BF16) for i in range(2)]
            qrt = [sb(es, "qrt%d" % i, [128, 2, H, 16], F32) for i in range(2)]
            qmst = [sb(es, "qmst%d" % i, [128, H, 512], BF16) for i in range(2)]
            pT = [ps(es, "pT%d" % i, [128, DC, 128], BF16) for i in range(2)]
            pP = [ps(es, "pP%d" % i, [128, 512], F32) for i in range(3)]
            pC = ps(es, "pC", [128, 4, 256], F32)
            pQ = ps(es, "pQ", [128, 512], F32)
            for v in stv:
                P.op("pool", lambda e, v=v: e.memset(v.t[:], 1.0), writes=[v.r])
            ctr = {"x": 0, "u": 0, "pT": 0, "pP": 0, "st4": 0, "stv": 0}

            def nxt(lst, key):
                v = lst[ctr[key] % len(lst)]
                ctr[key] += 1
                return v

            def rope_tok(x1, x2, cs, sn, o1, o2, tA, tB, rd, tmp_r, out_r):
                P.op("dve", lambda e: e.tensor_tensor(out=tA, in0=x1, in1=cs, op=ALU.mult), reads=rd, writes=[tmp_r])
                P.op("dve", lambda e: e.tensor_tensor(out=tB, in0=x2, in1=sn, op=ALU.mult), reads=rd, writes=[tmp_r])
                P.op("dve", lambda e: e.tensor_tensor(out=o1, in0=tA, in1=tB, op=ALU.subtract), reads=[tmp_r], writes=[out_r])
                P.op("dve", lambda e: e.tensor_tensor(out=tA, in0=x2, in1=cs, op=ALU.mult), reads=rd, writes=[tmp_r])
                P.op("dve", lambda e: e.tensor_tensor(out=tB, in0=x1, in1=sn, op=ALU.mult), reads=rd, writes=[tmp_r])
                P.op("dve", lambda e: e.tensor_tensor(out=o2, in0=tA, in1=tB, op=ALU.add), reads=[tmp_r], writes=[out_r])

            chunks = [("A", ci) for ci in range(NCH)] + [("B", gi) for gi in range(NG)]
            xtiles = {}

            def chunk_src(k):
                typ, i = chunks[k]
                return (xb if typ == "A" else xo), i

            def front_loads(k):
                src, i = chunk_src(k)
                xs = [nxt(xbuf, "x") for _ in range(4)]
                xtiles[k] = xs
                for t in range(4):
                    load("sp", xs[t].t[:], src[(4 * i + t) * 128:(4 * i + t + 1) * 128, :], [xs[t].r])

            def front_stats(k):
                xs = xtiles[k]
                s4 = ss4[k % 2]; r4 = rs4[k % 2]
                for t in range(4):
                    P.op("act", lambda e, t=t: e.activation(out=junk.t[:], in_=xs[t].t[:], func=AF.Square, accum_out=s4.t[:, t:t + 1]),
                         reads=[xs[t].r], writes=[junk.r, s4.r])
                rstd_from_ss(s4.t[:], r4.t[:], s4.r, r4.r, D)

            def front_norm_T(k):
                xs = xtiles[k]
                r4 = rs4[k % 2]
                uTt = uT[k % 2]
                for t in range(4):
                    ub = nxt(ubuf, "u"); pt = nxt(pT, "pT")
                    P.op("dve", lambda e, t=t, ub=ub: e.scalar_tensor_tensor(out=ub.t[:], in0=xs[t].t[:], scalar=r4.t[:, t:t + 1], in1=gm["mix"].t[:],
                                                                       op0=ALU.mult, op1=ALU.mult),
                         reads=[xs[t].r, r4.r, gm["mix"].r], writes=[ub.r])
                    for c in range(DC):
                        tr(pt.t[:, c, :], ub.t[:, c * 128:(c + 1) * 128], idb.t[:], [ub.r, idb.r], [pt.r])
                    copy_op("act", uTt.t[:, :, t * 128:(t + 1) * 128], pt.t[:], [pt.r], [uTt.r])

            def fm_proj(uTt, col0, dst4, scale=None):
                for pr in range(4):
                    p_ = nxt(pP, "pP")
                    for c in range(DC):
                        mm(p_.t[:, :], win.t[:, c, col0 + pr * 128:col0 + (pr + 1) * 128], uTt.t[:, c, :],
                           c == 0, c == DC - 1, [win.r, uTt.r], [p_.r])
                    copy_op(evac_eng(), dst4.t[:, pr, :], p_.t[:, :], [p_.r], [dst4.r], scale=scale)

            def proj_A1(k):
                ci = chunks[k][1]
                uTt = uT[k % 2]
                ks = nxt(st4, "st4")
                fm_proj(uTt, C_KSB, ks)
                load("sp", KTs[:, :, ci * 512:(ci + 1) * 512].rearrange("a p n -> p a n"), ks.t[:], [], reads=[ks.r])
                for t in range(4):
                    for c in range(DC):
                        mm(pC.t[:, t, 0:160], uTt.t[:, c, t * 128:(t + 1) * 128], win.t[:, c, C_CKV:C_CKV + 160],
                           c == 0, c == DC - 1, [win.r, uTt.r], [pC.r])
                s4 = ssc[k % 2]; r4 = rsc[k % 2]
                for t in range(4):
                    P.op("act", lambda e, t=t: e.activation(out=junk.t[:, 0:128], in_=pC.t[:, t, 0:128], func=AF.Square, accum_out=s4.t[:, t:t + 1]),
                         reads=[pC.r], writes=[junk.r, s4.r])
                rstd_from_ss(s4.t[:], r4.t[:], s4.r, r4.r, 128)

            def proj_A2(k):
                ci = chunks[k][1]
                uTt = uT[k % 2]
                r4 = rsc[k % 2]
                ckT = ckvT[k % 2]
                krt = krT[k % 2]
                cks = []
                for t in range(4):
                    kb = 4 * ci + t
                    ck = ckr[t]
                    tm = rtmp[t % 2]
                    cks.append(ck)
                    P.op("dve", lambda e, ck=ck, t=t: e.scalar_tensor_tensor(out=ck.t[:, 0:128], in0=pC.t[:, t, 0:128], scalar=r4.t[:, t:t + 1],
                                                                       in1=gm["kv"].t[:], op0=ALU.mult, op1=ALU.mult),
                         reads=[pC.r, r4.r, gm["kv"].r], writes=[ck.r])
                    rope_tok(pC.t[:, t, 128:144], pC.t[:, t, 144:160], cosb.t[:, kb, :], sinb.t[:, kb, :],
                             ck.t[:, 128:144], ck.t[:, 144:160], tm.t[:, 0, :], tm.t[:, 1, :],
                             [pC.r, cosb.r, sinb.r], tm.r, ck.r)
                vs_ = nxt(stv, "stv")
                for t in range(4):
                    p_ = nxt(pP, "pP")
                    for c in range(DC):
                        mm(p_.t[:, :], uTt.t[:, c, t * 128:(t + 1) * 128], win.t[:, c, C_VSB:C_VSB + 512],
                           c == 0, c == DC - 1, [win.r, uTt.r], [p_.r])
                    copy_op("act", vs_.t[:, :, t, 0:64], p_.t[:, :].rearrange("p (h d) -> p h d", d=64), [p_.r], [vs_.r])
                load("sp", Vs[:, :, 4 * ci:4 * ci + 4, :].rearrange("h p t e -> p h t e"), vs_.t[:], [], reads=[vs_.r])
                pt = nxt(pT, "pT")
                for t in range(4):
                    tr(pt.t[:, t, :], cks[t].t[:, 0:128], idb.t[:], [cks[t].r, idb.r], [pt.r])
                    tr(pt.t[0:32, 4 + t, :], cks[t].t[:, 128:160], idb.t[:], [cks[t].r, idb.r], [pt.r])
                copy_op("dve", ckT.t[:, :].rearrange("p (t n) -> p t n", n=128), pt.t[:, 0:4, :], [pt.r], [ckT.r])
                copy_op("dve", krt.t[:, :].rearrange("p (t n) -> p t n", n=128), pt.t[0:32, 4:8, :], [pt.r], [krt.r])
                load("sp", KR[:, ci * 512:(ci + 1) * 512], krt.t[:], [], reads=[krt.r])
                kn = nxt(st4, "st4")
                for pr in range(4):
                    p_ = nxt(pP, "pP")
                    mm(p_.t[:, :], wuk.t[:, pr * 128:(pr + 1) * 128], ckT.t[:, :], True, True, [wuk.r, ckT.r], [p_.r])
                    copy_op(evac_eng(), kn.t[:, pr, :], p_.t[:, :], [p_.r], [kn.r])
                load("sp", KTm[:, :, ci * 512:(ci + 1) * 512].rearrange("a p n -> p a n"), kn.t[:], [], reads=[kn.r])
                vm_ = nxt(stv, "stv")
                for t in range(4):
                    p_ = nxt(pP, "pP")
                    mm(p_.t[:, :], ckT.t[:, t * 128:(t + 1) * 128], wuv.t[:, :], True, True, [wuv.r, ckT.r], [p_.r])
                    copy_op(evac_eng(), vm_.t[:, :, t, 0:64], p_.t[:, :].rearrange("p (h d) -> p h d", d=64), [p_.r], [vm_.r])
                load("sp", Vm[:, :, 4 * ci:4 * ci + 4, :].rearrange("h p t e -> p h t e"), vm_.t[:], [], reads=[vm_.r])

            def proj_B1(k):
                gi = chunks[k][1]
                uTt = uT[k % 2]
                qs = nxt(st4, "st4")
                fm_proj(uTt, C_QSB, qs, scale=0.125)
                load("sp", QTs[:, :, gi * 512:(gi + 1) * 512].rearrange("a p n -> p a n"), qs.t[:], [], reads=[qs.r])
                for t in range(4):
                    for c in range(DC):
                        mm(pC.t[:, t, 0:256], uTt.t[:, c, t * 128:(t + 1) * 128], win.t[:, c, C_CQ:C_CQ + 256],
                           c == 0, c == DC - 1, [win.r, uTt.r], [pC.r])
                s4 = ssc[k % 2]; r4 = rsc[k % 2]
                for t in range(4):
                    P.op("act", lambda e, t=t: e.activation(out=junk.t[:, 0:256], in_=pC.t[:, t, 0:256], func=AF.Square, accum_out=s4.t[:, t:t + 1]),
                         reads=[pC.r], writes=[junk.r, s4.r])
                rstd_from_ss(s4.t[:], r4.t[:], s4.r, r4.r, 256)

            def proj_B2(k):
                gi = chunks[k][1]
                r4 = rsc[k % 2]
                cqt = cqT[k % 2]
                for t in range(4):
                    cq = cqn[t]
                    P.op("dve", lambda e, cq=cq, t=t: e.scalar_tensor_tensor(out=cq.t[:], in0=pC.t[:, t, 0:256], scalar=r4.t[:, t:t + 1],
                                                                       in1=gm["q"].t[:], op0=ALU.mult, op1=ALU.mult),
                         reads=[pC.r, r4.r, gm["q"].r], writes=[cq.r])
                pt = nxt(pT, "pT")
                for t in range(4):
                    for k2 in range(2):
                        tr(pt.t[:, 2 * t + k2, :], cqn[t].t[:, k2 * 128:(k2 + 1) * 128], idb.t[:], [cqn[t].r, idb.r], [pt.r])
                for k2 in range(2):
                    copy_op(evac_eng(), cqt.t[:, k2, :].rearrange("p (t n) -> p t n", n=128),
                            pt.t[:, :, :].rearrange("p (t k) n -> p t k n", k=2)[:, :, k2, :], [pt.r], [cqt.r])
                qm = qmst[k % 2]
                for t in range(4):
                    pos = 4 * gi + t
                    qt = qtok[t % 2]
                    qr = qrt[t % 2]
                    pa = nxt(pP, "pP")
                    for k2 in range(2):
                        mm(pa.t[:, 0:512], cqt.t[:, k2, t * 128:(t + 1) * 128], wuq.t[:, k2 * 768:k2 * 768 + 512],
                           k2 == 0, k2 == 1, [wuq.r, cqt.r], [pa.r])
                    for k2 in range(2):
                        mm(pQ.t[:, 0:256], cqt.t[:, k2, t * 128:(t + 1) * 128], wuq.t[:, k2 * 768 + 512:k2 * 768 + 768],
                           k2 == 0, k2 == 1, [wuq.r, cqt.r], [pQ.r])
                    qf = junk
                    copy_op("act", qf.t[:, 0:512], pa.t[:, 0:512], [pa.r], [qf.r])
                    copy_op("dve", qf.t[:, 512:768], pQ.t[:, 0:256], [pQ.r], [qf.r])
                    q3 = qf.t[:, 0:768].rearrange("p (h d) -> p h d", d=96)
                    copy_op("act", qt.t[:, :, 0:64], q3[:, :, 0:64], [qf.r], [qt.r])
                    rope_tok(q3[:, :, 64:80], q3[:, :, 80:96], coso.t[:, pos, :].unsqueeze(1).broadcast_to([128, H, 16]),
                             sino.t[:, pos, :].unsqueeze(1).broadcast_to([128, H, 16]),
                             qt.t[:, :, 64:80], qt.t[:, :, 80:96], qr.t[:, 0, :, :], qr.t[:, 1, :, :],
                             [qf.r, coso.r, sino.r], qr.r, qt.r)
                    pt = nxt(pT, "pT")
                    for h in range(H):
                        tr(pt.t[0:96, h, :], qt.t[:, h, :], idb.t[:], [qt.r, idb.r], [pt.r])
                    copy_op(evac_eng(), qm.t[0:96, :, t * 128:(t + 1) * 128], pt.t[0:96, :, :], [pt.r], [qm.r])
                load("sp", QTm[:, :, gi * 512:(gi + 1) * 512].rearrange("h r n -> r h n"), qm.t[0:96, :, :], [], reads=[qm.r])

            NK = len(chunks)
            front_loads(0)
            if NK > 1:
                front_loads(1)
            front_stats(0)
            front_norm_T(0)
            for k in range(NK):
                if k + 2 < NK:
                    front_loads(k + 2)
                if k + 1 < NK:
                    front_stats(k + 1)
                (proj_A1 if chunks[k][0] == "A" else proj_B1)(k)
                if k + 1 < NK:
                    front_norm_T(k + 1)
                (proj_A2 if chunks[k][0] == "A" else proj_B2)(k)
            P.end_phase()
            es.__exit__(None, None, None)
        if stop_after == "AB":
            return nc, dict(KTm=KTm, KR=KR, Vm=Vm, KTs=KTs, Vs=Vs, QTm=QTm, QTs=QTs)

        merged = sb(g, "merged", [128, NBO, D], F32)
        mres = [Res("m") for _ in range(NBO)]
        with ExitStack() as es:
            trib = sb(es, "trib", [128, 2, 128], BF16)
            maskb = sb(es, "maskb", [128, 16 * 128], BF16)
            mst = sb(es, "mst", [128, 2048], F32)
            mst2 = sb(es, "mst2", [128, 256], F32)
            load("sp", mst.t[:, 0:2048], mask_d[:, :], [mst.r])
            P.op("pool", lambda e: e.tensor_copy(out=maskb.t[:], in_=mst.t[:, 0:2048]), reads=[mst.r], writes=[maskb.r])
            load("sp", mst2.t[:, 0:256], tri_d.rearrange("p a b -> p (a b)"), [mst2.r])
            P.op("pool", lambda e: e.tensor_copy(out=trib.t[:].rearrange("p a b -> p (a b)"), in_=mst2.t[:, 0:256]),
                 reads=[mst2.r], writes=[trib.r])
            Kb = [sb(es, "Kb%d" % i, [128, S], BF16) for i in range(2)]
            Vb = [sb(es, "Vb%d" % i, [128, NB, 65], BF16) for i in range(2)]
            Qb = [sb(es, "Qb%d" % i, [128, SO], BF16) for i in range(2)]
            eb2 = [sb(es, "eb2_%d" % i, [128, 2, 512], F32) for i in range(4)]
            lb2 = [sb(es, "lb2_%d" % i, [128, 2, 512], BF16) for i in range(2)]
            gb2 = [sb(es, "gb2_%d" % i, [128, 2, 512], F32) for i in range(2)]
            ab2 = [sb(es, "ab2_%d" % i, [128, 2, 512], BF16) for i in range(2)]
            ab = [sb(es, "ab%d" % i, [128, 512], BF16) for i in range(2)]
            ot = [sb(es, "ot%d" % i, [128, 512], F32) for i in range(2)]
            ot2 = sb(es, "ot2", [128, 4, 128], F32)
            rinv = [sb(es, "rinv%d" % i, [128, 4, 1], F32) for i in range(2)]
            zb2 = [ps(es, "zb2_%d" % i, [128, 2, 512], F32) for i in range(2)]
            zres = [[Res("z", True), Res("z", True)] for _ in range(2)]
            cb = ps(es, "cb", [128, 512], F32)
            oacc = [ps(es, "oacc%d" % i, [128, 512], F32) for i in range(2)]
            tp = ps(es, "tp", [128, 4, 128], F32)
            mk4 = maskb.t[:].rearrange("p (a b c q) -> p a b c q", a=2, b=2, c=4)
            halfpos = NBO // 2

            jobs = [("m", h) for h in range(H)] + [("s", h) for h in range(H)]

            def job_loads(k):
                typ, h = jobs[k]
                sl = k % 2
                Kt, Vt, Qt = Kb[sl], Vb[sl], Qb[sl]
                r0 = (h % 2) * 64
                if typ == "m":
                    load("sp", Kt.t[0:64, :], KTm[h // 2, r0:r0 + 64, :], [Kt.r])
                    load("sp", Kt.t[64:96, :], KR[:, :], [Kt.r])
                    load("sp", Vt.t[:], Vm[h], [Vt.r])
                    load("sp", Qt.t[0:96, :], QTm[h], [Qt.r])
                else:
                    load("sp", Kt.t[0:64, :], KTs[h // 2, r0:r0 + 64, :], [Kt.r])
                    load("sp", Kt.t[64:128, :], KTs[h // 2, r0:r0 + 64, :], [Kt.r])
                    load("sp", Vt.t[:], Vs[h], [Vt.r])
                    load("sp", Qt.t[0:64, :], QTs[h // 2, r0:r0 + 64, :], [Qt.r])
                    load("sp", Qt.t[64:128, :], QTs[h // 2, r0:r0 + 64, :], [Qt.r])

            gctr = [0]

            def tiles_for(gi):
                out = []
                for kb in range(16 * gi + 15, -1, -1):
                    pm = kb // 4
                    if pm >= 4 * gi:
                        lo = pm - 4 * gi
                        out.append((kb, lo * 128, True, 0 if pm < halfpos else 1, kb % 4))
                    else:
                        out.append((kb, 0, False, 0, 0))
                return out

            def finish_group(typ, h, gi, oa):
                o_ = ot[gctr[0] % 2]
                rv = rinv[gctr[0] % 2]
                nr = 65 if typ == "m" else 64
                copy_op("dve", o_.t[0:nr, :], oa.t[0:nr, :], [oa.r], [o_.r])
                for t in range(4):
                    tr(tp.t[:, t, 0:nr], o_.t[0:nr, t * 128:(t + 1) * 128], idf.t[0:nr, 0:nr], [o_.r, idf.r], [tp.r])
                if typ == "m":
                    P.op("dve", lambda e: e.reciprocal(out=rv.t[:], in_=tp.t[:, :, 64:65]), reads=[tp.r], writes=[rv.r])
                    for t in range(4):
                        P.op("dve", lambda e, t=t: e.tensor_scalar(out=merged.t[:, 4 * gi + t, h * 64:(h + 1) * 64], in0=tp.t[:, t, 0:64],
                                                                   scalar1=rv.t[:, t, :], scalar2=None, op0=ALU.mult),
                             reads=[tp.r, rv.r], writes=[mres[4 * gi + t]])
                else:
                    P.op("dve", lambda e: e.tensor_copy(out=merged.t[:, 4 * gi:4 * gi + 4, 512 + h * 64:512 + (h + 1) * 64], in_=tp.t[:, :, 0:64]),
                         reads=[tp.r], writes=[mres[4 * gi + t_] for t_ in range(4)])

            def zero_bank(bank, Qt):
                mm(bank.t[:, :], zerob.t[:, :], maskb.t[:, 0:512], True, True, [zerob.r, maskb.r], [bank.r])

            def run_mla(k, h):
                sl = k % 2
                Kt, Vt, Qt = Kb[sl], Vb[sl], Qb[sl]
                for gi in range(NG):
                    oa = oacc[gctr[0] % 2]
                    zero_bank(oa, Qt)
                    tl = tiles_for(gi)
                    npair = len(tl) // 2

                    def S2(p, tl=tl, gi=gi):
                        (kba, c0, msk, hf, ra), (kbb, c0b, mskb, hfb, rb) = tl[2 * p], tl[2 * p + 1]
                        assert c0 == c0b and msk == mskb and hf == hfb
                        zt = zb2[p % 2].t; zr = zres[p % 2]
                        q0 = gi * 512 + c0
                        for j, kb_, r_ in ((0, kba, ra), (1, kbb, rb)):
                            mm(zt[:, j, c0:512], Kt.t[0:96, kb_ * 128:(kb_ + 1) * 128], Qt.t[0:96, q0:(gi + 1) * 512],
                               True, not msk, [Kt.r, Qt.r], [zr[j]])
                            if msk:
                                mm(zt[:, j, c0:c0 + 128], idb.t[:, :], mk4[:, 0, hf, r_, :], False, True, [idb.r, maskb.r], [zr[j]])

                    def P2(p, tl=tl, oa=oa):
                        (kba, c0, msk, hf, ra), (kbb, _, _, _, rb) = tl[2 * p], tl[2 * p + 1]
                        zt = zb2[p % 2].t; zr = zres[p % 2]
                        a_ = ab2[p % 2]
                        o_ap = a_.t[:, :, c0:512]; i_ap = zt[:, :, c0:512]
                        P.op("act", lambda e, o_ap=o_ap, i_ap=i_ap: e.activation(out=o_ap, in_=i_ap, func=AF.Exp, scale=MLA_SCALE),
                             reads=[zr[0], zr[1]], writes=[a_.r])
                        mm(oa.t[0:65, c0:512], Vt.t[:, kba, 0:65], a_.t[:, 0, c0:512], False, False, [Vt.r, a_.r], [oa.r], skip=True)
                        mm(oa.t[0:65, c0:512], Vt.t[:, kbb, 0:65], a_.t[:, 1, c0:512], False, False, [Vt.r, a_.r], [oa.r], skip=True)

                    S2(0)
                    for p in range(npair):
                        if p + 1 < npair:
                            S2(p + 1)
                        P2(p)
                    finish_group("m", h, gi, oa)
                    gctr[0] += 1

            def finish_sb(h, gi, oa):
                o_ = ot[gctr[0] % 2]
                copy_op("dve", o_.t[:, :], oa.t[:, :], [oa.r], [o_.r])
                for t in range(4):
                    tr(tp.t[:, t, :], o_.t[:, t * 128:(t + 1) * 128], idf.t[:, :], [o_.r, idf.r], [tp.r])
                copy_op("dve", ot2.t[:], tp.t[:], [tp.r], [ot2.r])
                P.op("dve", lambda e: e.tensor_tensor(out=merged.t[:, 4 * gi:4 * gi + 4, 512 + h * 64:512 + (h + 1) * 64],
                                                      in0=ot2.t[:, :, 0:64], in1=ot2.t[:, :, 64:128], op=ALU.add),
                     reads=[ot2.r], writes=[mres[4 * gi + t_] for t_ in range(4)])

            def run_sb(k, h):
                sl = k % 2
                Kt, Vt, Qt = Kb[sl], Vb[sl], Qb[sl]
                for gi in range(NG):
                    oa = oacc[gctr[0] % 2]
                    zero_bank(oa, Qt)
                    zero_bank(cb, Qt)
                    tl = tiles_for(gi)
                    npair = len(tl) // 2

                    def pr_(p):
                        ta, tb_ = tl[2 * p], tl[2 * p + 1]
                        assert ta[1] == tb_[1] and ta[2] == tb_[2] and ta[3] == tb_[3]
                        return ta, tb_

                    def Zp(p):
                        (kba, c0, msk, hf, ra), (kbb, _, _, _, rb) = pr_(p)
                        zt = zb2[p % 2].t; zr = zres[p % 2]
                        q0 = gi * 512 + c0
                        mm(zt[:, 0, c0:512], Kt.t[0:64, kba * 128:(kba + 1) * 128], Qt.t[0:64, q0:(gi + 1) * 512],
                           True, not msk, [Kt.r, Qt.r], [zr[0]])
                        mm(zt[:, 1, c0:512], Kt.t[64:128, kbb * 128:(kbb + 1) * 128], Qt.t[64:128, q0:(gi + 1) * 512],
                           True, not msk, [Kt.r, Qt.r], [zr[1]])
                        if msk:
                            mm(zt[:, 0, c0:c0 + 128], idb.t[:, :], mk4[:, 1, hf, ra, :], False, True, [idb.r, maskb.r], [zr[0]])
                            mm(zt[:, 1, c0:c0 + 128], idb.t[:, :], mk4[:, 1, hf, rb, :], False, True, [idb.r, maskb.r], [zr[1]])

                    def Ep(p):
                        c0 = pr_(p)[0][1]
                        zt = zb2[p % 2].t; zr = zres[p % 2]; e_ = eb2[p % 4]
                        P.op("act", lambda e: e.activation(out=e_.t[:, :, c0:512], in_=zt[:, :, c0:512], func=AF.Exp),
                             reads=[zr[0], zr[1]], writes=[e_.r])

                    def Lp(p):
                        c0 = pr_(p)[0][1]
                        e_ = eb2[p % 4]; l_ = lb2[p % 2]
                        P.op("act", lambda e: e.activation(out=l_.t[:, :, c0:512], in_=e_.t[:, :, c0:512], func=AF.Ln, bias=onec.t[:]),
                             reads=[e_.r, onec.r], writes=[l_.r])

                    def Tri(p, j):
                        c0 = pr_(p)[0][1]
                        l_ = lb2[p % 2]
                        mm(cb.t[:, c0:512], trib.t[:, 0, :], l_.t[:, j, c0:512], False, False, [trib.r, l_.r], [cb.r], skip=True)

                    def G(p, j):
                        c0 = pr_(p)[0][1]
                        g_ = gb2[p % 2]
                        P.op("act", lambda e: e.activation(out=g_.t[:, j, c0:512], in_=cb.t[:, c0:512], func=AF.Exp, scale=-1.0),
                             reads=[cb.r], writes=[g_.r])

                    def OmT(p, j):
                        c0 = pr_(p)[0][1]
                        l_ = lb2[p % 2]
                        mm(cb.t[:, c0:512], trib.t[:, 1, :], l_.t[:, j, c0:512], False, False, [trib.r, l_.r], [cb.r], skip=True)

                    def Amul(p):
                        c0 = pr_(p)[0][1]
                        g_ = gb2[p % 2]; a_ = ab2[p % 2]; e_ = eb2[p % 4]
                        P.op("dve", lambda e: e.tensor_tensor(out=a_.t[:, :, c0:512], in0=e_.t[:, :, c0:512], in1=g_.t[:, :, c0:512], op=ALU.mult),
                             reads=[e_.r, g_.r], writes=[a_.r])

                    def AVp(p):
                        (kba, c0, _, _, _), (kbb, _, _, _, _) = pr_(p)
                        a_ = ab2[p % 2]
                        mm(oa.t[0:64, c0:512], Vt.t[:, kba, 0:64], a_.t[:, 0, c0:512], False, False, [Vt.r, a_.r], [oa.r], skip=True)
                        o_ap = oa.t[64:128, c0:512]; l_ap = Vt.t[:, kbb, 0:64]; r_ap = a_.t[:, 1, c0:512]
                        P.op("pe", lambda e, o_ap=o_ap, l_ap=l_ap, r_ap=r_ap: e.matmul(o_ap, lhsT=l_ap, rhs=r_ap, start=False, stop=False,
                                                                                     skip_group_check=True, tile_position=(0, 64)),
                             reads=[Vt.r, a_.r], writes=[oa.r])

                    Zp(0)
                    if npair > 1:
                        Zp(1)
                    Ep(0)
                    if npair > 2:
                        Zp(2)
                    if npair > 1:
                        Ep(1)
                    Lp(0)
                    for p in range(npair + 1):
                        if 0 <= p - 1 < npair:
                            OmT(p - 1, 1)
                        if p < npair:
                            Tri(p, 0)
                            G(p, 0)
                        if p + 3 < npair:
                            Zp(p + 3)
                        if p + 1 < npair:
                            Lp(p + 1)
                        if 0 <= p - 1 < npair:
                            Amul(p - 1)
                            AVp(p - 1)
                        if p < npair:
                            OmT(p, 0)
                            Tri(p, 1)
                            G(p, 1)
                        if p + 2 < npair:
                            Ep(p + 2)
                    finish_sb(h, gi, oa)
                    gctr[0] += 1

            job_loads(0)
            for k in range(len(jobs)):
                if k + 1 < len(jobs):
                    job_loads(k + 1)
                typ, h = jobs[k]
                if typ == "m":
                    run_mla(k, h)
                else:
                    run_sb(k, h)
            P.end_phase()
        if stop_after == "C":
            return nc, dict(merged=merged)

        with ExitStack() as es:
            fT = sb(es, "fT", [128, DC, SO], BF16)
            load_gains(es, ("ffn", "fin"))
            g_o = sb(es, "g_o_sb", [128, DC], F32)
            load("sp", g_o.t[:], g_o_d[:, :], [g_o.r])
            sst = sb(es, "sst", [128, NBO, 2], F32)
            rst = sb(es, "rst", [128, NBO, 2], F32)
            junk = sb(es, "junkd", [128, D], F32)
            wg = [sb(es, "wg0", [128, DC, 512], BF16), None]
            wu = [sb(es, "wu0", [128, DC, 512], BF16), None]
            wd = [sb(es, "wd0", [128, 4, D], BF16), None]
            wstf = [sb(es, "wstf%d" % i, [128, 1024], F32) for i in range(2)]
            NFG = (FC + 3) // 4
            wctr = [0]

            def stage_cast(dst_ap, src_ap, n, dst_r):
                i = wctr[0]; wctr[0] += 1
                st = wstf[i % 2]
                load("sp", st.t[:, 0:n], src_ap, [st.r])
                copy_op("pool", dst_ap, st.t[:, 0:n], [st.r], [dst_r])

            def ffn_load_list(fg):
                f0 = fg * 512
                nf = min(512, DFF - f0)
                sl = fg % 2
                lst = []
                for c in range(DC):
                    lst.append((wg[sl].t[:, c, 0:nf], w_g_d[:, c, f0:f0 + nf], nf, wg[sl].r))
                    lst.append((wu[sl].t[:, c, 0:nf], w_u_d[:, c, f0:f0 + nf], nf, wu[sl].r))
                for fc in range(nf // 128):
                    lst.append((wd[sl].t[:, fc, :], w_d_d[:, fg * 4 + fc, :], D, wd[sl].r))
                return lst

            def ffn_loads(fg):
                for a_ in ffn_load_list(fg):
                    stage_cast(*a_)

            with ExitStack() as e1:
                wo = sb(e1, "wo", [128, DC, D], BF16)
                mn = [sb(e1, "mn%d" % i, [128, D], BF16) for i in range(3)]
                mT = [sb(e1, "mT%d" % i, [128, DC, 128], BF16) for i in range(3)]
                xs2 = [sb(e1, "xs2_%d" % i, [128, D], F32) for i in range(3)]
                pT = [ps(e1, "pTd%d" % i, [128, DC, 128], BF16) for i in range(3)]
                pO = [ps(e1, "pO%d" % i, [128, 1024], F32) for i in range(2)]
                for c in range(DC):
                    st = wstf[c % 2]
                    load("sp", st.t[:, 0:D], w_o_d[:, c, :], [st.r])
                    P.op("act", lambda e, st=st, c=c: e.activation(out=wo.t[:, c, :], in_=st.t[:, 0:D], func=AF.Copy, scale=g_o.t[:, c:c + 1]),
                         reads=[st.r, g_o.r], writes=[wo.r])
                for t in range(NBO):
                    for hf in range(2):
                        P.op("act", lambda e, t=t, hf=hf: e.activation(out=junk.t[:, 0:512], in_=merged.t[:, t, hf * 512:(hf + 1) * 512],
                                                                       func=AF.Square, accum_out=sst.t[:, t, hf:hf + 1]),
                             reads=[mres[t]], writes=[junk.r, sst.r])
                rstd_from_ss(sst.t[:], rst.t[:], sst.r, rst.r, 512)
                pre0 = ffn_load_list(0)

                def prep(t):
                    m_ = mn[t % 3]; mt = mT[t % 3]; pt = pT[t % 3]; xs = xs2[t % 3]
                    load("sp", xs.t[:], xo[t * 128:(t + 1) * 128, :], [xs.r])
                    for hf in range(2):
                        P.op("act", lambda e, t=t, hf=hf, m_=m_: e.activation(
                            out=m_.t[:, hf * 512:(hf + 1) * 512], in_=merged.t[:, t, hf * 512:(hf + 1) * 512],
                            func=AF.Copy, scale=rst.t[:, t, hf:hf + 1]),
                            reads=[mres[t], rst.r], writes=[m_.r])
                    for c in range(DC):
                        tr(pt.t[:, c, :], m_.t[:, c * 128:(c + 1) * 128], idb.t[:], [m_.r, idb.r], [pt.r])
                    copy_op("act", mt.t[:], pt.t[:], [pt.r], [mt.r])

                def fin(t):
                    mt = mT[t % 3]; po = pO[t % 2]; xs = xs2[t % 3]
                    for nh in range(2):
                        for c in range(DC):
                            mm(po.t[:, nh * 512:(nh + 1) * 512], mt.t[:, c, :], wo.t[:, c, nh * 512:(nh + 1) * 512],
                               c == 0, c == DC - 1, [mt.r, wo.r], [po.r])
                    P.op("dve", lambda e, t=t, po=po, xs=xs: e.tensor_tensor(out=merged.t[:, t, :], in0=po.t[:, :], in1=xs.t[:], op=ALU.add),
                         reads=[po.r, xs.r], writes=[mres[t]])

                def fin_sq(t):
                    P.op("act", lambda e, t=t: e.activation(out=junk.t[:], in_=merged.t[:, t, :], func=AF.Square, accum_out=sst.t[:, t, 0:1]),
                         reads=[mres[t]], writes=[junk.r, sst.r])

                prep(0)
                if NBO > 1:
                    prep(1)
                for t in range(NBO):
                    if t + 2 < NBO:
                        prep(t + 2)
                    for _ in range(2):
                        if pre0:
                            stage_cast(*pre0.pop(0))
                    fin(t)
                    if t >= 1:
                        fin_sq(t - 1)
                fin_sq(NBO - 1)
                while pre0:
                    stage_cast(*pre0.pop(0))
                rstd_from_ss(sst.t[:], rst.t[:], sst.r, rst.r, D)
                for t in range(NBO):
                    m_ = mn[t % 2]; pt = pT[t % 2]
                    P.op("dve", lambda e, t=t, m_=m_: e.scalar_tensor_tensor(out=m_.t[:], in0=merged.t[:, t, :], scalar=rst.t[:, t, 0:1],
                                                                       in1=gm["ffn"].t[:], op0=ALU.mult, op1=ALU.mult),
                         reads=[mres[t], rst.r, gm["ffn"].r], writes=[m_.r])
                    for c in range(DC):
                        tr(pt.t[:, c, :], m_.t[:, c * 128:(c + 1) * 128], idb.t[:], [m_.r, idb.r], [pt.r])
                    copy_op("act" if t % 2 == 0 else "dve", fT.t[:, :, t * 128:(t + 1) * 128], pt.t[:], [pt.r], [fT.r])
                P.end_phase()

            with ExitStack() as e2:
                wg[1] = sb(e2, "wg1", [128, DC, 512], BF16)
                wu[1] = sb(e2, "wu1", [128, DC, 512], BF16)
                wd[1] = sb(e2, "wd1", [128, 4, D], BF16)
                aT = [sb(e2, "aT%d" % i, [128, 4, 512], BF16) for i in range(2)]
                sg = [sb(e2, "sg%d" % i, [128, 512], F32) for i in range(2)]
                yt = [sb(e2, "yt%d" % i, [128, D], F32) for i in range(3)]
                pg = [ps(e2, "pg%d" % i, [128, 512], F32) for i in range(2)]
                pu = [ps(e2, "pu%d" % i, [128, 512], F32) for i in range(2)]
                pd = [ps(e2, "pd%d" % i, [128, 1024], F32) for i in range(2)]
                k = 0
                pend = None

                def down(k_, sl_, nfc_, tg_):
                    a_ = aT[k_ % 2]
                    for tt in range(4):
                        t = tg_ * 4 + tt
                        pd_ = pd[(k_ * 4 + tt) % 2]
                        for nh in range(2):
                            for fc in range(nfc_):
                                mm(pd_.t[:, nh * 512:(nh + 1) * 512], a_.t[:, fc, tt * 128:(tt + 1) * 128], wd[sl_].t[:, fc, nh * 512:(nh + 1) * 512],
                                   fc == 0, fc == nfc_ - 1, [a_.r, wd[sl_].r], [pd_.r])
                        P.op("dve", lambda e, t=t, pd_=pd_: e.tensor_tensor(out=merged.t[:, t, :], in0=pd_.t[:, :], in1=merged.t[:, t, :], op=ALU.add),
                             reads=[pd_.r, mres[t]], writes=[mres[t]])

                if NFG > 1:
                    ffn_loads(1)
                for fg in range(NFG):
                    f0 = fg * 512
                    nfc = min(512, DFF - f0) // 128
                    sl = fg % 2
                    for tg in range(NG):
                        a_ = aT[k % 2]
                        for fc in range(nfc):
                            pg_ = pg[(k * 4 + fc) % 2]; pu_ = pu[(k * 4 + fc) % 2]; sg_ = sg[(k * 4 + fc) % 2]
                            for c in range(DC):
                                mm(pg_.t[:, :], wg[sl].t[:, c, fc * 128:(fc + 1) * 128], fT.t[:, c, tg * 512:(tg + 1) * 512],
                                   c == 0, c == DC - 1, [wg[sl].r, fT.r], [pg_.r])
                            for c in range(DC):
                                mm(pu_.t[:, :], wu[sl].t[:, c, fc * 128:(fc + 1) * 128], fT.t[:, c, tg * 512:(tg + 1) * 512],
                                   c == 0, c == DC - 1, [wu[sl].r, fT.r], [pu_.r])
                            P.op("act", lambda e, pg_=pg_, sg_=sg_: e.activation(out=sg_.t[:], in_=pg_.t[:, :], func=AF.Silu),
                                 reads=[pg_.r], writes=[sg_.r])
                            P.op("dve", lambda e, pu_=pu_, sg_=sg_, a_=a_, fc=fc: e.tensor_tensor(out=a_.t[:, fc, :], in0=pu_.t[:, :], in1=sg_.t[:],
                                                                                             op=ALU.mult),
                                 reads=[pu_.r, sg_.r], writes=[a_.r])
                        if pend is not None:
                            down(*pend)
                        pend = (k, sl, nfc, tg)
                        k += 1
                        if tg == 0 and fg >= 1 and fg + 1 < NFG:
                            ffn_loads(fg + 1)
                down(*pend)
                for t in range(NBO):
                    P.op("act", lambda e, t=t: e.activation(out=junk.t[:], in_=merged.t[:, t, :], func=AF.Square, accum_out=sst.t[:, t, 0:1]),
                         reads=[mres[t]], writes=[junk.r, sst.r])
                rstd_from_ss(sst.t[:], rst.t[:], sst.r, rst.r, D)
                for t in range(NBO):
                    y_ = yt[t % 3]
                    P.op("dve", lambda e, t=t, y_=y_: e.scalar_tensor_tensor(out=y_.t[:], in0=merged.t[:, t, :], scalar=rst.t[:, t, 0:1],
                                                                       in1=gm["fin"].t[:], op0=ALU.mult, op1=ALU.mult),
                         reads=[mres[t], rst.r, gm["fin"].r], writes=[y_.r])
                    load("sp", y[t * 128:(t + 1) * 128, :], y_.t[:], [], reads=[y_.r])
                P.end_phase()
    return nc, {}


def host_inputs(inputs, S):
    NB = S // 128
    f = np.float32
    x = np.asarray(inputs["x"], f)
    pos = np.asarray(inputs["positions"], np.int32)

    def kchunk(w):
        K, E = w.shape
        return np.ascontiguousarray(w.reshape(K // 128, 128, E).transpose(1, 0, 2))

    def rep(v):
        return np.ascontiguousarray(np.broadcast_to(np.asarray(v, f).reshape(1, -1), (128, v.size)))

    w_in = kchunk(np.asarray(inputs["w_in"], f)[0])
    w_uq = kchunk(np.asarray(inputs["w_uq"], f)[0]).reshape(128, 2 * 768)
    wukv = np.asarray(inputs["w_ukv"], f)[0].reshape(128, H, 128)
    w_uk = np.ascontiguousarray(wukv[:, :, 0:64].reshape(128, 512))
    w_uv = np.ascontiguousarray(wukv[:, :, 64:128].reshape(128, 512))
    w_o = kchunk(np.asarray(inputs["w_o"], f)[0])
    w_g = kchunk(np.asarray(inputs["w_gate"], f)[0])
    w_u = kchunk(np.asarray(inputs["w_up"], f)[0])
    w_d = kchunk(np.asarray(inputs["w_down"], f)[0])
    ident = np.eye(128, dtype=f)
    jj = np.arange(128)[:, None]
    ss_ = np.arange(128)[None, :]
    tri = np.stack([(jj >= ss_).astype(f), (jj < ss_).astype(f)], axis=1)
    causal = (jj <= ss_).astype(f)
    strict = (jj < ss_).astype(f)
    common = dict(ident=ident, tri=np.ascontiguousarray(tri), w_in=w_in, w_uq=w_uq, w_uk=w_uk, w_uv=w_uv, w_o=w_o,
                  w_gate=w_g, w_up=w_u, w_down=w_d,
                  g_mix=rep(inputs["norm_mix"][0]), g_q=rep(inputs["q_latent_norm"][0]), g_kv=rep(inputs["kv_latent_norm"][0]),
                  g_mla=rep(inputs["out_norm_mla"][0]), g_sb=rep(inputs["out_norm_sb"][0]), g_ffn=rep(inputs["norm_ffn"][0]),
                  g_fin=rep(inputs["norm_final"]),
                  g_o=np.ascontiguousarray(np.concatenate([np.asarray(inputs["out_norm_mla"], f)[0],
                                                           np.asarray(inputs["out_norm_sb"], f)[0]]).reshape(DC, 128).T))
    maps = []
    for c in range(8):
        b, j = c // 4, c % 4
        ob = own_blocks(j, NB)
        rows = np.concatenate([np.arange(k * 128, (k + 1) * 128) for k in ob])
        masks = np.zeros((128, 2, 2, 4, 128), f)
        for ti, tm in enumerate((causal, strict)):
            for hf, off in enumerate((j, 3 - j)):
                for r in range(4):
                    if r < off:
                        masks[:, ti, hf, r, :] = 0.0
                    elif r == off:
                        masks[:, ti, hf, r, :] = NEG * (1.0 - tm)
                    else:
                        masks[:, ti, hf, r, :] = NEG
        m = dict(common)
        m["xb"] = np.ascontiguousarray(x[b])
        m["xo"] = np.ascontiguousarray(x[b][rows])
        m["posb"] = np.ascontiguousarray(pos[b].reshape(NB, 128).T)
        m["poso"] = np.ascontiguousarray(pos[b][rows].reshape(len(ob), 128).T)
        m["masks"] = masks.reshape(128, 16 * 128)
        maps.append(m)
    return maps


def assemble(results, S, B=2):
    NB = S // 128
    out = np.zeros((B, S, D), np.float32)
    for c in range(8):
        b, j = c // 4, c % 4
        ob = own_blocks(j, NB)
        yy = np.asarray(results[c]["y"], np.float32)
        for i, k in enumerate(ob):
            out[b, k * 128:(k + 1) * 128, :] = yy[i * 128:(i + 1) * 128, :]
    return out


_NC_CACHE = {}


def kernel(**inputs):
    S = int(np.asarray(inputs["x"]).shape[1])
    if S not in _NC_CACHE:
        _NC_CACHE[S] = build(S, Prog, Res)[0]
    nc = _NC_CACHE[S]
    maps = host_inputs(inputs, S)
    res = run_bass_kernel_spmd(nc, maps, core_ids=list(range(8)))
    return assemble(res.results, S, B=int(np.asarray(inputs["x"]).shape[0]))
```

```python
import math
from contextlib import ExitStack
import numpy as np
import concourse.bass as bass
import concourse.mybir as mybir
from concourse.bass_utils import run_bass_kernel_spmd

F32 = mybir.dt.float32
BF16 = mybir.dt.bfloat16
I32 = mybir.dt.int32
AF = mybir.ActivationFunctionType
ALU = mybir.AluOpType


ENGS = ("pe", "act", "dve", "pool", "sp")


class Res:
    __slots__ = ("name", "w", "r", "excl")

    def __init__(self, name, excl=False):
        self.name = name
        self.w = None
        self.r = []
        self.excl = excl


class Op:
    __slots__ = ("fn", "waits", "sig", "dma", "idx")

    def __init__(self, fn, waits, dma, idx):
        self.fn = fn
        self.waits = waits
        self.sig = False
        self.dma = dma
        self.idx = idx


class Prog:
    NDMA = 40

    def __init__(self, nc):
        self.nc = nc
        self.sem = {e: nc.alloc_semaphore("s_" + e) for e in ENGS}
        self.dsem = [nc.alloc_semaphore("d%d" % i) for i in range(self.NDMA)]
        self.dcount = [0] * self.NDMA
        self.dnext = 0
        self.ops = {e: [] for e in ENGS}
        self.start = {e: 0 for e in ENGS}
        self.base = {e: 0 for e in ENGS}
        self.seen = {e: {x: 0 for x in ENGS} for e in ENGS}
        self.seen_d = {e: [0] * self.NDMA for e in ENGS}

    def _deps(self, reads, writes):
        ev = []
        for r in reads:
            if r.excl:
                writes = list(writes) + [r]
                continue
            if r.w is not None:
                ev.append(r.w)
        for w in writes:
            if w.w is not None:
                ev.append(w.w)
            ev.extend(w.r)
        return ev

    def _filter(self, eng, evs):
        out = []
        best = {}
        for e in evs:
            if e[0] == "c":
                _, x, i = e
                if x == "pe" and eng == "pe":
                    continue
                if i <= self.seen[eng][x]:
                    continue
                if i > best.get(("c", x), 0):
                    best[("c", x)] = i
            else:
                _, s, c = e
                if c <= self.seen_d[eng][s]:
                    continue
                if c > best.get(("d", s), 0):
                    best[("d", s)] = c
        for k, v in best.items():
            if k[0] == "c":
                self.seen[eng][k[1]] = v
                self.ops[k[1]][v - 1].sig = True
                out.append(("c", k[1], v))
            else:
                self.seen_d[eng][k[1]] = v
                out.append(("d", k[1], v))
        return out

    def _commit(self, ev, reads, writes):
        for r in reads:
            if r.excl:
                r.w = ev
                r.r = []
            else:
                r.r.append(ev)
        for w in writes:
            w.w = ev
            w.r = []

    def op(self, eng, fn, reads=(), writes=()):
        waits = self._filter(eng, self._deps(reads, writes))
        lst = self.ops[eng]
        o = Op(fn, waits, None, len(lst) + 1)
        lst.append(o)
        self._commit(("c", eng, o.idx), reads, writes)
        return o

    def dma(self, q, fn, reads=(), writes=()):
        s = self.dnext
        self.dnext = (self.dnext + 1) % self.NDMA
        evs = self._deps(reads, writes)
        if self.dcount[s] > 0:
            evs.append(("d", s, self.dcount[s]))
        waits = self._filter(q, evs)
        self.dcount[s] += 16
        lst = self.ops[q]
        o = Op(fn, waits, (s, self.dcount[s]), len(lst) + 1)
        lst.append(o)
        self._commit(("d", s, self.dcount[s]), reads, writes)
        return o

    def barrier(self):
        evs = [("d", s, self.dcount[s]) for s in range(self.NDMA) if self.dcount[s] > 0]
        waits = self._filter("sp", evs)
        lst = self.ops["sp"]
        o = Op(lambda e: e.nop(), waits, None, len(lst) + 1)
        lst.append(o)
        for x in ENGS:
            for s in range(self.NDMA):
                self.seen_d[x][s] = self.dcount[s]
            for y in ENGS:
                self.seen[x][y] = len(self.ops[y])

    def end_phase(self):
        self.barrier()
        self.replay()

    def replay(self):
        nc = self.nc
        sigcnt = {}
        for e in ENGS:
            c = self.base[e]
            arr = []
            for o in self.ops[e][self.start[e]:]:
                if o.sig:
                    c += 1
                arr.append(c)
            sigcnt[e] = arr

        def val(x, idx1):
            st = self.start[x]
            if idx1 - 1 < st:
                return self._hist[x][idx1 - 1]
            return sigcnt[x][idx1 - 1 - st]

        if not hasattr(self, "_hist"):
            self._hist = {e: [] for e in ENGS}

        def emit(ename, eng):
            for o in self.ops[ename][self.start[ename]:]:
                for w in o.waits:
                    if w[0] == "c":
                        eng.wait_ge(self.sem[w[1]], val(w[1], w[2]))
                    else:
                        eng.wait_ge(self.dsem[w[1]], w[2])
                ins = o.fn(eng)
                if o.dma is not None:
                    ins.then_inc(self.dsem[o.dma[0]], 16)
                elif o.sig:
                    ins.then_inc(self.sem[ename], 1)

        with nc.Block() as block:
            @block.tensor
            def _(t):
                emit("pe", t)

            @block.scalar
            def _(t):
                emit("act", t)

            @block.vector
            def _(t):
                emit("dve", t)

            @block.gpsimd
            def _(t):
                emit("pool", t)

            @block.sync
            def _(t):
                emit("sp", t)

        for e in ENGS:
            self._hist[e].extend(sigcnt[e])
            self.base[e] = sigcnt[e][-1] if sigcnt[e] else self.base[e]
            self.start[e] = len(self.ops[e])


D = 1024
DC = 8
H = 8
DFF = 2816
FC = 22
EPS = 1e-6
INW = 1952
C_CQ, C_CKV, C_KR, C_QSB, C_KSB, C_VSB = 0, 256, 384, 416, 928, 1440
MLA_SCALE = 1.0 / math.sqrt(96.0)
NEG = -30000.0


def own_blocks(j, NB):
    half = NB // 8
    return [4 * m + j for m in range(half)] + [4 * m + 3 - j for m in range(half, 2 * half)]


def build(S, Prog, Res, stop_after=None):
    NB = S // 128
    NBO = NB // 4
    NG = NBO // 4
    NCH = NB // 4
    SO = NBO * 128
    nc = bass.Bass("TRN2", target_bir_lowering=False)

    def din(name, shape, dt=F32):
        return nc.dram_tensor(name, list(shape), dt, kind="ExternalInput").ap()

    xb = din("xb", [S, D])
    xo = din("xo", [SO, D])
    posb = din("posb", [128, NB], I32)
    poso = din("poso", [128, NBO], I32)
    ident_d = din("ident", [128, 128])
    tri_d = din("tri", [128, 2, 128])
    w_in_d = din("w_in", [128, DC, INW])
    w_uq_d = din("w_uq", [128, 2 * 768])
    w_uk_d = din("w_uk", [128, 512])
    w_uv_d = din("w_uv", [128, 512])
    w_o_d = din("w_o", [128, DC, D])
    w_g_d = din("w_gate", [128, DC, DFF])
    w_u_d = din("w_up", [128, DC, DFF])
    w_d_d = din("w_down", [128, FC, D])
    g_d = {"mix": din("g_mix", [128, D]), "q": din("g_q", [128, 256]), "kv": din("g_kv", [128, 128]),
           "mla": din("g_mla", [128, 512]), "sb": din("g_sb", [128, 512]), "ffn": din("g_ffn", [128, D]),
           "fin": din("g_fin", [128, D])}
    g_o_d = din("g_o", [128, DC])
    mask_d = din("masks", [128, 16 * 128])
    y = nc.dram_tensor("y", [SO, D], F32, kind="ExternalOutput").ap()

    KTm = nc.dram_tensor("KTm", [4, 128, S], BF16).ap()
    KR = nc.dram_tensor("KR", [32, S], BF16).ap()
    Vm = nc.dram_tensor("Vm", [H, 128, NB, 65], BF16).ap()
    KTs = nc.dram_tensor("KTs", [4, 128, S], BF16).ap()
    Vs = nc.dram_tensor("Vs", [H, 128, NB, 65], BF16).ap()
    QTm = nc.dram_tensor("QTm", [H, 96, SO], BF16).ap()
    QTs = nc.dram_tensor("QTs", [4, 128, SO], BF16).ap()

    P = Prog(nc)
    inv_freq = (10000.0 ** (-np.arange(0, 32, 2, dtype=np.float32) / np.float32(32))).astype(np.float32)

    class T:
        def __init__(self, t, excl=False):
            self.t = t
            self.r = Res("r", excl)

    def sb(es, name, shape, dt):
        return T(es.enter_context(nc.sbuf_tensor(name, list(shape), dt)))

    def ps(es, name, shape, dt=F32):
        return T(es.enter_context(nc.psum_tensor(name, list(shape), dt)), excl=True)

    rrs = {"evac": 0}

    def evac_eng():
        rrs["evac"] ^= 1
        return "act" if rrs["evac"] else "dve"

    def copy_op(eng, out, in_, reads, writes, scale=None):
        if eng == "act":
            if scale is None:
                P.op("act", lambda e: e.activation(out=out, in_=in_, func=AF.Copy), reads=reads, writes=writes)
            else:
                P.op("act", lambda e: e.activation(out=out, in_=in_, func=AF.Copy, scale=scale), reads=reads, writes=writes)
        elif scale is None:
            P.op(eng, lambda e: e.tensor_copy(out=out, in_=in_), reads=reads, writes=writes)
        else:
            P.op(eng, lambda e: e.tensor_scalar(out=out, in0=in_, scalar1=scale, scalar2=None, op0=ALU.mult), reads=reads, writes=writes)

    def mm(out, lhsT, rhs, start, stop, reads, writes, skip=False):
        if skip:
            P.op("pe", lambda e: e.matmul(out, lhsT=lhsT, rhs=rhs, start=start, stop=stop, skip_group_check=True), reads=reads, writes=writes)
        else:
            P.op("pe", lambda e: e.matmul(out, lhsT=lhsT, rhs=rhs, start=start, stop=stop), reads=reads, writes=writes)

    def tr(out, in_, ident, reads, writes):
        P.op("pe", lambda e: e.transpose(out=out, in_=in_, identity=ident), reads=reads, writes=writes)

    def load(q, out, in_, writes, reads=()):
        P.dma(q, lambda e: e.dma_start(out=out, in_=in_), reads=reads, writes=writes)

    with ExitStack() as g:
        idf = sb(g, "idf", [128, 128], F32)
        idb = sb(g, "idb", [128, 128], BF16)
        zerob = sb(g, "zerob", [128, 128], BF16)
        epsc = sb(g, "epsc", [128, 1], F32)
        onec = sb(g, "onec", [128, 1], F32)
        pic = sb(g, "pic", [128, 1], F32)
        gm = {}

        def load_gains(es_, names):
            for nm in names:
                gm[nm] = sb(es_, "gs_" + nm, [128, g_d[nm].shape[1]], F32)
                load("sp", gm[nm].t[:], g_d[nm][:, :], [gm[nm].r])
        load("sp", idf.t[:], ident_d[:, :], [idf.r])
        P.op("pool", lambda e: e.tensor_copy(out=idb.t[:], in_=idf.t[:]), reads=[idf.r], writes=[idb.r])
        P.op("pool", lambda e: e.memset(zerob.t[:], 0.0), writes=[zerob.r])
        P.op("pool", lambda e: e.memset(epsc.t[:], EPS), writes=[epsc.r])
        P.op("pool", lambda e: e.memset(onec.t[:], 1.0), writes=[onec.r])
        P.op("pool", lambda e: e.memset(pic.t[:], math.pi), writes=[pic.r])

        def rstd_from_ss(ssap, rsap, rd, wr, n):
            P.op("act", lambda e: e.activation(out=rsap, in_=ssap, func=AF.Ln, scale=1.0 / n, bias=epsc.t[:]),
                 reads=[rd, epsc.r], writes=[wr])
            P.op("act", lambda e: e.activation(out=rsap, in_=rsap, func=AF.Exp, scale=-0.5), reads=[wr], writes=[wr])

        with ExitStack() as es:
            win = sb(es, "win", [128, DC, INW], BF16)
            wuq = sb(es, "wuq", [128, 2 * 768], BF16)
            wuk = sb(es, "wuk", [128, 512], BF16)
            wuv = sb(es, "wuv", [128, 512], BF16)
            cosb = sb(es, "cosb", [128, NB, 16], F32)
            sinb = sb(es, "sinb", [128, NB, 16], F32)
            coso = sb(es, "coso", [128, NBO, 16], F32)
            sino = sb(es, "sino", [128, NBO, 16], F32)
            load_gains(es, ("mix", "q", "kv"))
            es0 = es
            es = ExitStack()
            es.__enter__()
            wst = [sb(es, "wst%d" % i, [128, 2048], F32) for i in range(4)]
            ropi = sb(es, "ropi", [128, NB], I32)
            ropf = sb(es, "ropf", [128, NB], F32)
            ropt = sb(es, "ropt", [128, NB, 16], F32)
            ropk = sb(es, "ropk", [128, NB, 16], I32)
            ropg = sb(es, "ropg", [128, NB, 16], F32)
            roph = sb(es, "roph", [128, NB, 16], F32)

            def rope_table(pos_d, n, sink_sin, sink_cos):
                load("sp", ropi.t[:, 0:n], pos_d[:, :], [ropi.r])
                P.op("dve", lambda e: e.tensor_copy(out=ropf.t[:, 0:n], in_=ropi.t[:, 0:n]), reads=[ropi.r], writes=[ropf.r])
                for i in range(16):
                    P.op("dve", lambda e, i=i: e.tensor_scalar(out=ropt.t[:, 0:n, i], in0=ropf.t[:, 0:n], scalar1=float(inv_freq[i]),
                                                                scalar2=1.0 / (2 * math.pi), op0=ALU.mult, op1=ALU.mult),
                         reads=[ropf.r], writes=[ropt.r])
                for ph, sink in ((0.0, sink_sin), (0.25, sink_cos)):
                    P.op("dve", lambda e, ph=ph: e.tensor_scalar(out=ropg.t[:, 0:n, :], in0=ropt.t[:, 0:n, :], scalar1=ph, scalar2=None, op0=ALU.add),
                         reads=[ropt.r], writes=[ropg.r])
                    P.op("dve", lambda e: e.tensor_copy(out=ropk.t[:, 0:n, :], in_=ropg.t[:, 0:n, :]), reads=[ropg.r], writes=[ropk.r])
                    P.op("dve", lambda e: e.tensor_copy(out=roph.t[:, 0:n, :], in_=ropk.t[:, 0:n, :]), reads=[ropk.r], writes=[roph.r])
                    P.op("dve", lambda e: e.tensor_tensor(out=ropg.t[:, 0:n, :], in0=ropg.t[:, 0:n, :], in1=roph.t[:, 0:n, :], op=ALU.subtract),
                         reads=[ropg.r, roph.r], writes=[ropg.r])
                    P.op("dve", lambda e: e.scalar_tensor_tensor(out=roph.t[:, 0:n, :], in0=ropg.t[:, 0:n, :], scalar=0.0, in1=ropg.t[:, 0:n, :],
                                                                 op0=ALU.is_lt, op1=ALU.add), reads=[ropg.r], writes=[roph.r])
                    sink(roph, n)

            def sink_b(tile):
                def f(src, n):
                    P.op("act", lambda e: e.activation(out=tile.t[:], in_=src.t[:, 0:n, :], func=AF.Sin, scale=-2.0 * math.pi, bias=pic.t[:]),
                         reads=[src.r, pic.r], writes=[tile.r])
                return f

            rope_table(posb, NB, sink_b(sinb), sink_b(cosb))
            rope_table(poso, NBO, sink_b(sino), sink_b(coso))

            k = 0
            for c in range(DC):
                st = wst[k % 4]; k += 1
                load("sp", st.t[:, 0:INW], w_in_d[:, c, :], [st.r])
                copy_op(("act", "dve", "act", "pool")[c % 4], win.t[:, c, :], st.t[:, 0:INW], [st.r], [win.r])
            st = wst[k % 4]; k += 1
            load("sp", st.t[:, 0:1536], w_uq_d[:, :], [st.r])
            copy_op("act", wuq.t[:], st.t[:, 0:1536], [st.r], [wuq.r])
            st = wst[k % 4]; k += 1
            load("sp", st.t[:, 0:512], w_uk_d[:, :], [st.r])
            load("sp", st.t[:, 512:1024], w_uv_d[:, :], [st.r])
            copy_op("dve", wuk.t[:], st.t[:, 0:512], [st.r], [wuk.r])
            copy_op("dve", wuv.t[:], st.t[:, 512:1024], [st.r], [wuv.r])

            P.end_phase()
            es.__exit__(None, None, None)
            es = ExitStack()
            es.__enter__()
            xbuf = [sb(es, "xbuf%d" % i, [128, D], F32) for i in range(8)]
            junk = sb(es, "junk", [128, D], F32)
            ubuf = [sb(es, "ubuf%d" % i, [128, D], BF16) for i in range(3)]
            uT = [sb(es, "uT%d" % i, [128, DC, 512], BF16) for i in range(2)]
            ss4 = [sb(es, "ss4_%d" % i, [128, 4], F32) for i in range(2)]
            rs4 = [sb(es, "rs4_%d" % i, [128, 4], F32) for i in range(2)]
            ssc = [sb(es, "ssc_%d" % i, [128, 4], F32) for i in range(2)]
            rsc = [sb(es, "rsc_%d" % i, [128, 4], F32) for i in range(2)]
            st4 = [sb(es, "st4_%d" % i, [128, 4, 512], BF16) for i in range(3)]
            stv = [sb(es, "stv%d" % i, [128, H, 4, 65], BF16) for i in range(3)]
            ckr = [sb(es, "ckr%d" % i, [128, 160], BF16) for i in range(4)]
            rtmp = [sb(es, "rtmp%d" % i, [128, 2, 16], F32) for i in range(2)]
            ckvT = [sb(es, "ckvT%d" % i, [128, 512], BF16) for i in range(2)]
            krT = [sb(es, "krT%d" % i, [32, 512], BF16) for i in range(2)]
            cqn = [sb(es, "cqn%d" % i, [128, 256], BF16) for i in range(4)]
            cqT = [sb(es, "cqT%d" % i, [128, 2, 512], BF16) for i in range(2)]
            qtok = [sb(es, "qtok%d" % i, [128, H, 96], BF16) for i in range(2)]
            qrt = [sb(es, "qrt%d" % i, [128, 2, H, 16], F32) for i in range(2)]
            qmst = [sb(es, "qmst%d" % i, [128, H, 512], BF16) for i in range(2)]
            pT = [ps(es, "pT%d" % i, [128, DC, 128], BF16) for i in range(2)]
            pP = [ps(es, "pP%d" % i, [128, 512], F32) for i in range(3)]
            pC = ps(es, "pC", [128, 4, 256], F32)
            pQ = ps(es, "pQ", [128, 512], F32)
            for v in stv:
                P.op("pool", lambda e, v=v: e.memset(v.t[:], 1.0), writes=[v.r])
            ctr = {"x": 0, "u": 0, "pT": 0, "pP": 0, "st4": 0, "stv": 0}

            def nxt(lst, key):
                v = lst[ctr[key] % len(lst)]
                ctr[key] += 1
                return v

            def rope_tok(x1, x2, cs, sn, o1, o2, tA, tB, rd, tmp_r, out_r):
                P.op("dve", lambda e: e.tensor_tensor(out=tA, in0=x1, in1=cs, op=ALU.mult), reads=rd, writes=[tmp_r])
                P.op("dve", lambda e: e.tensor_tensor(out=tB, in0=x2, in1=sn, op=ALU.mult), reads=rd, writes=[tmp_r])
                P.op("dve", lambda e: e.tensor_tensor(out=o1, in0=tA, in1=tB, op=ALU.subtract), reads=[tmp_r], writes=[out_r])
                P.op("dve", lambda e: e.tensor_tensor(out=tA, in0=x2, in1=cs, op=ALU.mult), reads=rd, writes=[tmp_r])
                P.op("dve", lambda e: e.tensor_tensor(out=tB, in0=x1, in1=sn, op=ALU.mult), reads=rd, writes=[tmp_r])
                P.op("dve", lambda e: e.tensor_tensor(out=o2, in0=tA, in1=tB, op=ALU.add), reads=[tmp_r], writes=[out_r])

            chunks = [("A", ci) for ci in range(NCH)] + [("B", gi) for gi in range(NG)]
            xtiles = {}

            def chunk_src(k):
                typ, i = chunks[k]
                return (xb if typ == "A" else xo), i

            def front_loads(k):
                src, i = chunk_src(k)
                xs = [nxt(xbuf, "x") for _ in range(4)]
                xtiles[k] = xs
                for t in range(4):
                    load("sp", xs[t].t[:], src[(4 * i + t) * 128:(4 * i + t + 1) * 128, :], [xs[t].r])

            def front_stats(k):
                xs = xtiles[k]
                s4 = ss4[k % 2]; r4 = rs4[k % 2]
                for t in range(4):
                    P.op("act", lambda e, t=t: e.activation(out=junk.t[:], in_=xs[t].t[:], func=AF.Square, accum_out=s4.t[:, t:t + 1]),
                         reads=[xs[t].r], writes=[junk.r, s4.r])
                rstd_from_ss(s4.t[:], r4.t[:], s4.r, r4.r, D)

            def front_norm_T(k):
                xs = xtiles[k]
                r4 = rs4[k % 2]
                uTt = uT[k % 2]
                for t in range(4):
                    ub = nxt(ubuf, "u"); pt = nxt(pT, "pT")
                    P.op("dve", lambda e, t=t, ub=ub: e.scalar_tensor_tensor(out=ub.t[:], in0=xs[t].t[:], scalar=r4.t[:, t:t + 1], in1=gm["mix"].t[:],
                                                                       op0=ALU.mult, op1=ALU.mult),
                         reads=[xs[t].r, r4.r, gm["mix"].r], writes=[ub.r])
                    for c in range(DC):
                        tr(pt.t[:, c, :], ub.t[:, c * 128:(c + 1) * 128], idb.t[:], [ub.r, idb.r], [pt.r])
                    copy_op("act", uTt.t[:, :, t * 128:(t + 1) * 128], pt.t[:], [pt.r], [uTt.r])

            def fm_proj(uTt, col0, dst4, scale=None):
                for pr in range(4):
                    p_ = nxt(pP, "pP")
                    for c in range(DC):
                        mm(p_.t[:, :], win.t[:, c, col0 + pr * 128:col0 + (pr + 1) * 128], uTt.t[:, c, :],
                           c == 0, c == DC - 1, [win.r, uTt.r], [p_.r])
                    copy_op(evac_eng(), dst4.t[:, pr, :], p_.t[:, :], [p_.r], [dst4.r], scale=scale)

            def proj_A1(k):
                ci = chunks[k][1]
                uTt = uT[k % 2]
                ks = nxt(st4, "st4")
                fm_proj(uTt, C_KSB, ks)
                load("sp", KTs[:, :, ci * 512:(ci + 1) * 512].rearrange("a p n -> p a n"), ks.t[:], [], reads=[ks.r])
                for t in range(4):
                    for c in range(DC):
                        mm(pC.t[:, t, 0:160], uTt.t[:, c, t * 128:(t + 1) * 128], win.t[:, c, C_CKV:C_CKV + 160],
                           c == 0, c == DC - 1, [win.r, uTt.r], [pC.r])
                s4 = ssc[k % 2]; r4 = rsc[k % 2]
                for t in range(4):
                    P.op("act", lambda e, t=t: e.activation(out=junk.t[:, 0:128], in_=pC.t[:, t, 0:128], func=AF.Square, accum_out=s4.t[:, t:t + 1]),
                         reads=[pC.r], writes=[junk.r, s4.r])
                rstd_from_ss(s4.t[:], r4.t[:], s4.r, r4.r, 128)

            def proj_A2(k):
                ci = chunks[k][1]
                uTt = uT[k % 2]
                r4 = rsc[k % 2]
                ckT = ckvT[k % 2]
                krt = krT[k % 2]
                cks = []
                for t in range(4):
                    kb = 4 * ci + t
                    ck = ckr[t]
                    tm = rtmp[t % 2]
                    cks.append(ck)
                    P.op("dve", lambda e, ck=ck, t=t: e.scalar_tensor_tensor(out=ck.t[:, 0:128], in0=pC.t[:, t, 0:128], scalar=r4.t[:, t:t + 1],
                                                                       in1=gm["kv"].t[:], op0=ALU.mult, op1=ALU.mult),
                         reads=[pC.r, r4.r, gm["kv"].r], writes=[ck.r])
                    rope_tok(pC.t[:, t, 128:144], pC.t[:, t, 144:160], cosb.t[:, kb, :], sinb.t[:, kb, :],
                             ck.t[:, 128:144], ck.t[:, 144:160], tm.t[:, 0, :], tm.t[:, 1, :],
                             [pC.r, cosb.r, sinb.r], tm.r, ck.r)
                vs_ = nxt(stv, "stv")
                for t in range(4):
                    p_ = nxt(pP, "pP")
                    for c in range(DC):
                        mm(p_.t[:, :], uTt.t[:, c, t * 128:(t + 1) * 128], win.t[:, c, C_VSB:C_VSB + 512],
                           c == 0, c == DC - 1, [win.r, uTt.r], [p_.r])
                    copy_op("act", vs_.t[:, :, t, 0:64], p_.t[:, :].rearrange("p (h d) -> p h d", d=64), [p_.r], [vs_.r])
                load("sp", Vs[:, :, 4 * ci:4 * ci + 4, :].rearrange("h p t e -> p h t e"), vs_.t[:], [], reads=[vs_.r])
                pt = nxt(pT, "pT")
                for t in range(4):
                    tr(pt.t[:, t, :], cks[t].t[:, 0:128], idb.t[:], [cks[t].r, idb.r], [pt.r])
                    tr(pt.t[0:32, 4 + t, :], cks[t].t[:, 128:160], idb.t[:], [cks[t].r, idb.r], [pt.r])
                copy_op("dve", ckT.t[:, :].rearrange("p (t n) -> p t n", n=128), pt.t[:, 0:4, :], [pt.r], [ckT.r])
                copy_op("dve", krt.t[:, :].rearrange("p (t n) -> p t n", n=128), pt.t[0:32, 4:8, :], [pt.r], [krt.r])
                load("sp", KR[:, ci * 512:(ci + 1) * 512], krt.t[:], [], reads=[krt.r])
                kn = nxt(st4, "st4")
                for pr in range(4):
                    p_ = nxt(pP, "pP")
                    mm(p_.t[:, :], wuk.t[:, pr * 128:(pr + 1) * 128], ckT.t[:, :], True, True, [wuk.r, ckT.r], [p_.r])
                    copy_op(evac_eng(), kn.t[:, pr, :], p_.t[:, :], [p_.r], [kn.r])
                load("sp", KTm[:, :, ci * 512:(ci + 1) * 512].rearrange("a p n -> p a n"), kn.t[:], [], reads=[kn.r])
                vm_ = nxt(stv, "stv")
                for t in range(4):
                    p_ = nxt(pP, "pP")
                    mm(p_.t[:, :], ckT.t[:, t * 128:(t + 1) * 128], wuv.t[:, :], True, True, [wuv.r, ckT.r], [p_.r])
                    copy_op(evac_eng(), vm_.t[:, :, t, 0:64], p_.t[:, :].rearrange("p (h d) -> p h d", d=64), [p_.r], [vm_.r])
                load("sp", Vm[:, :, 4 * ci:4 * ci + 4, :].rearrange("h p t e -> p h t e"), vm_.t[:], [], reads=[vm_.r])

            def proj_B1(k):
                gi = chunks[k][1]
                uTt = uT[k % 2]
                qs = nxt(st4, "st4")
                fm_proj(uTt, C_QSB, qs, scale=0.125)
                load("sp", QTs[:, :, gi * 512:(gi + 1) * 512].rearrange("a p n -> p a n"), qs.t[:], [], reads=[qs.r])
                for t in range(4):
                    for c in range(DC):
                        mm(pC.t[:, t, 0:256], uTt.t[:, c, t * 128:(t + 1) * 128], win.t[:, c, C_CQ:C_CQ + 256],
                           c == 0, c == DC - 1, [win.r, uTt.r], [pC.r])
                s4 = ssc[k % 2]; r4 = rsc[k % 2]
                for t in range(4):
                    P.op("act", lambda e, t=t: e.activation(out=junk.t[:, 0:256], in_=pC.t[:, t, 0:256], func=AF.Square, accum_out=s4.t[:, t:t + 1]),
                         reads=[pC.r], writes=[junk.r, s4.r])
                rstd_from_ss(s4.t[:], r4.t[:], s4.r, r4.r, 256)

            def proj_B2(k):
                gi = chunks[k][1]
                r4 = rsc[k % 2]
                cqt = cqT[k % 2]
                for t in range(4):
                    cq = cqn[t]
                    P.op("dve", lambda e, cq=cq, t=t: e.scalar_tensor_tensor(out=cq.t[:], in0=pC.t[:, t, 0:256], scalar=r4.t[:, t:t + 1],
                                                                       in1=gm["q"].t[:], op0=ALU.mult, op1=ALU.mult),
                         reads=[pC.r, r4.r, gm["q"].r], writes=[cq.r])
                pt = nxt(pT, "pT")
                for t in range(4):
                    for k2 in range(2):
                        tr(pt.t[:, 2 * t + k2, :], cqn[t].t[:, k2 * 128:(k2 + 1) * 128], idb.t[:], [cqn[t].r, idb.r], [pt.r])
                for k2 in range(2):
                    copy_op(evac_eng(), cqt.t[:, k2, :].rearrange("p (t n) -> p t n", n=128),
                            pt.t[:, :, :].rearrange("p (t k) n -> p t k n", k=2)[:, :, k2, :], [pt.r], [cqt.r])
                qm = qmst[k % 2]
                for t in range(4):
                    pos = 4 * gi + t
                    qt = qtok[t % 2]
                    qr = qrt[t % 2]
                    pa = nxt(pP, "pP")
                    for k2 in range(2):
                        mm(pa.t[:, 0:512], cqt.t[:, k2, t * 128:(t + 1) * 128], wuq.t[:, k2 * 768:k2 * 768 + 512],
                           k2 == 0, k2 == 1, [wuq.r, cqt.r], [pa.r])
                    for k2 in range(2):
                        mm(pQ.t[:, 0:256], cqt.t[:, k2, t * 128:(t + 1) * 128], wuq.t[:, k2 * 768 + 512:k2 * 768 + 768],
                           k2 == 0, k2 == 1, [wuq.r, cqt.r], [pQ.r])
                    qf = junk
                    copy_op("act", qf.t[:, 0:512], pa.t[:, 0:512], [pa.r], [qf.r])
                    copy_op("dve", qf.t[:, 512:768], pQ.t[:, 0:256], [pQ.r], [qf.r])
                    q3 = qf.t[:, 0:768].rearrange("p (h d) -> p h d", d=96)
                    copy_op("act", qt.t[:, :, 0:64], q3[:, :, 0:64], [qf.r], [qt.r])
                    rope_tok(q3[:, :, 64:80], q3[:, :, 80:96], coso.t[:, pos, :].unsqueeze(1).broadcast_to([128, H, 16]),
                             sino.t[:, pos, :].unsqueeze(1).broadcast_to([128, H, 16]),
                             qt.t[:, :, 64:80], qt.t[:, :, 80:96], qr.t[:, 0, :, :], qr.t[:, 1, :, :],
                             [qf.r, coso.r, sino.r], qr.r, qt.r)
                    pt = nxt(pT, "pT")
                    for h in range(H):
                        tr(pt.t[0:96, h, :], qt.t[:, h, :], idb.t[:], [qt.r, idb.r], [pt.r])
                    copy_op(evac_eng(), qm.t[0:96, :, t * 128:(t + 1) * 128], pt.t[0:96, :, :], [pt.r], [qm.r])
                load("sp", QTm[:, :, gi * 512:(gi + 1) * 512].rearrange("h r n -> r h n"), qm.t[0:96, :, :], [], reads=[qm.r])

            NK = len(chunks)
            front_loads(0)
            if NK > 1:
                front_loads(1)
            front_stats(0)
            front_norm_T(0)
            for k in range(NK):
                if k + 2 < NK:
                    front_loads(k + 2)
                if k + 1 < NK:
                    front_stats(k + 1)
                (proj_A1 if chunks[k][0] == "A" else proj_B1)(k)
                if k + 1 < NK:
                    front_norm_T(k + 1)
                (proj_A2 if chunks[k][0] == "A" else proj_B2)(k)
            P.end_phase()
            es.__exit__(None, None, None)
        if stop_after == "AB":
            return nc, dict(KTm=KTm, KR=KR, Vm=Vm, KTs=KTs, Vs=Vs, QTm=QTm, QTs=QTs)

        merged = sb(g, "merged", [128, NBO, D], F32)
        mres = [Res("m") for _ in range(NBO)]
        with ExitStack() as es:
            trib = sb(es, "trib", [128, 2, 128], BF16)
            maskb = sb(es, "maskb", [128, 16 * 128], BF16)
            mst = sb(es, "mst", [128, 2048], F32)
            mst2 = sb(es, "mst2", [128, 256], F32)
            load("sp", mst.t[:, 0:2048], mask_d[:, :], [mst.r])
            P.op("pool", lambda e: e.tensor_copy(out=maskb.t[:], in_=mst.t[:, 0:2048]), reads=[mst.r], writes=[maskb.r])
            load("sp", mst2.t[:, 0:256], tri_d.rearrange("p a b -> p (a b)"), [mst2.r])
            P.op("pool", lambda e: e.tensor_copy(out=trib.t[:].rearrange("p a b -> p (a b)"), in_=mst2.t[:, 0:256]),
                 reads=[mst2.r], writes=[trib.r])
            Kb = [sb(es, "Kb%d" % i, [128, S], BF16) for i in range(2)]
            Vb = [sb(es, "Vb%d" % i, [128, NB, 65], BF16) for i in range(2)]
            Qb = [sb(es, "Qb%d" % i, [128, SO], BF16) for i in range(2)]
            eb2 = [sb(es, "eb2_%d" % i, [128, 2, 512], F32) for i in range(4)]
            lb2 = [sb(es, "lb2_%d" % i, [128, 2, 512], BF16) for i in range(2)]
            gb2 = [sb(es, "gb2_%d" % i, [128, 2, 512], F32) for i in range(2)]
            ab2 = [sb(es, "ab2_%d" % i, [128, 2, 512], BF16) for i in range(2)]
            ab = [sb(es, "ab%d" % i, [128, 512], BF16) for i in range(2)]
            ot = [sb(es, "ot%d" % i, [128, 512], F32) for i in range(2)]
            ot2 = sb(es, "ot2", [128, 4, 128], F32)
            rinv = [sb(es, "rinv%d" % i, [128, 4, 1], F32) for i in range(2)]
            zb2 = [ps(es, "zb2_%d" % i, [128, 2, 512], F32) for i in range(2)]
            zres = [[Res("z", True), Res("z", True)] for _ in range(2)]
            cb = ps(es, "cb", [128, 512], F32)
            oacc = [ps(es, "oacc%d" % i, [128, 512], F32) for i in range(2)]
            tp = ps(es, "tp", [128, 4, 128], F32)
            mk4 = maskb.t[:].rearrange("p (a b c q) -> p a b c q", a=2, b=2, c=4)
            halfpos = NBO // 2

            jobs = [("m", h) for h in range(H)] + [("s", h) for h in range(H)]

            def job_loads(k):
                typ, h = jobs[k]
                sl = k % 2
                Kt, Vt, Qt = Kb[sl], Vb[sl], Qb[sl]
                r0 = (h % 2) * 64
                if typ == "m":
                    load("sp", Kt.t[0:64, :], KTm[h // 2, r0:r0 + 64, :], [Kt.r])
                    load("sp", Kt.t[64:96, :], KR[:, :], [Kt.r])
                    load("sp", Vt.t[:], Vm[h], [Vt.r])
                    load("sp", Qt.t[0:96, :], QTm[h], [Qt.r])
                else:
                    load("sp", Kt.t[0:64, :], KTs[h // 2, r0:r0 + 64, :], [Kt.r])
                    load("sp", Kt.t[64:128, :], KTs[h // 2, r0:r0 + 64, :], [Kt.r])
                    load("sp", Vt.t[:], Vs[h], [Vt.r])
                    load("sp", Qt.t[0:64, :], QTs[h // 2, r0:r0 + 64, :], [Qt.r])
                    load("sp", Qt.t[64:128, :], QTs[h // 2, r0:r0 + 64, :], [Qt.r])

            gctr = [0]

            def tiles_for(gi):
                out = []
                for kb in range(16 * gi + 15, -1, -1):
                    pm = kb // 4
                    if pm >= 4 * gi:
                        lo = pm - 4 * gi
                        out.append((kb, lo * 128, True, 0 if pm < halfpos else 1, kb % 4))
                    else:
                        out.append((kb, 0, False, 0, 0))
                return out

            def finish_group(typ, h, gi, oa):
                o_ = ot[gctr[0] % 2]
                rv = rinv[gctr[0] % 2]
                nr = 65 if typ == "m" else 64
                copy_op("dve", o_.t[0:nr, :], oa.t[0:nr, :], [oa.r], [o_.r])
                for t in range(4):
                    tr(tp.t[:, t, 0:nr], o_.t[0:nr, t * 128:(t + 1) * 128], idf.t[0:nr, 0:nr], [o_.r, idf.r], [tp.r])
                if typ == "m":
                    P.op("dve", lambda e: e.reciprocal(out=rv.t[:], in_=tp.t[:, :, 64:65]), reads=[tp.r], writes=[rv.r])
                    for t in range(4):
                        P.op("dve", lambda e, t=t: e.tensor_scalar(out=merged.t[:, 4 * gi + t, h * 64:(h + 1) * 64], in0=tp.t[:, t, 0:64],
                                                                   scalar1=rv.t[:, t, :], scalar2=None, op0=ALU.mult),
                             reads=[tp.r, rv.r], writes=[mres[4 * gi + t]])
                else:
                    P.op("dve", lambda e: e.tensor_copy(out=merged.t[:, 4 * gi:4 * gi + 4, 512 + h * 64:512 + (h + 1) * 64], in_=tp.t[:, :, 0:64]),
                         reads=[tp.r], writes=[mres[4 * gi + t_] for t_ in range(4)])

            def zero_bank(bank, Qt):
                mm(bank.t[:, :], zerob.t[:, :], maskb.t[:, 0:512], True, True, [zerob.r, maskb.r], [bank.r])

            def job_items(g0):
                items = []
                for gi in range(NG):
                    tl = tiles_for(gi)
                    npair = len(tl) // 2
                    for p in range(npair):
                        ta, tb_ = tl[2 * p], tl[2 * p + 1]
                        assert ta[1] == tb_[1] and ta[2] == tb_[2] and ta[3] == tb_[3]
                        items.append(dict(gi=gi, ta=ta, tb=tb_, first=(p == 0), last=(p == npair - 1), oa=oacc[(g0 + gi) % 2]))
                return items

            def run_mla(k, h):
                sl = k % 2
                Kt, Vt, Qt = Kb[sl], Vb[sl], Qb[sl]
                items = job_items(gctr[0])
                n = len(items)

                def S2(i):
                    it = items[i]; gi = it["gi"]
                    (kba, c0, msk, hf, ra), (kbb, _, _, _, rb) = it["ta"], it["tb"]
                    zt = zb2[i % 2].t; zr = zres[i % 2]
                    q0 = gi * 512 + c0
                    for j, kb_, r_ in ((0, kba, ra), (1, kbb, rb)):
                        mm(zt[:, j, c0:512], Kt.t[0:96, kb_ * 128:(kb_ + 1) * 128], Qt.t[0:96, q0:(gi + 1) * 512],
                           True, not msk, [Kt.r, Qt.r], [zr[j]])
                        if msk:
                            mm(zt[:, j, c0:c0 + 128], idb.t[:, :], mk4[:, 0, hf, r_, :], False, True, [idb.r, maskb.r], [zr[j]])

                def P2(i):
                    it = items[i]; oa = it["oa"]
                    (kba, c0, msk, hf, ra), (kbb, _, _, _, rb) = it["ta"], it["tb"]
                    zt = zb2[i % 2].t; zr = zres[i % 2]
                    a_ = ab2[i % 2]
                    o_ap = a_.t[:, :, c0:512]; i_ap = zt[:, :, c0:512]
                    P.op("act", lambda e, o_ap=o_ap, i_ap=i_ap: e.activation(out=o_ap, in_=i_ap, func=AF.Exp, scale=MLA_SCALE),
                         reads=[zr[0], zr[1]], writes=[a_.r])
                    if it["first"]:
                        zero_bank(oa, Qt)
                    mm(oa.t[0:65, c0:512], Vt.t[:, kba, 0:65], a_.t[:, 0, c0:512], False, False, [Vt.r, a_.r], [oa.r], skip=True)
                    mm(oa.t[0:65, c0:512], Vt.t[:, kbb, 0:65], a_.t[:, 1, c0:512], False, False, [Vt.r, a_.r], [oa.r], skip=True)

                S2(0)
                for i in range(n):
                    if i + 1 < n:
                        S2(i + 1)
                    P2(i)
                    if items[i]["last"]:
                        finish_group("m", h, items[i]["gi"], items[i]["oa"])
                        gctr[0] += 1

            def finish_sb(h, gi, oa):
                o_ = ot[gctr[0] % 2]
                copy_op("dve", o_.t[:, :], oa.t[:, :], [oa.r], [o_.r])
                for t in range(4):
                    tr(tp.t[:, t, :], o_.t[:, t * 128:(t + 1) * 128], idf.t[:, :], [o_.r, idf.r], [tp.r])
                copy_op("dve", ot2.t[:], tp.t[:], [tp.r], [ot2.r])
                P.op("dve", lambda e: e.tensor_tensor(out=merged.t[:, 4 * gi:4 * gi + 4, 512 + h * 64:512 + (h + 1) * 64],
                                                      in0=ot2.t[:, :, 0:64], in1=ot2.t[:, :, 64:128], op=ALU.add),
                     reads=[ot2.r], writes=[mres[4 * gi + t_] for t_ in range(4)])

            def run_sb(k, h):
                sl = k % 2
                Kt, Vt, Qt = Kb[sl], Vb[sl], Qb[sl]
                items = job_items(gctr[0])
                n = len(items)

                def Zp(i):
                    it = items[i]; gi = it["gi"]
                    (kba, c0, msk, hf, ra), (kbb, _, _, _, rb) = it["ta"], it["tb"]
                    zt = zb2[i % 2].t; zr = zres[i % 2]
                    q0 = gi * 512 + c0
                    mm(zt[:, 0, c0:512], Kt.t[0:64, kba * 128:(kba + 1) * 128], Qt.t[0:64, q0:(gi + 1) * 512],
                       True, not msk, [Kt.r, Qt.r], [zr[0]])
                    mm(zt[:, 1, c0:512], Kt.t[64:128, kbb * 128:(kbb + 1) * 128], Qt.t[64:128, q0:(gi + 1) * 512],
                       True, not msk, [Kt.r, Qt.r], [zr[1]])
                    if msk:
                        mm(zt[:, 0, c0:c0 + 128], idb.t[:, :], mk4[:, 1, hf, ra, :], False, True, [idb.r, maskb.r], [zr[0]])
                        mm(zt[:, 1, c0:c0 + 128], idb.t[:, :], mk4[:, 1, hf, rb, :], False, True, [idb.r, maskb.r], [zr[1]])

                def Ep(i):
                    c0 = items[i]["ta"][1]
                    zt = zb2[i % 2].t; zr = zres[i % 2]; e_ = eb2[i % 4]
                    o_ap = e_.t[:, :, c0:512]; i_ap = zt[:, :, c0:512]
                    P.op("act", lambda e, o_ap=o_ap, i_ap=i_ap: e.activation(out=o_ap, in_=i_ap, func=AF.Exp),
                         reads=[zr[0], zr[1]], writes=[e_.r])

                def Lp(i):
                    c0 = items[i]["ta"][1]
                    e_ = eb2[i % 4]; l_ = lb2[i % 2]
                    o_ap = l_.t[:, :, c0:512]; i_ap = e_.t[:, :, c0:512]
                    P.op("act", lambda e, o_ap=o_ap, i_ap=i_ap: e.activation(out=o_ap, in_=i_ap, func=AF.Ln, bias=onec.t[:]),
                         reads=[e_.r, onec.r], writes=[l_.r])

                def Tri(i, j):
                    c0 = items[i]["ta"][1]
                    l_ = lb2[i % 2]
                    mm(cb.t[:, c0:512], trib.t[:, 0, :], l_.t[:, j, c0:512], False, False, [trib.r, l_.r], [cb.r], skip=True)

                def G(i, j):
                    c0 = items[i]["ta"][1]
                    g_ = gb2[i % 2]
                    o_ap = g_.t[:, j, c0:512]; i_ap = cb.t[:, c0:512]
                    P.op("act", lambda e, o_ap=o_ap, i_ap=i_ap: e.activation(out=o_ap, in_=i_ap, func=AF.Exp, scale=-1.0),
                         reads=[cb.r], writes=[g_.r])

                def OmT(i, j):
                    c0 = items[i]["ta"][1]
                    l_ = lb2[i % 2]
                    mm(cb.t[:, c0:512], trib.t[:, 1, :], l_.t[:, j, c0:512], False, False, [trib.r, l_.r], [cb.r], skip=True)

                def Amul(i):
                    c0 = items[i]["ta"][1]
                    g_ = gb2[i % 2]; a_ = ab2[i % 2]; e_ = eb2[i % 4]
                    o_ap = a_.t[:, :, c0:512]; x_ap = e_.t[:, :, c0:512]; y_ap = g_.t[:, :, c0:512]
                    P.op("dve", lambda e, o_ap=o_ap, x_ap=x_ap, y_ap=y_ap: e.tensor_tensor(out=o_ap, in0=x_ap, in1=y_ap, op=ALU.mult),
                         reads=[e_.r, g_.r], writes=[a_.r])

                def AVp(i):
                    it = items[i]; oa = it["oa"]
                    kba, c0 = it["ta"][0], it["ta"][1]
                    kbb = it["tb"][0]
                    a_ = ab2[i % 2]
                    mm(oa.t[0:64, c0:512], Vt.t[:, kba, 0:64], a_.t[:, 0, c0:512], False, False, [Vt.r, a_.r], [oa.r], skip=True)
                    o_ap = oa.t[64:128, c0:512]; l_ap = Vt.t[:, kbb, 0:64]; r_ap = a_.t[:, 1, c0:512]
                    P.op("pe", lambda e, o_ap=o_ap, l_ap=l_ap, r_ap=r_ap: e.matmul(o_ap, lhsT=l_ap, rhs=r_ap, start=False, stop=False,
                                                                                 skip_group_check=True, tile_position=(0, 64)),
                         reads=[Vt.r, a_.r], writes=[oa.r])

                Zp(0)
                if n > 1:
                    Zp(1)
                Ep(0)
                if n > 2:
                    Zp(2)
                if n > 1:
                    Ep(1)
                Lp(0)
                for st_ in range(n + 1):
                    if 0 <= st_ - 1 < n:
                        OmT(st_ - 1, 1)
                    if st_ < n:
                        if items[st_]["first"]:
                            zero_bank(cb, Qt)
                        Tri(st_, 0)
                        G(st_, 0)
                    if st_ + 3 < n:
                        Zp(st_ + 3)
                    if st_ + 1 < n:
                        Lp(st_ + 1)
                    if 0 <= st_ - 1 < n:
                        it = items[st_ - 1]
                        if it["first"]:
                            zero_bank(it["oa"], Qt)
                        Amul(st_ - 1)
                        AVp(st_ - 1)
                        if it["last"]:
                            finish_sb(h, it["gi"], it["oa"])
                            gctr[0] += 1
                    if st_ < n:
                        OmT(st_, 0)
                        Tri(st_, 1)
                        G(st_, 1)
                    if st_ + 2 < n:
                        Ep(st_ + 2)

            job_loads(0)
            for k in range(len(jobs)):
                if k + 1 < len(jobs):
                    job_loads(k + 1)
                typ, h = jobs[k]
                if typ == "m":
                    run_mla(k, h)
                else:
                    run_sb(k, h)
            P.end_phase()
        if stop_after == "C":
            return nc, dict(merged=merged)

        with ExitStack() as es:
            fT = sb(es, "fT", [128, DC, SO], BF16)
            load_gains(es, ("ffn", "fin"))
            g_o = sb(es, "g_o_sb", [128, DC], F32)
            load("sp", g_o.t[:], g_o_d[:, :], [g_o.r])
            sst = sb(es, "sst", [128, NBO, 2], F32)
            rst = sb(es, "rst", [128, NBO, 2], F32)
            junk = sb(es, "junkd", [128, D], F32)
            wg = [sb(es, "wg0", [128, DC, 512], BF16), None]
            wu = [sb(es, "wu0", [128, DC, 512], BF16), None]
            wd = [sb(es, "wd0", [128, 4, D], BF16), None]
            wstf = [sb(es, "wstf%d" % i, [128, 1024], F32) for i in range(2)]
            NFG = (FC + 3) // 4
            wctr = [0]

            def stage_cast(dst_ap, src_ap, n, dst_r):
                i = wctr[0]; wctr[0] += 1
                st = wstf[i % 2]
                load("sp", st.t[:, 0:n], src_ap, [st.r])
                copy_op("pool", dst_ap, st.t[:, 0:n], [st.r], [dst_r])

            def ffn_load_list(fg):
                f0 = fg * 512
                nf = min(512, DFF - f0)
                sl = fg % 2
                lst = []
                for c in range(DC):
                    lst.append((wg[sl].t[:, c, 0:nf], w_g_d[:, c, f0:f0 + nf], nf, wg[sl].r))
                    lst.append((wu[sl].t[:, c, 0:nf], w_u_d[:, c, f0:f0 + nf], nf, wu[sl].r))
                for fc in range(nf // 128):
                    lst.append((wd[sl].t[:, fc, :], w_d_d[:, fg * 4 + fc, :], D, wd[sl].r))
                return lst

            def ffn_loads(fg):
                for a_ in ffn_load_list(fg):
                    stage_cast(*a_)

            with ExitStack() as e1:
                wo = sb(e1, "wo", [128, DC, D], BF16)
                mn = [sb(e1, "mn%d" % i, [128, D], BF16) for i in range(3)]
                mT = [sb(e1, "mT%d" % i, [128, DC, 128], BF16) for i in range(3)]
                xs2 = [sb(e1, "xs2_%d" % i, [128, D], F32) for i in range(3)]
                pT = [ps(e1, "pTd%d" % i, [128, DC, 128], BF16) for i in range(3)]
                pO = [ps(e1, "pO%d" % i, [128, 1024], F32) for i in range(2)]
                for c in range(DC):
                    st = wstf[c % 2]
                    load("sp", st.t[:, 0:D], w_o_d[:, c, :], [st.r])
                    P.op("act", lambda e, st=st, c=c: e.activation(out=wo.t[:, c, :], in_=st.t[:, 0:D], func=AF.Copy, scale=g_o.t[:, c:c + 1]),
                         reads=[st.r, g_o.r], writes=[wo.r])
                for t in range(NBO):
                    for hf in range(2):
                        P.op("act", lambda e, t=t, hf=hf: e.activation(out=junk.t[:, 0:512], in_=merged.t[:, t, hf * 512:(hf + 1) * 512],
                                                                       func=AF.Square, accum_out=sst.t[:, t, hf:hf + 1]),
                             reads=[mres[t]], writes=[junk.r, sst.r])
                rstd_from_ss(sst.t[:], rst.t[:], sst.r, rst.r, 512)
                pre0 = ffn_load_list(0)

                def prep(t):
                    m_ = mn[t % 3]; mt = mT[t % 3]; pt = pT[t % 3]; xs = xs2[t % 3]
                    load("sp", xs.t[:], xo[t * 128:(t + 1) * 128, :], [xs.r])
                    for hf in range(2):
                        P.op("act", lambda e, t=t, hf=hf, m_=m_: e.activation(
                            out=m_.t[:, hf * 512:(hf + 1) * 512], in_=merged.t[:, t, hf * 512:(hf + 1) * 512],
                            func=AF.Copy, scale=rst.t[:, t, hf:hf + 1]),
                            reads=[mres[t], rst.r], writes=[m_.r])
                    for c in range(DC):
                        tr(pt.t[:, c, :], m_.t[:, c * 128:(c + 1) * 128], idb.t[:], [m_.r, idb.r], [pt.r])
                    copy_op("act", mt.t[:], pt.t[:], [pt.r], [mt.r])

                def fin(t):
                    mt = mT[t % 3]; po = pO[t % 2]; xs = xs2[t % 3]
                    for nh in range(2):
                        for c in range(DC):
                            mm(po.t[:, nh * 512:(nh + 1) * 512], mt.t[:, c, :], wo.t[:, c, nh * 512:(nh + 1) * 512],
                               c == 0, c == DC - 1, [mt.r, wo.r], [po.r])
                    P.op("dve", lambda e, t=t, po=po, xs=xs: e.tensor_tensor(out=merged.t[:, t, :], in0=po.t[:, :], in1=xs.t[:], op=ALU.add),
                         reads=[po.r, xs.r], writes=[mres[t]])

                def fin_sq(t):
                    P.op("act", lambda e, t=t: e.activation(out=junk.t[:], in_=merged.t[:, t, :], func=AF.Square, accum_out=sst.t[:, t, 0:1]),
                         reads=[mres[t]], writes=[junk.r, sst.r])

                prep(0)
                if NBO > 1:
                    prep(1)
                for t in range(NBO):
                    if t + 2 < NBO:
                        prep(t + 2)
                    for _ in range(2):
                        if pre0:
                            stage_cast(*pre0.pop(0))
                    fin(t)
                    if t >= 1:
                        fin_sq(t - 1)
                fin_sq(NBO - 1)
                while pre0:
                    stage_cast(*pre0.pop(0))
                rstd_from_ss(sst.t[:], rst.t[:], sst.r, rst.r, D)
                for t in range(NBO):
                    m_ = mn[t % 2]; pt = pT[t % 2]
                    P.op("dve", lambda e, t=t, m_=m_: e.scalar_tensor_tensor(out=m_.t[:], in0=merged.t[:, t, :], scalar=rst.t[:, t, 0:1],
                                                                       in1=gm["ffn"].t[:], op0=ALU.mult, op1=ALU.mult),
                         reads=[mres[t], rst.r, gm["ffn"].r], writes=[m_.r])
                    for c in range(DC):
                        tr(pt.t[:, c, :], m_.t[:, c * 128:(c + 1) * 128], idb.t[:], [m_.r, idb.r], [pt.r])
                    copy_op("act" if t % 2 == 0 else "dve", fT.t[:, :, t * 128:(t + 1) * 128], pt.t[:], [pt.r], [fT.r])
                P.end_phase()

            with ExitStack() as e2:
                wg[1] = sb(e2, "wg1", [128, DC, 512], BF16)
                wu[1] = sb(e2, "wu1", [128, DC, 512], BF16)
                wd[1] = sb(e2, "wd1", [128, 4, D], BF16)
                aT = [sb(e2, "aT%d" % i, [128, 4, 512], BF16) for i in range(2)]
                sg = [sb(e2, "sg%d" % i, [128, 512], F32) for i in range(2)]
                yt = [sb(e2, "yt%d" % i, [128, D], F32) for i in range(3)]
                pg = [ps(e2, "pg%d" % i, [128, 512], F32) for i in range(2)]
                pu = [ps(e2, "pu%d" % i, [128, 512], F32) for i in range(2)]
                pd = [ps(e2, "pd%d" % i, [128, 1024], F32) for i in range(2)]
                k = 0
                pend = None

                def down(k_, sl_, nfc_, tg_):
                    a_ = aT[k_ % 2]
                    for tt in range(4):
                        t = tg_ * 4 + tt
                        pd_ = pd[(k_ * 4 + tt) % 2]
                        for nh in range(2):
                            for fc in range(nfc_):
                                mm(pd_.t[:, nh * 512:(nh + 1) * 512], a_.t[:, fc, tt * 128:(tt + 1) * 128], wd[sl_].t[:, fc, nh * 512:(nh + 1) * 512],
                                   fc == 0, fc == nfc_ - 1, [a_.r, wd[sl_].r], [pd_.r])
                        P.op("dve", lambda e, t=t, pd_=pd_: e.tensor_tensor(out=merged.t[:, t, :], in0=pd_.t[:, :], in1=merged.t[:, t, :], op=ALU.add),
                             reads=[pd_.r, mres[t]], writes=[mres[t]])

                if NFG > 1:
                    ffn_loads(1)
                for fg in range(NFG):
                    f0 = fg * 512
                    nfc = min(512, DFF - f0) // 128
                    sl = fg % 2
                    for tg in range(NG):
                        a_ = aT[k % 2]
                        for fc in range(nfc):
                            pg_ = pg[(k * 4 + fc) % 2]; pu_ = pu[(k * 4 + fc) % 2]; sg_ = sg[(k * 4 + fc) % 2]
                            for c in range(DC):
                                mm(pg_.t[:, :], wg[sl].t[:, c, fc * 128:(fc + 1) * 128], fT.t[:, c, tg * 512:(tg + 1) * 512],
                                   c == 0, c == DC - 1, [wg[sl].r, fT.r], [pg_.r])
                            for c in range(DC):
                                mm(pu_.t[:, :], wu[sl].t[:, c, fc * 128:(fc + 1) * 128], fT.t[:, c, tg * 512:(tg + 1) * 512],
                                   c == 0, c == DC - 1, [wu[sl].r, fT.r], [pu_.r])
                            P.op("act", lambda e, pg_=pg_, sg_=sg_: e.activation(out=sg_.t[:], in_=pg_.t[:, :], func=AF.Silu),
                                 reads=[pg_.r], writes=[sg_.r])
                            P.op("dve", lambda e, pu_=pu_, sg_=sg_, a_=a_, fc=fc: e.tensor_tensor(out=a_.t[:, fc, :], in0=pu_.t[:, :], in1=sg_.t[:],
                                                                                             op=ALU.mult),
                                 reads=[pu_.r, sg_.r], writes=[a_.r])
                        if pend is not None:
                            down(*pend)
                        pend = (k, sl, nfc, tg)
                        k += 1
                        if tg == 0 and fg >= 1 and fg + 1 < NFG:
                            ffn_loads(fg + 1)
                down(*pend)
                for t in range(NBO):
                    P.op("act", lambda e, t=t: e.activation(out=junk.t[:], in_=merged.t[:, t, :], func=AF.Square, accum_out=sst.t[:, t, 0:1]),
                         reads=[mres[t]], writes=[junk.r, sst.r])
                rstd_from_ss(sst.t[:], rst.t[:], sst.r, rst.r, D)
                for t in range(NBO):
                    y_ = yt[t % 3]
                    P.op("dve", lambda e, t=t, y_=y_: e.scalar_tensor_tensor(out=y_.t[:], in0=merged.t[:, t, :], scalar=rst.t[:, t, 0:1],
                                                                       in1=gm["fin"].t[:], op0=ALU.mult, op1=ALU.mult),
                         reads=[mres[t], rst.r, gm["fin"].r], writes=[y_.r])
                    load("sp", y[t * 128:(t + 1) * 128, :], y_.t[:], [], reads=[y_.r])
                P.end_phase()
    return nc, {}


def host_inputs(inputs, S):
    NB = S // 128
    f = np.float32
    x = np.asarray(inputs["x"], f)
    pos = np.asarray(inputs["positions"], np.int32)

    def kchunk(w):
        K, E = w.shape
        return np.ascontiguousarray(w.reshape(K // 128, 128, E).transpose(1, 0, 2))

    def rep(v):
        return np.ascontiguousarray(np.broadcast_to(np.asarray(v, f).reshape(1, -1), (128, v.size)))

    w_in = kchunk(np.asarray(inputs["w_in"], f)[0])
    w_uq = kchunk(np.asarray(inputs["w_uq"], f)[0]).reshape(128, 2 * 768)
    wukv = np.asarray(inputs["w_ukv"], f)[0].reshape(128, H, 128)
    w_uk = np.ascontiguousarray(wukv[:, :, 0:64].reshape(128, 512))
    w_uv = np.ascontiguousarray(wukv[:, :, 64:128].reshape(128, 512))
    w_o = kchunk(np.asarray(inputs["w_o"], f)[0])
    w_g = kchunk(np.asarray(inputs["w_gate"], f)[0])
    w_u = kchunk(np.asarray(inputs["w_up"], f)[0])
    w_d = kchunk(np.asarray(inputs["w_down"], f)[0])
    ident = np.eye(128, dtype=f)
    jj = np.arange(128)[:, None]
    ss_ = np.arange(128)[None, :]
    tri = np.stack([(jj >= ss_).astype(f), (jj < ss_).astype(f)], axis=1)
    causal = (jj <= ss_).astype(f)
    strict = (jj < ss_).astype(f)
    common = dict(ident=ident, tri=np.ascontiguousarray(tri), w_in=w_in, w_uq=w_uq, w_uk=w_uk, w_uv=w_uv, w_o=w_o,
                  w_gate=w_g, w_up=w_u, w_down=w_d,
                  g_mix=rep(inputs["norm_mix"][0]), g_q=rep(inputs["q_latent_norm"][0]), g_kv=rep(inputs["kv_latent_norm"][0]),
                  g_mla=rep(inputs["out_norm_mla"][0]), g_sb=rep(inputs["out_norm_sb"][0]), g_ffn=rep(inputs["norm_ffn"][0]),
                  g_fin=rep(inputs["norm_final"]),
                  g_o=np.ascontiguousarray(np.concatenate([np.asarray(inputs["out_norm_mla"], f)[0],
                                                           np.asarray(inputs["out_norm_sb"], f)[0]]).reshape(DC, 128).T))
    maps = []
    for c in range(8):
        b, j = c // 4, c % 4
        ob = own_blocks(j, NB)
        rows = np.concatenate([np.arange(k * 128, (k + 1) * 128) for k in ob])
        masks = np.zeros((128, 2, 2, 4, 128), f)
        for ti, tm in enumerate((causal, strict)):
            for hf, off in enumerate((j, 3 - j)):
                for r in range(4):
                    if r < off:
                        masks[:, ti, hf, r, :] = 0.0
                    elif r == off:
                        masks[:, ti, hf, r, :] = NEG * (1.0 - tm)
                    else:
                        masks[:, ti, hf, r, :] = NEG
        m = dict(common)
        m["xb"] = np.ascontiguousarray(x[b])
        m["xo"] = np.ascontiguousarray(x[b][rows])
        m["posb"] = np.ascontiguousarray(pos[b].reshape(NB, 128).T)
        m["poso"] = np.ascontiguousarray(pos[b][rows].reshape(len(ob), 128).T)
        m["masks"] = masks.reshape(128, 16 * 128)
        maps.append(m)
    return maps


def assemble(results, S, B=2):
    NB = S // 128
    out = np.zeros((B, S, D), np.float32)
    for c in range(8):
        b, j = c // 4, c % 4
        ob = own_blocks(j, NB)
        yy = np.asarray(results[c]["y"], np.float32)
        for i, k in enumerate(ob):
            out[b, k * 128:(k + 1) * 128, :] = yy[i * 128:(i + 1) * 128, :]
    return out


_NC_CACHE = {}


def kernel(**inputs):
    S = int(np.asarray(inputs["x"]).shape[1])
    if S not in _NC_CACHE:
        _NC_CACHE[S] = build(S, Prog, Res)[0]
    nc = _NC_CACHE[S]
    maps = host_inputs(inputs, S)
    res = run_bass_kernel_spmd(nc, maps, core_ids=list(range(8)))
    return assemble(res.results, S, B=int(np.asarray(inputs["x"]).shape[0]))
```

```python
import math
from contextlib import ExitStack
import numpy as np
import concourse.bass as bass
import concourse.mybir as mybir
from concourse.bass_utils import run_bass_kernel_spmd

F32 = mybir.dt.float32
BF16 = mybir.dt.bfloat16
I32 = mybir.dt.int32
AF = mybir.ActivationFunctionType
ALU = mybir.AluOpType


ENGS = ("pe", "act", "dve", "pool", "sp")


class Res:
    __slots__ = ("name", "w", "r", "excl")

    def __init__(self, name, excl=False):
        self.name = name
        self.w = None
        self.r = []
        self.excl = excl


class Op:
    __slots__ = ("fn", "waits", "sig", "dma", "idx")

    def __init__(self, fn, waits, dma, idx):
        self.fn = fn
        self.waits = waits
        self.sig = False
        self.dma = dma
        self.idx = idx


class Prog:
    NDMA = 40

    def __init__(self, nc):
        self.nc = nc
        self.sem = {e: nc.alloc_semaphore("s_" + e) for e in ENGS}
        self.dsem = [nc.alloc_semaphore("d%d" % i) for i in range(self.NDMA)]
        self.dcount = [0] * self.NDMA
        self.dnext = 0
        self.ops = {e: [] for e in ENGS}
        self.start = {e: 0 for e in ENGS}
        self.base = {e: 0 for e in ENGS}
        self.seen = {e: {x: 0 for x in ENGS} for e in ENGS}
        self.seen_d = {e: [0] * self.NDMA for e in ENGS}

    def _deps(self, reads, writes):
        ev = []
        for r in reads:
            if r.excl:
                writes = list(writes) + [r]
                continue
            if r.w is not None:
                ev.append(r.w)
        for w in writes:
            if w.w is not None:
                ev.append(w.w)
            ev.extend(w.r)
        return ev

    def _filter(self, eng, evs):
        out = []
        best = {}
        for e in evs:
            if e[0] == "c":
                _, x, i = e
                if x == "pe" and eng == "pe":
                    continue
                if i <= self.seen[eng][x]:
                    continue
                if i > best.get(("c", x), 0):
                    best[("c", x)] = i
            else:
                _, s, c = e
                if c <= self.seen_d[eng][s]:
                    continue
                if c > best.get(("d", s), 0):
                    best[("d", s)] = c
        for k, v in best.items():
            if k[0] == "c":
                self.seen[eng][k[1]] = v
                self.ops[k[1]][v - 1].sig = True
                out.append(("c", k[1], v))
            else:
                self.seen_d[eng][k[1]] = v
                out.append(("d", k[1], v))
        return out

    def _commit(self, ev, reads, writes):
        for r in reads:
            if r.excl:
                r.w = ev
                r.r = []
            else:
                r.r.append(ev)
        for w in writes:
            w.w = ev
            w.r = []

    def op(self, eng, fn, reads=(), writes=()):
        waits = self._filter(eng, self._deps(reads, writes))
        lst = self.ops[eng]
        o = Op(fn, waits, None, len(lst) + 1)
        lst.append(o)
        self._commit(("c", eng, o.idx), reads, writes)
        return o

    def dma(self, q, fn, reads=(), writes=()):
        s = self.dnext
        self.dnext = (self.dnext + 1) % self.NDMA
        evs = self._deps(reads, writes)
        if self.dcount[s] > 0:
            evs.append(("d", s, self.dcount[s]))
        waits = self._filter(q, evs)
        self.dcount[s] += 16
        lst = self.ops[q]
        o = Op(fn, waits, (s, self.dcount[s]), len(lst) + 1)
        lst.append(o)
        self._commit(("d", s, self.dcount[s]), reads, writes)
        return o

    def barrier(self):
        evs = [("d", s, self.dcount[s]) for s in range(self.NDMA) if self.dcount[s] > 0]
        waits = self._filter("sp", evs)
        lst = self.ops["sp"]
        o = Op(lambda e: e.nop(), waits, None, len(lst) + 1)
        lst.append(o)
        for x in ENGS:
            for s in range(self.NDMA):
                self.seen_d[x][s] = self.dcount[s]
            for y in ENGS:
                self.seen[x][y] = len(self.ops[y])

    def end_phase(self):
        self.barrier()
        self.replay()

    def replay(self):
        nc = self.nc
        sigcnt = {}
        for e in ENGS:
            c = self.base[e]
            arr = []
            for o in self.ops[e][self.start[e]:]:
                if o.sig:
                    c += 1
                arr.append(c)
            sigcnt[e] = arr

        def val(x, idx1):
            st = self.start[x]
            if idx1 - 1 < st:
                return self._hist[x][idx1 - 1]
            return sigcnt[x][idx1 - 1 - st]

        if not hasattr(self, "_hist"):
            self._hist = {e: [] for e in ENGS}

        def emit(ename, eng):
            for o in self.ops[ename][self.start[ename]:]:
                for w in o.waits:
                    if w[0] == "c":
                        eng.wait_ge(self.sem[w[1]], val(w[1], w[2]))
                    else:
                        eng.wait_ge(self.dsem[w[1]], w[2])
                ins = o.fn(eng)
                if o.dma is not None:
                    ins.then_inc(self.dsem[o.dma[0]], 16)
                elif o.sig:
                    ins.then_inc(self.sem[ename], 1)

        with nc.Block() as block:
            @block.tensor
            def _(t):
                emit("pe", t)

            @block.scalar
            def _(t):
                emit("act", t)

            @block.vector
            def _(t):
                emit("dve", t)

            @block.gpsimd
            def _(t):
                emit("pool", t)

            @block.sync
            def _(t):
                emit("sp", t)

        for e in ENGS:
            self._hist[e].extend(sigcnt[e])
            self.base[e] = sigcnt[e][-1] if sigcnt[e] else self.base[e]
            self.start[e] = len(self.ops[e])


D = 1024
DC = 8
H = 8
DFF = 2816
FC = 22
EPS = 1e-6
INW = 1952
C_CQ, C_CKV, C_KR, C_QSB, C_KSB, C_VSB = 0, 256, 384, 416, 928, 1440
MLA_SCALE = 1.0 / math.sqrt(96.0)
NEG = -30000.0


def own_blocks(j, NB):
    half = NB // 8
    return [4 * m + j for m in range(half)] + [4 * m + 3 - j for m in range(half, 2 * half)]


def build(S, Prog, Res, stop_after=None):
    NB = S // 128
    NBO = NB // 4
    NG = NBO // 4
    NCH = NB // 4
    SO = NBO * 128
    nc = bass.Bass("TRN2", target_bir_lowering=False)

    def din(name, shape, dt=F32):
        return nc.dram_tensor(name, list(shape), dt, kind="ExternalInput").ap()

    xb = din("xb", [S, D])
    xo = din("xo", [SO, D])
    posb = din("posb", [128, NB], I32)
    poso = din("poso", [128, NBO], I32)
    ident_d = din("ident", [128, 128])
    tri_d = din("tri", [128, 2, 128])
    w_in_d = din("w_in", [128, DC, INW])
    w_uq_d = din("w_uq", [128, 2 * 768])
    w_uk_d = din("w_uk", [128, 512])
    w_uv_d = din("w_uv", [128, 512])
    w_o_d = din("w_o", [128, DC, D])
    w_g_d = din("w_gate", [128, DC, DFF])
    w_u_d = din("w_up", [128, DC, DFF])
    w_d_d = din("w_down", [128, FC, D])
    g_d = {"mix": din("g_mix", [128, D]), "q": din("g_q", [128, 256]), "kv": din("g_kv", [128, 128]),
           "mla": din("g_mla", [128, 512]), "sb": din("g_sb", [128, 512]), "ffn": din("g_ffn", [128, D]),
           "fin": din("g_fin", [128, D])}
    g_o_d = din("g_o", [128, DC])
    mask_d = din("masks", [128, 16 * 128])
    y = nc.dram_tensor("y", [SO, D], F32, kind="ExternalOutput").ap()

    KTm = nc.dram_tensor("KTm", [4, 128, S], BF16).ap()
    KR = nc.dram_tensor("KR", [32, S], BF16).ap()
    Vm = nc.dram_tensor("Vm", [H, 128, NB, 65], BF16).ap()
    KTs = nc.dram_tensor("KTs", [4, 128, S], BF16).ap()
    Vs = nc.dram_tensor("Vs", [H, 128, NB, 65], BF16).ap()
    QTm = nc.dram_tensor("QTm", [H, 96, SO], BF16).ap()
    QTs = nc.dram_tensor("QTs", [4, 128, SO], BF16).ap()

    P = Prog(nc)
    inv_freq = (10000.0 ** (-np.arange(0, 32, 2, dtype=np.float32) / np.float32(32))).astype(np.float32)

    class T:
        def __init__(self, t, excl=False):
            self.t = t
            self.r = Res("r", excl)

    def sb(es, name, shape, dt):
        return T(es.enter_context(nc.sbuf_tensor(name, list(shape), dt)))

    def ps(es, name, shape, dt=F32):
        return T(es.enter_context(nc.psum_tensor(name, list(shape), dt)), excl=True)

    rrs = {"evac": 0}

    def evac_eng():
        rrs["evac"] ^= 1
        return "act" if rrs["evac"] else "dve"

    def copy_op(eng, out, in_, reads, writes, scale=None):
        if eng == "act":
            if scale is None:
                P.op("act", lambda e: e.activation(out=out, in_=in_, func=AF.Copy), reads=reads, writes=writes)
            else:
                P.op("act", lambda e: e.activation(out=out, in_=in_, func=AF.Copy, scale=scale), reads=reads, writes=writes)
        elif scale is None:
            P.op(eng, lambda e: e.tensor_copy(out=out, in_=in_), reads=reads, writes=writes)
        else:
            P.op(eng, lambda e: e.tensor_scalar(out=out, in0=in_, scalar1=scale, scalar2=None, op0=ALU.mult), reads=reads, writes=writes)

    def mm(out, lhsT, rhs, start, stop, reads, writes, skip=False):
        if skip:
            P.op("pe", lambda e: e.matmul(out, lhsT=lhsT, rhs=rhs, start=start, stop=stop, skip_group_check=True), reads=reads, writes=writes)
        else:
            P.op("pe", lambda e: e.matmul(out, lhsT=lhsT, rhs=rhs, start=start, stop=stop), reads=reads, writes=writes)

    def tr(out, in_, ident, reads, writes):
        P.op("pe", lambda e: e.transpose(out=out, in_=in_, identity=ident), reads=reads, writes=writes)

    def load(q, out, in_, writes, reads=()):
        P.dma(q, lambda e: e.dma_start(out=out, in_=in_), reads=reads, writes=writes)

    with ExitStack() as g:
        idf = sb(g, "idf", [128, 128], F32)
        idb = sb(g, "idb", [128, 128], BF16)
        zerob = sb(g, "zerob", [128, 128], BF16)
        epsc = sb(g, "epsc", [128, 1], F32)
        onec = sb(g, "onec", [128, 1], F32)
        pic = sb(g, "pic", [128, 1], F32)
        gm = {}

        def load_gains(es_, names):
            for nm in names:
                gm[nm] = sb(es_, "gs_" + nm, [128, g_d[nm].shape[1]], F32)
                load("sp", gm[nm].t[:], g_d[nm][:, :], [gm[nm].r])
        load("sp", idf.t[:], ident_d[:, :], [idf.r])
        P.op("pool", lambda e: e.tensor_copy(out=idb.t[:], in_=idf.t[:]), reads=[idf.r], writes=[idb.r])
        P.op("pool", lambda e: e.memset(zerob.t[:], 0.0), writes=[zerob.r])
        P.op("pool", lambda e: e.memset(epsc.t[:], EPS), writes=[epsc.r])
        P.op("pool", lambda e: e.memset(onec.t[:], 1.0), writes=[onec.r])
        P.op("pool", lambda e: e.memset(pic.t[:], math.pi), writes=[pic.r])

        def rstd_from_ss(ssap, rsap, rd, wr, n):
            P.op("act", lambda e: e.activation(out=rsap, in_=ssap, func=AF.Ln, scale=1.0 / n, bias=epsc.t[:]),
                 reads=[rd, epsc.r], writes=[wr])
            P.op("act", lambda e: e.activation(out=rsap, in_=rsap, func=AF.Exp, scale=-0.5), reads=[wr], writes=[wr])

        with ExitStack() as es:
            win = sb(es, "win", [128, DC, INW], BF16)
            wuq = sb(es, "wuq", [128, 2 * 768], BF16)
            wuk = sb(es, "wuk", [128, 512], BF16)
            wuv = sb(es, "wuv", [128, 512], BF16)
            cosb = sb(es, "cosb", [128, NB, 16], F32)
            sinb = sb(es, "sinb", [128, NB, 16], F32)
            coso = sb(es, "coso", [128, NBO, 16], F32)
            sino = sb(es, "sino", [128, NBO, 16], F32)
            load_gains(es, ("mix", "q", "kv"))
            es0 = es
            es = ExitStack()
            es.__enter__()
            wst = [sb(es, "wst%d" % i, [128, 2048], F32) for i in range(4)]
            ropi = sb(es, "ropi", [128, NB], I32)
            ropf = sb(es, "ropf", [128, NB], F32)
            ropt = sb(es, "ropt", [128, NB, 16], F32)
            ropk = sb(es, "ropk", [128, NB, 16], I32)
            ropg = sb(es, "ropg", [128, NB, 16], F32)
            roph = sb(es, "roph", [128, NB, 16], F32)

            def rope_table(pos_d, n, sink_sin, sink_cos):
                load("sp", ropi.t[:, 0:n], pos_d[:, :], [ropi.r])
                P.op("dve", lambda e: e.tensor_copy(out=ropf.t[:, 0:n], in_=ropi.t[:, 0:n]), reads=[ropi.r], writes=[ropf.r])
                for i in range(16):
                    P.op("dve", lambda e, i=i: e.tensor_scalar(out=ropt.t[:, 0:n, i], in0=ropf.t[:, 0:n], scalar1=float(inv_freq[i]),
                                                                scalar2=1.0 / (2 * math.pi), op0=ALU.mult, op1=ALU.mult),
                         reads=[ropf.r], writes=[ropt.r])
                for ph, sink in ((0.0, sink_sin), (0.25, sink_cos)):
                    P.op("dve", lambda e, ph=ph: e.tensor_scalar(out=ropg.t[:, 0:n, :], in0=ropt.t[:, 0:n, :], scalar1=ph, scalar2=None, op0=ALU.add),
                         reads=[ropt.r], writes=[ropg.r])
                    P.op("dve", lambda e: e.tensor_copy(out=ropk.t[:, 0:n, :], in_=ropg.t[:, 0:n, :]), reads=[ropg.r], writes=[ropk.r])
                    P.op("dve", lambda e: e.tensor_copy(out=roph.t[:, 0:n, :], in_=ropk.t[:, 0:n, :]), reads=[ropk.r], writes=[roph.r])
                    P.op("dve", lambda e: e.tensor_tensor(out=ropg.t[:, 0:n, :], in0=ropg.t[:, 0:n, :], in1=roph.t[:, 0:n, :], op=ALU.subtract),
                         reads=[ropg.r, roph.r], writes=[ropg.r])
                    P.op("dve", lambda e: e.scalar_tensor_tensor(out=roph.t[:, 0:n, :], in0=ropg.t[:, 0:n, :], scalar=0.0, in1=ropg.t[:, 0:n, :],
                                                                 op0=ALU.is_lt, op1=ALU.add), reads=[ropg.r], writes=[roph.r])
                    sink(roph, n)

            def sink_b(tile):
                def f(src, n):
                    P.op("act", lambda e: e.activation(out=tile.t[:], in_=src.t[:, 0:n, :], func=AF.Sin, scale=-2.0 * math.pi, bias=pic.t[:]),
                         reads=[src.r, pic.r], writes=[tile.r])
                return f

            rope_table(posb, NB, sink_b(sinb), sink_b(cosb))
            rope_table(poso, NBO, sink_b(sino), sink_b(coso))

            k = 0
            for c in range(DC):
                st = wst[k % 4]; k += 1
                load("sp", st.t[:, 0:INW], w_in_d[:, c, :], [st.r])
                copy_op(("act", "dve", "act", "pool")[c % 4], win.t[:, c, :], st.t[:, 0:INW], [st.r], [win.r])
            st = wst[k % 4]; k += 1
            load("sp", st.t[:, 0:1536], w_uq_d[:, :], [st.r])
            copy_op("act", wuq.t[:], st.t[:, 0:1536], [st.r], [wuq.r])
            st = wst[k % 4]; k += 1
            load("sp", st.t[:, 0:512], w_uk_d[:, :], [st.r])
            load("sp", st.t[:, 512:1024], w_uv_d[:, :], [st.r])
            copy_op("dve", wuk.t[:], st.t[:, 0:512], [st.r], [wuk.r])
            copy_op("dve", wuv.t[:], st.t[:, 512:1024], [st.r], [wuv.r])

            P.end_phase()
            es.__exit__(None, None, None)
            es = ExitStack()
            es.__enter__()
            xbuf = [sb(es, "xbuf%d" % i, [128, D], F32) for i in range(8)]
            junk = sb(es, "junk", [128, D], F32)
            ubuf = [sb(es, "ubuf%d" % i, [128, D], BF16) for i in range(3)]
            uT = [sb(es, "uT%d" % i, [128, DC, 512], BF16) for i in range(2)]
            ss4 = [sb(es, "ss4_%d" % i, [128, 4], F32) for i in range(2)]
            rs4 = [sb(es, "rs4_%d" % i, [128, 4], F32) for i in range(2)]
            ssc = [sb(es, "ssc_%d" % i, [128, 4], F32) for i in range(2)]
            rsc = [sb(es, "rsc_%d" % i, [128, 4], F32) for i in range(2)]
            st4 = [sb(es, "st4_%d" % i, [128, 4, 512], BF16) for i in range(3)]
            stv = [sb(es, "stv%d" % i, [128, H, 4, 65], BF16) for i in range(3)]
            ckr = [sb(es, "ckr%d" % i, [128, 160], BF16) for i in range(4)]
            rtmp = [sb(es, "rtmp%d" % i, [128, 2, 16], F32) for i in range(2)]
            ckvT = [sb(es, "ckvT%d" % i, [128, 512], BF16) for i in range(2)]
            krT = [sb(es, "krT%d" % i, [32, 512], BF16) for i in range(2)]
            cqn = [sb(es, "cqn%d" % i, [128, 256], BF16) for i in range(4)]
            cqT = [sb(es, "cqT%d" % i, [128, 2, 512], BF16) for i in range(2)]
            qtok = [sb(es, "qtok%d" % i, [128, H, 96], BF16) for i in range(2)]
            qrt = [sb(es, "qrt%d" % i, [128, 2, H, 16], F32) for i in range(2)]
            qmst = [sb(es, "qmst%d" % i, [128, H, 512], BF16) for i in range(2)]
            pT = [ps(es, "pT%d" % i, [128, DC, 128], BF16) for i in range(2)]
            pP = [ps(es, "pP%d" % i, [128, 512], F32) for i in range(3)]
            pC = ps(es, "pC", [128, 4, 256], F32)
            pQ = ps(es, "pQ", [128, 512], F32)
            for v in stv:
                P.op("pool", lambda e, v=v: e.memset(v.t[:], 1.0), writes=[v.r])
            ctr = {"x": 0, "u": 0, "pT": 0, "pP": 0, "st4": 0, "stv": 0}

            def nxt(lst, key):
                v = lst[ctr[key] % len(lst)]
                ctr[key] += 1
                return v

            def rope_tok(x1, x2, cs, sn, o1, o2, tA, tB, rd, tmp_r, out_r):
                P.op("dve", lambda e: e.tensor_tensor(out=tA, in0=x1, in1=cs, op=ALU.mult), reads=rd, writes=[tmp_r])
                P.op("dve", lambda e: e.tensor_tensor(out=tB, in0=x2, in1=sn, op=ALU.mult), reads=rd, writes=[tmp_r])
                P.op("dve", lambda e: e.tensor_tensor(out=o1, in0=tA, in1=tB, op=ALU.subtract), reads=[tmp_r], writes=[out_r])
                P.op("dve", lambda e: e.tensor_tensor(out=tA, in0=x2, in1=cs, op=ALU.mult), reads=rd, writes=[tmp_r])
                P.op("dve", lambda e: e.tensor_tensor(out=tB, in0=x1, in1=sn, op=ALU.mult), reads=rd, writes=[tmp_r])
                P.op("dve", lambda e: e.tensor_tensor(out=o2, in0=tA, in1=tB, op=ALU.add), reads=[tmp_r], writes=[out_r])

            chunks = [("A", ci) for ci in range(NCH)] + [("B", gi) for gi in range(NG)]
            xtiles = {}

            def chunk_src(k):
                typ, i = chunks[k]
                return (xb if typ == "A" else xo), i

            def front_loads(k):
                src, i = chunk_src(k)
                xs = [nxt(xbuf, "x") for _ in range(4)]
                xtiles[k] = xs
                for t in range(4):
                    load("sp", xs[t].t[:], src[(4 * i + t) * 128:(4 * i + t + 1) * 128, :], [xs[t].r])

            def front_stats(k):
                xs = xtiles[k]
                s4 = ss4[k % 2]; r4 = rs4[k % 2]
                for t in range(4):
                    P.op("act", lambda e, t=t: e.activation(out=junk.t[:], in_=xs[t].t[:], func=AF.Square, accum_out=s4.t[:, t:t + 1]),
                         reads=[xs[t].r], writes=[junk.r, s4.r])
                rstd_from_ss(s4.t[:], r4.t[:], s4.r, r4.r, D)

            def front_norm_T(k):
                xs = xtiles[k]
                r4 = rs4[k % 2]
                uTt = uT[k % 2]
                for t in range(4):
                    ub = nxt(ubuf, "u"); pt = nxt(pT, "pT")
                    P.op("dve", lambda e, t=t, ub=ub: e.scalar_tensor_tensor(out=ub.t[:], in0=xs[t].t[:], scalar=r4.t[:, t:t + 1], in1=gm["mix"].t[:],
                                                                       op0=ALU.mult, op1=ALU.mult),
                         reads=[xs[t].r, r4.r, gm["mix"].r], writes=[ub.r])
                    for c in range(DC):
                        tr(pt.t[:, c, :], ub.t[:, c * 128:(c + 1) * 128], idb.t[:], [ub.r, idb.r], [pt.r])
                    copy_op("act", uTt.t[:, :, t * 128:(t + 1) * 128], pt.t[:], [pt.r], [uTt.r])

            def fm_proj(uTt, col0, dst4, scale=None):
                for pr in range(4):
                    p_ = nxt(pP, "pP")
                    for c in range(DC):
                        mm(p_.t[:, :], win.t[:, c, col0 + pr * 128:col0 + (pr + 1) * 128], uTt.t[:, c, :],
                           c == 0, c == DC - 1, [win.r, uTt.r], [p_.r])
                    copy_op(evac_eng(), dst4.t[:, pr, :], p_.t[:, :], [p_.r], [dst4.r], scale=scale)

            def proj_A1(k):
                ci = chunks[k][1]
                uTt = uT[k % 2]
                ks = nxt(st4, "st4")
                fm_proj(uTt, C_KSB, ks)
                load("sp", KTs[:, :, ci * 512:(ci + 1) * 512].rearrange("a p n -> p a n"), ks.t[:], [], reads=[ks.r])
                for t in range(4):
                    for c in range(DC):
                        mm(pC.t[:, t, 0:160], uTt.t[:, c, t * 128:(t + 1) * 128], win.t[:, c, C_CKV:C_CKV + 160],
                           c == 0, c == DC - 1, [win.r, uTt.r], [pC.r])
                s4 = ssc[k % 2]; r4 = rsc[k % 2]
                for t in range(4):
                    P.op("act", lambda e, t=t: e.activation(out=junk.t[:, 0:128], in_=pC.t[:, t, 0:128], func=AF.Square, accum_out=s4.t[:, t:t + 1]),
                         reads=[pC.r], writes=[junk.r, s4.r])
                rstd_from_ss(s4.t[:], r4.t[:], s4.r, r4.r, 128)

            def proj_A2(k):
                ci = chunks[k][1]
                uTt = uT[k % 2]
                r4 = rsc[k % 2]
                ckT = ckvT[k % 2]
                krt = krT[k % 2]
                cks = []
                for t in range(4):
                    kb = 4 * ci + t
                    ck = ckr[t]
                    tm = rtmp[t % 2]
                    cks.append(ck)
                    P.op("dve", lambda e, ck=ck, t=t: e.scalar_tensor_tensor(out=ck.t[:, 0:128], in0=pC.t[:, t, 0:128], scalar=r4.t[:, t:t + 1],
                                                                       in1=gm["kv"].t[:], op0=ALU.mult, op1=ALU.mult),
                         reads=[pC.r, r4.r, gm["kv"].r], writes=[ck.r])
                    rope_tok(pC.t[:, t, 128:144], pC.t[:, t, 144:160], cosb.t[:, kb, :], sinb.t[:, kb, :],
                             ck.t[:, 128:144], ck.t[:, 144:160], tm.t[:, 0, :], tm.t[:, 1, :],
                             [pC.r, cosb.r, sinb.r], tm.r, ck.r)
                vs_ = nxt(stv, "stv")
                for t in range(4):
                    p_ = nxt(pP, "pP")
                    for c in range(DC):
                        mm(p_.t[:, :], uTt.t[:, c, t * 128:(t + 1) * 128], win.t[:, c, C_VSB:C_VSB + 512],
                           c == 0, c == DC - 1, [win.r, uTt.r], [p_.r])
                    copy_op("act", vs_.t[:, :, t, 0:64], p_.t[:, :].rearrange("p (h d) -> p h d", d=64), [p_.r], [vs_.r])
                load("sp", Vs[:, :, 4 * ci:4 * ci + 4, :].rearrange("h p t e -> p h t e"), vs_.t[:], [], reads=[vs_.r])
                pt = nxt(pT, "pT")
                for t in range(4):
                    tr(pt.t[:, t, :], cks[t].t[:, 0:128], idb.t[:], [cks[t].r, idb.r], [pt.r])
                    tr(pt.t[0:32, 4 + t, :], cks[t].t[:, 128:160], idb.t[:], [cks[t].r, idb.r], [pt.r])
                copy_op("dve", ckT.t[:, :].rearrange("p (t n) -> p t n", n=128), pt.t[:, 0:4, :], [pt.r], [ckT.r])
                copy_op("dve", krt.t[:, :].rearrange("p (t n) -> p t n", n=128), pt.t[0:32, 4:8, :], [pt.r], [krt.r])
                load("sp", KR[:, ci * 512:(ci + 1) * 512], krt.t[:], [], reads=[krt.r])
                kn = nxt(st4, "st4")
                for pr in range(4):
                    p_ = nxt(pP, "pP")
                    mm(p_.t[:, :], wuk.t[:, pr * 128:(pr + 1) * 128], ckT.t[:, :], True, True, [wuk.r, ckT.r], [p_.r])
                    copy_op(evac_eng(), kn.t[:, pr, :], p_.t[:, :], [p_.r], [kn.r])
                load("sp", KTm[:, :, ci * 512:(ci + 1) * 512].rearrange("a p n -> p a n"), kn.t[:], [], reads=[kn.r])
                vm_ = nxt(stv, "stv")
                for t in range(4):
                    p_ = nxt(pP, "pP")
                    mm(p_.t[:, :], ckT.t[:, t * 128:(t + 1) * 128], wuv.t[:, :], True, True, [wuv.r, ckT.r], [p_.r])
                    copy_op(evac_eng(), vm_.t[:, :, t, 0:64], p_.t[:, :].rearrange("p (h d) -> p h d", d=64), [p_.r], [vm_.r])
                load("sp", Vm[:, :, 4 * ci:4 * ci + 4, :].rearrange("h p t e -> p h t e"), vm_.t[:], [], reads=[vm_.r])

            def proj_B1(k):
                gi = chunks[k][1]
                uTt = uT[k % 2]
                qs = nxt(st4, "st4")
                fm_proj(uTt, C_QSB, qs, scale=0.125)
                load("sp", QTs[:, :, gi * 512:(gi + 1) * 512].rearrange("a p n -> p a n"), qs.t[:], [], reads=[qs.r])
                for t in range(4):
                    for c in range(DC):
                        mm(pC.t[:, t, 0:256], uTt.t[:, c, t * 128:(t + 1) * 128], win.t[:, c, C_CQ:C_CQ + 256],
                           c == 0, c == DC - 1, [win.r, uTt.r], [pC.r])
                s4 = ssc[k % 2]; r4 = rsc[k % 2]
                for t in range(4):
                    P.op("act", lambda e, t=t: e.activation(out=junk.t[:, 0:256], in_=pC.t[:, t, 0:256], func=AF.Square, accum_out=s4.t[:, t:t + 1]),
                         reads=[pC.r], writes=[junk.r, s4.r])
                rstd_from_ss(s4.t[:], r4.t[:], s4.r, r4.r, 256)

            def proj_B2(k):
                gi = chunks[k][1]
                r4 = rsc[k % 2]
                cqt = cqT[k % 2]
                for t in range(4):
                    cq = cqn[t]
                    P.op("dve", lambda e, cq=cq, t=t: e.scalar_tensor_tensor(out=cq.t[:], in0=pC.t[:, t, 0:256], scalar=r4.t[:, t:t + 1],
                                                                       in1=gm["q"].t[:], op0=ALU.mult, op1=ALU.mult),
                         reads=[pC.r, r4.r, gm["q"].r], writes=[cq.r])
                pt = nxt(pT, "pT")
                for t in range(4):
                    for k2 in range(2):
                        tr(pt.t[:, 2 * t + k2, :], cqn[t].t[:, k2 * 128:(k2 + 1) * 128], idb.t[:], [cqn[t].r, idb.r], [pt.r])
                for k2 in range(2):
                    copy_op(evac_eng(), cqt.t[:, k2, :].rearrange("p (t n) -> p t n", n=128),
                            pt.t[:, :, :].rearrange("p (t k) n -> p t k n", k=2)[:, :, k2, :], [pt.r], [cqt.r])
                qm = qmst[k % 2]
                for t in range(4):
                    pos = 4 * gi + t
                    qt = qtok[t % 2]
                    qr = qrt[t % 2]
                    pa = nxt(pP, "pP")
                    for k2 in range(2):
                        mm(pa.t[:, 0:512], cqt.t[:, k2, t * 128:(t + 1) * 128], wuq.t[:, k2 * 768:k2 * 768 + 512],
                           k2 == 0, k2 == 1, [wuq.r, cqt.r], [pa.r])
                    for k2 in range(2):
                        mm(pQ.t[:, 0:256], cqt.t[:, k2, t * 128:(t + 1) * 128], wuq.t[:, k2 * 768 + 512:k2 * 768 + 768],
                           k2 == 0, k2 == 1, [wuq.r, cqt.r], [pQ.r])
                    qf = junk
                    copy_op("act", qf.t[:, 0:512], pa.t[:, 0:512], [pa.r], [qf.r])
                    copy_op("dve", qf.t[:, 512:768], pQ.t[:, 0:256], [pQ.r], [qf.r])
                    q3 = qf.t[:, 0:768].rearrange("p (h d) -> p h d", d=96)
                    copy_op("act", qt.t[:, :, 0:64], q3[:, :, 0:64], [qf.r], [qt.r])
                    rope_tok(q3[:, :, 64:80], q3[:, :, 80:96], coso.t[:, pos, :].unsqueeze(1).broadcast_to([128, H, 16]),
                             sino.t[:, pos, :].unsqueeze(1).broadcast_to([128, H, 16]),
                             qt.t[:, :, 64:80], qt.t[:, :, 80:96], qr.t[:, 0, :, :], qr.t[:, 1, :, :],
                             [qf.r, coso.r, sino.r], qr.r, qt.r)
                    pt = nxt(pT, "pT")
                    for h in range(H):
                        tr(pt.t[0:96, h, :], qt.t[:, h, :], idb.t[:], [qt.r, idb.r], [pt.r])
                    copy_op(evac_eng(), qm.t[0:96, :, t * 128:(t + 1) * 128], pt.t[0:96, :, :], [pt.r], [qm.r])
                load("sp", QTm[:, :, gi * 512:(gi + 1) * 512].rearrange("h r n -> r h n"), qm.t[0:96, :, :], [], reads=[qm.r])

            NK = len(chunks)
            front_loads(0)
            if NK > 1:
                front_loads(1)
            front_stats(0)
            front_norm_T(0)
            for k in range(NK):
                if k + 2 < NK:
                    front_loads(k + 2)
                if k + 1 < NK:
                    front_stats(k + 1)
                (proj_A1 if chunks[k][0] == "A" else proj_B1)(k)
                if k + 1 < NK:
                    front_norm_T(k + 1)
                (proj_A2 if chunks[k][0] == "A" else proj_B2)(k)
            P.end_phase()
            es.__exit__(None, None, None)
        if stop_after == "AB":
            return nc, dict(KTm=KTm, KR=KR, Vm=Vm, KTs=KTs, Vs=Vs, QTm=QTm, QTs=QTs)

        merged = sb(g, "merged", [128, NBO, D], F32)
        mres = [Res("m") for _ in range(NBO)]
        with ExitStack() as es:
            trib = sb(es, "trib", [128, 2, 128], BF16)
            maskb = sb(es, "maskb", [128, 16 * 128], BF16)
            mst = sb(es, "mst", [128, 2048], F32)
            mst2 = sb(es, "mst2", [128, 256], F32)
            load("sp", mst.t[:, 0:2048], mask_d[:, :], [mst.r])
            P.op("pool", lambda e: e.tensor_copy(out=maskb.t[:], in_=mst.t[:, 0:2048]), reads=[mst.r], writes=[maskb.r])
            load("sp", mst2.t[:, 0:256], tri_d.rearrange("p a b -> p (a b)"), [mst2.r])
            P.op("pool", lambda e: e.tensor_copy(out=trib.t[:].rearrange("p a b -> p (a b)"), in_=mst2.t[:, 0:256]),
                 reads=[mst2.r], writes=[trib.r])
            Kb = [sb(es, "Kb%d" % i, [128, S], BF16) for i in range(2)]
            Vb = [sb(es, "Vb%d" % i, [128, NB, 65], BF16) for i in range(2)]
            Qb = [sb(es, "Qb%d" % i, [128, SO], BF16) for i in range(2)]
            eb2 = [sb(es, "eb2_%d" % i, [128, 2, 512], F32) for i in range(4)]
            lb2 = [sb(es, "lb2_%d" % i, [128, 2, 512], BF16) for i in range(2)]
            gb2 = [sb(es, "gb2_%d" % i, [128, 2, 512], F32) for i in range(2)]
            ab2 = [sb(es, "ab2_%d" % i, [128, 2, 512], BF16) for i in range(2)]
            ab = [sb(es, "ab%d" % i, [128, 512], BF16) for i in range(2)]
            ot = [sb(es, "ot%d" % i, [128, 512], F32) for i in range(2)]
            ot2 = sb(es, "ot2", [128, 4, 128], F32)
            rinv = [sb(es, "rinv%d" % i, [128, 4, 1], F32) for i in range(2)]
            zb2 = [ps(es, "zb2_%d" % i, [128, 2, 512], F32) for i in range(2)]
            zres = [[Res("z", True), Res("z", True)] for _ in range(2)]
            cb = ps(es, "cb", [128, 512], F32)
            oacc = [ps(es, "oacc%d" % i, [128, 512], F32) for i in range(2)]
            tp = ps(es, "tp", [128, 4, 128], F32)
            mk4 = maskb.t[:].rearrange("p (a b c q) -> p a b c q", a=2, b=2, c=4)
            halfpos = NBO // 2

            jobs = [("m", h) for h in range(H)] + [("s", h) for h in range(H)]

            def job_loads(k):
                typ, h = jobs[k]
                sl = k % 2
                Kt, Vt, Qt = Kb[sl], Vb[sl], Qb[sl]
                r0 = (h % 2) * 64
                if typ == "m":
                    load("sp", Kt.t[0:64, :], KTm[h // 2, r0:r0 + 64, :], [Kt.r])
                    load("sp", Kt.t[64:96, :], KR[:, :], [Kt.r])
                    load("sp", Vt.t[:], Vm[h], [Vt.r])
                    load("sp", Qt.t[0:96, :], QTm[h], [Qt.r])
                else:
                    load("sp", Kt.t[0:64, :], KTs[h // 2, r0:r0 + 64, :], [Kt.r])
                    load("sp", Kt.t[64:128, :], KTs[h // 2, r0:r0 + 64, :], [Kt.r])
                    load("sp", Vt.t[:], Vs[h], [Vt.r])
                    load("sp", Qt.t[0:64, :], QTs[h // 2, r0:r0 + 64, :], [Qt.r])
                    load("sp", Qt.t[64:128, :], QTs[h // 2, r0:r0 + 64, :], [Qt.r])

            gctr = [0]

            def tiles_for(gi):
                out = []
                for kb in range(16 * gi + 15, -1, -1):
                    pm = kb // 4
                    if pm >= 4 * gi:
                        lo = pm - 4 * gi
                        out.append((kb, lo * 128, True, 0 if pm < halfpos else 1, kb % 4))
                    else:
                        out.append((kb, 0, False, 0, 0))
                return out

            def finish_group(typ, h, gi, oa):
                o_ = ot[gctr[0] % 2]
                rv = rinv[gctr[0] % 2]
                nr = 65 if typ == "m" else 64
                copy_op("dve", o_.t[0:nr, :], oa.t[0:nr, :], [oa.r], [o_.r])
                for t in range(4):
                    tr(tp.t[:, t, 0:nr], o_.t[0:nr, t * 128:(t + 1) * 128], idf.t[0:nr, 0:nr], [o_.r, idf.r], [tp.r])
                if typ == "m":
                    P.op("dve", lambda e: e.reciprocal(out=rv.t[:], in_=tp.t[:, :, 64:65]), reads=[tp.r], writes=[rv.r])
                    for t in range(4):
                        P.op("dve", lambda e, t=t: e.tensor_scalar(out=merged.t[:, 4 * gi + t, h * 64:(h + 1) * 64], in0=tp.t[:, t, 0:64],
                                                                   scalar1=rv.t[:, t, :], scalar2=None, op0=ALU.mult),
                             reads=[tp.r, rv.r], writes=[mres[4 * gi + t]])
                else:
                    P.op("dve", lambda e: e.tensor_copy(out=merged.t[:, 4 * gi:4 * gi + 4, 512 + h * 64:512 + (h + 1) * 64], in_=tp.t[:, :, 0:64]),
                         reads=[tp.r], writes=[mres[4 * gi + t_] for t_ in range(4)])

            def zero_bank(bank, Qt):
                mm(bank.t[:, :], zerob.t[:, :], maskb.t[:, 0:512], True, True, [zerob.r, maskb.r], [bank.r])

            def job_items(ks):
                items = []
                g = gctr[0]
                for k in ks:
                    typ, h = jobs[k]
                    sl = k % 2
                    for gi in range(NG):
                        tl = tiles_for(gi)
                        npair = len(tl) // 2
                        for p in range(npair):
                            ta, tb_ = tl[2 * p], tl[2 * p + 1]
                            assert ta[1] == tb_[1] and ta[2] == tb_[2] and ta[3] == tb_[3]
                            items.append(dict(k=k, h=h, gi=gi, ta=ta, tb=tb_, first=(p == 0), last=(p == npair - 1), oa=oacc[g % 2],
                                              jobfirst=(gi == 0 and p == 0), Kt=Kb[sl], Vt=Vb[sl], Qt=Qb[sl]))
                        g += 1
                return items

            def run_mla(ks):
                items = job_items(ks)
                n = len(items)

                def S2(i):
                    it = items[i]; gi = it["gi"]; Kt = it["Kt"]; Qt = it["Qt"]
                    (kba, c0, msk, hf, ra), (kbb, _, _, _, rb) = it["ta"], it["tb"]
                    zt = zb2[i % 2].t; zr = zres[i % 2]
                    q0 = gi * 512 + c0
                    for j, kb_, r_ in ((0, kba, ra), (1, kbb, rb)):
                        mm(zt[:, j, c0:512], Kt.t[0:96, kb_ * 128:(kb_ + 1) * 128], Qt.t[0:96, q0:(gi + 1) * 512],
                           True, not msk, [Kt.r, Qt.r], [zr[j]])
                        if msk:
                            mm(zt[:, j, c0:c0 + 128], idb.t[:, :], mk4[:, 0, hf, r_, :], False, True, [idb.r, maskb.r], [zr[j]])

                def P2(i):
                    it = items[i]; oa = it["oa"]; Vt = it["Vt"]; Qt = it["Qt"]
                    (kba, c0, msk, hf, ra), (kbb, _, _, _, rb) = it["ta"], it["tb"]
                    zt = zb2[i % 2].t; zr = zres[i % 2]
                    a_ = ab2[i % 2]
                    o_ap = a_.t[:, :, c0:512]; i_ap = zt[:, :, c0:512]
                    P.op("act", lambda e, o_ap=o_ap, i_ap=i_ap: e.activation(out=o_ap, in_=i_ap, func=AF.Exp, scale=MLA_SCALE),
                         reads=[zr[0], zr[1]], writes=[a_.r])
                    if it["first"]:
                        zero_bank(oa, Qt)
                    mm(oa.t[0:65, c0:512], Vt.t[:, kba, 0:65], a_.t[:, 0, c0:512], False, False, [Vt.r, a_.r], [oa.r], skip=True)
                    mm(oa.t[0:65, c0:512], Vt.t[:, kbb, 0:65], a_.t[:, 1, c0:512], False, False, [Vt.r, a_.r], [oa.r], skip=True)
                    if it["jobfirst"] and it["k"] + 1 < len(jobs):
                        job_loads(it["k"] + 1)

                S2(0)
                for i in range(n):
                    if i + 1 < n:
                        S2(i + 1)
                    P2(i)
                    if items[i]["last"]:
                        finish_group("m", items[i]["h"], items[i]["gi"], items[i]["oa"])
                        gctr[0] += 1

            def finish_sb(h, gi, oa):
                o_ = ot[gctr[0] % 2]
                copy_op("dve", o_.t[:, :], oa.t[:, :], [oa.r], [o_.r])
                for t in range(4):
                    tr(tp.t[:, t, :], o_.t[:, t * 128:(t + 1) * 128], idf.t[:, :], [o_.r, idf.r], [tp.r])
                copy_op("dve", ot2.t[:], tp.t[:], [tp.r], [ot2.r])
                P.op("dve", lambda e: e.tensor_tensor(out=merged.t[:, 4 * gi:4 * gi + 4, 512 + h * 64:512 + (h + 1) * 64],
                                                      in0=ot2.t[:, :, 0:64], in1=ot2.t[:, :, 64:128], op=ALU.add),
                     reads=[ot2.r], writes=[mres[4 * gi + t_] for t_ in range(4)])

            def run_sb(ks):
                items = job_items(ks)
                n = len(items)

                def Zp(i):
                    it = items[i]; gi = it["gi"]; Kt = it["Kt"]; Qt = it["Qt"]
                    (kba, c0, msk, hf, ra), (kbb, _, _, _, rb) = it["ta"], it["tb"]
                    zt = zb2[i % 2].t; zr = zres[i % 2]
                    q0 = gi * 512 + c0
                    mm(zt[:, 0, c0:512], Kt.t[0:64, kba * 128:(kba + 1) * 128], Qt.t[0:64, q0:(gi + 1) * 512],
                       True, not msk, [Kt.r, Qt.r], [zr[0]])
                    mm(zt[:, 1, c0:512], Kt.t[64:128, kbb * 128:(kbb + 1) * 128], Qt.t[64:128, q0:(gi + 1) * 512],
                       True, not msk, [Kt.r, Qt.r], [zr[1]])
                    if msk:
                        mm(zt[:, 0, c0:c0 + 128], idb.t[:, :], mk4[:, 1, hf, ra, :], False, True, [idb.r, maskb.r], [zr[0]])
                        mm(zt[:, 1, c0:c0 + 128], idb.t[:, :], mk4[:, 1, hf, rb, :], False, True, [idb.r, maskb.r], [zr[1]])

                def Ep(i):
                    c0 = items[i]["ta"][1]
                    zt = zb2[i % 2].t; zr = zres[i % 2]; e_ = eb2[i % 4]
                    o_ap = e_.t[:, :, c0:512]; i_ap = zt[:, :, c0:512]
                    P.op("act", lambda e, o_ap=o_ap, i_ap=i_ap: e.activation(out=o_ap, in_=i_ap, func=AF.Exp),
                         reads=[zr[0], zr[1]], writes=[e_.r])

                def Lp(i):
                    c0 = items[i]["ta"][1]
                    e_ = eb2[i % 4]; l_ = lb2[i % 2]
                    o_ap = l_.t[:, :, c0:512]; i_ap = e_.t[:, :, c0:512]
                    P.op("act", lambda e, o_ap=o_ap, i_ap=i_ap: e.activation(out=o_ap, in_=i_ap, func=AF.Ln, bias=onec.t[:]),
                         reads=[e_.r, onec.r], writes=[l_.r])

                def Tri(i, j):
                    c0 = items[i]["ta"][1]
                    l_ = lb2[i % 2]
                    mm(cb.t[:, c0:512], trib.t[:, 0, :], l_.t[:, j, c0:512], False, False, [trib.r, l_.r], [cb.r], skip=True)

                def G(i, j):
                    c0 = items[i]["ta"][1]
                    g_ = gb2[i % 2]
                    o_ap = g_.t[:, j, c0:512]; i_ap = cb.t[:, c0:512]
                    P.op("act", lambda e, o_ap=o_ap, i_ap=i_ap: e.activation(out=o_ap, in_=i_ap, func=AF.Exp, scale=-1.0),
                         reads=[cb.r], writes=[g_.r])

                def OmT(i, j):
                    c0 = items[i]["ta"][1]
                    l_ = lb2[i % 2]
                    mm(cb.t[:, c0:512], trib.t[:, 1, :], l_.t[:, j, c0:512], False, False, [trib.r, l_.r], [cb.r], skip=True)

                def Amul(i):
                    c0 = items[i]["ta"][1]
                    g_ = gb2[i % 2]; a_ = ab2[i % 2]; e_ = eb2[i % 4]
                    o_ap = a_.t[:, :, c0:512]; x_ap = e_.t[:, :, c0:512]; y_ap = g_.t[:, :, c0:512]
                    P.op("dve", lambda e, o_ap=o_ap, x_ap=x_ap, y_ap=y_ap: e.tensor_tensor(out=o_ap, in0=x_ap, in1=y_ap, op=ALU.mult),
                         reads=[e_.r, g_.r], writes=[a_.r])

                def AVp(i):
                    it = items[i]; oa = it["oa"]; Vt = it["Vt"]
                    kba, c0 = it["ta"][0], it["ta"][1]
                    kbb = it["tb"][0]
                    a_ = ab2[i % 2]
                    mm(oa.t[0:64, c0:512], Vt.t[:, kba, 0:64], a_.t[:, 0, c0:512], False, False, [Vt.r, a_.r], [oa.r], skip=True)
                    o_ap = oa.t[64:128, c0:512]; l_ap = Vt.t[:, kbb, 0:64]; r_ap = a_.t[:, 1, c0:512]
                    P.op("pe", lambda e, o_ap=o_ap, l_ap=l_ap, r_ap=r_ap: e.matmul(o_ap, lhsT=l_ap, rhs=r_ap, start=False, stop=False,
                                                                                 skip_group_check=True, tile_position=(0, 64)),
                         reads=[Vt.r, a_.r], writes=[oa.r])
                    if it["jobfirst"] and it["k"] + 1 < len(jobs):
                        job_loads(it["k"] + 1)

                Zp(0)
                if n > 1:
                    Zp(1)
                Ep(0)
                if n > 2:
                    Zp(2)
                if n > 1:
                    Ep(1)
                Lp(0)
                for st_ in range(n + 1):
                    if 0 <= st_ - 1 < n:
                        OmT(st_ - 1, 1)
                    if st_ < n:
                        if items[st_]["first"]:
                            zero_bank(cb, None)
                        Tri(st_, 0)
                        G(st_, 0)
                    if st_ + 3 < n:
                        Zp(st_ + 3)
                    if st_ + 1 < n:
                        Lp(st_ + 1)
                    if 0 <= st_ - 1 < n:
                        it = items[st_ - 1]
                        if it["first"]:
                            zero_bank(it["oa"], None)
                        Amul(st_ - 1)
                        AVp(st_ - 1)
                        if it["last"]:
                            finish_sb(it["h"], it["gi"], it["oa"])
                            gctr[0] += 1
                    if st_ < n:
                        OmT(st_, 0)
                        Tri(st_, 1)
                        G(st_, 1)
                    if st_ + 2 < n:
                        Ep(st_ + 2)

            job_loads(0)
            run_mla([k for k in range(len(jobs)) if jobs[k][0] == "m"])
            run_sb([k for k in range(len(jobs)) if jobs[k][0] == "s"])
            P.end_phase()
        if stop_after == "C":
            return nc, dict(merged=merged)

        with ExitStack() as es:
            fT = sb(es, "fT", [128, DC, SO], BF16)
            load_gains(es, ("ffn", "fin"))
            g_o = sb(es, "g_o_sb", [128, DC], F32)
            load("sp", g_o.t[:], g_o_d[:, :], [g_o.r])
            sst = sb(es, "sst", [128, NBO, 2], F32)
            rst = sb(es, "rst", [128, NBO, 2], F32)
            junk = sb(es, "junkd", [128, D], F32)
            wg = [sb(es, "wg0", [128, DC, 512], BF16), None]
            wu = [sb(es, "wu0", [128, DC, 512], BF16), None]
            wd = [sb(es, "wd0", [128, 4, D], BF16), None]
            wstf = [sb(es, "wstf%d" % i, [128, 1024], F32) for i in range(2)]
            NFG = (FC + 3) // 4
            wctr = [0]

            def stage_cast(dst_ap, src_ap, n, dst_r):
                i = wctr[0]; wctr[0] += 1
                st = wstf[i % 2]
                load("sp", st.t[:, 0:n], src_ap, [st.r])
                copy_op("pool", dst_ap, st.t[:, 0:n], [st.r], [dst_r])

            def ffn_load_list(fg):
                f0 = fg * 512
                nf = min(512, DFF - f0)
                sl = fg % 2
                lst = []
                for c in range(DC):
                    lst.append((wg[sl].t[:, c, 0:nf], w_g_d[:, c, f0:f0 + nf], nf, wg[sl].r))
                    lst.append((wu[sl].t[:, c, 0:nf], w_u_d[:, c, f0:f0 + nf], nf, wu[sl].r))
                for fc in range(nf // 128):
                    lst.append((wd[sl].t[:, fc, :], w_d_d[:, fg * 4 + fc, :], D, wd[sl].r))
                return lst

            def ffn_loads(fg):
                for a_ in ffn_load_list(fg):
                    stage_cast(*a_)

            with ExitStack() as e1:
                wo = sb(e1, "wo", [128, DC, D], BF16)
                mn = [sb(e1, "mn%d" % i, [128, D], BF16) for i in range(3)]
                mT = [sb(e1, "mT%d" % i, [128, DC, 128], BF16) for i in range(3)]
                xs2 = [sb(e1, "xs2_%d" % i, [128, D], F32) for i in range(3)]
                pT = [ps(e1, "pTd%d" % i, [128, DC, 128], BF16) for i in range(3)]
                pO = [ps(e1, "pO%d" % i, [128, 1024], F32) for i in range(2)]
                for c in range(DC):
                    st = wstf[c % 2]
                    load("sp", st.t[:, 0:D], w_o_d[:, c, :], [st.r])
                    P.op("act", lambda e, st=st, c=c: e.activation(out=wo.t[:, c, :], in_=st.t[:, 0:D], func=AF.Copy, scale=g_o.t[:, c:c + 1]),
                         reads=[st.r, g_o.r], writes=[wo.r])
                for t in range(NBO):
                    for hf in range(2):
                        P.op("act", lambda e, t=t, hf=hf: e.activation(out=junk.t[:, 0:512], in_=merged.t[:, t, hf * 512:(hf + 1) * 512],
                                                                       func=AF.Square, accum_out=sst.t[:, t, hf:hf + 1]),
                             reads=[mres[t]], writes=[junk.r, sst.r])
                rstd_from_ss(sst.t[:], rst.t[:], sst.r, rst.r, 512)
                pre0 = ffn_load_list(0)

                def prep(t):
                    m_ = mn[t % 3]; mt = mT[t % 3]; pt = pT[t % 3]; xs = xs2[t % 3]
                    load("sp", xs.t[:], xo[t * 128:(t + 1) * 128, :], [xs.r])
                    for hf in range(2):
                        P.op("act", lambda e, t=t, hf=hf, m_=m_: e.activation(
                            out=m_.t[:, hf * 512:(hf + 1) * 512], in_=merged.t[:, t, hf * 512:(hf + 1) * 512],
                            func=AF.Copy, scale=rst.t[:, t, hf:hf + 1]),
                            reads=[mres[t], rst.r], writes=[m_.r])
                    for c in range(DC):
                        tr(pt.t[:, c, :], m_.t[:, c * 128:(c + 1) * 128], idb.t[:], [m_.r, idb.r], [pt.r])
                    copy_op("act", mt.t[:], pt.t[:], [pt.r], [mt.r])

                def fin(t):
                    mt = mT[t % 3]; po = pO[t % 2]; xs = xs2[t % 3]
                    for nh in range(2):
                        for c in range(DC):
                            mm(po.t[:, nh * 512:(nh + 1) * 512], mt.t[:, c, :], wo.t[:, c, nh * 512:(nh + 1) * 512],
                               c == 0, c == DC - 1, [mt.r, wo.r], [po.r])
                    P.op("dve", lambda e, t=t, po=po, xs=xs: e.tensor_tensor(out=merged.t[:, t, :], in0=po.t[:, :], in1=xs.t[:], op=ALU.add),
                         reads=[po.r, xs.r], writes=[mres[t]])

                def fin_sq(t):
                    P.op("act", lambda e, t=t: e.activation(out=junk.t[:], in_=merged.t[:, t, :], func=AF.Square, accum_out=sst.t[:, t, 0:1]),
                         reads=[mres[t]], writes=[junk.r, sst.r])

                prep(0)
                if NBO > 1:
                    prep(1)
                for t in range(NBO):
                    if t + 2 < NBO:
                        prep(t + 2)
                    for _ in range(2):
                        if pre0:
                            stage_cast(*pre0.pop(0))
                    fin(t)
                    if t >= 1:
                        fin_sq(t - 1)
                fin_sq(NBO - 1)
                while pre0:
                    stage_cast(*pre0.pop(0))
                rstd_from_ss(sst.t[:], rst.t[:], sst.r, rst.r, D)
                for t in range(NBO):
                    m_ = mn[t % 2]; pt = pT[t % 2]
                    P.op("dve", lambda e, t=t, m_=m_: e.scalar_tensor_tensor(out=m_.t[:], in0=merged.t[:, t, :], scalar=rst.t[:, t, 0:1],
                                                                       in1=gm["ffn"].t[:], op0=ALU.mult, op1=ALU.mult),
                         reads=[mres[t], rst.r, gm["ffn"].r], writes=[m_.r])
                    for c in range(DC):
                        tr(pt.t[:, c, :], m_.t[:, c * 128:(c + 1) * 128], idb.t[:], [m_.r, idb.r], [pt.r])
                    copy_op("act" if t % 2 == 0 else "dve", fT.t[:, :, t * 128:(t + 1) * 128], pt.t[:], [pt.r], [fT.r])
                P.end_phase()

            with ExitStack() as e2:
                wg[1] = sb(e2, "wg1", [128, DC, 512], BF16)
                wu[1] = sb(e2, "wu1", [128, DC, 512], BF16)
                wd[1] = sb(e2, "wd1", [128, 4, D], BF16)
                aT = [sb(e2, "aT%d" % i, [128, 4, 512], BF16) for i in range(2)]
                sg = [sb(e2, "sg%d" % i, [128, 512], F32) for i in range(2)]
                yt = [sb(e2, "yt%d" % i, [128, D], F32) for i in range(3)]
                pg = [ps(e2, "pg%d" % i, [128, 512], F32) for i in range(2)]
                pu = [ps(e2, "pu%d" % i, [128, 512], F32) for i in range(2)]
                pd = [ps(e2, "pd%d" % i, [128, 1024], F32) for i in range(2)]
                k = 0
                pend = None

                def down(k_, sl_, nfc_, tg_):
                    a_ = aT[k_ % 2]
                    for tt in range(4):
                        t = tg_ * 4 + tt
                        pd_ = pd[(k_ * 4 + tt) % 2]
                        for nh in range(2):
                            for fc in range(nfc_):
                                mm(pd_.t[:, nh * 512:(nh + 1) * 512], a_.t[:, fc, tt * 128:(tt + 1) * 128], wd[sl_].t[:, fc, nh * 512:(nh + 1) * 512],
                                   fc == 0, fc == nfc_ - 1, [a_.r, wd[sl_].r], [pd_.r])
                        P.op("dve", lambda e, t=t, pd_=pd_: e.tensor_tensor(out=merged.t[:, t, :], in0=pd_.t[:, :], in1=merged.t[:, t, :], op=ALU.add),
                             reads=[pd_.r, mres[t]], writes=[mres[t]])

                if NFG > 1:
                    ffn_loads(1)
                for fg in range(NFG):
                    f0 = fg * 512
                    nfc = min(512, DFF - f0) // 128
                    sl = fg % 2
                    for tg in range(NG):
                        a_ = aT[k % 2]
                        for fc in range(nfc):
                            pg_ = pg[(k * 4 + fc) % 2]; pu_ = pu[(k * 4 + fc) % 2]; sg_ = sg[(k * 4 + fc) % 2]
                            for c in range(DC):
                                mm(pg_.t[:, :], wg[sl].t[:, c, fc * 128:(fc + 1) * 128], fT.t[:, c, tg * 512:(tg + 1) * 512],
                                   c == 0, c == DC - 1, [wg[sl].r, fT.r], [pg_.r])
                            for c in range(DC):
                                mm(pu_.t[:, :], wu[sl].t[:, c, fc * 128:(fc + 1) * 128], fT.t[:, c, tg * 512:(tg + 1) * 512],
                                   c == 0, c == DC - 1, [wu[sl].r, fT.r], [pu_.r])
                            P.op("act", lambda e, pg_=pg_, sg_=sg_: e.activation(out=sg_.t[:], in_=pg_.t[:, :], func=AF.Silu),
                                 reads=[pg_.r], writes=[sg_.r])
                            P.op("dve", lambda e, pu_=pu_, sg_=sg_, a_=a_, fc=fc: e.tensor_tensor(out=a_.t[:, fc, :], in0=pu_.t[:, :], in1=sg_.t[:],
                                                                                             op=ALU.mult),
                                 reads=[pu_.r, sg_.r], writes=[a_.r])
                        if pend is not None:
                            down(*pend)
                        pend = (k, sl, nfc, tg)
                        k += 1
                        if tg == 0 and fg >= 1 and fg + 1 < NFG:
                            ffn_loads(fg + 1)
                down(*pend)
                for t in range(NBO):
                    P.op("act", lambda e, t=t: e.activation(out=junk.t[:], in_=merged.t[:, t, :], func=AF.Square, accum_out=sst.t[:, t, 0:1]),
                         reads=[mres[t]], writes=[junk.r, sst.r])
                rstd_from_ss(sst.t[:], rst.t[:], sst.r, rst.r, D)
                for t in range(NBO):
                    y_ = yt[t % 3]
                    P.op("dve", lambda e, t=t, y_=y_: e.scalar_tensor_tensor(out=y_.t[:], in0=merged.t[:, t, :], scalar=rst.t[:, t, 0:1],
                                                                       in1=gm["fin"].t[:], op0=ALU.mult, op1=ALU.mult),
                         reads=[mres[t], rst.r, gm["fin"].r], writes=[y_.r])
                    load("sp", y[t * 128:(t + 1) * 128, :], y_.t[:], [], reads=[y_.r])
                P.end_phase()
    return nc, {}


def host_inputs(inputs, S):
    NB = S // 128
    f = np.float32
    x = np.asarray(inputs["x"], f)
    pos = np.asarray(inputs["positions"], np.int32)

    def kchunk(w):
        K, E = w.shape
        return np.ascontiguousarray(w.reshape(K // 128, 128, E).transpose(1, 0, 2))

    def rep(v):
        return np.ascontiguousarray(np.broadcast_to(np.asarray(v, f).reshape(1, -1), (128, v.size)))

    w_in = kchunk(np.asarray(inputs["w_in"], f)[0])
    w_uq = kchunk(np.asarray(inputs["w_uq"], f)[0]).reshape(128, 2 * 768)
    wukv = np.asarray(inputs["w_ukv"], f)[0].reshape(128, H, 128)
    w_uk = np.ascontiguousarray(wukv[:, :, 0:64].reshape(128, 512))
    w_uv = np.ascontiguousarray(wukv[:, :, 64:128].reshape(128, 512))
    w_o = kchunk(np.asarray(inputs["w_o"], f)[0])
    w_g = kchunk(np.asarray(inputs["w_gate"], f)[0])
    w_u = kchunk(np.asarray(inputs["w_up"], f)[0])
    w_d = kchunk(np.asarray(inputs["w_down"], f)[0])
    ident = np.eye(128, dtype=f)
    jj = np.arange(128)[:, None]
    ss_ = np.arange(128)[None, :]
    tri = np.stack([(jj >= ss_).astype(f), (jj < ss_).astype(f)], axis=1)
    causal = (jj <= ss_).astype(f)
    strict = (jj < ss_).astype(f)
    common = dict(ident=ident, tri=np.ascontiguousarray(tri), w_in=w_in, w_uq=w_uq, w_uk=w_uk, w_uv=w_uv, w_o=w_o,
                  w_gate=w_g, w_up=w_u, w_down=w_d,
                  g_mix=rep(inputs["norm_mix"][0]), g_q=rep(inputs["q_latent_norm"][0]), g_kv=rep(inputs["kv_latent_norm"][0]),
                  g_mla=rep(inputs["out_norm_mla"][0]), g_sb=rep(inputs["out_norm_sb"][0]), g_ffn=rep(inputs["norm_ffn"][0]),
                  g_fin=rep(inputs["norm_final"]),
                  g_o=np.ascontiguousarray(np.concatenate([np.asarray(inputs["out_norm_mla"], f)[0],
                                                           np.asarray(inputs["out_norm_sb"], f)[0]]).reshape(DC, 128).T))
    maps = []
    for c in range(8):
        b, j = c // 4, c % 4
        ob = own_blocks(j, NB)
        rows = np.concatenate([np.arange(k * 128, (k + 1) * 128) for k in ob])
        masks = np.zeros((128, 2, 2, 4, 128), f)
        for ti, tm in enumerate((causal, strict)):
            for hf, off in enumerate((j, 3 - j)):
                for r in range(4):
                    if r < off:
                        masks[:, ti, hf, r, :] = 0.0
                    elif r == off:
                        masks[:, ti, hf, r, :] = NEG * (1.0 - tm)
                    else:
                        masks[:, ti, hf, r, :] = NEG
        m = dict(common)
        m["xb"] = np.ascontiguousarray(x[b])
        m["xo"] = np.ascontiguousarray(x[b][rows])
        m["posb"] = np.ascontiguousarray(pos[b].reshape(NB, 128).T)
        m["poso"] = np.ascontiguousarray(pos[b][rows].reshape(len(ob), 128).T)
        m["masks"] = masks.reshape(128, 16 * 128)
        maps.append(m)
    return maps


def assemble(results, S, B=2):
    NB = S // 128
    out = np.zeros((B, S, D), np.float32)
    for c in range(8):
        b, j = c // 4, c % 4
        ob = own_blocks(j, NB)
        yy = np.asarray(results[c]["y"], np.float32)
        for i, k in enumerate(ob):
            out[b, k * 128:(k + 1) * 128, :] = yy[i * 128:(i + 1) * 128, :]
    return out


_NC_CACHE = {}


def kernel(**inputs):
    S = int(np.asarray(inputs["x"]).shape[1])
    if S not in _NC_CACHE:
        _NC_CACHE[S] = build(S, Prog, Res)[0]
    nc = _NC_CACHE[S]
    maps = host_inputs(inputs, S)
    res = run_bass_kernel_spmd(nc, maps, core_ids=list(range(8)))
    return assemble(res.results, S, B=int(np.asarray(inputs["x"]).shape[0]))
```

```python
import math
from contextlib import ExitStack
import numpy as np
import concourse.bass as bass
import concourse.mybir as mybir
from concourse.bass_utils import run_bass_kernel_spmd

F32 = mybir.dt.float32
BF16 = mybir.dt.bfloat16
I32 = mybir.dt.int32
AF = mybir.ActivationFunctionType
ALU = mybir.AluOpType


ENGS = ("pe", "act", "dve", "pool", "sp")


class Res:
    __slots__ = ("name", "w", "r", "excl")

    def __init__(self, name, excl=False):
        self.name = name
        self.w = None
        self.r = []
        self.excl = excl


class Op:
    __slots__ = ("fn", "waits", "sig", "dma", "idx")

    def __init__(self, fn, waits, dma, idx):
        self.fn = fn
        self.waits = waits
        self.sig = False
        self.dma = dma
        self.idx = idx


class Prog:
    NDMA = 40

    def __init__(self, nc):
        self.nc = nc
        self.sem = {e: nc.alloc_semaphore("s_" + e) for e in ENGS}
        self.dsem = [nc.alloc_semaphore("d%d" % i) for i in range(self.NDMA)]
        self.dcount = [0] * self.NDMA
        self.dnext = 0
        self.ops = {e: [] for e in ENGS}
        self.start = {e: 0 for e in ENGS}
        self.base = {e: 0 for e in ENGS}
        self.seen = {e: {x: 0 for x in ENGS} for e in ENGS}
        self.seen_d = {e: [0] * self.NDMA for e in ENGS}

    def _deps(self, reads, writes):
        ev = []
        for r in reads:
            if r.excl:
                writes = list(writes) + [r]
                continue
            if r.w is not None:
                ev.append(r.w)
        for w in writes:
            if w.w is not None:
                ev.append(w.w)
            ev.extend(w.r)
        return ev

    def _filter(self, eng, evs):
        out = []
        best = {}
        for e in evs:
            if e[0] == "c":
                _, x, i = e
                if x == "pe" and eng == "pe":
                    continue
                if i <= self.seen[eng][x]:
                    continue
                if i > best.get(("c", x), 0):
                    best[("c", x)] = i
            else:
                _, s, c = e
                if c <= self.seen_d[eng][s]:
                    continue
                if c > best.get(("d", s), 0):
                    best[("d", s)] = c
        for k, v in best.items():
            if k[0] == "c":
                self.seen[eng][k[1]] = v
                self.ops[k[1]][v - 1].sig = True
                out.append(("c", k[1], v))
            else:
                self.seen_d[eng][k[1]] = v
                out.append(("d", k[1], v))
        return out

    def _commit(self, ev, reads, writes):
        for r in reads:
            if r.excl:
                r.w = ev
                r.r = []
            else:
                r.r.append(ev)
        for w in writes:
            w.w = ev
            w.r = []

    def op(self, eng, fn, reads=(), writes=()):
        waits = self._filter(eng, self._deps(reads, writes))
        lst = self.ops[eng]
        o = Op(fn, waits, None, len(lst) + 1)
        lst.append(o)
        self._commit(("c", eng, o.idx), reads, writes)
        return o

    def dma(self, q, fn, reads=(), writes=()):
        s = self.dnext
        self.dnext = (self.dnext + 1) % self.NDMA
        evs = self._deps(reads, writes)
        if self.dcount[s] > 0:
            evs.append(("d", s, self.dcount[s]))
        waits = self._filter(q, evs)
        self.dcount[s] += 16
        lst = self.ops[q]
        o = Op(fn, waits, (s, self.dcount[s]), len(lst) + 1)
        lst.append(o)
        self._commit(("d", s, self.dcount[s]), reads, writes)
        return o

    def barrier(self):
        evs = [("d", s, self.dcount[s]) for s in range(self.NDMA) if self.dcount[s] > 0]
        waits = self._filter("sp", evs)
        lst = self.ops["sp"]
        o = Op(lambda e: e.nop(), waits, None, len(lst) + 1)
        lst.append(o)
        for x in ENGS:
            for s in range(self.NDMA):
                self.seen_d[x][s] = self.dcount[s]
            for y in ENGS:
                self.seen[x][y] = len(self.ops[y])

    def end_phase(self):
        self.barrier()
        self.replay()

    def replay(self):
        nc = self.nc
        sigcnt = {}
        for e in ENGS:
            c = self.base[e]
            arr = []
            for o in self.ops[e][self.start[e]:]:
                if o.sig:
                    c += 1
                arr.append(c)
            sigcnt[e] = arr

        def val(x, idx1):
            st = self.start[x]
            if idx1 - 1 < st:
                return self._hist[x][idx1 - 1]
            return sigcnt[x][idx1 - 1 - st]

        if not hasattr(self, "_hist"):
            self._hist = {e: [] for e in ENGS}

        def emit(ename, eng):
            for o in self.ops[ename][self.start[ename]:]:
                for w in o.waits:
                    if w[0] == "c":
                        eng.wait_ge(self.sem[w[1]], val(w[1], w[2]))
                    else:
                        eng.wait_ge(self.dsem[w[1]], w[2])
                ins = o.fn(eng)
                if o.dma is not None:
                    ins.then_inc(self.dsem[o.dma[0]], 16)
                elif o.sig:
                    ins.then_inc(self.sem[ename], 1)

        with nc.Block() as block:
            @block.tensor
            def _(t):
                emit("pe", t)

            @block.scalar
            def _(t):
                emit("act", t)

            @block.vector
            def _(t):
                emit("dve", t)

            @block.gpsimd
            def _(t):
                emit("pool", t)

            @block.sync
            def _(t):
                emit("sp", t)

        for e in ENGS:
            self._hist[e].extend(sigcnt[e])
            self.base[e] = sigcnt[e][-1] if sigcnt[e] else self.base[e]
            self.start[e] = len(self.ops[e])


D = 1024
DC = 8
H = 8
DFF = 2816
FC = 22
EPS = 1e-6
INW = 1952
C_CQ, C_CKV, C_KR, C_QSB, C_KSB, C_VSB = 0, 256, 384, 416, 928, 1440
MLA_SCALE = 1.0 / math.sqrt(96.0)
NEG = -30000.0


def own_blocks(j, NB):
    half = NB // 8
    return [4 * m + j for m in range(half)] + [4 * m + 3 - j for m in range(half, 2 * half)]


def build(S, Prog, Res, stop_after=None):
    NB = S // 128
    NBO = NB // 4
    NG = NBO // 4
    NCH = NB // 4
    SO = NBO * 128
    nc = bass.Bass("TRN2", target_bir_lowering=False)

    def din(name, shape, dt=F32):
        return nc.dram_tensor(name, list(shape), dt, kind="ExternalInput").ap()

    xb = din("xb", [S, D])
    xo = din("xo", [SO, D])
    posb = din("posb", [128, NB], I32)
    poso = din("poso", [128, NBO], I32)
    ident_d = din("ident", [128, 128])
    tri_d = din("tri", [128, 2, 128])
    w_in_d = din("w_in", [128, DC, INW])
    w_uq_d = din("w_uq", [128, 2 * 768])
    w_uk_d = din("w_uk", [128, 512])
    w_uv_d = din("w_uv", [128, 512])
    w_o_d = din("w_o", [128, DC, D])
    w_g_d = din("w_gate", [128, DC, DFF])
    w_u_d = din("w_up", [128, DC, DFF])
    w_d_d = din("w_down", [128, FC, D])
    g_d = {"mix": din("g_mix", [128, D]), "q": din("g_q", [128, 256]), "kv": din("g_kv", [128, 128]),
           "mla": din("g_mla", [128, 512]), "sb": din("g_sb", [128, 512]), "ffn": din("g_ffn", [128, D]),
           "fin": din("g_fin", [128, D])}
    g_o_d = din("g_o", [128, DC])
    mask_d = din("masks", [128, 16 * 128])
    y = nc.dram_tensor("y", [SO, D], F32, kind="ExternalOutput").ap()

    KTm = nc.dram_tensor("KTm", [4, 128, S], BF16).ap()
    KR = nc.dram_tensor("KR", [32, S], BF16).ap()
    Vm = nc.dram_tensor("Vm", [H, 128, NB, 65], BF16).ap()
    KTs = nc.dram_tensor("KTs", [4, 128, S], BF16).ap()
    Vs = nc.dram_tensor("Vs", [H, 128, NB, 65], BF16).ap()
    QTm = nc.dram_tensor("QTm", [H, 96, SO], BF16).ap()
    QTs = nc.dram_tensor("QTs", [4, 128, SO], BF16).ap()

    P = Prog(nc)
    inv_freq = (10000.0 ** (-np.arange(0, 32, 2, dtype=np.float32) / np.float32(32))).astype(np.float32)

    class T:
        def __init__(self, t, excl=False):
            self.t = t
            self.r = Res("r", excl)

    def sb(es, name, shape, dt):
        return T(es.enter_context(nc.sbuf_tensor(name, list(shape), dt)))

    def ps(es, name, shape, dt=F32):
        return T(es.enter_context(nc.psum_tensor(name, list(shape), dt)), excl=True)

    rrs = {"evac": 0}

    def evac_eng():
        rrs["evac"] ^= 1
        return "act" if rrs["evac"] else "dve"

    def copy_op(eng, out, in_, reads, writes, scale=None):
        if eng == "act":
            if scale is None:
                P.op("act", lambda e: e.activation(out=out, in_=in_, func=AF.Copy), reads=reads, writes=writes)
            else:
                P.op("act", lambda e: e.activation(out=out, in_=in_, func=AF.Copy, scale=scale), reads=reads, writes=writes)
        elif scale is None:
            P.op(eng, lambda e: e.tensor_copy(out=out, in_=in_), reads=reads, writes=writes)
        else:
            P.op(eng, lambda e: e.tensor_scalar(out=out, in0=in_, scalar1=scale, scalar2=None, op0=ALU.mult), reads=reads, writes=writes)

    def mm(out, lhsT, rhs, start, stop, reads, writes, skip=False):
        if skip:
            P.op("pe", lambda e: e.matmul(out, lhsT=lhsT, rhs=rhs, start=start, stop=stop, skip_group_check=True), reads=reads, writes=writes)
        else:
            P.op("pe", lambda e: e.matmul(out, lhsT=lhsT, rhs=rhs, start=start, stop=stop), reads=reads, writes=writes)

    def tr(out, in_, ident, reads, writes):
        P.op("pe", lambda e: e.transpose(out=out, in_=in_, identity=ident), reads=reads, writes=writes)

    def load(q, out, in_, writes, reads=()):
        P.dma(q, lambda e: e.dma_start(out=out, in_=in_), reads=reads, writes=writes)

    with ExitStack() as g:
        idf = sb(g, "idf", [128, 128], F32)
        idb = sb(g, "idb", [128, 128], BF16)
        zerob = sb(g, "zerob", [128, 128], BF16)
        epsc = sb(g, "epsc", [128, 1], F32)
        onec = sb(g, "onec", [128, 1], F32)
        pic = sb(g, "pic", [128, 1], F32)
        gm = {}

        def load_gains(es_, names):
            for nm in names:
                gm[nm] = sb(es_, "gs_" + nm, [128, g_d[nm].shape[1]], F32)
                load("sp", gm[nm].t[:], g_d[nm][:, :], [gm[nm].r])
        load("sp", idf.t[:], ident_d[:, :], [idf.r])
        P.op("pool", lambda e: e.tensor_copy(out=idb.t[:], in_=idf.t[:]), reads=[idf.r], writes=[idb.r])
        P.op("pool", lambda e: e.memset(zerob.t[:], 0.0), writes=[zerob.r])
        P.op("pool", lambda e: e.memset(epsc.t[:], EPS), writes=[epsc.r])
        P.op("pool", lambda e: e.memset(onec.t[:], 1.0), writes=[onec.r])
        P.op("pool", lambda e: e.memset(pic.t[:], math.pi), writes=[pic.r])

        def rstd_from_ss(ssap, rsap, rd, wr, n):
            P.op("act", lambda e: e.activation(out=rsap, in_=ssap, func=AF.Ln, scale=1.0 / n, bias=epsc.t[:]),
                 reads=[rd, epsc.r], writes=[wr])
            P.op("act", lambda e: e.activation(out=rsap, in_=rsap, func=AF.Exp, scale=-0.5), reads=[wr], writes=[wr])

        with ExitStack() as es:
            win = sb(es, "win", [128, DC, INW], BF16)
            wuq = sb(es, "wuq", [128, 2 * 768], BF16)
            wuk = sb(es, "wuk", [128, 512], BF16)
            wuv = sb(es, "wuv", [128, 512], BF16)
            cosb = sb(es, "cosb", [128, NB, 16], F32)
            sinb = sb(es, "sinb", [128, NB, 16], F32)
            coso = sb(es, "coso", [128, NBO, 16], F32)
            sino = sb(es, "sino", [128, NBO, 16], F32)
            load_gains(es, ("mix", "q", "kv"))
            es0 = es
            es = ExitStack()
            es.__enter__()
            wst = [sb(es, "wst%d" % i, [128, 2048], F32) for i in range(4)]
            ropi = sb(es, "ropi", [128, NB], I32)
            ropf = sb(es, "ropf", [128, NB], F32)
            ropt = sb(es, "ropt", [128, NB, 16], F32)
            ropk = sb(es, "ropk", [128, NB, 16], I32)
            ropg = sb(es, "ropg", [128, NB, 16], F32)
            roph = sb(es, "roph", [128, NB, 16], F32)

            def rope_table(pos_d, n, sink_sin, sink_cos):
                load("sp", ropi.t[:, 0:n], pos_d[:, :], [ropi.r])
                P.op("dve", lambda e: e.tensor_copy(out=ropf.t[:, 0:n], in_=ropi.t[:, 0:n]), reads=[ropi.r], writes=[ropf.r])
                for i in range(16):
                    P.op("dve", lambda e, i=i: e.tensor_scalar(out=ropt.t[:, 0:n, i], in0=ropf.t[:, 0:n], scalar1=float(inv_freq[i]),
                                                                scalar2=1.0 / (2 * math.pi), op0=ALU.mult, op1=ALU.mult),
                         reads=[ropf.r], writes=[ropt.r])
                for ph, sink in ((0.0, sink_sin), (0.25, sink_cos)):
                    P.op("dve", lambda e, ph=ph: e.tensor_scalar(out=ropg.t[:, 0:n, :], in0=ropt.t[:, 0:n, :], scalar1=ph, scalar2=None, op0=ALU.add),
                         reads=[ropt.r], writes=[ropg.r])
                    P.op("dve", lambda e: e.tensor_copy(out=ropk.t[:, 0:n, :], in_=ropg.t[:, 0:n, :]), reads=[ropg.r], writes=[ropk.r])
                    P.op("dve", lambda e: e.tensor_copy(out=roph.t[:, 0:n, :], in_=ropk.t[:, 0:n, :]), reads=[ropk.r], writes=[roph.r])
                    P.op("dve", lambda e: e.tensor_tensor(out=ropg.t[:, 0:n, :], in0=ropg.t[:, 0:n, :], in1=roph.t[:, 0:n, :], op=ALU.subtract),
                         reads=[ropg.r, roph.r], writes=[ropg.r])
                    P.op("dve", lambda e: e.scalar_tensor_tensor(out=roph.t[:, 0:n, :], in0=ropg.t[:, 0:n, :], scalar=0.0, in1=ropg.t[:, 0:n, :],
                                                                 op0=ALU.is_lt, op1=ALU.add), reads=[ropg.r], writes=[roph.r])
                    sink(roph, n)

            def sink_b(tile):
                def f(src, n):
                    P.op("act", lambda e: e.activation(out=tile.t[:], in_=src.t[:, 0:n, :], func=AF.Sin, scale=-2.0 * math.pi, bias=pic.t[:]),
                         reads=[src.r, pic.r], writes=[tile.r])
                return f

            rope_table(posb, NB, sink_b(sinb), sink_b(cosb))
            rope_table(poso, NBO, sink_b(sino), sink_b(coso))

            k = 0
            for c in range(DC):
                st = wst[k % 4]; k += 1
                load("sp", st.t[:, 0:INW], w_in_d[:, c, :], [st.r])
                copy_op(("act", "dve", "act", "pool")[c % 4], win.t[:, c, :], st.t[:, 0:INW], [st.r], [win.r])
            st = wst[k % 4]; k += 1
            load("sp", st.t[:, 0:1536], w_uq_d[:, :], [st.r])
            copy_op("act", wuq.t[:], st.t[:, 0:1536], [st.r], [wuq.r])
            st = wst[k % 4]; k += 1
            load("sp", st.t[:, 0:512], w_uk_d[:, :], [st.r])
            load("sp", st.t[:, 512:1024], w_uv_d[:, :], [st.r])
            copy_op("dve", wuk.t[:], st.t[:, 0:512], [st.r], [wuk.r])
            copy_op("dve", wuv.t[:], st.t[:, 512:1024], [st.r], [wuv.r])

            P.end_phase()
            es.__exit__(None, None, None)
            es = ExitStack()
            es.__enter__()
            xbuf = [sb(es, "xbuf%d" % i, [128, D], F32) for i in range(8)]
            junk = sb(es, "junk", [128, D], F32)
            ubuf = [sb(es, "ubuf%d" % i, [128, D], BF16) for i in range(3)]
            uT = [sb(es, "uT%d" % i, [128, DC, 512], BF16) for i in range(2)]
            ss4 = [sb(es, "ss4_%d" % i, [128, 4], F32) for i in range(2)]
            rs4 = [sb(es, "rs4_%d" % i, [128, 4], F32) for i in range(2)]
            ssc = [sb(es, "ssc_%d" % i, [128, 4], F32) for i in range(2)]
            rsc = [sb(es, "rsc_%d" % i, [128, 4], F32) for i in range(2)]
            st4 = [sb(es, "st4_%d" % i, [128, 4, 512], BF16) for i in range(3)]
            stv = [sb(es, "stv%d" % i, [128, H, 4, 65], BF16) for i in range(3)]
            ckr = [sb(es, "ckr%d" % i, [128, 160], BF16) for i in range(4)]
            rtmp = [sb(es, "rtmp%d" % i, [128, 2, 16], F32) for i in range(2)]
            ckvT = [sb(es, "ckvT%d" % i, [128, 512], BF16) for i in range(2)]
            krT = [sb(es, "krT%d" % i, [32, 512], BF16) for i in range(2)]
            cqn = [sb(es, "cqn%d" % i, [128, 256], BF16) for i in range(4)]
            cqT = [sb(es, "cqT%d" % i, [128, 2, 512], BF16) for i in range(2)]
            qtok = [sb(es, "qtok%d" % i, [128, H, 96], BF16) for i in range(2)]
            qrt = [sb(es, "qrt%d" % i, [128, 2, H, 16], F32) for i in range(2)]
            qmst = [sb(es, "qmst%d" % i, [128, H, 512], BF16) for i in range(2)]
            pT = [ps(es, "pT%d" % i, [128, DC, 128], BF16) for i in range(2)]
            pP = [ps(es, "pP%d" % i, [128, 512], F32) for i in range(3)]
            pC = ps(es, "pC", [128, 4, 256], F32)
            pQ = ps(es, "pQ", [128, 512], F32)
            for v in stv:
                P.op("pool", lambda e, v=v: e.memset(v.t[:], 1.0), writes=[v.r])
            ctr = {"x": 0, "u": 0, "pT": 0, "pP": 0, "st4": 0, "stv": 0}

            def nxt(lst, key):
                v = lst[ctr[key] % len(lst)]
                ctr[key] += 1
                return v

            def rope_tok(x1, x2, cs, sn, o1, o2, tA, tB, rd, tmp_r, out_r):
                P.op("dve", lambda e: e.tensor_tensor(out=tA, in0=x1, in1=cs, op=ALU.mult), reads=rd, writes=[tmp_r])
                P.op("dve", lambda e: e.tensor_tensor(out=tB, in0=x2, in1=sn, op=ALU.mult), reads=rd, writes=[tmp_r])
                P.op("dve", lambda e: e.tensor_tensor(out=o1, in0=tA, in1=tB, op=ALU.subtract), reads=[tmp_r], writes=[out_r])
                P.op("dve", lambda e: e.tensor_tensor(out=tA, in0=x2, in1=cs, op=ALU.mult), reads=rd, writes=[tmp_r])
                P.op("dve", lambda e: e.tensor_tensor(out=tB, in0=x1, in1=sn, op=ALU.mult), reads=rd, writes=[tmp_r])
                P.op("dve", lambda e: e.tensor_tensor(out=o2, in0=tA, in1=tB, op=ALU.add), reads=[tmp_r], writes=[out_r])

            chunks = [("A", ci) for ci in range(NCH)] + [("B", gi) for gi in range(NG)]
            xtiles = {}

            def chunk_src(k):
                typ, i = chunks[k]
                return (xb if typ == "A" else xo), i

            def front_loads(k):
                src, i = chunk_src(k)
                xs = [nxt(xbuf, "x") for _ in range(4)]
                xtiles[k] = xs
                for t in range(4):
                    load("sp", xs[t].t[:], src[(4 * i + t) * 128:(4 * i + t + 1) * 128, :], [xs[t].r])

            def front_stats(k):
                xs = xtiles[k]
                s4 = ss4[k % 2]; r4 = rs4[k % 2]
                for t in range(4):
                    P.op("act", lambda e, t=t: e.activation(out=junk.t[:], in_=xs[t].t[:], func=AF.Square, accum_out=s4.t[:, t:t + 1]),
                         reads=[xs[t].r], writes=[junk.r, s4.r])
                rstd_from_ss(s4.t[:], r4.t[:], s4.r, r4.r, D)

            def front_norm_T(k):
                xs = xtiles[k]
                r4 = rs4[k % 2]
                uTt = uT[k % 2]
                for t in range(4):
                    ub = nxt(ubuf, "u"); pt = nxt(pT, "pT")
                    P.op("dve", lambda e, t=t, ub=ub: e.scalar_tensor_tensor(out=ub.t[:], in0=xs[t].t[:], scalar=r4.t[:, t:t + 1], in1=gm["mix"].t[:],
                                                                       op0=ALU.mult, op1=ALU.mult),
                         reads=[xs[t].r, r4.r, gm["mix"].r], writes=[ub.r])
                    for c in range(DC):
                        tr(pt.t[:, c, :], ub.t[:, c * 128:(c + 1) * 128], idb.t[:], [ub.r, idb.r], [pt.r])
                    copy_op("act", uTt.t[:, :, t * 128:(t + 1) * 128], pt.t[:], [pt.r], [uTt.r])

            def fm_proj(uTt, col0, dst4, scale=None):
                for pr in range(4):
                    p_ = nxt(pP, "pP")
                    for c in range(DC):
                        mm(p_.t[:, :], win.t[:, c, col0 + pr * 128:col0 + (pr + 1) * 128], uTt.t[:, c, :],
                           c == 0, c == DC - 1, [win.r, uTt.r], [p_.r])
                    copy_op(evac_eng(), dst4.t[:, pr, :], p_.t[:, :], [p_.r], [dst4.r], scale=scale)

            def proj_A1(k):
                ci = chunks[k][1]
                uTt = uT[k % 2]
                ks = nxt(st4, "st4")
                fm_proj(uTt, C_KSB, ks)
                load("sp", KTs[:, :, ci * 512:(ci + 1) * 512].rearrange("a p n -> p a n"), ks.t[:], [], reads=[ks.r])
                for t in range(4):
                    for c in range(DC):
                        mm(pC.t[:, t, 0:160], uTt.t[:, c, t * 128:(t + 1) * 128], win.t[:, c, C_CKV:C_CKV + 160],
                           c == 0, c == DC - 1, [win.r, uTt.r], [pC.r])
                s4 = ssc[k % 2]; r4 = rsc[k % 2]
                for t in range(4):
                    P.op("act", lambda e, t=t: e.activation(out=junk.t[:, 0:128], in_=pC.t[:, t, 0:128], func=AF.Square, accum_out=s4.t[:, t:t + 1]),
                         reads=[pC.r], writes=[junk.r, s4.r])
                rstd_from_ss(s4.t[:], r4.t[:], s4.r, r4.r, 128)

            def proj_A2(k):
                ci = chunks[k][1]
                uTt = uT[k % 2]
                r4 = rsc[k % 2]
                ckT = ckvT[k % 2]
                krt = krT[k % 2]
                cks = []
                for t in range(4):
                    kb = 4 * ci + t
                    ck = ckr[t]
                    tm = rtmp[t % 2]
                    cks.append(ck)
                    P.op("dve", lambda e, ck=ck, t=t: e.scalar_tensor_tensor(out=ck.t[:, 0:128], in0=pC.t[:, t, 0:128], scalar=r4.t[:, t:t + 1],
                                                                       in1=gm["kv"].t[:], op0=ALU.mult, op1=ALU.mult),
                         reads=[pC.r, r4.r, gm["kv"].r], writes=[ck.r])
                    rope_tok(pC.t[:, t, 128:144], pC.t[:, t, 144:160], cosb.t[:, kb, :], sinb.t[:, kb, :],
                             ck.t[:, 128:144], ck.t[:, 144:160], tm.t[:, 0, :], tm.t[:, 1, :],
                             [pC.r, cosb.r, sinb.r], tm.r, ck.r)
                vs_ = nxt(stv, "stv")
                for t in range(4):
                    p_ = nxt(pP, "pP")
                    for c in range(DC):
                        mm(p_.t[:, :], uTt.t[:, c, t * 128:(t + 1) * 128], win.t[:, c, C_VSB:C_VSB + 512],
                           c == 0, c == DC - 1, [win.r, uTt.r], [p_.r])
                    copy_op("act", vs_.t[:, :, t, 0:64], p_.t[:, :].rearrange("p (h d) -> p h d", d=64), [p_.r], [vs_.r])
                load("sp", Vs[:, :, 4 * ci:4 * ci + 4, :].rearrange("h p t e -> p h t e"), vs_.t[:], [], reads=[vs_.r])
                pt = nxt(pT, "pT")
                for t in range(4):
                    tr(pt.t[:, t, :], cks[t].t[:, 0:128], idb.t[:], [cks[t].r, idb.r], [pt.r])
                    tr(pt.t[0:32, 4 + t, :], cks[t].t[:, 128:160], idb.t[:], [cks[t].r, idb.r], [pt.r])
                copy_op("dve", ckT.t[:, :].rearrange("p (t n) -> p t n", n=128), pt.t[:, 0:4, :], [pt.r], [ckT.r])
                copy_op("dve", krt.t[:, :].rearrange("p (t n) -> p t n", n=128), pt.t[0:32, 4:8, :], [pt.r], [krt.r])
                load("sp", KR[:, ci * 512:(ci + 1) * 512], krt.t[:], [], reads=[krt.r])
                kn = nxt(st4, "st4")
                for pr in range(4):
                    p_ = nxt(pP, "pP")
                    mm(p_.t[:, :], wuk.t[:, pr * 128:(pr + 1) * 128], ckT.t[:, :], True, True, [wuk.r, ckT.r], [p_.r])
                    copy_op(evac_eng(), kn.t[:, pr, :], p_.t[:, :], [p_.r], [kn.r])
                load("sp", KTm[:, :, ci * 512:(ci + 1) * 512].rearrange("a p n -> p a n"), kn.t[:], [], reads=[kn.r])
                vm_ = nxt(stv, "stv")
                for t in range(4):
                    p_ = nxt(pP, "pP")
                    mm(p_.t[:, :], ckT.t[:, t * 128:(t + 1) * 128], wuv.t[:, :], True, True, [wuv.r, ckT.r], [p_.r])
                    copy_op(evac_eng(), vm_.t[:, :, t, 0:64], p_.t[:, :].rearrange("p (h d) -> p h d", d=64), [p_.r], [vm_.r])
                load("sp", Vm[:, :, 4 * ci:4 * ci + 4, :].rearrange("h p t e -> p h t e"), vm_.t[:], [], reads=[vm_.r])

            def proj_B1(k):
                gi = chunks[k][1]
                uTt = uT[k % 2]
                qs = nxt(st4, "st4")
                fm_proj(uTt, C_QSB, qs, scale=0.125)
                load("sp", QTs[:, :, gi * 512:(gi + 1) * 512].rearrange("a p n -> p a n"), qs.t[:], [], reads=[qs.r])
                for t in range(4):
                    for c in range(DC):
                        mm(pC.t[:, t, 0:256], uTt.t[:, c, t * 128:(t + 1) * 128], win.t[:, c, C_CQ:C_CQ + 256],
                           c == 0, c == DC - 1, [win.r, uTt.r], [pC.r])
                s4 = ssc[k % 2]; r4 = rsc[k % 2]
                for t in range(4):
                    P.op("act", lambda e, t=t: e.activation(out=junk.t[:, 0:256], in_=pC.t[:, t, 0:256], func=AF.Square, accum_out=s4.t[:, t:t + 1]),
                         reads=[pC.r], writes=[junk.r, s4.r])
                rstd_from_ss(s4.t[:], r4.t[:], s4.r, r4.r, 256)

            def proj_B2(k):
                gi = chunks[k][1]
                r4 = rsc[k % 2]
                cqt = cqT[k % 2]
                for t in range(4):
                    cq = cqn[t]
                    P.op("dve", lambda e, cq=cq, t=t: e.scalar_tensor_tensor(out=cq.t[:], in0=pC.t[:, t, 0:256], scalar=r4.t[:, t:t + 1],
                                                                       in1=gm["q"].t[:], op0=ALU.mult, op1=ALU.mult),
                         reads=[pC.r, r4.r, gm["q"].r], writes=[cq.r])
                pt = nxt(pT, "pT")
                for t in range(4):
                    for k2 in range(2):
                        tr(pt.t[:, 2 * t + k2, :], cqn[t].t[:, k2 * 128:(k2 + 1) * 128], idb.t[:], [cqn[t].r, idb.r], [pt.r])
                for k2 in range(2):
                    copy_op(evac_eng(), cqt.t[:, k2, :].rearrange("p (t n) -> p t n", n=128),
                            pt.t[:, :, :].rearrange("p (t k) n -> p t k n", k=2)[:, :, k2, :], [pt.r], [cqt.r])
                qm = qmst[k % 2]
                for t in range(4):
                    pos = 4 * gi + t
                    qt = qtok[t % 2]
                    qr = qrt[t % 2]
                    pa = nxt(pP, "pP")
                    for k2 in range(2):
                        mm(pa.t[:, 0:512], cqt.t[:, k2, t * 128:(t + 1) * 128], wuq.t[:, k2 * 768:k2 * 768 + 512],
                           k2 == 0, k2 == 1, [wuq.r, cqt.r], [pa.r])
                    for k2 in range(2):
                        mm(pQ.t[:, 0:256], cqt.t[:, k2, t * 128:(t + 1) * 128], wuq.t[:, k2 * 768 + 512:k2 * 768 + 768],
                           k2 == 0, k2 == 1, [wuq.r, cqt.r], [pQ.r])
                    qf = junk
                    copy_op("act", qf.t[:, 0:512], pa.t[:, 0:512], [pa.r], [qf.r])
                    copy_op("dve", qf.t[:, 512:768], pQ.t[:, 0:256], [pQ.r], [qf.r])
                    q3 = qf.t[:, 0:768].rearrange("p (h d) -> p h d", d=96)
                    copy_op("act", qt.t[:, :, 0:64], q3[:, :, 0:64], [qf.r], [qt.r])
                    rope_tok(q3[:, :, 64:80], q3[:, :, 80:96], coso.t[:, pos, :].unsqueeze(1).broadcast_to([128, H, 16]),
                             sino.t[:, pos, :].unsqueeze(1).broadcast_to([128, H, 16]),
                             qt.t[:, :, 64:80], qt.t[:, :, 80:96], qr.t[:, 0, :, :], qr.t[:, 1, :, :],
                             [qf.r, coso.r, sino.r], qr.r, qt.r)
                    pt = nxt(pT, "pT")
                    for h in range(H):
                        tr(pt.t[0:96, h, :], qt.t[:, h, :], idb.t[:], [qt.r, idb.r], [pt.r])
                    copy_op(evac_eng(), qm.t[0:96, :, t * 128:(t + 1) * 128], pt.t[0:96, :, :], [pt.r], [qm.r])
                load("sp", QTm[:, :, gi * 512:(gi + 1) * 512].rearrange("h r n -> r h n"), qm.t[0:96, :, :], [], reads=[qm.r])

            NK = len(chunks)
            front_loads(0)
            if NK > 1:
                front_loads(1)
            front_stats(0)
            front_norm_T(0)
            for k in range(NK):
                if k + 2 < NK:
                    front_loads(k + 2)
                if k + 1 < NK:
                    front_stats(k + 1)
                (proj_A1 if chunks[k][0] == "A" else proj_B1)(k)
                if k + 1 < NK:
                    front_norm_T(k + 1)
                (proj_A2 if chunks[k][0] == "A" else proj_B2)(k)
            P.end_phase()
            es.__exit__(None, None, None)
        if stop_after == "AB":
            return nc, dict(KTm=KTm, KR=KR, Vm=Vm, KTs=KTs, Vs=Vs, QTm=QTm, QTs=QTs)

        merged = sb(g, "merged", [128, NBO, D], F32)
        mres = [Res("m") for _ in range(NBO)]
        with ExitStack() as es:
            trib = sb(es, "trib", [128, 2, 128], BF16)
            maskb = sb(es, "maskb", [128, 16 * 128], BF16)
            mst = sb(es, "mst", [128, 2048], F32)
            mst2 = sb(es, "mst2", [128, 256], F32)
            load("sp", mst.t[:, 0:2048], mask_d[:, :], [mst.r])
            P.op("pool", lambda e: e.tensor_copy(out=maskb.t[:], in_=mst.t[:, 0:2048]), reads=[mst.r], writes=[maskb.r])
            load("sp", mst2.t[:, 0:256], tri_d.rearrange("p a b -> p (a b)"), [mst2.r])
            P.op("pool", lambda e: e.tensor_copy(out=trib.t[:].rearrange("p a b -> p (a b)"), in_=mst2.t[:, 0:256]),
                 reads=[mst2.r], writes=[trib.r])
            Kb = [sb(es, "Kb%d" % i, [128, S], BF16) for i in range(2)]
            Vb = [sb(es, "Vb%d" % i, [128, NB, 65], BF16) for i in range(2)]
            Qb = [sb(es, "Qb%d" % i, [128, SO], BF16) for i in range(2)]
            eb2 = [sb(es, "eb2_%d" % i, [128, 2, 512], F32) for i in range(4)]
            lb2 = [sb(es, "lb2_%d" % i, [128, 2, 512], BF16) for i in range(2)]
            gb2 = [sb(es, "gb2_%d" % i, [128, 2, 512], F32) for i in range(2)]
            ab2 = [sb(es, "ab2_%d" % i, [128, 2, 512], BF16) for i in range(2)]
            ab = [sb(es, "ab%d" % i, [128, 512], BF16) for i in range(2)]
            ot = [sb(es, "ot%d" % i, [128, 512], F32) for i in range(2)]
            ot2 = sb(es, "ot2", [128, 4, 128], F32)
            rinv = [sb(es, "rinv%d" % i, [128, 4, 1], F32) for i in range(2)]
            zb2 = [ps(es, "zb2_%d" % i, [128, 2, 512], F32) for i in range(2)]
            zres = [[Res("z", True), Res("z", True)] for _ in range(2)]
            cb = ps(es, "cb", [128, 512], F32)
            oacc = [ps(es, "oacc%d" % i, [128, 512], F32) for i in range(2)]
            tp = ps(es, "tp", [128, 4, 128], F32)
            mk4 = maskb.t[:].rearrange("p (a b c q) -> p a b c q", a=2, b=2, c=4)
            halfpos = NBO // 2

            jobs = [("m", h) for h in range(H)] + [("s", h) for h in range(H)]

            def job_loads(k):
                typ, h = jobs[k]
                sl = k % 2
                Kt, Vt, Qt = Kb[sl], Vb[sl], Qb[sl]
                r0 = (h % 2) * 64
                if typ == "m":
                    load("sp", Kt.t[0:64, :], KTm[h // 2, r0:r0 + 64, :], [Kt.r])
                    load("sp", Kt.t[64:96, :], KR[:, :], [Kt.r])
                    load("sp", Vt.t[:], Vm[h], [Vt.r])
                    load("sp", Qt.t[0:96, :], QTm[h], [Qt.r])
                else:
                    load("sp", Kt.t[0:64, :], KTs[h // 2, r0:r0 + 64, :], [Kt.r])
                    load("sp", Kt.t[64:128, :], KTs[h // 2, r0:r0 + 64, :], [Kt.r])
                    load("sp", Vt.t[:], Vs[h], [Vt.r])
                    load("sp", Qt.t[0:64, :], QTs[h // 2, r0:r0 + 64, :], [Qt.r])
                    load("sp", Qt.t[64:128, :], QTs[h // 2, r0:r0 + 64, :], [Qt.r])

            gctr = [0]

            def tiles_for(gi):
                out = []
                for kb in range(16 * gi + 15, -1, -1):
                    pm = kb // 4
                    if pm >= 4 * gi:
                        lo = pm - 4 * gi
                        out.append((kb, lo * 128, True, 0 if pm < halfpos else 1, kb % 4))
                    else:
                        out.append((kb, 0, False, 0, 0))
                return out

            def finish_group(typ, h, gi, oa):
                o_ = ot[gctr[0] % 2]
                rv = rinv[gctr[0] % 2]
                nr = 65 if typ == "m" else 64
                copy_op("dve", o_.t[0:nr, :], oa.t[0:nr, :], [oa.r], [o_.r])
                for t in range(4):
                    tr(tp.t[:, t, 0:nr], o_.t[0:nr, t * 128:(t + 1) * 128], idf.t[0:nr, 0:nr], [o_.r, idf.r], [tp.r])
                if typ == "m":
                    P.op("dve", lambda e: e.reciprocal(out=rv.t[:], in_=tp.t[:, :, 64:65]), reads=[tp.r], writes=[rv.r])
                    for t in range(4):
                        P.op("dve", lambda e, t=t: e.tensor_scalar(out=merged.t[:, 4 * gi + t, h * 64:(h + 1) * 64], in0=tp.t[:, t, 0:64],
                                                                   scalar1=rv.t[:, t, :], scalar2=None, op0=ALU.mult),
                             reads=[tp.r, rv.r], writes=[mres[4 * gi + t]])
                else:
                    P.op("dve", lambda e: e.tensor_copy(out=merged.t[:, 4 * gi:4 * gi + 4, 512 + h * 64:512 + (h + 1) * 64], in_=tp.t[:, :, 0:64]),
                         reads=[tp.r], writes=[mres[4 * gi + t_] for t_ in range(4)])

            def zero_bank(bank, Qt):
                mm(bank.t[:, :], zerob.t[:, :], maskb.t[:, 0:512], True, True, [zerob.r, maskb.r], [bank.r])

            def job_items(ks):
                items = []
                g = gctr[0]
                for k in ks:
                    typ, h = jobs[k]
                    sl = k % 2
                    for gi in range(NG):
                        tl = tiles_for(gi)
                        npair = len(tl) // 2
                        for p in range(npair):
                            ta, tb_ = tl[2 * p], tl[2 * p + 1]
                            assert ta[1] == tb_[1] and ta[2] == tb_[2] and ta[3] == tb_[3]
                            items.append(dict(k=k, h=h, gi=gi, ta=ta, tb=tb_, first=(p == 0), last=(p == npair - 1), oa=oacc[g % 2],
                                              jobfirst=(gi == 0 and p == 0), Kt=Kb[sl], Vt=Vb[sl], Qt=Qb[sl]))
                        g += 1
                return items

            def run_mla(ks):
                items = job_items(ks)
                n = len(items)

                def S2(i):
                    it = items[i]; gi = it["gi"]; Kt = it["Kt"]; Qt = it["Qt"]
                    (kba, c0, msk, hf, ra), (kbb, _, _, _, rb) = it["ta"], it["tb"]
                    zt = zb2[i % 2].t; zr = zres[i % 2]
                    q0 = gi * 512 + c0
                    for j, kb_, r_ in ((0, kba, ra), (1, kbb, rb)):
                        mm(zt[:, j, c0:512], Kt.t[0:96, kb_ * 128:(kb_ + 1) * 128], Qt.t[0:96, q0:(gi + 1) * 512],
                           True, not msk, [Kt.r, Qt.r], [zr[j]])
                        if msk:
                            mm(zt[:, j, c0:c0 + 128], idb.t[:, :], mk4[:, 0, hf, r_, :], False, True, [idb.r, maskb.r], [zr[j]])

                def P2(i):
                    it = items[i]; oa = it["oa"]; Vt = it["Vt"]; Qt = it["Qt"]
                    (kba, c0, msk, hf, ra), (kbb, _, _, _, rb) = it["ta"], it["tb"]
                    zt = zb2[i % 2].t; zr = zres[i % 2]
                    a_ = ab2[i % 2]
                    o_ap = a_.t[:, :, c0:512]; i_ap = zt[:, :, c0:512]
                    P.op("act", lambda e, o_ap=o_ap, i_ap=i_ap: e.activation(out=o_ap, in_=i_ap, func=AF.Exp, scale=MLA_SCALE),
                         reads=[zr[0], zr[1]], writes=[a_.r])
                    if it["first"]:
                        zero_bank(oa, Qt)
                    mm(oa.t[0:65, c0:512], Vt.t[:, kba, 0:65], a_.t[:, 0, c0:512], False, False, [Vt.r, a_.r], [oa.r], skip=True)
                    mm(oa.t[0:65, c0:512], Vt.t[:, kbb, 0:65], a_.t[:, 1, c0:512], False, False, [Vt.r, a_.r], [oa.r], skip=True)
                    if it["jobfirst"] and it["k"] + 1 < len(jobs):
                        job_loads(it["k"] + 1)

                S2(0)
                for i in range(n):
                    if i + 1 < n:
                        S2(i + 1)
                    P2(i)
                    if items[i]["last"]:
                        finish_group("m", items[i]["h"], items[i]["gi"], items[i]["oa"])
                        gctr[0] += 1

            def finish_sb(h, gi, oa):
                o_ = ot[gctr[0] % 2]
                copy_op("dve", o_.t[:, :], oa.t[:, :], [oa.r], [o_.r])
                for t in range(4):
                    tr(tp.t[:, t, :], o_.t[:, t * 128:(t + 1) * 128], idf.t[:, :], [o_.r, idf.r], [tp.r])
                copy_op("dve", ot2.t[:], tp.t[:], [tp.r], [ot2.r])
                P.op("dve", lambda e: e.tensor_tensor(out=merged.t[:, 4 * gi:4 * gi + 4, 512 + h * 64:512 + (h + 1) * 64],
                                                      in0=ot2.t[:, :, 0:64], in1=ot2.t[:, :, 64:128], op=ALU.add),
                     reads=[ot2.r], writes=[mres[4 * gi + t_] for t_ in range(4)])

            def run_sb(ks):
                items = job_items(ks)
                n = len(items)

                def Zp(i):
                    it = items[i]; gi = it["gi"]; Kt = it["Kt"]; Qt = it["Qt"]
                    (kba, c0, msk, hf, ra), (kbb, _, _, _, rb) = it["ta"], it["tb"]
                    zt = zb2[i % 2].t; zr = zres[i % 2]
                    q0 = gi * 512 + c0
                    mm(zt[:, 0, c0:512], Kt.t[0:64, kba * 128:(kba + 1) * 128], Qt.t[0:64, q0:(gi + 1) * 512],
                       True, not msk, [Kt.r, Qt.r], [zr[0]])
                    mm(zt[:, 1, c0:512], Kt.t[64:128, kbb * 128:(kbb + 1) * 128], Qt.t[64:128, q0:(gi + 1) * 512],
                       True, not msk, [Kt.r, Qt.r], [zr[1]])
                    if msk:
                        mm(zt[:, 0, c0:c0 + 128], idb.t[:, :], mk4[:, 1, hf, ra, :], False, True, [idb.r, maskb.r], [zr[0]])
                        mm(zt[:, 1, c0:c0 + 128], idb.t[:, :], mk4[:, 1, hf, rb, :], False, True, [idb.r, maskb.r], [zr[1]])

                def Ep(i):
                    c0 = items[i]["ta"][1]
                    zt = zb2[i % 2].t; zr = zres[i % 2]; e_ = eb2[i % 4]
                    o_ap = e_.t[:, :, c0:512]; i_ap = zt[:, :, c0:512]
                    P.op("act", lambda e, o_ap=o_ap, i_ap=i_ap: e.activation(out=o_ap, in_=i_ap, func=AF.Exp),
                         reads=[zr[0], zr[1]], writes=[e_.r])

                def Lp(i):
                    c0 = items[i]["ta"][1]
                    e_ = eb2[i % 4]; l_ = lb2[i % 2]
                    o_ap = l_.t[:, :, c0:512]; i_ap = e_.t[:, :, c0:512]
                    P.op("act", lambda e, o_ap=o_ap, i_ap=i_ap: e.activation(out=o_ap, in_=i_ap, func=AF.Ln, bias=onec.t[:]),
                         reads=[e_.r, onec.r], writes=[l_.r])

                def Tri(i, j):
                    c0 = items[i]["ta"][1]
                    l_ = lb2[i % 2]
                    mm(cb.t[:, c0:512], trib.t[:, 0, :], l_.t[:, j, c0:512], False, False, [trib.r, l_.r], [cb.r], skip=True)

                def G(i, j):
                    c0 = items[i]["ta"][1]
                    g_ = gb2[i % 2]
                    o_ap = g_.t[:, j, c0:512]; i_ap = cb.t[:, c0:512]
                    P.op("act", lambda e, o_ap=o_ap, i_ap=i_ap: e.activation(out=o_ap, in_=i_ap, func=AF.Exp, scale=-1.0),
                         reads=[cb.r], writes=[g_.r])

                def OmT(i, j):
                    c0 = items[i]["ta"][1]
                    l_ = lb2[i % 2]
                    mm(cb.t[:, c0:512], trib.t[:, 1, :], l_.t[:, j, c0:512], False, False, [trib.r, l_.r], [cb.r], skip=True)

                def Amul(i):
                    c0 = items[i]["ta"][1]
                    g_ = gb2[i % 2]; a_ = ab2[i % 2]; e_ = eb2[i % 4]
                    o_ap = a_.t[:, :, c0:512]; x_ap = e_.t[:, :, c0:512]; y_ap = g_.t[:, :, c0:512]
                    P.op("dve", lambda e, o_ap=o_ap, x_ap=x_ap, y_ap=y_ap: e.tensor_tensor(out=o_ap, in0=x_ap, in1=y_ap, op=ALU.mult),
                         reads=[e_.r, g_.r], writes=[a_.r])

                def AVp(i):
                    it = items[i]; oa = it["oa"]; Vt = it["Vt"]
                    kba, c0 = it["ta"][0], it["ta"][1]
                    kbb = it["tb"][0]
                    a_ = ab2[i % 2]
                    mm(oa.t[0:64, c0:512], Vt.t[:, kba, 0:64], a_.t[:, 0, c0:512], False, False, [Vt.r, a_.r], [oa.r], skip=True)
                    o_ap = oa.t[64:128, c0:512]; l_ap = Vt.t[:, kbb, 0:64]; r_ap = a_.t[:, 1, c0:512]
                    P.op("pe", lambda e, o_ap=o_ap, l_ap=l_ap, r_ap=r_ap: e.matmul(o_ap, lhsT=l_ap, rhs=r_ap, start=False, stop=False,
                                                                                 skip_group_check=True, tile_position=(0, 64)),
                         reads=[Vt.r, a_.r], writes=[oa.r])
                    if it["jobfirst"] and it["k"] + 1 < len(jobs):
                        job_loads(it["k"] + 1)

                Zp(0)
                if n > 1:
                    Zp(1)
                Ep(0)
                if n > 2:
                    Zp(2)
                if n > 1:
                    Ep(1)
                Lp(0)
                for st_ in range(n + 1):
                    if 0 <= st_ - 1 < n:
                        OmT(st_ - 1, 1)
                    if st_ < n:
                        if items[st_]["first"]:
                            zero_bank(cb, None)
                        Tri(st_, 0)
                        G(st_, 0)
                    if st_ + 3 < n:
                        Zp(st_ + 3)
                    if st_ + 1 < n:
                        Lp(st_ + 1)
                    if 0 <= st_ - 1 < n:
                        Amul(st_ - 1)
                    if st_ < n:
                        OmT(st_, 0)
                        Tri(st_, 1)
                        G(st_, 1)
                    if 0 <= st_ - 1 < n:
                        it = items[st_ - 1]
                        if it["first"]:
                            zero_bank(it["oa"], None)
                        AVp(st_ - 1)
                        if it["last"]:
                            finish_sb(it["h"], it["gi"], it["oa"])
                            gctr[0] += 1
                    if st_ + 2 < n:
                        Ep(st_ + 2)

            job_loads(0)
            run_mla([k for k in range(len(jobs)) if jobs[k][0] == "m"])
            run_sb([k for k in range(len(jobs)) if jobs[k][0] == "s"])
            P.end_phase()
        if stop_after == "C":
            return nc, dict(merged=merged)

        with ExitStack() as es:
            fT = sb(es, "fT", [128, DC, SO], BF16)
            load_gains(es, ("ffn", "fin"))
            g_o = sb(es, "g_o_sb", [128, DC], F32)
            load("sp", g_o.t[:], g_o_d[:, :], [g_o.r])
            sst = sb(es, "sst", [128, NBO, 2], F32)
            rst = sb(es, "rst", [128, NBO, 2], F32)
            junk = sb(es, "junkd", [128, D], F32)
            wg = [sb(es, "wg0", [128, DC, 512], BF16), None]
            wu = [sb(es, "wu0", [128, DC, 512], BF16), None]
            wd = [sb(es, "wd0", [128, 4, D], BF16), None]
            wstf = [sb(es, "wstf%d" % i, [128, 1024], F32) for i in range(2)]
            NFG = (FC + 3) // 4
            wctr = [0]

            def stage_cast(dst_ap, src_ap, n, dst_r):
                i = wctr[0]; wctr[0] += 1
                st = wstf[i % 2]
                load("sp", st.t[:, 0:n], src_ap, [st.r])
                copy_op("pool", dst_ap, st.t[:, 0:n], [st.r], [dst_r])

            def ffn_load_list(fg):
                f0 = fg * 512
                nf = min(512, DFF - f0)
                sl = fg % 2
                lst = []
                for c in range(DC):
                    lst.append((wg[sl].t[:, c, 0:nf], w_g_d[:, c, f0:f0 + nf], nf, wg[sl].r))
                    lst.append((wu[sl].t[:, c, 0:nf], w_u_d[:, c, f0:f0 + nf], nf, wu[sl].r))
                for fc in range(nf // 128):
                    lst.append((wd[sl].t[:, fc, :], w_d_d[:, fg * 4 + fc, :], D, wd[sl].r))
                return lst

            def ffn_loads(fg):
                for a_ in ffn_load_list(fg):
                    stage_cast(*a_)

            with ExitStack() as e1:
                wo = sb(e1, "wo", [128, DC, D], BF16)
                mn = [sb(e1, "mn%d" % i, [128, D], BF16) for i in range(3)]
                mT = [sb(e1, "mT%d" % i, [128, DC, 128], BF16) for i in range(3)]
                xs2 = [sb(e1, "xs2_%d" % i, [128, D], F32) for i in range(3)]
                pT = [ps(e1, "pTd%d" % i, [128, DC, 128], BF16) for i in range(3)]
                pO = [ps(e1, "pO%d" % i, [128, 1024], F32) for i in range(2)]
                for c in range(DC):
                    st = wstf[c % 2]
                    load("sp", st.t[:, 0:D], w_o_d[:, c, :], [st.r])
                    P.op("act", lambda e, st=st, c=c: e.activation(out=wo.t[:, c, :], in_=st.t[:, 0:D], func=AF.Copy, scale=g_o.t[:, c:c + 1]),
                         reads=[st.r, g_o.r], writes=[wo.r])
                for t in range(NBO):
                    for hf in range(2):
                        P.op("act", lambda e, t=t, hf=hf: e.activation(out=junk.t[:, 0:512], in_=merged.t[:, t, hf * 512:(hf + 1) * 512],
                                                                       func=AF.Square, accum_out=sst.t[:, t, hf:hf + 1]),
                             reads=[mres[t]], writes=[junk.r, sst.r])
                rstd_from_ss(sst.t[:], rst.t[:], sst.r, rst.r, 512)
                pre0 = ffn_load_list(0)

                def prep(t):
                    m_ = mn[t % 3]; mt = mT[t % 3]; pt = pT[t % 3]; xs = xs2[t % 3]
                    load("sp", xs.t[:], xo[t * 128:(t + 1) * 128, :], [xs.r])
                    for hf in range(2):
                        P.op("act", lambda e, t=t, hf=hf, m_=m_: e.activation(
                            out=m_.t[:, hf * 512:(hf + 1) * 512], in_=merged.t[:, t, hf * 512:(hf + 1) * 512],
                            func=AF.Copy, scale=rst.t[:, t, hf:hf + 1]),
                            reads=[mres[t], rst.r], writes=[m_.r])
                    for c in range(DC):
                        tr(pt.t[:, c, :], m_.t[:, c * 128:(c + 1) * 128], idb.t[:], [m_.r, idb.r], [pt.r])
                    copy_op("act", mt.t[:], pt.t[:], [pt.r], [mt.r])

                def fin(t):
                    mt = mT[t % 3]; po = pO[t % 2]; xs = xs2[t % 3]
                    for nh in range(2):
                        for c in range(DC):
                            mm(po.t[:, nh * 512:(nh + 1) * 512], mt.t[:, c, :], wo.t[:, c, nh * 512:(nh + 1) * 512],
                               c == 0, c == DC - 1, [mt.r, wo.r], [po.r])
                    P.op("dve", lambda e, t=t, po=po, xs=xs: e.tensor_tensor(out=merged.t[:, t, :], in0=po.t[:, :], in1=xs.t[:], op=ALU.add),
                         reads=[po.r, xs.r], writes=[mres[t]])

                def fin_sq(t):
                    P.op("act", lambda e, t=t: e.activation(out=junk.t[:], in_=merged.t[:, t, :], func=AF.Square, accum_out=sst.t[:, t, 0:1]),
                         reads=[mres[t]], writes=[junk.r, sst.r])

                prep(0)
                if NBO > 1:
                    prep(1)
                for t in range(NBO):
                    if t + 2 < NBO:
                        prep(t + 2)
                    for _ in range(2):
                        if pre0:
                            stage_cast(*pre0.pop(0))
                    fin(t)
                    if t >= 1:
                        fin_sq(t - 1)
                fin_sq(NBO - 1)
                while pre0:
                    stage_cast(*pre0.pop(0))
                rstd_from_ss(sst.t[:], rst.t[:], sst.r, rst.r, D)
                for t in range(NBO):
                    m_ = mn[t % 2]; pt = pT[t % 2]
                    P.op("dve", lambda e, t=t, m_=m_: e.scalar_tensor_tensor(out=m_.t[:], in0=merged.t[:, t, :], scalar=rst.t[:, t, 0:1],
                                                                       in1=gm["ffn"].t[:], op0=ALU.mult, op1=ALU.mult),
                         reads=[mres[t], rst.r, gm["ffn"].r], writes=[m_.r])
                    for c in range(DC):
                        tr(pt.t[:, c, :], m_.t[:, c * 128:(c + 1) * 128], idb.t[:], [m_.r, idb.r], [pt.r])
                    copy_op("act" if t % 2 == 0 else "dve", fT.t[:, :, t * 128:(t + 1) * 128], pt.t[:], [pt.r], [fT.r])
                P.end_phase()

            with ExitStack() as e2:
                wg[1] = sb(e2, "wg1", [128, DC, 512], BF16)
                wu[1] = sb(e2, "wu1", [128, DC, 512], BF16)
                wd[1] = sb(e2, "wd1", [128, 4, D], BF16)
                aT = [sb(e2, "aT%d" % i, [128, 4, 512], BF16) for i in range(2)]
                sg = [sb(e2, "sg%d" % i, [128, 512], F32) for i in range(2)]
                yt = [sb(e2, "yt%d" % i, [128, D], F32) for i in range(3)]
                pg = [ps(e2, "pg%d" % i, [128, 512], F32) for i in range(2)]
                pu = [ps(e2, "pu%d" % i, [128, 512], F32) for i in range(2)]
                pd = [ps(e2, "pd%d" % i, [128, 1024], F32) for i in range(2)]
                k = 0
                pend = None

                def down(k_, sl_, nfc_, tg_):
                    a_ = aT[k_ % 2]
                    for tt in range(4):
                        t = tg_ * 4 + tt
                        pd_ = pd[(k_ * 4 + tt) % 2]
                        for nh in range(2):
                            for fc in range(nfc_):
                                mm(pd_.t[:, nh * 512:(nh + 1) * 512], a_.t[:, fc, tt * 128:(tt + 1) * 128], wd[sl_].t[:, fc, nh * 512:(nh + 1) * 512],
                                   fc == 0, fc == nfc_ - 1, [a_.r, wd[sl_].r], [pd_.r])
                        P.op("dve", lambda e, t=t, pd_=pd_: e.tensor_tensor(out=merged.t[:, t, :], in0=pd_.t[:, :], in1=merged.t[:, t, :], op=ALU.add),
                             reads=[pd_.r, mres[t]], writes=[mres[t]])

                if NFG > 1:
                    ffn_loads(1)
                for fg in range(NFG):
                    f0 = fg * 512
                    nfc = min(512, DFF - f0) // 128
                    sl = fg % 2
                    for tg in range(NG):
                        a_ = aT[k % 2]
                        for fc in range(nfc):
                            pg_ = pg[(k * 4 + fc) % 2]; pu_ = pu[(k * 4 + fc) % 2]; sg_ = sg[(k * 4 + fc) % 2]
                            for c in range(DC):
                                mm(pg_.t[:, :], wg[sl].t[:, c, fc * 128:(fc + 1) * 128], fT.t[:, c, tg * 512:(tg + 1) * 512],
                                   c == 0, c == DC - 1, [wg[sl].r, fT.r], [pg_.r])
                            for c in range(DC):
                                mm(pu_.t[:, :], wu[sl].t[:, c, fc * 128:(fc + 1) * 128], fT.t[:, c, tg * 512:(tg + 1) * 512],
                                   c == 0, c == DC - 1, [wu[sl].r, fT.r], [pu_.r])
                            P.op("act", lambda e, pg_=pg_, sg_=sg_: e.activation(out=sg_.t[:], in_=pg_.t[:, :], func=AF.Silu),
                                 reads=[pg_.r], writes=[sg_.r])
                            P.op("dve", lambda e, pu_=pu_, sg_=sg_, a_=a_, fc=fc: e.tensor_tensor(out=a_.t[:, fc, :], in0=pu_.t[:, :], in1=sg_.t[:],
                                                                                             op=ALU.mult),
                                 reads=[pu_.r, sg_.r], writes=[a_.r])
                        if pend is not None:
                            down(*pend)
                        pend = (k, sl, nfc, tg)
                        k += 1
                        if tg == 0 and fg >= 1 and fg + 1 < NFG:
                            ffn_loads(fg + 1)
                down(*pend)
                for t in range(NBO):
                    P.op("act", lambda e, t=t: e.activation(out=junk.t[:], in_=merged.t[:, t, :], func=AF.Square, accum_out=sst.t[:, t, 0:1]),
                         reads=[mres[t]], writes=[junk.r, sst.r])
                rstd_from_ss(sst.t[:], rst.t[:], sst.r, rst.r, D)
                for t in range(NBO):
                    y_ = yt[t % 3]
                    P.op("dve", lambda e, t=t, y_=y_: e.scalar_tensor_tensor(out=y_.t[:], in0=merged.t[:, t, :], scalar=rst.t[:, t, 0:1],
                                                                       in1=gm["fin"].t[:], op0=ALU.mult, op1=ALU.mult),
                         reads=[mres[t], rst.r, gm["fin"].r], writes=[y_.r])
                    load("sp", y[t * 128:(t + 1) * 128, :], y_.t[:], [], reads=[y_.r])
                P.end_phase()
    return nc, {}


def host_inputs(inputs, S):
    NB = S // 128
    f = np.float32
    x = np.asarray(inputs["x"], f)
    pos = np.asarray(inputs["positions"], np.int32)

    def kchunk(w):
        K, E = w.shape
        return np.ascontiguousarray(w.reshape(K // 128, 128, E).transpose(1, 0, 2))

    def rep(v):
        return np.ascontiguousarray(np.broadcast_to(np.asarray(v, f).reshape(1, -1), (128, v.size)))

    w_in = kchunk(np.asarray(inputs["w_in"], f)[0])
    w_uq = kchunk(np.asarray(inputs["w_uq"], f)[0]).reshape(128, 2 * 768)
    wukv = np.asarray(inputs["w_ukv"], f)[0].reshape(128, H, 128)
    w_uk = np.ascontiguousarray(wukv[:, :, 0:64].reshape(128, 512))
    w_uv = np.ascontiguousarray(wukv[:, :, 64:128].reshape(128, 512))
    w_o = kchunk(np.asarray(inputs["w_o"], f)[0])
    w_g = kchunk(np.asarray(inputs["w_gate"], f)[0])
    w_u = kchunk(np.asarray(inputs["w_up"], f)[0])
    w_d = kchunk(np.asarray(inputs["w_down"], f)[0])
    ident = np.eye(128, dtype=f)
    jj = np.arange(128)[:, None]
    ss_ = np.arange(128)[None, :]
    tri = np.stack([(jj >= ss_).astype(f), (jj < ss_).astype(f)], axis=1)
    causal = (jj <= ss_).astype(f)
    strict = (jj < ss_).astype(f)
    common = dict(ident=ident, tri=np.ascontiguousarray(tri), w_in=w_in, w_uq=w_uq, w_uk=w_uk, w_uv=w_uv, w_o=w_o,
                  w_gate=w_g, w_up=w_u, w_down=w_d,
                  g_mix=rep(inputs["norm_mix"][0]), g_q=rep(inputs["q_latent_norm"][0]), g_kv=rep(inputs["kv_latent_norm"][0]),
                  g_mla=rep(inputs["out_norm_mla"][0]), g_sb=rep(inputs["out_norm_sb"][0]), g_ffn=rep(inputs["norm_ffn"][0]),
                  g_fin=rep(inputs["norm_final"]),
                  g_o=np.ascontiguousarray(np.concatenate([np.asarray(inputs["out_norm_mla"], f)[0],
                                                           np.asarray(inputs["out_norm_sb"], f)[0]]).reshape(DC, 128).T))
    maps = []
    for c in range(8):
        b, j = c // 4, c % 4
        ob = own_blocks(j, NB)
        rows = np.concatenate([np.arange(k * 128, (k + 1) * 128) for k in ob])
        masks = np.zeros((128, 2, 2, 4, 128), f)
        for ti, tm in enumerate((causal, strict)):
            for hf, off in enumerate((j, 3 - j)):
                for r in range(4):
                    if r < off:
                        masks[:, ti, hf, r, :] = 0.0
                    elif r == off:
                        masks[:, ti, hf, r, :] = NEG * (1.0 - tm)
                    else:
                        masks[:, ti, hf, r, :] = NEG
        m = dict(common)
        m["xb"] = np.ascontiguousarray(x[b])
        m["xo"] = np.ascontiguousarray(x[b][rows])
        m["posb"] = np.ascontiguousarray(pos[b].reshape(NB, 128).T)
        m["poso"] = np.ascontiguousarray(pos[b][rows].reshape(len(ob), 128).T)
        m["masks"] = masks.reshape(128, 16 * 128)
        maps.append(m)
    return maps


def assemble(results, S, B=2):
    NB = S // 128
    out = np.zeros((B, S, D), np.float32)
    for c in range(8):
        b, j = c // 4, c % 4
        ob = own_blocks(j, NB)
        yy = np.asarray(results[c]["y"], np.float32)
        for i, k in enumerate(ob):
            out[b, k * 128:(k + 1) * 128, :] = yy[i * 128:(i + 1) * 128, :]
    return out


_NC_CACHE = {}


def kernel(**inputs):
    S = int(np.asarray(inputs["x"]).shape[1])
    if S not in _NC_CACHE:
        _NC_CACHE[S] = build(S, Prog, Res)[0]
    nc = _NC_CACHE[S]
    maps = host_inputs(inputs, S)
    res = run_bass_kernel_spmd(nc, maps, core_ids=list(range(8)))
    return assemble(res.results, S, B=int(np.asarray(inputs["x"]).shape[0]))
```

```python
import math
from contextlib import ExitStack
import numpy as np
import concourse.bass as bass
import concourse.mybir as mybir
from concourse.bass_utils import run_bass_kernel_spmd

F32 = mybir.dt.float32
BF16 = mybir.dt.bfloat16
I32 = mybir.dt.int32
AF = mybir.ActivationFunctionType
ALU = mybir.AluOpType


ENGS = ("pe", "act", "dve", "pool", "sp")


class Res:
    __slots__ = ("name", "w", "r", "excl")

    def __init__(self, name, excl=False):
        self.name = name
        self.w = None
        self.r = []
        self.excl = excl


class Op:
    __slots__ = ("fn", "waits", "sig", "dma", "idx")

    def __init__(self, fn, waits, dma, idx):
        self.fn = fn
        self.waits = waits
        self.sig = False
        self.dma = dma
        self.idx = idx


class Prog:
    NDMA = 40

    def __init__(self, nc):
        self.nc = nc
        self.sem = {e: nc.alloc_semaphore("s_" + e) for e in ENGS}
        self.dsem = [nc.alloc_semaphore("d%d" % i) for i in range(self.NDMA)]
        self.dcount = [0] * self.NDMA
        self.dnext = 0
        self.ops = {e: [] for e in ENGS}
        self.start = {e: 0 for e in ENGS}
        self.base = {e: 0 for e in ENGS}
        self.seen = {e: {x: 0 for x in ENGS} for e in ENGS}
        self.seen_d = {e: [0] * self.NDMA for e in ENGS}

    def _deps(self, reads, writes):
        ev = []
        for r in reads:
            if r.excl:
                writes = list(writes) + [r]
                continue
            if r.w is not None:
                ev.append(r.w)
        for w in writes:
            if w.w is not None:
                ev.append(w.w)
            ev.extend(w.r)
        return ev

    def _filter(self, eng, evs):
        out = []
        best = {}
        for e in evs:
            if e[0] == "c":
                _, x, i = e
                if x == "pe" and eng == "pe":
                    continue
                if i <= self.seen[eng][x]:
                    continue
                if i > best.get(("c", x), 0):
                    best[("c", x)] = i
            else:
                _, s, c = e
                if c <= self.seen_d[eng][s]:
                    continue
                if c > best.get(("d", s), 0):
                    best[("d", s)] = c
        for k, v in best.items():
            if k[0] == "c":
                self.seen[eng][k[1]] = v
                self.ops[k[1]][v - 1].sig = True
                out.append(("c", k[1], v))
            else:
                self.seen_d[eng][k[1]] = v
                out.append(("d", k[1], v))
        return out

    def _commit(self, ev, reads, writes):
        for r in reads:
            if r.excl:
                r.w = ev
                r.r = []
            else:
                r.r.append(ev)
        for w in writes:
            w.w = ev
            w.r = []

    def op(self, eng, fn, reads=(), writes=()):
        waits = self._filter(eng, self._deps(reads, writes))
        lst = self.ops[eng]
        o = Op(fn, waits, None, len(lst) + 1)
        lst.append(o)
        self._commit(("c", eng, o.idx), reads, writes)
        return o

    def dma(self, q, fn, reads=(), writes=()):
        s = self.dnext
        self.dnext = (self.dnext + 1) % self.NDMA
        evs = self._deps(reads, writes)
        if self.dcount[s] > 0:
            evs.append(("d", s, self.dcount[s]))
        waits = self._filter(q, evs)
        self.dcount[s] += 16
        lst = self.ops[q]
        o = Op(fn, waits, (s, self.dcount[s]), len(lst) + 1)
        lst.append(o)
        self._commit(("d", s, self.dcount[s]), reads, writes)
        return o

    def barrier(self):
        evs = [("d", s, self.dcount[s]) for s in range(self.NDMA) if self.dcount[s] > 0]
        waits = self._filter("sp", evs)
        lst = self.ops["sp"]
        o = Op(lambda e: e.nop(), waits, None, len(lst) + 1)
        lst.append(o)
        for x in ENGS:
            for s in range(self.NDMA):
                self.seen_d[x][s] = self.dcount[s]
            for y in ENGS:
                self.seen[x][y] = len(self.ops[y])

    def end_phase(self):
        self.barrier()
        self.replay()

    def replay(self):
        nc = self.nc
        sigcnt = {}
        for e in ENGS:
            c = self.base[e]
            arr = []
            for o in self.ops[e][self.start[e]:]:
                if o.sig:
                    c += 1
                arr.append(c)
            sigcnt[e] = arr

        def val(x, idx1):
            st = self.start[x]
            if idx1 - 1 < st:
                return self._hist[x][idx1 - 1]
            return sigcnt[x][idx1 - 1 - st]

        if not hasattr(self, "_hist"):
            self._hist = {e: [] for e in ENGS}

        def emit(ename, eng):
            for o in self.ops[ename][self.start[ename]:]:
                for w in o.waits:
                    if w[0] == "c":
                        eng.wait_ge(self.sem[w[1]], val(w[1], w[2]))
                    else:
                        eng.wait_ge(self.dsem[w[1]], w[2])
                ins = o.fn(eng)
                if o.dma is not None:
                    ins.then_inc(self.dsem[o.dma[0]], 16)
                elif o.sig:
                    ins.then_inc(self.sem[ename], 1)

        with nc.Block() as block:
            @block.tensor
            def _(t):
                emit("pe", t)

            @block.scalar
            def _(t):
                emit("act", t)

            @block.vector
            def _(t):
                emit("dve", t)

            @block.gpsimd
            def _(t):
                emit("pool", t)

            @block.sync
            def _(t):
                emit("sp", t)

        for e in ENGS:
            self._hist[e].extend(sigcnt[e])
            self.base[e] = sigcnt[e][-1] if sigcnt[e] else self.base[e]
            self.start[e] = len(self.ops[e])


D = 1024
DC = 8
H = 8
DFF = 2816
FC = 22
EPS = 1e-6
INW = 1952
C_CQ, C_CKV, C_KR, C_QSB, C_KSB, C_VSB = 0, 256, 384, 416, 928, 1440
MLA_SCALE = 1.0 / math.sqrt(96.0)
NEG = -30000.0


def own_blocks(j, NB):
    half = NB // 8
    return [4 * m + j for m in range(half)] + [4 * m + 3 - j for m in range(half, 2 * half)]


def build(S, Prog, Res, stop_after=None):
    NB = S // 128
    NBO = NB // 4
    NG = NBO // 4
    NCH = NB // 4
    SO = NBO * 128
    nc = bass.Bass("TRN2", target_bir_lowering=False)

    def din(name, shape, dt=F32):
        return nc.dram_tensor(name, list(shape), dt, kind="ExternalInput").ap()

    xb = din("xb", [S, D])
    xo = din("xo", [SO, D])
    posb = din("posb", [128, NB], I32)
    poso = din("poso", [128, NBO], I32)
    ident_d = din("ident", [128, 128])
    tri_d = din("tri", [128, 2, 128])
    w_in_d = din("w_in", [128, DC, INW])
    w_uq_d = din("w_uq", [128, 2 * 768])
    w_uk_d = din("w_uk", [128, 512])
    w_uv_d = din("w_uv", [128, 512])
    w_o_d = din("w_o", [128, DC, D])
    w_g_d = din("w_gate", [128, DC, DFF])
    w_u_d = din("w_up", [128, DC, DFF])
    w_d_d = din("w_down", [128, FC, D])
    g_d = {"mix": din("g_mix", [128, D]), "q": din("g_q", [128, 256]), "kv": din("g_kv", [128, 128]),
           "mla": din("g_mla", [128, 512]), "sb": din("g_sb", [128, 512]), "ffn": din("g_ffn", [128, D]),
           "fin": din("g_fin", [128, D])}
    g_o_d = din("g_o", [128, DC])
    mask_d = din("masks", [128, 16 * 128])
    y = nc.dram_tensor("y", [SO, D], F32, kind="ExternalOutput").ap()

    KTm = nc.dram_tensor("KTm", [4, 128, S], BF16).ap()
    KR = nc.dram_tensor("KR", [32, S], BF16).ap()
    Vm = nc.dram_tensor("Vm", [H, 128, NB, 65], BF16).ap()
    KTs = nc.dram_tensor("KTs", [4, 128, S], BF16).ap()
    Vs = nc.dram_tensor("Vs", [H, 128, NB, 65], BF16).ap()
    QTm = nc.dram_tensor("QTm", [H, 96, SO], BF16).ap()
    QTs = nc.dram_tensor("QTs", [4, 128, SO], BF16).ap()

    P = Prog(nc)
    inv_freq = (10000.0 ** (-np.arange(0, 32, 2, dtype=np.float32) / np.float32(32))).astype(np.float32)

    class T:
        def __init__(self, t, excl=False):
            self.t = t
            self.r = Res("r", excl)

    def sb(es, name, shape, dt):
        return T(es.enter_context(nc.sbuf_tensor(name, list(shape), dt)))

    def ps(es, name, shape, dt=F32):
        return T(es.enter_context(nc.psum_tensor(name, list(shape), dt)), excl=True)

    rrs = {"evac": 0}

    def evac_eng():
        rrs["evac"] ^= 1
        return "act" if rrs["evac"] else "dve"

    def copy_op(eng, out, in_, reads, writes, scale=None):
        if eng == "act":
            if scale is None:
                P.op("act", lambda e: e.activation(out=out, in_=in_, func=AF.Copy), reads=reads, writes=writes)
            else:
                P.op("act", lambda e: e.activation(out=out, in_=in_, func=AF.Copy, scale=scale), reads=reads, writes=writes)
        elif scale is None:
            P.op(eng, lambda e: e.tensor_copy(out=out, in_=in_), reads=reads, writes=writes)
        else:
            P.op(eng, lambda e: e.tensor_scalar(out=out, in0=in_, scalar1=scale, scalar2=None, op0=ALU.mult), reads=reads, writes=writes)

    def mm(out, lhsT, rhs, start, stop, reads, writes, skip=False):
        if skip:
            P.op("pe", lambda e: e.matmul(out, lhsT=lhsT, rhs=rhs, start=start, stop=stop, skip_group_check=True), reads=reads, writes=writes)
        else:
            P.op("pe", lambda e: e.matmul(out, lhsT=lhsT, rhs=rhs, start=start, stop=stop), reads=reads, writes=writes)

    def tr(out, in_, ident, reads, writes):
        P.op("pe", lambda e: e.transpose(out=out, in_=in_, identity=ident), reads=reads, writes=writes)

    def load(q, out, in_, writes, reads=()):
        P.dma(q, lambda e: e.dma_start(out=out, in_=in_), reads=reads, writes=writes)

    with ExitStack() as g:
        idf = sb(g, "idf", [128, 128], F32)
        idb = sb(g, "idb", [128, 128], BF16)
        zerob = sb(g, "zerob", [128, 128], BF16)
        epsc = sb(g, "epsc", [128, 1], F32)
        onec = sb(g, "onec", [128, 1], F32)
        pic = sb(g, "pic", [128, 1], F32)
        gm = {}

        def load_gains(es_, names):
            for nm in names:
                gm[nm] = sb(es_, "gs_" + nm, [128, g_d[nm].shape[1]], F32)
                load("sp", gm[nm].t[:], g_d[nm][:, :], [gm[nm].r])
        load("sp", idf.t[:], ident_d[:, :], [idf.r])
        P.op("pool", lambda e: e.tensor_copy(out=idb.t[:], in_=idf.t[:]), reads=[idf.r], writes=[idb.r])
        P.op("pool", lambda e: e.memset(zerob.t[:], 0.0), writes=[zerob.r])
        P.op("pool", lambda e: e.memset(epsc.t[:], EPS), writes=[epsc.r])
        P.op("pool", lambda e: e.memset(onec.t[:], 1.0), writes=[onec.r])
        P.op("pool", lambda e: e.memset(pic.t[:], math.pi), writes=[pic.r])

        def rstd_from_ss(ssap, rsap, rd, wr, n):
            P.op("act", lambda e: e.activation(out=rsap, in_=ssap, func=AF.Ln, scale=1.0 / n, bias=epsc.t[:]),
                 reads=[rd, epsc.r], writes=[wr])
            P.op("act", lambda e: e.activation(out=rsap, in_=rsap, func=AF.Exp, scale=-0.5), reads=[wr], writes=[wr])

        with ExitStack() as es:
            win = sb(es, "win", [128, DC, INW], BF16)
            wuq = sb(es, "wuq", [128, 2 * 768], BF16)
            wuk = sb(es, "wuk", [128, 512], BF16)
            wuv = sb(es, "wuv", [128, 512], BF16)
            cosb = sb(es, "cosb", [128, NB, 16], F32)
            sinb = sb(es, "sinb", [128, NB, 16], F32)
            coso = sb(es, "coso", [128, NBO, 16], F32)
            sino = sb(es, "sino", [128, NBO, 16], F32)
            load_gains(es, ("mix", "q", "kv"))
            es0 = es
            es = ExitStack()
            es.__enter__()
            wst = [sb(es, "wst%d" % i, [128, 2048], F32) for i in range(4)]
            ropi = sb(es, "ropi", [128, NB], I32)
            ropf = sb(es, "ropf", [128, NB], F32)
            ropt = sb(es, "ropt", [128, NB, 16], F32)
            ropk = sb(es, "ropk", [128, NB, 16], I32)
            ropg = sb(es, "ropg", [128, NB, 16], F32)
            roph = sb(es, "roph", [128, NB, 16], F32)

            def rope_table(pos_d, n, sink_sin, sink_cos):
                load("sp", ropi.t[:, 0:n], pos_d[:, :], [ropi.r])
                P.op("dve", lambda e: e.tensor_copy(out=ropf.t[:, 0:n], in_=ropi.t[:, 0:n]), reads=[ropi.r], writes=[ropf.r])
                for i in range(16):
                    P.op("dve", lambda e, i=i: e.tensor_scalar(out=ropt.t[:, 0:n, i], in0=ropf.t[:, 0:n], scalar1=float(inv_freq[i]),
                                                                scalar2=1.0 / (2 * math.pi), op0=ALU.mult, op1=ALU.mult),
                         reads=[ropf.r], writes=[ropt.r])
                for ph, sink in ((0.0, sink_sin), (0.25, sink_cos)):
                    P.op("dve", lambda e, ph=ph: e.tensor_scalar(out=ropg.t[:, 0:n, :], in0=ropt.t[:, 0:n, :], scalar1=ph, scalar2=None, op0=ALU.add),
                         reads=[ropt.r], writes=[ropg.r])
                    P.op("dve", lambda e: e.tensor_copy(out=ropk.t[:, 0:n, :], in_=ropg.t[:, 0:n, :]), reads=[ropg.r], writes=[ropk.r])
                    P.op("dve", lambda e: e.tensor_copy(out=roph.t[:, 0:n, :], in_=ropk.t[:, 0:n, :]), reads=[ropk.r], writes=[roph.r])
                    P.op("dve", lambda e: e.tensor_tensor(out=ropg.t[:, 0:n, :], in0=ropg.t[:, 0:n, :], in1=roph.t[:, 0:n, :], op=ALU.subtract),
                         reads=[ropg.r, roph.r], writes=[ropg.r])
                    P.op("dve", lambda e: e.scalar_tensor_tensor(out=roph.t[:, 0:n, :], in0=ropg.t[:, 0:n, :], scalar=0.0, in1=ropg.t[:, 0:n, :],
                                                                 op0=ALU.is_lt, op1=ALU.add), reads=[ropg.r], writes=[roph.r])
                    sink(roph, n)

            def sink_b(tile):
                def f(src, n):
                    P.op("act", lambda e: e.activation(out=tile.t[:], in_=src.t[:, 0:n, :], func=AF.Sin, scale=-2.0 * math.pi, bias=pic.t[:]),
                         reads=[src.r, pic.r], writes=[tile.r])
                return f

            rope_table(posb, NB, sink_b(sinb), sink_b(cosb))
            rope_table(poso, NBO, sink_b(sino), sink_b(coso))

            k = 0
            for c in range(DC):
                st = wst[k % 4]; k += 1
                load("sp", st.t[:, 0:INW], w_in_d[:, c, :], [st.r])
                copy_op(("act", "dve", "act", "pool")[c % 4], win.t[:, c, :], st.t[:, 0:INW], [st.r], [win.r])
            st = wst[k % 4]; k += 1
            load("sp", st.t[:, 0:1536], w_uq_d[:, :], [st.r])
            copy_op("act", wuq.t[:], st.t[:, 0:1536], [st.r], [wuq.r])
            st = wst[k % 4]; k += 1
            load("sp", st.t[:, 0:512], w_uk_d[:, :], [st.r])
            load("sp", st.t[:, 512:1024], w_uv_d[:, :], [st.r])
            copy_op("dve", wuk.t[:], st.t[:, 0:512], [st.r], [wuk.r])
            copy_op("dve", wuv.t[:], st.t[:, 512:1024], [st.r], [wuv.r])

            P.end_phase()
            es.__exit__(None, None, None)
            es = ExitStack()
            es.__enter__()
            xbuf = [sb(es, "xbuf%d" % i, [128, D], F32) for i in range(8)]
            junk = sb(es, "junk", [128, D], F32)
            ubuf = [sb(es, "ubuf%d" % i, [128, D], BF16) for i in range(3)]
            uT = [sb(es, "uT%d" % i, [128, DC, 512], BF16) for i in range(2)]
            ss4 = [sb(es, "ss4_%d" % i, [128, 4], F32) for i in range(2)]
            rs4 = [sb(es, "rs4_%d" % i, [128, 4], F32) for i in range(2)]
            ssc = [sb(es, "ssc_%d" % i, [128, 4], F32) for i in range(2)]
            rsc = [sb(es, "rsc_%d" % i, [128, 4], F32) for i in range(2)]
            st4 = [sb(es, "st4_%d" % i, [128, 4, 512], BF16) for i in range(3)]
            stv = [sb(es, "stv%d" % i, [128, H, 4, 65], BF16) for i in range(3)]
            ckr = [sb(es, "ckr%d" % i, [128, 160], BF16) for i in range(4)]
            rtmp = [sb(es, "rtmp%d" % i, [128, 2, 16], F32) for i in range(2)]
            ckvT = [sb(es, "ckvT%d" % i, [128, 512], BF16) for i in range(2)]
            krT = [sb(es, "krT%d" % i, [32, 512], BF16) for i in range(2)]
            cqn = [sb(es, "cqn%d" % i, [128, 256], BF16) for i in range(4)]
            cqT = [sb(es, "cqT%d" % i, [128, 2, 512], BF16) for i in range(2)]
            qtok = [sb(es, "qtok%d" % i, [128, H, 96], BF16) for i in range(2)]
            qrt = [sb(es, "qrt%d" % i, [128, 2, H, 16], F32) for i in range(2)]
            qmst = [sb(es, "qmst%d" % i, [128, H, 512], BF16) for i in range(2)]
            qfb = [sb(es, "qfb%d" % i, [128, 768], F32) for i in range(2)]
            pT = [ps(es, "pT%d" % i, [128, DC, 128], BF16) for i in range(2)]
            pP = [ps(es, "pP%d" % i, [128, 512], F32) for i in range(3)]
            pC = ps(es, "pC", [128, 4, 256], F32)
            pQ = ps(es, "pQ", [128, 512], F32)
            for v in stv:
                P.op("pool", lambda e, v=v: e.memset(v.t[:], 1.0), writes=[v.r])
            ctr = {"x": 0, "u": 0, "pT": 0, "pP": 0, "st4": 0, "stv": 0}

            def nxt(lst, key):
                v = lst[ctr[key] % len(lst)]
                ctr[key] += 1
                return v

            def rope_tok(x1, x2, cs, sn, o1, o2, tA, tB, rd, tmp_r, out_r):
                P.op("dve", lambda e: e.tensor_tensor(out=tA, in0=x1, in1=cs, op=ALU.mult), reads=rd, writes=[tmp_r])
                P.op("dve", lambda e: e.tensor_tensor(out=tB, in0=x2, in1=sn, op=ALU.mult), reads=rd, writes=[tmp_r])
                P.op("dve", lambda e: e.tensor_tensor(out=o1, in0=tA, in1=tB, op=ALU.subtract), reads=[tmp_r], writes=[out_r])
                P.op("dve", lambda e: e.tensor_tensor(out=tA, in0=x2, in1=cs, op=ALU.mult), reads=rd, writes=[tmp_r])
                P.op("dve", lambda e: e.tensor_tensor(out=tB, in0=x1, in1=sn, op=ALU.mult), reads=rd, writes=[tmp_r])
                P.op("dve", lambda e: e.tensor_tensor(out=o2, in0=tA, in1=tB, op=ALU.add), reads=[tmp_r], writes=[out_r])

            chunks = [("A", ci) for ci in range(NCH)] + [("B", gi) for gi in range(NG)]
            xtiles = {}

            def chunk_src(k):
                typ, i = chunks[k]
                return (xb if typ == "A" else xo), i

            def front_loads(k):
                src, i = chunk_src(k)
                xs = [nxt(xbuf, "x") for _ in range(4)]
                xtiles[k] = xs
                for t in range(4):
                    load("sp", xs[t].t[:], src[(4 * i + t) * 128:(4 * i + t + 1) * 128, :], [xs[t].r])

            def front_stats(k):
                xs = xtiles[k]
                s4 = ss4[k % 2]; r4 = rs4[k % 2]
                for t in range(4):
                    P.op("act", lambda e, t=t: e.activation(out=junk.t[:], in_=xs[t].t[:], func=AF.Square, accum_out=s4.t[:, t:t + 1]),
                         reads=[xs[t].r], writes=[junk.r, s4.r])
                rstd_from_ss(s4.t[:], r4.t[:], s4.r, r4.r, D)

            def front_norm_T(k):
                xs = xtiles[k]
                r4 = rs4[k % 2]
                uTt = uT[k % 2]
                for t in range(4):
                    ub = nxt(ubuf, "u"); pt = nxt(pT, "pT")
                    P.op("dve", lambda e, t=t, ub=ub: e.scalar_tensor_tensor(out=ub.t[:], in0=xs[t].t[:], scalar=r4.t[:, t:t + 1], in1=gm["mix"].t[:],
                                                                       op0=ALU.mult, op1=ALU.mult),
                         reads=[xs[t].r, r4.r, gm["mix"].r], writes=[ub.r])
                    for c in range(DC):
                        tr(pt.t[:, c, :], ub.t[:, c * 128:(c + 1) * 128], idb.t[:], [ub.r, idb.r], [pt.r])
                    copy_op("act", uTt.t[:, :, t * 128:(t + 1) * 128], pt.t[:], [pt.r], [uTt.r])

            def fm_proj(uTt, col0, dst4, scale=None):
                for pr in range(4):
                    p_ = nxt(pP, "pP")
                    for c in range(DC):
                        mm(p_.t[:, :], win.t[:, c, col0 + pr * 128:col0 + (pr + 1) * 128], uTt.t[:, c, :],
                           c == 0, c == DC - 1, [win.r, uTt.r], [p_.r])
                    copy_op(evac_eng(), dst4.t[:, pr, :], p_.t[:, :], [p_.r], [dst4.r], scale=scale)

            def proj_A1(k):
                ci = chunks[k][1]
                uTt = uT[k % 2]
                ks = nxt(st4, "st4")
                fm_proj(uTt, C_KSB, ks)
                load("sp", KTs[:, :, ci * 512:(ci + 1) * 512].rearrange("a p n -> p a n"), ks.t[:], [], reads=[ks.r])
                for t in range(4):
                    for c in range(DC):
                        mm(pC.t[:, t, 0:160], uTt.t[:, c, t * 128:(t + 1) * 128], win.t[:, c, C_CKV:C_CKV + 160],
                           c == 0, c == DC - 1, [win.r, uTt.r], [pC.r])
                s4 = ssc[k % 2]; r4 = rsc[k % 2]
                for t in range(4):
                    P.op("act", lambda e, t=t: e.activation(out=junk.t[:, 0:128], in_=pC.t[:, t, 0:128], func=AF.Square, accum_out=s4.t[:, t:t + 1]),
                         reads=[pC.r], writes=[junk.r, s4.r])
                rstd_from_ss(s4.t[:], r4.t[:], s4.r, r4.r, 128)

            def proj_A2(k):
                ci = chunks[k][1]
                uTt = uT[k % 2]
                r4 = rsc[k % 2]
                ckT = ckvT[k % 2]
                krt = krT[k % 2]
                cks = []
                for t in range(4):
                    kb = 4 * ci + t
                    ck = ckr[t]
                    tm = rtmp[t % 2]
                    cks.append(ck)
                    P.op("dve", lambda e, ck=ck, t=t: e.scalar_tensor_tensor(out=ck.t[:, 0:128], in0=pC.t[:, t, 0:128], scalar=r4.t[:, t:t + 1],
                                                                       in1=gm["kv"].t[:], op0=ALU.mult, op1=ALU.mult),
                         reads=[pC.r, r4.r, gm["kv"].r], writes=[ck.r])
                    rope_tok(pC.t[:, t, 128:144], pC.t[:, t, 144:160], cosb.t[:, kb, :], sinb.t[:, kb, :],
                             ck.t[:, 128:144], ck.t[:, 144:160], tm.t[:, 0, :], tm.t[:, 1, :],
                             [pC.r, cosb.r, sinb.r], tm.r, ck.r)
                vs_ = nxt(stv, "stv")
                for t in range(4):
                    p_ = nxt(pP, "pP")
                    for c in range(DC):
                        mm(p_.t[:, :], uTt.t[:, c, t * 128:(t + 1) * 128], win.t[:, c, C_VSB:C_VSB + 512],
                           c == 0, c == DC - 1, [win.r, uTt.r], [p_.r])
                    copy_op("act", vs_.t[:, :, t, 0:64], p_.t[:, :].rearrange("p (h d) -> p h d", d=64), [p_.r], [vs_.r])
                load("sp", Vs[:, :, 4 * ci:4 * ci + 4, :].rearrange("h p t e -> p h t e"), vs_.t[:], [], reads=[vs_.r])
                pt = nxt(pT, "pT")
                for t in range(4):
                    tr(pt.t[:, t, :], cks[t].t[:, 0:128], idb.t[:], [cks[t].r, idb.r], [pt.r])
                    tr(pt.t[0:32, 4 + t, :], cks[t].t[:, 128:160], idb.t[:], [cks[t].r, idb.r], [pt.r])
                copy_op("dve", ckT.t[:, :].rearrange("p (t n) -> p t n", n=128), pt.t[:, 0:4, :], [pt.r], [ckT.r])
                copy_op("dve", krt.t[:, :].rearrange("p (t n) -> p t n", n=128), pt.t[0:32, 4:8, :], [pt.r], [krt.r])
                load("sp", KR[:, ci * 512:(ci + 1) * 512], krt.t[:], [], reads=[krt.r])
                kn = nxt(st4, "st4")
                for pr in range(4):
                    p_ = nxt(pP, "pP")
                    mm(p_.t[:, :], wuk.t[:, pr * 128:(pr + 1) * 128], ckT.t[:, :], True, True, [wuk.r, ckT.r], [p_.r])
                    copy_op(evac_eng(), kn.t[:, pr, :], p_.t[:, :], [p_.r], [kn.r])
                load("sp", KTm[:, :, ci * 512:(ci + 1) * 512].rearrange("a p n -> p a n"), kn.t[:], [], reads=[kn.r])
                vm_ = nxt(stv, "stv")
                for t in range(4):
                    p_ = nxt(pP, "pP")
                    mm(p_.t[:, :], ckT.t[:, t * 128:(t + 1) * 128], wuv.t[:, :], True, True, [wuv.r, ckT.r], [p_.r])
                    copy_op(evac_eng(), vm_.t[:, :, t, 0:64], p_.t[:, :].rearrange("p (h d) -> p h d", d=64), [p_.r], [vm_.r])
                load("sp", Vm[:, :, 4 * ci:4 * ci + 4, :].rearrange("h p t e -> p h t e"), vm_.t[:], [], reads=[vm_.r])

            def proj_B1(k):
                gi = chunks[k][1]
                uTt = uT[k % 2]
                qs = nxt(st4, "st4")
                fm_proj(uTt, C_QSB, qs, scale=0.125)
                load("sp", QTs[:, :, gi * 512:(gi + 1) * 512].rearrange("a p n -> p a n"), qs.t[:], [], reads=[qs.r])
                for t in range(4):
                    for c in range(DC):
                        mm(pC.t[:, t, 0:256], uTt.t[:, c, t * 128:(t + 1) * 128], win.t[:, c, C_CQ:C_CQ + 256],
                           c == 0, c == DC - 1, [win.r, uTt.r], [pC.r])
                s4 = ssc[k % 2]; r4 = rsc[k % 2]
                for t in range(4):
                    P.op("act", lambda e, t=t: e.activation(out=junk.t[:, 0:256], in_=pC.t[:, t, 0:256], func=AF.Square, accum_out=s4.t[:, t:t + 1]),
                         reads=[pC.r], writes=[junk.r, s4.r])
                rstd_from_ss(s4.t[:], r4.t[:], s4.r, r4.r, 256)

            def proj_B2(k):
                gi = chunks[k][1]
                r4 = rsc[k % 2]
                cqt = cqT[k % 2]
                for t in range(4):
                    cq = cqn[t]
                    P.op("dve", lambda e, cq=cq, t=t: e.scalar_tensor_tensor(out=cq.t[:], in0=pC.t[:, t, 0:256], scalar=r4.t[:, t:t + 1],
                                                                       in1=gm["q"].t[:], op0=ALU.mult, op1=ALU.mult),
                         reads=[pC.r, r4.r, gm["q"].r], writes=[cq.r])
                pt = nxt(pT, "pT")
                for t in range(4):
                    for k2 in range(2):
                        tr(pt.t[:, 2 * t + k2, :], cqn[t].t[:, k2 * 128:(k2 + 1) * 128], idb.t[:], [cqn[t].r, idb.r], [pt.r])
                for k2 in range(2):
                    copy_op(evac_eng(), cqt.t[:, k2, :].rearrange("p (t n) -> p t n", n=128),
                            pt.t[:, :, :].rearrange("p (t k) n -> p t k n", k=2)[:, :, k2, :], [pt.r], [cqt.r])
                qm = qmst[k % 2]

                def qstage1(t):
                    pa = nxt(pP, "pP")
                    pb = nxt(pP, "pP")
                    qf = qfb[t % 2]
                    for k2 in range(2):
                        mm(pa.t[:, 0:512], cqt.t[:, k2, t * 128:(t + 1) * 128], wuq.t[:, k2 * 768:k2 * 768 + 512],
                           k2 == 0, k2 == 1, [wuq.r, cqt.r], [pa.r])
                    for k2 in range(2):
                        mm(pb.t[:, 0:256], cqt.t[:, k2, t * 128:(t + 1) * 128], wuq.t[:, k2 * 768 + 512:k2 * 768 + 768],
                           k2 == 0, k2 == 1, [wuq.r, cqt.r], [pb.r])
                    copy_op("act", qf.t[:, 0:512], pa.t[:, 0:512], [pa.r], [qf.r])
                    copy_op("dve", qf.t[:, 512:768], pb.t[:, 0:256], [pb.r], [qf.r])

                def qstage2(t):
                    pos = 4 * gi + t
                    qt = qtok[t % 2]
                    qr = qrt[t % 2]
                    qf = qfb[t % 2]
                    q3 = qf.t[:, 0:768].rearrange("p (h d) -> p h d", d=96)
                    copy_op("act", qt.t[:, :, 0:64], q3[:, :, 0:64], [qf.r], [qt.r])
                    rope_tok(q3[:, :, 64:80], q3[:, :, 80:96], coso.t[:, pos, :].unsqueeze(1).broadcast_to([128, H, 16]),
                             sino.t[:, pos, :].unsqueeze(1).broadcast_to([128, H, 16]),
                             qt.t[:, :, 64:80], qt.t[:, :, 80:96], qr.t[:, 0, :, :], qr.t[:, 1, :, :],
                             [qf.r, coso.r, sino.r], qr.r, qt.r)
                    pt = nxt(pT, "pT")
                    for h in range(H):
                        tr(pt.t[0:96, h, :], qt.t[:, h, :], idb.t[:], [qt.r, idb.r], [pt.r])
                    copy_op(evac_eng(), qm.t[0:96, :, t * 128:(t + 1) * 128], pt.t[0:96, :, :], [pt.r], [qm.r])

                qstage1(0)
                for t in range(4):
                    if t + 1 < 4:
                        qstage1(t + 1)
                    qstage2(t)
                load("sp", QTm[:, :, gi * 512:(gi + 1) * 512].rearrange("h r n -> r h n"), qm.t[0:96, :, :], [], reads=[qm.r])

            NK = len(chunks)
            front_loads(0)
            if NK > 1:
                front_loads(1)
            front_stats(0)
            front_norm_T(0)
            for k in range(NK):
                if k + 2 < NK:
                    front_loads(k + 2)
                if k + 1 < NK:
                    front_stats(k + 1)
                (proj_A1 if chunks[k][0] == "A" else proj_B1)(k)
                if k + 1 < NK:
                    front_norm_T(k + 1)
                (proj_A2 if chunks[k][0] == "A" else proj_B2)(k)
            P.end_phase()
            es.__exit__(None, None, None)
        if stop_after == "AB":
            return nc, dict(KTm=KTm, KR=KR, Vm=Vm, KTs=KTs, Vs=Vs, QTm=QTm, QTs=QTs)

        merged = sb(g, "merged", [128, NBO, D], F32)
        mres = [Res("m") for _ in range(NBO)]
        with ExitStack() as es:
            trib = sb(es, "trib", [128, 2, 128], BF16)
            maskb = sb(es, "maskb", [128, 16 * 128], BF16)
            mst = sb(es, "mst", [128, 2048], F32)
            mst2 = sb(es, "mst2", [128, 256], F32)
            load("sp", mst.t[:, 0:2048], mask_d[:, :], [mst.r])
            P.op("pool", lambda e: e.tensor_copy(out=maskb.t[:], in_=mst.t[:, 0:2048]), reads=[mst.r], writes=[maskb.r])
            load("sp", mst2.t[:, 0:256], tri_d.rearrange("p a b -> p (a b)"), [mst2.r])
            P.op("pool", lambda e: e.tensor_copy(out=trib.t[:].rearrange("p a b -> p (a b)"), in_=mst2.t[:, 0:256]),
                 reads=[mst2.r], writes=[trib.r])
            Kb = [sb(es, "Kb%d" % i, [128, S], BF16) for i in range(2)]
            Vb = [sb(es, "Vb%d" % i, [128, NB, 65], BF16) for i in range(2)]
            Qb = [sb(es, "Qb%d" % i, [128, SO], BF16) for i in range(2)]
            eb2 = [sb(es, "eb2_%d" % i, [128, 2, 512], F32) for i in range(4)]
            lb2 = [sb(es, "lb2_%d" % i, [128, 2, 512], BF16) for i in range(2)]
            gb2 = [sb(es, "gb2_%d" % i, [128, 2, 512], F32) for i in range(2)]
            ab2 = [sb(es, "ab2_%d" % i, [128, 2, 512], BF16) for i in range(2)]
            ab = [sb(es, "ab%d" % i, [128, 512], BF16) for i in range(2)]
            ot = [sb(es, "ot%d" % i, [128, 512], F32) for i in range(2)]
            ot2 = sb(es, "ot2", [128, 4, 128], F32)
            rinv = [sb(es, "rinv%d" % i, [128, 4, 1], F32) for i in range(2)]
            zb2 = [ps(es, "zb2_%d" % i, [128, 2, 512], F32) for i in range(2)]
            zres = [[Res("z", True), Res("z", True)] for _ in range(2)]
            cb = ps(es, "cb", [128, 512], F32)
            oacc = [ps(es, "oacc%d" % i, [128, 512], F32) for i in range(2)]
            tp = ps(es, "tp", [128, 4, 128], F32)
            mk4 = maskb.t[:].rearrange("p (a b c q) -> p a b c q", a=2, b=2, c=4)
            halfpos = NBO // 2

            jobs = [("m", h) for h in range(H)] + [("s", h) for h in range(H)]

            def job_loads(k):
                typ, h = jobs[k]
                sl = k % 2
                Kt, Vt, Qt = Kb[sl], Vb[sl], Qb[sl]
                r0 = (h % 2) * 64
                if typ == "m":
                    load("sp", Kt.t[0:64, :], KTm[h // 2, r0:r0 + 64, :], [Kt.r])
                    load("sp", Kt.t[64:96, :], KR[:, :], [Kt.r])
                    load("sp", Vt.t[:], Vm[h], [Vt.r])
                    load("sp", Qt.t[0:96, :], QTm[h], [Qt.r])
                else:
                    load("sp", Kt.t[0:64, :], KTs[h // 2, r0:r0 + 64, :], [Kt.r])
                    load("sp", Kt.t[64:128, :], KTs[h // 2, r0:r0 + 64, :], [Kt.r])
                    load("sp", Vt.t[:], Vs[h], [Vt.r])
                    load("sp", Qt.t[0:64, :], QTs[h // 2, r0:r0 + 64, :], [Qt.r])
                    load("sp", Qt.t[64:128, :], QTs[h // 2, r0:r0 + 64, :], [Qt.r])

            gctr = [0]

            def tiles_for(gi):
                out = []
                for kb in range(16 * gi + 15, -1, -1):
                    pm = kb // 4
                    if pm >= 4 * gi:
                        lo = pm - 4 * gi
                        out.append((kb, lo * 128, True, 0 if pm < halfpos else 1, kb % 4))
                    else:
                        out.append((kb, 0, False, 0, 0))
                return out

            def finish_group(typ, h, gi, oa):
                o_ = ot[gctr[0] % 2]
                rv = rinv[gctr[0] % 2]
                nr = 65 if typ == "m" else 64
                copy_op("dve", o_.t[0:nr, :], oa.t[0:nr, :], [oa.r], [o_.r])
                for t in range(4):
                    tr(tp.t[:, t, 0:nr], o_.t[0:nr, t * 128:(t + 1) * 128], idf.t[0:nr, 0:nr], [o_.r, idf.r], [tp.r])
                if typ == "m":
                    P.op("dve", lambda e: e.reciprocal(out=rv.t[:], in_=tp.t[:, :, 64:65]), reads=[tp.r], writes=[rv.r])
                    for t in range(4):
                        P.op("dve", lambda e, t=t: e.tensor_scalar(out=merged.t[:, 4 * gi + t, h * 64:(h + 1) * 64], in0=tp.t[:, t, 0:64],
                                                                   scalar1=rv.t[:, t, :], scalar2=None, op0=ALU.mult),
                             reads=[tp.r, rv.r], writes=[mres[4 * gi + t]])
                else:
                    P.op("dve", lambda e: e.tensor_copy(out=merged.t[:, 4 * gi:4 * gi + 4, 512 + h * 64:512 + (h + 1) * 64], in_=tp.t[:, :, 0:64]),
                         reads=[tp.r], writes=[mres[4 * gi + t_] for t_ in range(4)])

            def zero_bank(bank, Qt):
                mm(bank.t[:, :], zerob.t[:, :], maskb.t[:, 0:512], True, True, [zerob.r, maskb.r], [bank.r])

            def job_items(ks):
                items = []
                g = gctr[0]
                for k in ks:
                    typ, h = jobs[k]
                    sl = k % 2
                    for gi in range(NG):
                        tl = tiles_for(gi)
                        npair = len(tl) // 2
                        for p in range(npair):
                            ta, tb_ = tl[2 * p], tl[2 * p + 1]
                            assert ta[1] == tb_[1] and ta[2] == tb_[2] and ta[3] == tb_[3]
                            items.append(dict(k=k, h=h, gi=gi, ta=ta, tb=tb_, first=(p == 0), last=(p == npair - 1), oa=oacc[g % 2],
                                              jobfirst=(gi == 0 and p == 0), Kt=Kb[sl], Vt=Vb[sl], Qt=Qb[sl]))
                        g += 1
                return items

            def run_mla(ks):
                items = job_items(ks)
                n = len(items)

                def S2(i):
                    it = items[i]; gi = it["gi"]; Kt = it["Kt"]; Qt = it["Qt"]
                    (kba, c0, msk, hf, ra), (kbb, _, _, _, rb) = it["ta"], it["tb"]
                    zt = zb2[i % 2].t; zr = zres[i % 2]
                    q0 = gi * 512 + c0
                    for j, kb_, r_ in ((0, kba, ra), (1, kbb, rb)):
                        mm(zt[:, j, c0:512], Kt.t[0:96, kb_ * 128:(kb_ + 1) * 128], Qt.t[0:96, q0:(gi + 1) * 512],
                           True, not msk, [Kt.r, Qt.r], [zr[j]])
                        if msk:
                            mm(zt[:, j, c0:c0 + 128], idb.t[:, :], mk4[:, 0, hf, r_, :], False, True, [idb.r, maskb.r], [zr[j]])

                def P2(i):
                    it = items[i]; oa = it["oa"]; Vt = it["Vt"]; Qt = it["Qt"]
                    (kba, c0, msk, hf, ra), (kbb, _, _, _, rb) = it["ta"], it["tb"]
                    zt = zb2[i % 2].t; zr = zres[i % 2]
                    a_ = ab2[i % 2]
                    o_ap = a_.t[:, :, c0:512]; i_ap = zt[:, :, c0:512]
                    P.op("act", lambda e, o_ap=o_ap, i_ap=i_ap: e.activation(out=o_ap, in_=i_ap, func=AF.Exp, scale=MLA_SCALE),
                         reads=[zr[0], zr[1]], writes=[a_.r])
                    if it["first"]:
                        zero_bank(oa, Qt)
                    mm(oa.t[0:65, c0:512], Vt.t[:, kba, 0:65], a_.t[:, 0, c0:512], False, False, [Vt.r, a_.r], [oa.r], skip=True)
                    mm(oa.t[0:65, c0:512], Vt.t[:, kbb, 0:65], a_.t[:, 1, c0:512], False, False, [Vt.r, a_.r], [oa.r], skip=True)
                    if it["jobfirst"] and it["k"] + 1 < len(jobs):
                        job_loads(it["k"] + 1)

                S2(0)
                for i in range(n):
                    if i + 1 < n:
                        S2(i + 1)
                    P2(i)
                    if items[i]["last"]:
                        finish_group("m", items[i]["h"], items[i]["gi"], items[i]["oa"])
                        gctr[0] += 1

            def finish_sb(h, gi, oa):
                o_ = ot[gctr[0] % 2]
                copy_op("dve", o_.t[:, :], oa.t[:, :], [oa.r], [o_.r])
                for t in range(4):
                    tr(tp.t[:, t, :], o_.t[:, t * 128:(t + 1) * 128], idf.t[:, :], [o_.r, idf.r], [tp.r])
                copy_op("dve", ot2.t[:], tp.t[:], [tp.r], [ot2.r])
                P.op("dve", lambda e: e.tensor_tensor(out=merged.t[:, 4 * gi:4 * gi + 4, 512 + h * 64:512 + (h + 1) * 64],
                                                      in0=ot2.t[:, :, 0:64], in1=ot2.t[:, :, 64:128], op=ALU.add),
                     reads=[ot2.r], writes=[mres[4 * gi + t_] for t_ in range(4)])

            def run_sb(ks):
                items = job_items(ks)
                n = len(items)

                def Zp(i):
                    it = items[i]; gi = it["gi"]; Kt = it["Kt"]; Qt = it["Qt"]
                    (kba, c0, msk, hf, ra), (kbb, _, _, _, rb) = it["ta"], it["tb"]
                    zt = zb2[i % 2].t; zr = zres[i % 2]
                    q0 = gi * 512 + c0
                    mm(zt[:, 0, c0:512], Kt.t[0:64, kba * 128:(kba + 1) * 128], Qt.t[0:64, q0:(gi + 1) * 512],
                       True, not msk, [Kt.r, Qt.r], [zr[0]])
                    mm(zt[:, 1, c0:512], Kt.t[64:128, kbb * 128:(kbb + 1) * 128], Qt.t[64:128, q0:(gi + 1) * 512],
                       True, not msk, [Kt.r, Qt.r], [zr[1]])
                    if msk:
                        mm(zt[:, 0, c0:c0 + 128], idb.t[:, :], mk4[:, 1, hf, ra, :], False, True, [idb.r, maskb.r], [zr[0]])
                        mm(zt[:, 1, c0:c0 + 128], idb.t[:, :], mk4[:, 1, hf, rb, :], False, True, [idb.r, maskb.r], [zr[1]])

                def Ep(i):
                    c0 = items[i]["ta"][1]
                    zt = zb2[i % 2].t; zr = zres[i % 2]; e_ = eb2[i % 4]
                    o_ap = e_.t[:, :, c0:512]; i_ap = zt[:, :, c0:512]
                    P.op("act", lambda e, o_ap=o_ap, i_ap=i_ap: e.activation(out=o_ap, in_=i_ap, func=AF.Exp),
                         reads=[zr[0], zr[1]], writes=[e_.r])

                def Lp(i):
                    c0 = items[i]["ta"][1]
                    e_ = eb2[i % 4]; l_ = lb2[i % 2]
                    o_ap = l_.t[:, :, c0:512]; i_ap = e_.t[:, :, c0:512]
                    P.op("act", lambda e, o_ap=o_ap, i_ap=i_ap: e.activation(out=o_ap, in_=i_ap, func=AF.Ln, bias=onec.t[:]),
                         reads=[e_.r, onec.r], writes=[l_.r])

                def Tri(i, j):
                    c0 = items[i]["ta"][1]
                    l_ = lb2[i % 2]
                    mm(cb.t[:, c0:512], trib.t[:, 0, :], l_.t[:, j, c0:512], False, False, [trib.r, l_.r], [cb.r], skip=True)

                def G(i, j):
                    c0 = items[i]["ta"][1]
                    g_ = gb2[i % 2]
                    o_ap = g_.t[:, j, c0:512]; i_ap = cb.t[:, c0:512]
                    P.op("act", lambda e, o_ap=o_ap, i_ap=i_ap: e.activation(out=o_ap, in_=i_ap, func=AF.Exp, scale=-1.0),
                         reads=[cb.r], writes=[g_.r])

                def OmT(i, j):
                    c0 = items[i]["ta"][1]
                    l_ = lb2[i % 2]
                    mm(cb.t[:, c0:512], trib.t[:, 1, :], l_.t[:, j, c0:512], False, False, [trib.r, l_.r], [cb.r], skip=True)

                def Amul(i):
                    c0 = items[i]["ta"][1]
                    g_ = gb2[i % 2]; a_ = ab2[i % 2]; e_ = eb2[i % 4]
                    o_ap = a_.t[:, :, c0:512]; x_ap = e_.t[:, :, c0:512]; y_ap = g_.t[:, :, c0:512]
                    P.op("dve", lambda e, o_ap=o_ap, x_ap=x_ap, y_ap=y_ap: e.tensor_tensor(out=o_ap, in0=x_ap, in1=y_ap, op=ALU.mult),
                         reads=[e_.r, g_.r], writes=[a_.r])

                def AVp(i):
                    it = items[i]; oa = it["oa"]; Vt = it["Vt"]
                    kba, c0 = it["ta"][0], it["ta"][1]
                    kbb = it["tb"][0]
                    a_ = ab2[i % 2]
                    mm(oa.t[0:64, c0:512], Vt.t[:, kba, 0:64], a_.t[:, 0, c0:512], False, False, [Vt.r, a_.r], [oa.r], skip=True)
                    o_ap = oa.t[64:128, c0:512]; l_ap = Vt.t[:, kbb, 0:64]; r_ap = a_.t[:, 1, c0:512]
                    P.op("pe", lambda e, o_ap=o_ap, l_ap=l_ap, r_ap=r_ap: e.matmul(o_ap, lhsT=l_ap, rhs=r_ap, start=False, stop=False,
                                                                                 skip_group_check=True, tile_position=(0, 64)),
                         reads=[Vt.r, a_.r], writes=[oa.r])
                    if it["jobfirst"] and it["k"] + 1 < len(jobs):
                        job_loads(it["k"] + 1)

                Zp(0)
                if n > 1:
                    Zp(1)
                Ep(0)
                if n > 2:
                    Zp(2)
                if n > 1:
                    Ep(1)
                Lp(0)
                for st_ in range(n + 1):
                    if 0 <= st_ - 1 < n:
                        OmT(st_ - 1, 1)
                    if st_ < n:
                        if items[st_]["first"]:
                            zero_bank(cb, None)
                        Tri(st_, 0)
                        G(st_, 0)
                    if st_ + 3 < n:
                        Zp(st_ + 3)
                    if st_ + 1 < n:
                        Lp(st_ + 1)
                    if 0 <= st_ - 1 < n:
                        Amul(st_ - 1)
                    if st_ < n:
                        OmT(st_, 0)
                        Tri(st_, 1)
                        G(st_, 1)
                    if 0 <= st_ - 1 < n:
                        it = items[st_ - 1]
                        if it["first"]:
                            zero_bank(it["oa"], None)
                        AVp(st_ - 1)
                        if it["last"]:
                            finish_sb(it["h"], it["gi"], it["oa"])
                            gctr[0] += 1
                    if st_ + 2 < n:
                        Ep(st_ + 2)

            job_loads(0)
            run_mla([k for k in range(len(jobs)) if jobs[k][0] == "m"])
            run_sb([k for k in range(len(jobs)) if jobs[k][0] == "s"])
            P.end_phase()
        if stop_after == "C":
            return nc, dict(merged=merged)

        with ExitStack() as es:
            fT = sb(es, "fT", [128, DC, SO], BF16)
            load_gains(es, ("ffn", "fin"))
            g_o = sb(es, "g_o_sb", [128, DC], F32)
            load("sp", g_o.t[:], g_o_d[:, :], [g_o.r])
            sst = sb(es, "sst", [128, NBO, 2], F32)
            rst = sb(es, "rst", [128, NBO, 2], F32)
            junk = sb(es, "junkd", [128, D], F32)
            wg = [sb(es, "wg0", [128, DC, 512], BF16), None]
            wu = [sb(es, "wu0", [128, DC, 512], BF16), None]
            wd = [sb(es, "wd0", [128, 4, D], BF16), None]
            wstf = [sb(es, "wstf%d" % i, [128, 1024], F32) for i in range(2)]
            NFG = (FC + 3) // 4
            wctr = [0]

            def stage_cast(dst_ap, src_ap, n, dst_r):
                i = wctr[0]; wctr[0] += 1
                st = wstf[i % 2]
                load("sp", st.t[:, 0:n], src_ap, [st.r])
                copy_op("pool", dst_ap, st.t[:, 0:n], [st.r], [dst_r])

            def ffn_load_list(fg):
                f0 = fg * 512
                nf = min(512, DFF - f0)
                sl = fg % 2
                lst = []
                for c in range(DC):
                    lst.append((wg[sl].t[:, c, 0:nf], w_g_d[:, c, f0:f0 + nf], nf, wg[sl].r))
                    lst.append((wu[sl].t[:, c, 0:nf], w_u_d[:, c, f0:f0 + nf], nf, wu[sl].r))
                for fc in range(nf // 128):
                    lst.append((wd[sl].t[:, fc, :], w_d_d[:, fg * 4 + fc, :], D, wd[sl].r))
                return lst

            def ffn_loads(fg):
                for a_ in ffn_load_list(fg):
                    stage_cast(*a_)

            with ExitStack() as e1:
                wo = sb(e1, "wo", [128, DC, D], BF16)
                mn = [sb(e1, "mn%d" % i, [128, D], BF16) for i in range(3)]
                mT = [sb(e1, "mT%d" % i, [128, DC, 128], BF16) for i in range(3)]
                xs2 = [sb(e1, "xs2_%d" % i, [128, D], F32) for i in range(3)]
                pT = [ps(e1, "pTd%d" % i, [128, DC, 128], BF16) for i in range(3)]
                pO = [ps(e1, "pO%d" % i, [128, 1024], F32) for i in range(2)]
                for c in range(DC):
                    st = wstf[c % 2]
                    load("sp", st.t[:, 0:D], w_o_d[:, c, :], [st.r])
                    P.op("act", lambda e, st=st, c=c: e.activation(out=wo.t[:, c, :], in_=st.t[:, 0:D], func=AF.Copy, scale=g_o.t[:, c:c + 1]),
                         reads=[st.r, g_o.r], writes=[wo.r])
                for t in range(NBO):
                    for hf in range(2):
                        P.op("act", lambda e, t=t, hf=hf: e.activation(out=junk.t[:, 0:512], in_=merged.t[:, t, hf * 512:(hf + 1) * 512],
                                                                       func=AF.Square, accum_out=sst.t[:, t, hf:hf + 1]),
                             reads=[mres[t]], writes=[junk.r, sst.r])
                rstd_from_ss(sst.t[:], rst.t[:], sst.r, rst.r, 512)
                pre0 = ffn_load_list(0)

                def prep(t):
                    m_ = mn[t % 3]; mt = mT[t % 3]; pt = pT[t % 3]; xs = xs2[t % 3]
                    load("sp", xs.t[:], xo[t * 128:(t + 1) * 128, :], [xs.r])
                    for hf in range(2):
                        P.op("act", lambda e, t=t, hf=hf, m_=m_: e.activation(
                            out=m_.t[:, hf * 512:(hf + 1) * 512], in_=merged.t[:, t, hf * 512:(hf + 1) * 512],
                            func=AF.Copy, scale=rst.t[:, t, hf:hf + 1]),
                            reads=[mres[t], rst.r], writes=[m_.r])
                    for c in range(DC):
                        tr(pt.t[:, c, :], m_.t[:, c * 128:(c + 1) * 128], idb.t[:], [m_.r, idb.r], [pt.r])
                    copy_op("act", mt.t[:], pt.t[:], [pt.r], [mt.r])

                def fin(t):
                    mt = mT[t % 3]; po = pO[t % 2]; xs = xs2[t % 3]
                    for nh in range(2):
                        for c in range(DC):
                            mm(po.t[:, nh * 512:(nh + 1) * 512], mt.t[:, c, :], wo.t[:, c, nh * 512:(nh + 1) * 512],
                               c == 0, c == DC - 1, [mt.r, wo.r], [po.r])
                    P.op("dve", lambda e, t=t, po=po, xs=xs: e.tensor_tensor(out=merged.t[:, t, :], in0=po.t[:, :], in1=xs.t[:], op=ALU.add),
                         reads=[po.r, xs.r], writes=[mres[t]])

                def fin_sq(t):
                    P.op("act", lambda e, t=t: e.activation(out=junk.t[:], in_=merged.t[:, t, :], func=AF.Square, accum_out=sst.t[:, t, 0:1]),
                         reads=[mres[t]], writes=[junk.r, sst.r])

                prep(0)
                if NBO > 1:
                    prep(1)
                for t in range(NBO):
                    if t + 2 < NBO:
                        prep(t + 2)
                    for _ in range(2):
                        if pre0:
                            stage_cast(*pre0.pop(0))
                    fin(t)
                    if t >= 1:
                        fin_sq(t - 1)
                fin_sq(NBO - 1)
                while pre0:
                    stage_cast(*pre0.pop(0))
                rstd_from_ss(sst.t[:], rst.t[:], sst.r, rst.r, D)
                for t in range(NBO):
                    m_ = mn[t % 2]; pt = pT[t % 2]
                    P.op("dve", lambda e, t=t, m_=m_: e.scalar_tensor_tensor(out=m_.t[:], in0=merged.t[:, t, :], scalar=rst.t[:, t, 0:1],
                                                                       in1=gm["ffn"].t[:], op0=ALU.mult, op1=ALU.mult),
                         reads=[mres[t], rst.r, gm["ffn"].r], writes=[m_.r])
                    for c in range(DC):
                        tr(pt.t[:, c, :], m_.t[:, c * 128:(c + 1) * 128], idb.t[:], [m_.r, idb.r], [pt.r])
                    copy_op("act" if t % 2 == 0 else "dve", fT.t[:, :, t * 128:(t + 1) * 128], pt.t[:], [pt.r], [fT.r])
                P.end_phase()

            with ExitStack() as e2:
                wg[1] = sb(e2, "wg1", [128, DC, 512], BF16)
                wu[1] = sb(e2, "wu1", [128, DC, 512], BF16)
                wd[1] = sb(e2, "wd1", [128, 4, D], BF16)
                aT = [sb(e2, "aT%d" % i, [128, 4, 512], BF16) for i in range(2)]
                sg = [sb(e2, "sg%d" % i, [128, 512], F32) for i in range(2)]
                yt = [sb(e2, "yt%d" % i, [128, D], F32) for i in range(3)]
                pg = [ps(e2, "pg%d" % i, [128, 512], F32) for i in range(2)]
                pu = [ps(e2, "pu%d" % i, [128, 512], F32) for i in range(2)]
                pd = [ps(e2, "pd%d" % i, [128, 1024], F32) for i in range(2)]
                k = 0
                pend = None

                def down(k_, sl_, nfc_, tg_):
                    a_ = aT[k_ % 2]
                    for tt in range(4):
                        t = tg_ * 4 + tt
                        pd_ = pd[(k_ * 4 + tt) % 2]
                        for nh in range(2):
                            for fc in range(nfc_):
                                mm(pd_.t[:, nh * 512:(nh + 1) * 512], a_.t[:, fc, tt * 128:(tt + 1) * 128], wd[sl_].t[:, fc, nh * 512:(nh + 1) * 512],
                                   fc == 0, fc == nfc_ - 1, [a_.r, wd[sl_].r], [pd_.r])
                        P.op("dve", lambda e, t=t, pd_=pd_: e.tensor_tensor(out=merged.t[:, t, :], in0=pd_.t[:, :], in1=merged.t[:, t, :], op=ALU.add),
                             reads=[pd_.r, mres[t]], writes=[mres[t]])

                if NFG > 1:
                    ffn_loads(1)
                for fg in range(NFG):
                    f0 = fg * 512
                    nfc = min(512, DFF - f0) // 128
                    sl = fg % 2
                    for tg in range(NG):
                        a_ = aT[k % 2]
                        for fc in range(nfc):
                            pg_ = pg[(k * 4 + fc) % 2]; pu_ = pu[(k * 4 + fc) % 2]; sg_ = sg[(k * 4 + fc) % 2]
                            for c in range(DC):
                                mm(pg_.t[:, :], wg[sl].t[:, c, fc * 128:(fc + 1) * 128], fT.t[:, c, tg * 512:(tg + 1) * 512],
                                   c == 0, c == DC - 1, [wg[sl].r, fT.r], [pg_.r])
                            for c in range(DC):
                                mm(pu_.t[:, :], wu[sl].t[:, c, fc * 128:(fc + 1) * 128], fT.t[:, c, tg * 512:(tg + 1) * 512],
                                   c == 0, c == DC - 1, [wu[sl].r, fT.r], [pu_.r])
                            P.op("act", lambda e, pg_=pg_, sg_=sg_: e.activation(out=sg_.t[:], in_=pg_.t[:, :], func=AF.Silu),
                                 reads=[pg_.r], writes=[sg_.r])
                            P.op("dve", lambda e, pu_=pu_, sg_=sg_, a_=a_, fc=fc: e.tensor_tensor(out=a_.t[:, fc, :], in0=pu_.t[:, :], in1=sg_.t[:],
                                                                                             op=ALU.mult),
                                 reads=[pu_.r, sg_.r], writes=[a_.r])
                        if pend is not None:
                            down(*pend)
                        pend = (k, sl, nfc, tg)
                        k += 1
                        if tg == 0 and fg >= 1 and fg + 1 < NFG:
                            ffn_loads(fg + 1)
                down(*pend)
                for t in range(NBO):
                    P.op("act", lambda e, t=t: e.activation(out=junk.t[:], in_=merged.t[:, t, :], func=AF.Square, accum_out=sst.t[:, t, 0:1]),
                         reads=[mres[t]], writes=[junk.r, sst.r])
                rstd_from_ss(sst.t[:], rst.t[:], sst.r, rst.r, D)
                for t in range(NBO):
                    y_ = yt[t % 3]
                    P.op("dve", lambda e, t=t, y_=y_: e.scalar_tensor_tensor(out=y_.t[:], in0=merged.t[:, t, :], scalar=rst.t[:, t, 0:1],
                                                                       in1=gm["fin"].t[:], op0=ALU.mult, op1=ALU.mult),
                         reads=[mres[t], rst.r, gm["fin"].r], writes=[y_.r])
                    load("sp", y[t * 128:(t + 1) * 128, :], y_.t[:], [], reads=[y_.r])
                P.end_phase()
    return nc, {}


def host_inputs(inputs, S):
    NB = S // 128
    f = np.float32
    x = np.asarray(inputs["x"], f)
    pos = np.asarray(inputs["positions"], np.int32)

    def kchunk(w):
        K, E = w.shape
        return np.ascontiguousarray(w.reshape(K // 128, 128, E).transpose(1, 0, 2))

    def rep(v):
        return np.ascontiguousarray(np.broadcast_to(np.asarray(v, f).reshape(1, -1), (128, v.size)))

    w_in = kchunk(np.asarray(inputs["w_in"], f)[0])
    w_uq = kchunk(np.asarray(inputs["w_uq"], f)[0]).reshape(128, 2 * 768)
    wukv = np.asarray(inputs["w_ukv"], f)[0].reshape(128, H, 128)
    w_uk = np.ascontiguousarray(wukv[:, :, 0:64].reshape(128, 512))
    w_uv = np.ascontiguousarray(wukv[:, :, 64:128].reshape(128, 512))
    w_o = kchunk(np.asarray(inputs["w_o"], f)[0])
    w_g = kchunk(np.asarray(inputs["w_gate"], f)[0])
    w_u = kchunk(np.asarray(inputs["w_up"], f)[0])
    w_d = kchunk(np.asarray(inputs["w_down"], f)[0])
    ident = np.eye(128, dtype=f)
    jj = np.arange(128)[:, None]
    ss_ = np.arange(128)[None, :]
    tri = np.stack([(jj >= ss_).astype(f), (jj < ss_).astype(f)], axis=1)
    causal = (jj <= ss_).astype(f)
    strict = (jj < ss_).astype(f)
    common = dict(ident=ident, tri=np.ascontiguousarray(tri), w_in=w_in, w_uq=w_uq, w_uk=w_uk, w_uv=w_uv, w_o=w_o,
                  w_gate=w_g, w_up=w_u, w_down=w_d,
                  g_mix=rep(inputs["norm_mix"][0]), g_q=rep(inputs["q_latent_norm"][0]), g_kv=rep(inputs["kv_latent_norm"][0]),
                  g_mla=rep(inputs["out_norm_mla"][0]), g_sb=rep(inputs["out_norm_sb"][0]), g_ffn=rep(inputs["norm_ffn"][0]),
                  g_fin=rep(inputs["norm_final"]),
                  g_o=np.ascontiguousarray(np.concatenate([np.asarray(inputs["out_norm_mla"], f)[0],
                                                           np.asarray(inputs["out_norm_sb"], f)[0]]).reshape(DC, 128).T))
    maps = []
    for c in range(8):
        b, j = c // 4, c % 4
        ob = own_blocks(j, NB)
        rows = np.concatenate([np.arange(k * 128, (k + 1) * 128) for k in ob])
        masks = np.zeros((128, 2, 2, 4, 128), f)
        for ti, tm in enumerate((causal, strict)):
            for hf, off in enumerate((j, 3 - j)):
                for r in range(4):
                    if r < off:
                        masks[:, ti, hf, r, :] = 0.0
                    elif r == off:
                        masks[:, ti, hf, r, :] = NEG * (1.0 - tm)
                    else:
                        masks[:, ti, hf, r, :] = NEG
        m = dict(common)
        m["xb"] = np.ascontiguousarray(x[b])
        m["xo"] = np.ascontiguousarray(x[b][rows])
        m["posb"] = np.ascontiguousarray(pos[b].reshape(NB, 128).T)
        m["poso"] = np.ascontiguousarray(pos[b][rows].reshape(len(ob), 128).T)
        m["masks"] = masks.reshape(128, 16 * 128)
        maps.append(m)
    return maps


def assemble(results, S, B=2):
    NB = S // 128
    out = np.zeros((B, S, D), np.float32)
    for c in range(8):
        b, j = c // 4, c % 4
        ob = own_blocks(j, NB)
        yy = np.asarray(results[c]["y"], np.float32)
        for i, k in enumerate(ob):
            out[b, k * 128:(k + 1) * 128, :] = yy[i * 128:(i + 1) * 128, :]
    return out


_NC_CACHE = {}


def kernel(**inputs):
    S = int(np.asarray(inputs["x"]).shape[1])
    if S not in _NC_CACHE:
        _NC_CACHE[S] = build(S, Prog, Res)[0]
    nc = _NC_CACHE[S]
    maps = host_inputs(inputs, S)
    res = run_bass_kernel_spmd(nc, maps, core_ids=list(range(8)))
    return assemble(res.results, S, B=int(np.asarray(inputs["x"]).shape[0]))
```

```python
import math
from contextlib import ExitStack
import numpy as np
import concourse.bass as bass
import concourse.mybir as mybir
from concourse.bass_utils import run_bass_kernel_spmd

F32 = mybir.dt.float32
BF16 = mybir.dt.bfloat16
I32 = mybir.dt.int32
AF = mybir.ActivationFunctionType
ALU = mybir.AluOpType


ENGS = ("pe", "act", "dve", "pool", "sp")


class Res:
    __slots__ = ("name", "w", "r", "excl")

    def __init__(self, name, excl=False):
        self.name = name
        self.w = None
        self.r = []
        self.excl = excl


class Op:
    __slots__ = ("fn", "waits", "sig", "dma", "idx")

    def __init__(self, fn, waits, dma, idx):
        self.fn = fn
        self.waits = waits
        self.sig = False
        self.dma = dma
        self.idx = idx


class Prog:
    NDMA = 40

    def __init__(self, nc):
        self.nc = nc
        self.sem = {e: nc.alloc_semaphore("s_" + e) for e in ENGS}
        self.dsem = [nc.alloc_semaphore("d%d" % i) for i in range(self.NDMA)]
        self.dcount = [0] * self.NDMA
        self.dnext = 0
        self.ops = {e: [] for e in ENGS}
        self.start = {e: 0 for e in ENGS}
        self.base = {e: 0 for e in ENGS}
        self.seen = {e: {x: 0 for x in ENGS} for e in ENGS}
        self.seen_d = {e: [0] * self.NDMA for e in ENGS}

    def _deps(self, reads, writes):
        ev = []
        for r in reads:
            if r.excl:
                writes = list(writes) + [r]
                continue
            if r.w is not None:
                ev.append(r.w)
        for w in writes:
            if w.w is not None:
                ev.append(w.w)
            ev.extend(w.r)
        return ev

    def _filter(self, eng, evs):
        out = []
        best = {}
        for e in evs:
            if e[0] == "c":
                _, x, i = e
                if x == "pe" and eng == "pe":
                    continue
                if i <= self.seen[eng][x]:
                    continue
                if i > best.get(("c", x), 0):
                    best[("c", x)] = i
            else:
                _, s, c = e
                if c <= self.seen_d[eng][s]:
                    continue
                if c > best.get(("d", s), 0):
                    best[("d", s)] = c
        for k, v in best.items():
            if k[0] == "c":
                self.seen[eng][k[1]] = v
                self.ops[k[1]][v - 1].sig = True
                out.append(("c", k[1], v))
            else:
                self.seen_d[eng][k[1]] = v
                out.append(("d", k[1], v))
        return out

    def _commit(self, ev, reads, writes):
        for r in reads:
            if r.excl:
                r.w = ev
                r.r = []
            else:
                r.r.append(ev)
        for w in writes:
            w.w = ev
            w.r = []

    def op(self, eng, fn, reads=(), writes=()):
        waits = self._filter(eng, self._deps(reads, writes))
        lst = self.ops[eng]
        o = Op(fn, waits, None, len(lst) + 1)
        lst.append(o)
        self._commit(("c", eng, o.idx), reads, writes)
        return o

    def dma(self, q, fn, reads=(), writes=()):
        s = self.dnext
        self.dnext = (self.dnext + 1) % self.NDMA
        evs = self._deps(reads, writes)
        if self.dcount[s] > 0:
            evs.append(("d", s, self.dcount[s]))
        waits = self._filter(q, evs)
        self.dcount[s] += 16
        lst = self.ops[q]
        o = Op(fn, waits, (s, self.dcount[s]), len(lst) + 1)
        lst.append(o)
        self._commit(("d", s, self.dcount[s]), reads, writes)
        return o

    def barrier(self):
        evs = [("d", s, self.dcount[s]) for s in range(self.NDMA) if self.dcount[s] > 0]
        waits = self._filter("sp", evs)
        lst = self.ops["sp"]
        o = Op(lambda e: e.nop(), waits, None, len(lst) + 1)
        lst.append(o)
        for x in ENGS:
            for s in range(self.NDMA):
                self.seen_d[x][s] = self.dcount[s]
            for y in ENGS:
                self.seen[x][y] = len(self.ops[y])

    def end_phase(self):
        self.barrier()
        self.replay()

    def replay(self):
        nc = self.nc
        sigcnt = {}
        for e in ENGS:
            c = self.base[e]
            arr = []
            for o in self.ops[e][self.start[e]:]:
                if o.sig:
                    c += 1
                arr.append(c)
            sigcnt[e] = arr

        def val(x, idx1):
            st = self.start[x]
            if idx1 - 1 < st:
                return self._hist[x][idx1 - 1]
            return sigcnt[x][idx1 - 1 - st]

        if not hasattr(self, "_hist"):
            self._hist = {e: [] for e in ENGS}

        def emit(ename, eng):
            for o in self.ops[ename][self.start[ename]:]:
                for w in o.waits:
                    if w[0] == "c":
                        eng.wait_ge(self.sem[w[1]], val(w[1], w[2]))
                    else:
                        eng.wait_ge(self.dsem[w[1]], w[2])
                ins = o.fn(eng)
                if o.dma is not None:
                    ins.then_inc(self.dsem[o.dma[0]], 16)
                elif o.sig:
                    ins.then_inc(self.sem[ename], 1)

        with nc.Block() as block:
            @block.tensor
            def _(t):
                emit("pe", t)

            @block.scalar
            def _(t):
                emit("act", t)

            @block.vector
            def _(t):
                emit("dve", t)

            @block.gpsimd
            def _(t):
                emit("pool", t)

            @block.sync
            def _(t):
                emit("sp", t)

        for e in ENGS:
            self._hist[e].extend(sigcnt[e])
            self.base[e] = sigcnt[e][-1] if sigcnt[e] else self.base[e]
            self.start[e] = len(self.ops[e])


D = 1024
DC = 8
H = 8
DFF = 2816
FC = 22
EPS = 1e-6
INW = 1952
C_CQ, C_CKV, C_KR, C_QSB, C_KSB, C_VSB = 0, 256, 384, 416, 928, 1440
MLA_SCALE = 1.0 / math.sqrt(96.0)
NEG = -30000.0


def own_blocks(j, NB):
    half = NB // 8
    return [4 * m + j for m in range(half)] + [4 * m + 3 - j for m in range(half, 2 * half)]


def build(S, Prog, Res, stop_after=None):
    NB = S // 128
    NBO = NB // 4
    NG = NBO // 4
    NCH = NB // 4
    SO = NBO * 128
    nc = bass.Bass("TRN2", target_bir_lowering=False)

    def din(name, shape, dt=F32):
        return nc.dram_tensor(name, list(shape), dt, kind="ExternalInput").ap()

    xb = din("xb", [S, D])
    xo = din("xo", [SO, D])
    posb = din("posb", [128, NB], I32)
    poso = din("poso", [128, NBO], I32)
    ident_d = din("ident", [128, 128])
    tri_d = din("tri", [128, 2, 128])
    w_in_d = din("w_in", [128, DC, INW])
    w_uq_d = din("w_uq", [128, 2 * 768])
    w_uk_d = din("w_uk", [128, 512])
    w_uv_d = din("w_uv", [128, 512])
    w_o_d = din("w_o", [128, DC, D])
    w_g_d = din("w_gate", [128, DC, DFF])
    w_u_d = din("w_up", [128, DC, DFF])
    w_d_d = din("w_down", [128, FC, D])
    g_d = {"mix": din("g_mix", [128, D]), "q": din("g_q", [128, 256]), "kv": din("g_kv", [128, 128]),
           "mla": din("g_mla", [128, 512]), "sb": din("g_sb", [128, 512]), "ffn": din("g_ffn", [128, D]),
           "fin": din("g_fin", [128, D])}
    g_o_d = din("g_o", [128, DC])
    mask_d = din("masks", [128, 16 * 128])
    y = nc.dram_tensor("y", [SO, D], F32, kind="ExternalOutput").ap()

    KTm = nc.dram_tensor("KTm", [4, 128, S], BF16).ap()
    KR = nc.dram_tensor("KR", [32, S], BF16).ap()
    Vm = nc.dram_tensor("Vm", [H, 128, NB, 65], BF16).ap()
    KTs = nc.dram_tensor("KTs", [4, 128, S], BF16).ap()
    Vs = nc.dram_tensor("Vs", [H, 128, NB, 65], BF16).ap()
    QTm = nc.dram_tensor("QTm", [H, 96, SO], BF16).ap()
    QTs = nc.dram_tensor("QTs", [4, 128, SO], BF16).ap()

    P = Prog(nc)
    inv_freq = (10000.0 ** (-np.arange(0, 32, 2, dtype=np.float32) / np.float32(32))).astype(np.float32)

    class T:
        def __init__(self, t, excl=False):
            self.t = t
            self.r = Res("r", excl)

    def sb(es, name, shape, dt):
        return T(es.enter_context(nc.sbuf_tensor(name, list(shape), dt)))

    def ps(es, name, shape, dt=F32):
        return T(es.enter_context(nc.psum_tensor(name, list(shape), dt)), excl=True)

    rrs = {"evac": 0}

    def evac_eng():
        rrs["evac"] ^= 1
        return "act" if rrs["evac"] else "dve"

    def copy_op(eng, out, in_, reads, writes, scale=None):
        if eng == "act":
            if scale is None:
                P.op("act", lambda e: e.activation(out=out, in_=in_, func=AF.Copy), reads=reads, writes=writes)
            else:
                P.op("act", lambda e: e.activation(out=out, in_=in_, func=AF.Copy, scale=scale), reads=reads, writes=writes)
        elif scale is None:
            P.op(eng, lambda e: e.tensor_copy(out=out, in_=in_), reads=reads, writes=writes)
        else:
            P.op(eng, lambda e: e.tensor_scalar(out=out, in0=in_, scalar1=scale, scalar2=None, op0=ALU.mult), reads=reads, writes=writes)

    def mm(out, lhsT, rhs, start, stop, reads, writes, skip=False):
        if skip:
            P.op("pe", lambda e: e.matmul(out, lhsT=lhsT, rhs=rhs, start=start, stop=stop, skip_group_check=True), reads=reads, writes=writes)
        else:
            P.op("pe", lambda e: e.matmul(out, lhsT=lhsT, rhs=rhs, start=start, stop=stop), reads=reads, writes=writes)

    def tr(out, in_, ident, reads, writes):
        P.op("pe", lambda e: e.transpose(out=out, in_=in_, identity=ident), reads=reads, writes=writes)

    def load(q, out, in_, writes, reads=()):
        P.dma(q, lambda e: e.dma_start(out=out, in_=in_), reads=reads, writes=writes)

    with ExitStack() as g:
        idf = sb(g, "idf", [128, 128], F32)
        idb = sb(g, "idb", [128, 128], BF16)
        zerob = sb(g, "zerob", [128, 128], BF16)
        epsc = sb(g, "epsc", [128, 1], F32)
        onec = sb(g, "onec", [128, 1], F32)
        pic = sb(g, "pic", [128, 1], F32)
        gm = {}

        def load_gains(es_, names):
            for nm in names:
                gm[nm] = sb(es_, "gs_" + nm, [128, g_d[nm].shape[1]], F32)
                load("sp", gm[nm].t[:], g_d[nm][:, :], [gm[nm].r])
        load("sp", idf.t[:], ident_d[:, :], [idf.r])
        P.op("pool", lambda e: e.tensor_copy(out=idb.t[:], in_=idf.t[:]), reads=[idf.r], writes=[idb.r])
        P.op("pool", lambda e: e.memset(zerob.t[:], 0.0), writes=[zerob.r])
        P.op("pool", lambda e: e.memset(epsc.t[:], EPS), writes=[epsc.r])
        P.op("pool", lambda e: e.memset(onec.t[:], 1.0), writes=[onec.r])
        P.op("pool", lambda e: e.memset(pic.t[:], math.pi), writes=[pic.r])

        def rstd_from_ss(ssap, rsap, rd, wr, n):
            P.op("act", lambda e: e.activation(out=rsap, in_=ssap, func=AF.Ln, scale=1.0 / n, bias=epsc.t[:]),
                 reads=[rd, epsc.r], writes=[wr])
            P.op("act", lambda e: e.activation(out=rsap, in_=rsap, func=AF.Exp, scale=-0.5), reads=[wr], writes=[wr])

        with ExitStack() as es:
            win = sb(es, "win", [128, DC, INW], BF16)
            wuq = sb(es, "wuq", [128, 2 * 768], BF16)
            wuk = sb(es, "wuk", [128, 512], BF16)
            wuv = sb(es, "wuv", [128, 512], BF16)
            cosb = sb(es, "cosb", [128, NB, 16], F32)
            sinb = sb(es, "sinb", [128, NB, 16], F32)
            coso = sb(es, "coso", [128, NBO, 16], F32)
            sino = sb(es, "sino", [128, NBO, 16], F32)
            load_gains(es, ("mix", "q", "kv"))
            es0 = es
            es = ExitStack()
            es.__enter__()
            wst = [sb(es, "wst%d" % i, [128, 2048], F32) for i in range(4)]
            ropi = sb(es, "ropi", [128, NB], I32)
            ropf = sb(es, "ropf", [128, NB], F32)
            ropt = sb(es, "ropt", [128, NB, 16], F32)
            ropk = sb(es, "ropk", [128, NB, 16], I32)
            ropg = sb(es, "ropg", [128, NB, 16], F32)
            roph = sb(es, "roph", [128, NB, 16], F32)

            def rope_table(pos_d, n, sink_sin, sink_cos):
                load("sp", ropi.t[:, 0:n], pos_d[:, :], [ropi.r])
                P.op("dve", lambda e: e.tensor_copy(out=ropf.t[:, 0:n], in_=ropi.t[:, 0:n]), reads=[ropi.r], writes=[ropf.r])
                for i in range(16):
                    P.op("dve", lambda e, i=i: e.tensor_scalar(out=ropt.t[:, 0:n, i], in0=ropf.t[:, 0:n], scalar1=float(inv_freq[i]),
                                                                scalar2=1.0 / (2 * math.pi), op0=ALU.mult, op1=ALU.mult),
                         reads=[ropf.r], writes=[ropt.r])
                for ph, sink in ((0.0, sink_sin), (0.25, sink_cos)):
                    P.op("dve", lambda e, ph=ph: e.tensor_scalar(out=ropg.t[:, 0:n, :], in0=ropt.t[:, 0:n, :], scalar1=ph, scalar2=None, op0=ALU.add),
                         reads=[ropt.r], writes=[ropg.r])
                    P.op("dve", lambda e: e.tensor_copy(out=ropk.t[:, 0:n, :], in_=ropg.t[:, 0:n, :]), reads=[ropg.r], writes=[ropk.r])
                    P.op("dve", lambda e: e.tensor_copy(out=roph.t[:, 0:n, :], in_=ropk.t[:, 0:n, :]), reads=[ropk.r], writes=[roph.r])
                    P.op("dve", lambda e: e.tensor_tensor(out=ropg.t[:, 0:n, :], in0=ropg.t[:, 0:n, :], in1=roph.t[:, 0:n, :], op=ALU.subtract),
                         reads=[ropg.r, roph.r], writes=[ropg.r])
                    P.op("dve", lambda e: e.scalar_tensor_tensor(out=roph.t[:, 0:n, :], in0=ropg.t[:, 0:n, :], scalar=0.0, in1=ropg.t[:, 0:n, :],
                                                                 op0=ALU.is_lt, op1=ALU.add), reads=[ropg.r], writes=[roph.r])
                    sink(roph, n)

            def sink_b(tile):
                def f(src, n):
                    P.op("act", lambda e: e.activation(out=tile.t[:], in_=src.t[:, 0:n, :], func=AF.Sin, scale=-2.0 * math.pi, bias=pic.t[:]),
                         reads=[src.r, pic.r], writes=[tile.r])
                return f

            rope_table(posb, NB, sink_b(sinb), sink_b(cosb))
            rope_table(poso, NBO, sink_b(sino), sink_b(coso))

            k = 0
            for c in range(DC):
                st = wst[k % 4]; k += 1
                load("sp", st.t[:, 0:INW], w_in_d[:, c, :], [st.r])
                copy_op(("act", "dve", "act", "pool")[c % 4], win.t[:, c, :], st.t[:, 0:INW], [st.r], [win.r])
            st = wst[k % 4]; k += 1
            load("sp", st.t[:, 0:1536], w_uq_d[:, :], [st.r])
            copy_op("act", wuq.t[:], st.t[:, 0:1536], [st.r], [wuq.r])
            st = wst[k % 4]; k += 1
            load("sp", st.t[:, 0:512], w_uk_d[:, :], [st.r])
            load("sp", st.t[:, 512:1024], w_uv_d[:, :], [st.r])
            copy_op("dve", wuk.t[:], st.t[:, 0:512], [st.r], [wuk.r])
            copy_op("dve", wuv.t[:], st.t[:, 512:1024], [st.r], [wuv.r])

            P.end_phase()
            es.__exit__(None, None, None)
            es = ExitStack()
            es.__enter__()
            xbuf = [sb(es, "xbuf%d" % i, [128, D], F32) for i in range(8)]
            junk = sb(es, "junk", [128, D], F32)
            ubuf = [sb(es, "ubuf%d" % i, [128, D], BF16) for i in range(3)]
            uT = [sb(es, "uT%d" % i, [128, DC, 512], BF16) for i in range(2)]
            ss4 = [sb(es, "ss4_%d" % i, [128, 4], F32) for i in range(2)]
            rs4 = [sb(es, "rs4_%d" % i, [128, 4], F32) for i in range(2)]
            ssc = [sb(es, "ssc_%d" % i, [128, 4], F32) for i in range(2)]
            rsc = [sb(es, "rsc_%d" % i, [128, 4], F32) for i in range(2)]
            st4 = [sb(es, "st4_%d" % i, [128, 4, 512], BF16) for i in range(3)]
            stv = [sb(es, "stv%d" % i, [128, H, 4, 65], BF16) for i in range(3)]
            ckr = [sb(es, "ckr%d" % i, [128, 160], BF16) for i in range(4)]
            rtmp = [sb(es, "rtmp%d" % i, [128, 2, 16], F32) for i in range(2)]
            ckvT = [sb(es, "ckvT%d" % i, [128, 512], BF16) for i in range(2)]
            krT = [sb(es, "krT%d" % i, [32, 512], BF16) for i in range(2)]
            cqn = [sb(es, "cqn%d" % i, [128, 256], BF16) for i in range(4)]
            cqT = [sb(es, "cqT%d" % i, [128, 2, 512], BF16) for i in range(2)]
            qtok = [sb(es, "qtok%d" % i, [128, H, 96], BF16) for i in range(2)]
            qrt = [sb(es, "qrt%d" % i, [128, 2, H, 16], F32) for i in range(2)]
            qmst = [sb(es, "qmst%d" % i, [128, H, 512], BF16) for i in range(2)]
            qfb = [sb(es, "qfb%d" % i, [128, 768], F32) for i in range(2)]
            pT = [ps(es, "pT%d" % i, [128, DC, 128], BF16) for i in range(2)]
            pP = [ps(es, "pP%d" % i, [128, 512], F32) for i in range(3)]
            pC = ps(es, "pC", [128, 4, 256], F32)
            pQ = ps(es, "pQ", [128, 512], F32)
            for v in stv:
                P.op("pool", lambda e, v=v: e.memset(v.t[:], 1.0), writes=[v.r])
            ctr = {"x": 0, "u": 0, "pT": 0, "pP": 0, "st4": 0, "stv": 0}

            def nxt(lst, key):
                v = lst[ctr[key] % len(lst)]
                ctr[key] += 1
                return v

            def rope_tok(x1, x2, cs, sn, o1, o2, tA, tB, rd, tmp_r, out_r):
                P.op("dve", lambda e: e.tensor_tensor(out=tA, in0=x1, in1=cs, op=ALU.mult), reads=rd, writes=[tmp_r])
                P.op("dve", lambda e: e.tensor_tensor(out=tB, in0=x2, in1=sn, op=ALU.mult), reads=rd, writes=[tmp_r])
                P.op("dve", lambda e: e.tensor_tensor(out=o1, in0=tA, in1=tB, op=ALU.subtract), reads=[tmp_r], writes=[out_r])
                P.op("dve", lambda e: e.tensor_tensor(out=tA, in0=x2, in1=cs, op=ALU.mult), reads=rd, writes=[tmp_r])
                P.op("dve", lambda e: e.tensor_tensor(out=tB, in0=x1, in1=sn, op=ALU.mult), reads=rd, writes=[tmp_r])
                P.op("dve", lambda e: e.tensor_tensor(out=o2, in0=tA, in1=tB, op=ALU.add), reads=[tmp_r], writes=[out_r])

            chunks = [("A", ci) for ci in range(NCH)] + [("B", gi) for gi in range(NG)]
            xtiles = {}

            def chunk_src(k):
                typ, i = chunks[k]
                return (xb if typ == "A" else xo), i

            def front_loads(k):
                src, i = chunk_src(k)
                xs = [nxt(xbuf, "x") for _ in range(4)]
                xtiles[k] = xs
                for t in range(4):
                    load("sp", xs[t].t[:], src[(4 * i + t) * 128:(4 * i + t + 1) * 128, :], [xs[t].r])

            def front_stats(k):
                xs = xtiles[k]
                s4 = ss4[k % 2]; r4 = rs4[k % 2]
                for t in range(4):
                    P.op("act", lambda e, t=t: e.activation(out=junk.t[:], in_=xs[t].t[:], func=AF.Square, accum_out=s4.t[:, t:t + 1]),
                         reads=[xs[t].r], writes=[junk.r, s4.r])
                rstd_from_ss(s4.t[:], r4.t[:], s4.r, r4.r, D)

            def front_norm_T(k):
                xs = xtiles[k]
                r4 = rs4[k % 2]
                uTt = uT[k % 2]
                for t in range(4):
                    ub = nxt(ubuf, "u"); pt = nxt(pT, "pT")
                    P.op("dve", lambda e, t=t, ub=ub: e.scalar_tensor_tensor(out=ub.t[:], in0=xs[t].t[:], scalar=r4.t[:, t:t + 1], in1=gm["mix"].t[:],
                                                                       op0=ALU.mult, op1=ALU.mult),
                         reads=[xs[t].r, r4.r, gm["mix"].r], writes=[ub.r])
                    for c in range(DC):
                        tr(pt.t[:, c, :], ub.t[:, c * 128:(c + 1) * 128], idb.t[:], [ub.r, idb.r], [pt.r])
                    copy_op("act", uTt.t[:, :, t * 128:(t + 1) * 128], pt.t[:], [pt.r], [uTt.r])

            def fm_proj(uTt, col0, dst4, scale=None):
                for pr in range(4):
                    p_ = nxt(pP, "pP")
                    for c in range(DC):
                        mm(p_.t[:, :], win.t[:, c, col0 + pr * 128:col0 + (pr + 1) * 128], uTt.t[:, c, :],
                           c == 0, c == DC - 1, [win.r, uTt.r], [p_.r])
                    copy_op(evac_eng(), dst4.t[:, pr, :], p_.t[:, :], [p_.r], [dst4.r], scale=scale)

            def proj_A1(k):
                ci = chunks[k][1]
                uTt = uT[k % 2]
                ks = nxt(st4, "st4")
                fm_proj(uTt, C_KSB, ks)
                load("sp", KTs[:, :, ci * 512:(ci + 1) * 512].rearrange("a p n -> p a n"), ks.t[:], [], reads=[ks.r])
                for t in range(4):
                    for c in range(DC):
                        mm(pC.t[:, t, 0:160], uTt.t[:, c, t * 128:(t + 1) * 128], win.t[:, c, C_CKV:C_CKV + 160],
                           c == 0, c == DC - 1, [win.r, uTt.r], [pC.r])
                s4 = ssc[k % 2]; r4 = rsc[k % 2]
                for t in range(4):
                    P.op("act", lambda e, t=t: e.activation(out=junk.t[:, 0:128], in_=pC.t[:, t, 0:128], func=AF.Square, accum_out=s4.t[:, t:t + 1]),
                         reads=[pC.r], writes=[junk.r, s4.r])
                rstd_from_ss(s4.t[:], r4.t[:], s4.r, r4.r, 128)

            def proj_A2(k):
                ci = chunks[k][1]
                uTt = uT[k % 2]
                r4 = rsc[k % 2]
                ckT = ckvT[k % 2]
                krt = krT[k % 2]
                cks = []
                for t in range(4):
                    kb = 4 * ci + t
                    ck = ckr[t]
                    tm = rtmp[t % 2]
                    cks.append(ck)
                    P.op("dve", lambda e, ck=ck, t=t: e.scalar_tensor_tensor(out=ck.t[:, 0:128], in0=pC.t[:, t, 0:128], scalar=r4.t[:, t:t + 1],
                                                                       in1=gm["kv"].t[:], op0=ALU.mult, op1=ALU.mult),
                         reads=[pC.r, r4.r, gm["kv"].r], writes=[ck.r])
                    rope_tok(pC.t[:, t, 128:144], pC.t[:, t, 144:160], cosb.t[:, kb, :], sinb.t[:, kb, :],
                             ck.t[:, 128:144], ck.t[:, 144:160], tm.t[:, 0, :], tm.t[:, 1, :],
                             [pC.r, cosb.r, sinb.r], tm.r, ck.r)
                vs_ = nxt(stv, "stv")
                for t in range(4):
                    p_ = nxt(pP, "pP")
                    for c in range(DC):
                        mm(p_.t[:, :], uTt.t[:, c, t * 128:(t + 1) * 128], win.t[:, c, C_VSB:C_VSB + 512],
                           c == 0, c == DC - 1, [win.r, uTt.r], [p_.r])
                    copy_op("act", vs_.t[:, :, t, 0:64], p_.t[:, :].rearrange("p (h d) -> p h d", d=64), [p_.r], [vs_.r])
                load("sp", Vs[:, :, 4 * ci:4 * ci + 4, :].rearrange("h p t e -> p h t e"), vs_.t[:], [], reads=[vs_.r])
                pt = nxt(pT, "pT")
                for t in range(4):
                    tr(pt.t[:, t, :], cks[t].t[:, 0:128], idb.t[:], [cks[t].r, idb.r], [pt.r])
                    tr(pt.t[0:32, 4 + t, :], cks[t].t[:, 128:160], idb.t[:], [cks[t].r, idb.r], [pt.r])
                copy_op("dve", ckT.t[:, :].rearrange("p (t n) -> p t n", n=128), pt.t[:, 0:4, :], [pt.r], [ckT.r])
                copy_op("dve", krt.t[:, :].rearrange("p (t n) -> p t n", n=128), pt.t[0:32, 4:8, :], [pt.r], [krt.r])
                load("sp", KR[:, ci * 512:(ci + 1) * 512], krt.t[:], [], reads=[krt.r])
                kn = nxt(st4, "st4")
                for pr in range(4):
                    p_ = nxt(pP, "pP")
                    mm(p_.t[:, :], wuk.t[:, pr * 128:(pr + 1) * 128], ckT.t[:, :], True, True, [wuk.r, ckT.r], [p_.r])
                    copy_op(evac_eng(), kn.t[:, pr, :], p_.t[:, :], [p_.r], [kn.r])
                load("sp", KTm[:, :, ci * 512:(ci + 1) * 512].rearrange("a p n -> p a n"), kn.t[:], [], reads=[kn.r])
                vm_ = nxt(stv, "stv")
                for t in range(4):
                    p_ = nxt(pP, "pP")
                    mm(p_.t[:, :], ckT.t[:, t * 128:(t + 1) * 128], wuv.t[:, :], True, True, [wuv.r, ckT.r], [p_.r])
                    copy_op(evac_eng(), vm_.t[:, :, t, 0:64], p_.t[:, :].rearrange("p (h d) -> p h d", d=64), [p_.r], [vm_.r])
                load("sp", Vm[:, :, 4 * ci:4 * ci + 4, :].rearrange("h p t e -> p h t e"), vm_.t[:], [], reads=[vm_.r])

            def proj_B1(k):
                gi = chunks[k][1]
                uTt = uT[k % 2]
                qs = nxt(st4, "st4")
                fm_proj(uTt, C_QSB, qs, scale=0.125)
                load("sp", QTs[:, :, gi * 512:(gi + 1) * 512].rearrange("a p n -> p a n"), qs.t[:], [], reads=[qs.r])
                for t in range(4):
                    for c in range(DC):
                        mm(pC.t[:, t, 0:256], uTt.t[:, c, t * 128:(t + 1) * 128], win.t[:, c, C_CQ:C_CQ + 256],
                           c == 0, c == DC - 1, [win.r, uTt.r], [pC.r])
                s4 = ssc[k % 2]; r4 = rsc[k % 2]
                for t in range(4):
                    P.op("act", lambda e, t=t: e.activation(out=junk.t[:, 0:256], in_=pC.t[:, t, 0:256], func=AF.Square, accum_out=s4.t[:, t:t + 1]),
                         reads=[pC.r], writes=[junk.r, s4.r])
                rstd_from_ss(s4.t[:], r4.t[:], s4.r, r4.r, 256)

            def proj_B2(k):
                gi = chunks[k][1]
                r4 = rsc[k % 2]
                cqt = cqT[k % 2]
                for t in range(4):
                    cq = cqn[t]
                    P.op("dve", lambda e, cq=cq, t=t: e.scalar_tensor_tensor(out=cq.t[:], in0=pC.t[:, t, 0:256], scalar=r4.t[:, t:t + 1],
                                                                       in1=gm["q"].t[:], op0=ALU.mult, op1=ALU.mult),
                         reads=[pC.r, r4.r, gm["q"].r], writes=[cq.r])
                pt = nxt(pT, "pT")
                for t in range(4):
                    for k2 in range(2):
                        tr(pt.t[:, 2 * t + k2, :], cqn[t].t[:, k2 * 128:(k2 + 1) * 128], idb.t[:], [cqn[t].r, idb.r], [pt.r])
                for k2 in range(2):
                    copy_op(evac_eng(), cqt.t[:, k2, :].rearrange("p (t n) -> p t n", n=128),
                            pt.t[:, :, :].rearrange("p (t k) n -> p t k n", k=2)[:, :, k2, :], [pt.r], [cqt.r])
                qm = qmst[k % 2]

                def qstage1(t):
                    pa = nxt(pP, "pP")
                    pb = nxt(pP, "pP")
                    qf = qfb[t % 2]
                    for k2 in range(2):
                        mm(pa.t[:, 0:512], cqt.t[:, k2, t * 128:(t + 1) * 128], wuq.t[:, k2 * 768:k2 * 768 + 512],
                           k2 == 0, k2 == 1, [wuq.r, cqt.r], [pa.r])
                    for k2 in range(2):
                        mm(pb.t[:, 0:256], cqt.t[:, k2, t * 128:(t + 1) * 128], wuq.t[:, k2 * 768 + 512:k2 * 768 + 768],
                           k2 == 0, k2 == 1, [wuq.r, cqt.r], [pb.r])
                    copy_op("act", qf.t[:, 0:512], pa.t[:, 0:512], [pa.r], [qf.r])
                    copy_op("dve", qf.t[:, 512:768], pb.t[:, 0:256], [pb.r], [qf.r])

                def qstage2(t):
                    pos = 4 * gi + t
                    qt = qtok[t % 2]
                    qr = qrt[t % 2]
                    qf = qfb[t % 2]
                    q3 = qf.t[:, 0:768].rearrange("p (h d) -> p h d", d=96)
                    copy_op("act", qt.t[:, :, 0:64], q3[:, :, 0:64], [qf.r], [qt.r])
                    rope_tok(q3[:, :, 64:80], q3[:, :, 80:96], coso.t[:, pos, :].unsqueeze(1).broadcast_to([128, H, 16]),
                             sino.t[:, pos, :].unsqueeze(1).broadcast_to([128, H, 16]),
                             qt.t[:, :, 64:80], qt.t[:, :, 80:96], qr.t[:, 0, :, :], qr.t[:, 1, :, :],
                             [qf.r, coso.r, sino.r], qr.r, qt.r)
                    pt = nxt(pT, "pT")
                    for h in range(H):
                        tr(pt.t[0:96, h, :], qt.t[:, h, :], idb.t[:], [qt.r, idb.r], [pt.r])
                    copy_op(evac_eng(), qm.t[0:96, :, t * 128:(t + 1) * 128], pt.t[0:96, :, :], [pt.r], [qm.r])

                qstage1(0)
                for t in range(4):
                    if t + 1 < 4:
                        qstage1(t + 1)
                    qstage2(t)
                load("sp", QTm[:, :, gi * 512:(gi + 1) * 512].rearrange("h r n -> r h n"), qm.t[0:96, :, :], [], reads=[qm.r])

            NK = len(chunks)
            front_loads(0)
            if NK > 1:
                front_loads(1)
            front_stats(0)
            front_norm_T(0)
            for k in range(NK):
                if k + 2 < NK:
                    front_loads(k + 2)
                if k + 1 < NK:
                    front_stats(k + 1)
                (proj_A1 if chunks[k][0] == "A" else proj_B1)(k)
                if k + 1 < NK:
                    front_norm_T(k + 1)
                (proj_A2 if chunks[k][0] == "A" else proj_B2)(k)
            P.end_phase()
            es.__exit__(None, None, None)
        if stop_after == "AB":
            return nc, dict(KTm=KTm, KR=KR, Vm=Vm, KTs=KTs, Vs=Vs, QTm=QTm, QTs=QTs)

        merged = sb(g, "merged", [128, NBO, D], F32)
        mres = [Res("m") for _ in range(NBO)]
        with ExitStack() as es:
            trib = sb(es, "trib", [128, 2, 128], BF16)
            maskb = sb(es, "maskb", [128, 16 * 128], BF16)
            mst = sb(es, "mst", [128, 2048], F32)
            mst2 = sb(es, "mst2", [128, 256], F32)
            load("sp", mst.t[:, 0:2048], mask_d[:, :], [mst.r])
            P.op("pool", lambda e: e.tensor_copy(out=maskb.t[:], in_=mst.t[:, 0:2048]), reads=[mst.r], writes=[maskb.r])
            load("sp", mst2.t[:, 0:256], tri_d.rearrange("p a b -> p (a b)"), [mst2.r])
            P.op("pool", lambda e: e.tensor_copy(out=trib.t[:].rearrange("p a b -> p (a b)"), in_=mst2.t[:, 0:256]),
                 reads=[mst2.r], writes=[trib.r])
            Kb = [sb(es, "Kb%d" % i, [128, S], BF16) for i in range(2)]
            Vb = [sb(es, "Vb%d" % i, [128, NB, 65], BF16) for i in range(2)]
            Qb = [sb(es, "Qb%d" % i, [128, SO], BF16) for i in range(2)]
            eb2 = [sb(es, "eb2_%d" % i, [128, 2, 512], F32) for i in range(4)]
            lb2 = [sb(es, "lb2_%d" % i, [128, 2, 512], BF16) for i in range(2)]
            gb2 = [sb(es, "gb2_%d" % i, [128, 2, 512], F32) for i in range(2)]
            ab2 = [sb(es, "ab2_%d" % i, [128, 2, 512], BF16) for i in range(3)]
            ab = [sb(es, "ab%d" % i, [128, 512], BF16) for i in range(2)]
            ot = [sb(es, "ot%d" % i, [128, 512], F32) for i in range(2)]
            ot2 = sb(es, "ot2", [128, 4, 128], F32)
            rinv = [sb(es, "rinv%d" % i, [128, 4, 1], F32) for i in range(2)]
            zb2 = [ps(es, "zb2_%d" % i, [128, 2, 512], F32) for i in range(2)]
            zres = [[Res("z", True), Res("z", True)] for _ in range(2)]
            cb = ps(es, "cb", [128, 512], F32)
            oacc = [ps(es, "oacc%d" % i, [128, 512], F32) for i in range(2)]
            tp = ps(es, "tp", [128, 4, 128], F32)
            mk4 = maskb.t[:].rearrange("p (a b c q) -> p a b c q", a=2, b=2, c=4)
            halfpos = NBO // 2

            jobs = [("m", h) for h in range(H)] + [("s", h) for h in range(H)]

            def job_loads(k):
                typ, h = jobs[k]
                sl = k % 2
                Kt, Vt, Qt = Kb[sl], Vb[sl], Qb[sl]
                r0 = (h % 2) * 64
                if typ == "m":
                    load("sp", Kt.t[0:64, :], KTm[h // 2, r0:r0 + 64, :], [Kt.r])
                    load("sp", Kt.t[64:96, :], KR[:, :], [Kt.r])
                    load("sp", Vt.t[:], Vm[h], [Vt.r])
                    load("sp", Qt.t[0:96, :], QTm[h], [Qt.r])
                else:
                    load("sp", Kt.t[0:64, :], KTs[h // 2, r0:r0 + 64, :], [Kt.r])
                    load("sp", Kt.t[64:128, :], KTs[h // 2, r0:r0 + 64, :], [Kt.r])
                    load("sp", Vt.t[:], Vs[h], [Vt.r])
                    load("sp", Qt.t[0:64, :], QTs[h // 2, r0:r0 + 64, :], [Qt.r])
                    load("sp", Qt.t[64:128, :], QTs[h // 2, r0:r0 + 64, :], [Qt.r])

            gctr = [0]

            def tiles_for(gi):
                out = []
                for kb in range(16 * gi + 15, -1, -1):
                    pm = kb // 4
                    if pm >= 4 * gi:
                        lo = pm - 4 * gi
                        out.append((kb, lo * 128, True, 0 if pm < halfpos else 1, kb % 4))
                    else:
                        out.append((kb, 0, False, 0, 0))
                return out

            def finish_group(typ, h, gi, oa):
                o_ = ot[gctr[0] % 2]
                rv = rinv[gctr[0] % 2]
                nr = 65 if typ == "m" else 64
                copy_op("dve", o_.t[0:nr, :], oa.t[0:nr, :], [oa.r], [o_.r])
                for t in range(4):
                    tr(tp.t[:, t, 0:nr], o_.t[0:nr, t * 128:(t + 1) * 128], idf.t[0:nr, 0:nr], [o_.r, idf.r], [tp.r])
                if typ == "m":
                    P.op("dve", lambda e: e.reciprocal(out=rv.t[:], in_=tp.t[:, :, 64:65]), reads=[tp.r], writes=[rv.r])
                    for t in range(4):
                        P.op("dve", lambda e, t=t: e.tensor_scalar(out=merged.t[:, 4 * gi + t, h * 64:(h + 1) * 64], in0=tp.t[:, t, 0:64],
                                                                   scalar1=rv.t[:, t, :], scalar2=None, op0=ALU.mult),
                             reads=[tp.r, rv.r], writes=[mres[4 * gi + t]])
                else:
                    P.op("dve", lambda e: e.tensor_copy(out=merged.t[:, 4 * gi:4 * gi + 4, 512 + h * 64:512 + (h + 1) * 64], in_=tp.t[:, :, 0:64]),
                         reads=[tp.r], writes=[mres[4 * gi + t_] for t_ in range(4)])

            def zero_bank(bank, Qt):
                mm(bank.t[:, :], zerob.t[:, :], maskb.t[:, 0:512], True, True, [zerob.r, maskb.r], [bank.r])

            def job_items(ks):
                items = []
                g = gctr[0]
                for k in ks:
                    typ, h = jobs[k]
                    sl = k % 2
                    for gi in range(NG):
                        tl = tiles_for(gi)
                        npair = len(tl) // 2
                        for p in range(npair):
                            ta, tb_ = tl[2 * p], tl[2 * p + 1]
                            assert ta[1] == tb_[1] and ta[2] == tb_[2] and ta[3] == tb_[3]
                            items.append(dict(k=k, h=h, gi=gi, ta=ta, tb=tb_, first=(p == 0), last=(p == npair - 1), oa=oacc[g % 2],
                                              jobfirst=(gi == 0 and p == 0), Kt=Kb[sl], Vt=Vb[sl], Qt=Qb[sl]))
                        g += 1
                return items

            def run_mla(ks):
                items = job_items(ks)
                n = len(items)

                def S2(i):
                    it = items[i]; gi = it["gi"]; Kt = it["Kt"]; Qt = it["Qt"]
                    (kba, c0, msk, hf, ra), (kbb, _, _, _, rb) = it["ta"], it["tb"]
                    zt = zb2[i % 2].t; zr = zres[i % 2]
                    q0 = gi * 512 + c0
                    for j, kb_, r_ in ((0, kba, ra), (1, kbb, rb)):
                        mm(zt[:, j, c0:512], Kt.t[0:96, kb_ * 128:(kb_ + 1) * 128], Qt.t[0:96, q0:(gi + 1) * 512],
                           True, not msk, [Kt.r, Qt.r], [zr[j]])
                        if msk:
                            mm(zt[:, j, c0:c0 + 128], idb.t[:, :], mk4[:, 0, hf, r_, :], False, True, [idb.r, maskb.r], [zr[j]])

                def P2a(i):
                    it = items[i]
                    c0 = it["ta"][1]
                    zt = zb2[i % 2].t; zr = zres[i % 2]
                    a_ = ab2[i % 3]
                    o_ap = a_.t[:, :, c0:512]; i_ap = zt[:, :, c0:512]
                    P.op("act", lambda e, o_ap=o_ap, i_ap=i_ap: e.activation(out=o_ap, in_=i_ap, func=AF.Exp, scale=MLA_SCALE),
                         reads=[zr[0], zr[1]], writes=[a_.r])

                def P2b(i):
                    it = items[i]; oa = it["oa"]; Vt = it["Vt"]
                    kba, c0 = it["ta"][0], it["ta"][1]
                    kbb = it["tb"][0]
                    a_ = ab2[i % 3]
                    if it["first"]:
                        zero_bank(oa, None)
                    mm(oa.t[0:65, c0:512], Vt.t[:, kba, 0:65], a_.t[:, 0, c0:512], False, False, [Vt.r, a_.r], [oa.r], skip=True)
                    mm(oa.t[0:65, c0:512], Vt.t[:, kbb, 0:65], a_.t[:, 1, c0:512], False, False, [Vt.r, a_.r], [oa.r], skip=True)
                    if it["jobfirst"] and it["k"] + 1 < len(jobs):
                        job_loads(it["k"] + 1)

                S2(0)
                if n > 1:
                    S2(1)
                for i in range(n):
                    P2a(i)
                    if i + 2 < n:
                        S2(i + 2)
                    P2b(i)
                    if items[i]["last"]:
                        finish_group("m", items[i]["h"], items[i]["gi"], items[i]["oa"])
                        gctr[0] += 1

            def finish_sb(h, gi, oa):
                o_ = ot[gctr[0] % 2]
                copy_op("dve", o_.t[:, :], oa.t[:, :], [oa.r], [o_.r])
                for t in range(4):
                    tr(tp.t[:, t, :], o_.t[:, t * 128:(t + 1) * 128], idf.t[:, :], [o_.r, idf.r], [tp.r])
                copy_op("dve", ot2.t[:], tp.t[:], [tp.r], [ot2.r])
                P.op("dve", lambda e: e.tensor_tensor(out=merged.t[:, 4 * gi:4 * gi + 4, 512 + h * 64:512 + (h + 1) * 64],
                                                      in0=ot2.t[:, :, 0:64], in1=ot2.t[:, :, 64:128], op=ALU.add),
                     reads=[ot2.r], writes=[mres[4 * gi + t_] for t_ in range(4)])

            def run_sb(ks):
                items = job_items(ks)
                n = len(items)

                def Zp(i):
                    it = items[i]; gi = it["gi"]; Kt = it["Kt"]; Qt = it["Qt"]
                    (kba, c0, msk, hf, ra), (kbb, _, _, _, rb) = it["ta"], it["tb"]
                    zt = zb2[i % 2].t; zr = zres[i % 2]
                    q0 = gi * 512 + c0
                    mm(zt[:, 0, c0:512], Kt.t[0:64, kba * 128:(kba + 1) * 128], Qt.t[0:64, q0:(gi + 1) * 512],
                       True, not msk, [Kt.r, Qt.r], [zr[0]])
                    mm(zt[:, 1, c0:512], Kt.t[64:128, kbb * 128:(kbb + 1) * 128], Qt.t[64:128, q0:(gi + 1) * 512],
                       True, not msk, [Kt.r, Qt.r], [zr[1]])
                    if msk:
                        mm(zt[:, 0, c0:c0 + 128], idb.t[:, :], mk4[:, 1, hf, ra, :], False, True, [idb.r, maskb.r], [zr[0]])
                        mm(zt[:, 1, c0:c0 + 128], idb.t[:, :], mk4[:, 1, hf, rb, :], False, True, [idb.r, maskb.r], [zr[1]])

                def Ep(i):
                    c0 = items[i]["ta"][1]
                    zt = zb2[i % 2].t; zr = zres[i % 2]; e_ = eb2[i % 4]
                    o_ap = e_.t[:, :, c0:512]; i_ap = zt[:, :, c0:512]
                    P.op("act", lambda e, o_ap=o_ap, i_ap=i_ap: e.activation(out=o_ap, in_=i_ap, func=AF.Exp),
                         reads=[zr[0], zr[1]], writes=[e_.r])

                def Lp(i):
                    c0 = items[i]["ta"][1]
                    e_ = eb2[i % 4]; l_ = lb2[i % 2]
                    o_ap = l_.t[:, :, c0:512]; i_ap = e_.t[:, :, c0:512]
                    P.op("act", lambda e, o_ap=o_ap, i_ap=i_ap: e.activation(out=o_ap, in_=i_ap, func=AF.Ln, bias=onec.t[:]),
                         reads=[e_.r, onec.r], writes=[l_.r])

                def Tri(i, j):
                    c0 = items[i]["ta"][1]
                    l_ = lb2[i % 2]
                    mm(cb.t[:, c0:512], trib.t[:, 0, :], l_.t[:, j, c0:512], False, False, [trib.r, l_.r], [cb.r], skip=True)

                def G(i, j):
                    c0 = items[i]["ta"][1]
                    g_ = gb2[i % 2]
                    o_ap = g_.t[:, j, c0:512]; i_ap = cb.t[:, c0:512]
                    P.op("act", lambda e, o_ap=o_ap, i_ap=i_ap: e.activation(out=o_ap, in_=i_ap, func=AF.Exp, scale=-1.0),
                         reads=[cb.r], writes=[g_.r])

                def OmT(i, j):
                    c0 = items[i]["ta"][1]
                    l_ = lb2[i % 2]
                    mm(cb.t[:, c0:512], trib.t[:, 1, :], l_.t[:, j, c0:512], False, False, [trib.r, l_.r], [cb.r], skip=True)

                def Amul(i):
                    c0 = items[i]["ta"][1]
                    g_ = gb2[i % 2]; a_ = ab2[i % 2]; e_ = eb2[i % 4]
                    o_ap = a_.t[:, :, c0:512]; x_ap = e_.t[:, :, c0:512]; y_ap = g_.t[:, :, c0:512]
                    P.op("dve", lambda e, o_ap=o_ap, x_ap=x_ap, y_ap=y_ap: e.tensor_tensor(out=o_ap, in0=x_ap, in1=y_ap, op=ALU.mult),
                         reads=[e_.r, g_.r], writes=[a_.r])

                def AVp(i):
                    it = items[i]; oa = it["oa"]; Vt = it["Vt"]
                    kba, c0 = it["ta"][0], it["ta"][1]
                    kbb = it["tb"][0]
                    a_ = ab2[i % 2]
                    mm(oa.t[0:64, c0:512], Vt.t[:, kba, 0:64], a_.t[:, 0, c0:512], False, False, [Vt.r, a_.r], [oa.r], skip=True)
                    o_ap = oa.t[64:128, c0:512]; l_ap = Vt.t[:, kbb, 0:64]; r_ap = a_.t[:, 1, c0:512]
                    P.op("pe", lambda e, o_ap=o_ap, l_ap=l_ap, r_ap=r_ap: e.matmul(o_ap, lhsT=l_ap, rhs=r_ap, start=False, stop=False,
                                                                                 skip_group_check=True, tile_position=(0, 64)),
                         reads=[Vt.r, a_.r], writes=[oa.r])
                    if it["jobfirst"] and it["k"] + 1 < len(jobs):
                        job_loads(it["k"] + 1)

                Zp(0)
                if n > 1:
                    Zp(1)
                Ep(0)
                if n > 2:
                    Zp(2)
                if n > 1:
                    Ep(1)
                Lp(0)
                for st_ in range(n + 1):
                    if 0 <= st_ - 1 < n:
                        OmT(st_ - 1, 1)
                    if st_ < n:
                        if items[st_]["first"]:
                            zero_bank(cb, None)
                        Tri(st_, 0)
                        G(st_, 0)
                    if st_ + 3 < n:
                        Zp(st_ + 3)
                    if st_ + 1 < n:
                        Lp(st_ + 1)
                    if 0 <= st_ - 1 < n:
                        Amul(st_ - 1)
                    if st_ < n:
                        OmT(st_, 0)
                        Tri(st_, 1)
                        G(st_, 1)
                    if 0 <= st_ - 1 < n:
                        it = items[st_ - 1]
                        if it["first"]:
                            zero_bank(it["oa"], None)
                        AVp(st_ - 1)
                        if it["last"]:
                            finish_sb(it["h"], it["gi"], it["oa"])
                            gctr[0] += 1
                    if st_ + 2 < n:
                        Ep(st_ + 2)

            job_loads(0)
            run_mla([k for k in range(len(jobs)) if jobs[k][0] == "m"])
            run_sb([k for k in range(len(jobs)) if jobs[k][0] == "s"])
            P.end_phase()
        if stop_after == "C":
            return nc, dict(merged=merged)

        with ExitStack() as es:
            fT = sb(es, "fT", [128, DC, SO], BF16)
            load_gains(es, ("ffn", "fin"))
            g_o = sb(es, "g_o_sb", [128, DC], F32)
            load("sp", g_o.t[:], g_o_d[:, :], [g_o.r])
            sst = sb(es, "sst", [128, NBO, 2], F32)
            rst = sb(es, "rst", [128, NBO, 2], F32)
            junk = sb(es, "junkd", [128, D], F32)
            wg = [sb(es, "wg0", [128, DC, 512], BF16), None]
            wu = [sb(es, "wu0", [128, DC, 512], BF16), None]
            wd = [sb(es, "wd0", [128, 4, D], BF16), None]
            wstf = [sb(es, "wstf%d" % i, [128, 1024], F32) for i in range(2)]
            NFG = (FC + 3) // 4
            wctr = [0]

            def stage_cast(dst_ap, src_ap, n, dst_r):
                i = wctr[0]; wctr[0] += 1
                st = wstf[i % 2]
                load("sp", st.t[:, 0:n], src_ap, [st.r])
                copy_op("pool", dst_ap, st.t[:, 0:n], [st.r], [dst_r])

            def ffn_load_list(fg):
                f0 = fg * 512
                nf = min(512, DFF - f0)
                sl = fg % 2
                lst = []
                for c in range(DC):
                    lst.append((wg[sl].t[:, c, 0:nf], w_g_d[:, c, f0:f0 + nf], nf, wg[sl].r))
                    lst.append((wu[sl].t[:, c, 0:nf], w_u_d[:, c, f0:f0 + nf], nf, wu[sl].r))
                for fc in range(nf // 128):
                    lst.append((wd[sl].t[:, fc, :], w_d_d[:, fg * 4 + fc, :], D, wd[sl].r))
                return lst

            def ffn_loads(fg):
                for a_ in ffn_load_list(fg):
                    stage_cast(*a_)

            with ExitStack() as e1:
                wo = sb(e1, "wo", [128, DC, D], BF16)
                mn = [sb(e1, "mn%d" % i, [128, D], BF16) for i in range(3)]
                mT = [sb(e1, "mT%d" % i, [128, DC, 128], BF16) for i in range(3)]
                xs2 = [sb(e1, "xs2_%d" % i, [128, D], F32) for i in range(3)]
                pT = [ps(e1, "pTd%d" % i, [128, DC, 128], BF16) for i in range(3)]
                pO = [ps(e1, "pO%d" % i, [128, 1024], F32) for i in range(2)]
                for c in range(DC):
                    st = wstf[c % 2]
                    load("sp", st.t[:, 0:D], w_o_d[:, c, :], [st.r])
                    P.op("act", lambda e, st=st, c=c: e.activation(out=wo.t[:, c, :], in_=st.t[:, 0:D], func=AF.Copy, scale=g_o.t[:, c:c + 1]),
                         reads=[st.r, g_o.r], writes=[wo.r])
                for t in range(NBO):
                    for hf in range(2):
                        P.op("act", lambda e, t=t, hf=hf: e.activation(out=junk.t[:, 0:512], in_=merged.t[:, t, hf * 512:(hf + 1) * 512],
                                                                       func=AF.Square, accum_out=sst.t[:, t, hf:hf + 1]),
                             reads=[mres[t]], writes=[junk.r, sst.r])
                rstd_from_ss(sst.t[:], rst.t[:], sst.r, rst.r, 512)
                pre0 = ffn_load_list(0)

                def prep(t):
                    m_ = mn[t % 3]; mt = mT[t % 3]; pt = pT[t % 3]; xs = xs2[t % 3]
                    load("sp", xs.t[:], xo[t * 128:(t + 1) * 128, :], [xs.r])
                    for hf in range(2):
                        P.op("act", lambda e, t=t, hf=hf, m_=m_: e.activation(
                            out=m_.t[:, hf * 512:(hf + 1) * 512], in_=merged.t[:, t, hf * 512:(hf + 1) * 512],
                            func=AF.Copy, scale=rst.t[:, t, hf:hf + 1]),
                            reads=[mres[t], rst.r], writes=[m_.r])
                    for c in range(DC):
                        tr(pt.t[:, c, :], m_.t[:, c * 128:(c + 1) * 128], idb.t[:], [m_.r, idb.r], [pt.r])
                    copy_op("act", mt.t[:], pt.t[:], [pt.r], [mt.r])

                def fin(t):
                    mt = mT[t % 3]; po = pO[t % 2]; xs = xs2[t % 3]
                    for nh in range(2):
                        for c in range(DC):
                            mm(po.t[:, nh * 512:(nh + 1) * 512], mt.t[:, c, :], wo.t[:, c, nh * 512:(nh + 1) * 512],
                               c == 0, c == DC - 1, [mt.r, wo.r], [po.r])
                    P.op("dve", lambda e, t=t, po=po, xs=xs: e.tensor_tensor(out=merged.t[:, t, :], in0=po.t[:, :], in1=xs.t[:], op=ALU.add),
                         reads=[po.r, xs.r], writes=[mres[t]])

                def fin_sq(t):
                    P.op("act", lambda e, t=t: e.activation(out=junk.t[:], in_=merged.t[:, t, :], func=AF.Square, accum_out=sst.t[:, t, 0:1]),
                         reads=[mres[t]], writes=[junk.r, sst.r])

                prep(0)
                if NBO > 1:
                    prep(1)
                for t in range(NBO):
                    if t + 2 < NBO:
                        prep(t + 2)
                    for _ in range(2):
                        if pre0:
                            stage_cast(*pre0.pop(0))
                    fin(t)
                    if t >= 1:
                        fin_sq(t - 1)
                fin_sq(NBO - 1)
                while pre0:
                    stage_cast(*pre0.pop(0))
                rstd_from_ss(sst.t[:], rst.t[:], sst.r, rst.r, D)
                for t in range(NBO):
                    m_ = mn[t % 2]; pt = pT[t % 2]
                    P.op("dve", lambda e, t=t, m_=m_: e.scalar_tensor_tensor(out=m_.t[:], in0=merged.t[:, t, :], scalar=rst.t[:, t, 0:1],
                                                                       in1=gm["ffn"].t[:], op0=ALU.mult, op1=ALU.mult),
                         reads=[mres[t], rst.r, gm["ffn"].r], writes=[m_.r])
                    for c in range(DC):
                        tr(pt.t[:, c, :], m_.t[:, c * 128:(c + 1) * 128], idb.t[:], [m_.r, idb.r], [pt.r])
                    copy_op("act" if t % 2 == 0 else "dve", fT.t[:, :, t * 128:(t + 1) * 128], pt.t[:], [pt.r], [fT.r])
                P.end_phase()

            with ExitStack() as e2:
                wg[1] = sb(e2, "wg1", [128, DC, 512], BF16)
                wu[1] = sb(e2, "wu1", [128, DC, 512], BF16)
                wd[1] = sb(e2, "wd1", [128, 4, D], BF16)
                aT = [sb(e2, "aT%d" % i, [128, 4, 512], BF16) for i in range(2)]
                sg = [sb(e2, "sg%d" % i, [128, 512], F32) for i in range(2)]
                yt = [sb(e2, "yt%d" % i, [128, D], F32) for i in range(3)]
                pg = [ps(e2, "pg%d" % i, [128, 512], F32) for i in range(2)]
                pu = [ps(e2, "pu%d" % i, [128, 512], F32) for i in range(2)]
                pd = [ps(e2, "pd%d" % i, [128, 1024], F32) for i in range(2)]
                k = 0
                pend = None

                def down(k_, sl_, nfc_, tg_):
                    a_ = aT[k_ % 2]
                    for tt in range(4):
                        t = tg_ * 4 + tt
                        pd_ = pd[(k_ * 4 + tt) % 2]
                        for nh in range(2):
                            for fc in range(nfc_):
                                mm(pd_.t[:, nh * 512:(nh + 1) * 512], a_.t[:, fc, tt * 128:(tt + 1) * 128], wd[sl_].t[:, fc, nh * 512:(nh + 1) * 512],
                                   fc == 0, fc == nfc_ - 1, [a_.r, wd[sl_].r], [pd_.r])
                        P.op("dve", lambda e, t=t, pd_=pd_: e.tensor_tensor(out=merged.t[:, t, :], in0=pd_.t[:, :], in1=merged.t[:, t, :], op=ALU.add),
                             reads=[pd_.r, mres[t]], writes=[mres[t]])

                if NFG > 1:
                    ffn_loads(1)
                for fg in range(NFG):
                    f0 = fg * 512
                    nfc = min(512, DFF - f0) // 128
                    sl = fg % 2
                    for tg in range(NG):
                        a_ = aT[k % 2]
                        for fc in range(nfc):
                            pg_ = pg[(k * 4 + fc) % 2]; pu_ = pu[(k * 4 + fc) % 2]; sg_ = sg[(k * 4 + fc) % 2]
                            for c in range(DC):
                                mm(pg_.t[:, :], wg[sl].t[:, c, fc * 128:(fc + 1) * 128], fT.t[:, c, tg * 512:(tg + 1) * 512],
                                   c == 0, c == DC - 1, [wg[sl].r, fT.r], [pg_.r])
                            for c in range(DC):
                                mm(pu_.t[:, :], wu[sl].t[:, c, fc * 128:(fc + 1) * 128], fT.t[:, c, tg * 512:(tg + 1) * 512],
                                   c == 0, c == DC - 1, [wu[sl].r, fT.r], [pu_.r])
                            P.op("act", lambda e, pg_=pg_, sg_=sg_: e.activation(out=sg_.t[:], in_=pg_.t[:, :], func=AF.Silu),
                                 reads=[pg_.r], writes=[sg_.r])
                            P.op("dve", lambda e, pu_=pu_, sg_=sg_, a_=a_, fc=fc: e.tensor_tensor(out=a_.t[:, fc, :], in0=pu_.t[:, :], in1=sg_.t[:],
                                                                                             op=ALU.mult),
                                 reads=[pu_.r, sg_.r], writes=[a_.r])
                        if pend is not None:
                            down(*pend)
                        pend = (k, sl, nfc, tg)
                        k += 1
                        if tg == 0 and fg >= 1 and fg + 1 < NFG:
                            ffn_loads(fg + 1)
                down(*pend)
                for t in range(NBO):
                    P.op("act", lambda e, t=t: e.activation(out=junk.t[:], in_=merged.t[:, t, :], func=AF.Square, accum_out=sst.t[:, t, 0:1]),
                         reads=[mres[t]], writes=[junk.r, sst.r])
                rstd_from_ss(sst.t[:], rst.t[:], sst.r, rst.r, D)
                for t in range(NBO):
                    y_ = yt[t % 3]
                    P.op("dve", lambda e, t=t, y_=y_: e.scalar_tensor_tensor(out=y_.t[:], in0=merged.t[:, t, :], scalar=rst.t[:, t, 0:1],
                                                                       in1=gm["fin"].t[:], op0=ALU.mult, op1=ALU.mult),
                         reads=[mres[t], rst.r, gm["fin"].r], writes=[y_.r])
                    load("sp", y[t * 128:(t + 1) * 128, :], y_.t[:], [], reads=[y_.r])
                P.end_phase()
    return nc, {}


def host_inputs(inputs, S):
    NB = S // 128
    f = np.float32
    x = np.asarray(inputs["x"], f)
    pos = np.asarray(inputs["positions"], np.int32)

    def kchunk(w):
        K, E = w.shape
        return np.ascontiguousarray(w.reshape(K // 128, 128, E).transpose(1, 0, 2))

    def rep(v):
        return np.ascontiguousarray(np.broadcast_to(np.asarray(v, f).reshape(1, -1), (128, v.size)))

    w_in = kchunk(np.asarray(inputs["w_in"], f)[0])
    w_uq = kchunk(np.asarray(inputs["w_uq"], f)[0]).reshape(128, 2 * 768)
    wukv = np.asarray(inputs["w_ukv"], f)[0].reshape(128, H, 128)
    w_uk = np.ascontiguousarray(wukv[:, :, 0:64].reshape(128, 512))
    w_uv = np.ascontiguousarray(wukv[:, :, 64:128].reshape(128, 512))
    w_o = kchunk(np.asarray(inputs["w_o"], f)[0])
    w_g = kchunk(np.asarray(inputs["w_gate"], f)[0])
    w_u = kchunk(np.asarray(inputs["w_up"], f)[0])
    w_d = kchunk(np.asarray(inputs["w_down"], f)[0])
    ident = np.eye(128, dtype=f)
    jj = np.arange(128)[:, None]
    ss_ = np.arange(128)[None, :]
    tri = np.stack([(jj >= ss_).astype(f), (jj < ss_).astype(f)], axis=1)
    causal = (jj <= ss_).astype(f)
    strict = (jj < ss_).astype(f)
    common = dict(ident=ident, tri=np.ascontiguousarray(tri), w_in=w_in, w_uq=w_uq, w_uk=w_uk, w_uv=w_uv, w_o=w_o,
                  w_gate=w_g, w_up=w_u, w_down=w_d,
                  g_mix=rep(inputs["norm_mix"][0]), g_q=rep(inputs["q_latent_norm"][0]), g_kv=rep(inputs["kv_latent_norm"][0]),
                  g_mla=rep(inputs["out_norm_mla"][0]), g_sb=rep(inputs["out_norm_sb"][0]), g_ffn=rep(inputs["norm_ffn"][0]),
                  g_fin=rep(inputs["norm_final"]),
                  g_o=np.ascontiguousarray(np.concatenate([np.asarray(inputs["out_norm_mla"], f)[0],
                                                           np.asarray(inputs["out_norm_sb"], f)[0]]).reshape(DC, 128).T))
    maps = []
    for c in range(8):
        b, j = c // 4, c % 4
        ob = own_blocks(j, NB)
        rows = np.concatenate([np.arange(k * 128, (k + 1) * 128) for k in ob])
        masks = np.zeros((128, 2, 2, 4, 128), f)
        for ti, tm in enumerate((causal, strict)):
            for hf, off in enumerate((j, 3 - j)):
                for r in range(4):
                    if r < off:
                        masks[:, ti, hf, r, :] = 0.0
                    elif r == off:
                        masks[:, ti, hf, r, :] = NEG * (1.0 - tm)
                    else:
                        masks[:, ti, hf, r, :] = NEG
        m = dict(common)
        m["xb"] = np.ascontiguousarray(x[b])
        m["xo"] = np.ascontiguousarray(x[b][rows])
        m["posb"] = np.ascontiguousarray(pos[b].reshape(NB, 128).T)
        m["poso"] = np.ascontiguousarray(pos[b][rows].reshape(len(ob), 128).T)
        m["masks"] = masks.reshape(128, 16 * 128)
        maps.append(m)
    return maps


def assemble(results, S, B=2):
    NB = S // 128
    out = np.zeros((B, S, D), np.float32)
    for c in range(8):
        b, j = c // 4, c % 4
        ob = own_blocks(j, NB)
        yy = np.asarray(results[c]["y"], np.float32)
        for i, k in enumerate(ob):
            out[b, k * 128:(k + 1) * 128, :] = yy[i * 128:(i + 1) * 128, :]
    return out


_NC_CACHE = {}


def kernel(**inputs):
    S = int(np.asarray(inputs["x"]).shape[1])
    if S not in _NC_CACHE:
        _NC_CACHE[S] = build(S, Prog, Res)[0]
    nc = _NC_CACHE[S]
    maps = host_inputs(inputs, S)
    res = run_bass_kernel_spmd(nc, maps, core_ids=list(range(8)))
    return assemble(res.results, S, B=int(np.asarray(inputs["x"]).shape[0]))
```
